# Optimizing a Trainium2 kernel written in Bass

```python
import math
import jax, jax.numpy as jnp
from jax import lax
import numpy as np

D_MODEL = 1024
BATCH = 16
SEQ = 2048
DEPTH = 2

LRU_WIDTH = D_MODEL
LRU_BLOCKS = 8
LRU_BLOCK = LRU_WIDTH // LRU_BLOCKS
CONV_WIDTH = 4
LRU_C = 8.0
HEAD_DIM = 64
N_Q_HEADS = D_MODEL // HEAD_DIM
N_KV_HEADS = 2
GQA_GROUP = N_Q_HEADS // N_KV_HEADS
ATTN_WIDTH = N_Q_HEADS * HEAD_DIM
KV_WIDTH = N_KV_HEADS * HEAD_DIM
WINDOW = 128
REL_BUCKETS = 32
REL_MAX_DIST = 128
CHUNK = 128
GMLP_WIDTH = 2 * D_MODEL
GMLP_GROUPS = 8
GMLP_GROUP_DIM = GMLP_WIDTH // GMLP_GROUPS

EPS = 1e-6
N_EVEN = (DEPTH + 1) // 2
N_ODD = DEPTH // 2

IN_A_SPLITS = (LRU_WIDTH, LRU_WIDTH, ATTN_WIDTH, KV_WIDTH, KV_WIDTH, ATTN_WIDTH)
IN_A_WIDTH = sum(IN_A_SPLITS)
OUT_A_WIDTH = LRU_WIDTH + ATTN_WIDTH
IN_C_WIDTH = 3 * GMLP_WIDTH

kernel_name = "hybrid_rglru_swa_sink_chunked_gmlp"


def rms_norm(x, g):
    xf = x.astype(jnp.float32)
    y = xf * lax.rsqrt(jnp.mean(xf * xf, axis=-1, keepdims=True) + EPS)
    return (y * g.astype(jnp.float32)).astype(x.dtype)


def layer_norm(x, g, b):
    xf = x.astype(jnp.float32)
    mu = jnp.mean(xf, axis=-1, keepdims=True)
    xc = xf - mu
    y = xc * lax.rsqrt(jnp.mean(xc * xc, axis=-1, keepdims=True) + EPS)
    return (y * g.astype(jnp.float32) + b.astype(jnp.float32)).astype(x.dtype)


def split_cols(z, sizes):
    idx = list(np.cumsum(sizes)[:-1])
    return jnp.split(z, idx, axis=-1)


def causal_depthwise_conv(x, w, b):
    c = x.shape[-1]
    y = lax.conv_general_dilated(
        x, w[:, None, :].astype(x.dtype), window_strides=(1,),
        padding=[(CONV_WIDTH - 1, 0)],
        dimension_numbers=("NWC", "WIO", "NWC"), feature_group_count=c)
    return y + b


def block_diag_linear(x, w, b):
    bsz, s, _ = x.shape
    xb = x.reshape(bsz, s, LRU_BLOCKS, LRU_BLOCK)
    y = jnp.einsum("bsnk,nkj->bsnj", xb, w)
    return y.reshape(bsz, s, LRU_WIDTH) + b


def rg_lru(x, gate_a_w, gate_a_b, gate_x_w, gate_x_b, lam):
    r = jax.nn.sigmoid(block_diag_linear(x, gate_a_w, gate_a_b).astype(jnp.float32))
    i = jax.nn.sigmoid(block_diag_linear(x, gate_x_w, gate_x_b).astype(jnp.float32))
    log_a = -LRU_C * r * jax.nn.softplus(-lam.astype(jnp.float32))
    a = jnp.exp(log_a)
    mult = jnp.sqrt(-jnp.expm1(2.0 * log_a))
    bterm = mult * i * x.astype(jnp.float32)

    def combine(e1, e2):
        a1, b1 = e1
        a2, b2 = e2
        return a1 * a2, a2 * b1 + b2

    _, h = lax.associative_scan(combine, (a, bterm), axis=1)
    return h.astype(x.dtype)


def t5_causal_bucket(dist):
    max_exact = REL_BUCKETS // 2
    is_small = dist < max_exact
    df = jnp.maximum(dist, 1).astype(jnp.float32)
    large = max_exact + (jnp.log(df / max_exact) / math.log(REL_MAX_DIST / max_exact)
                         * (REL_BUCKETS - max_exact)).astype(jnp.int32)
    large = jnp.minimum(large, REL_BUCKETS - 1)
    return jnp.where(is_small, dist, large)


def sliding_window_gqa(q, k, v, q_g, k_g, sinks, rel_bias):
    bsz, s, _, _ = q.shape
    nblk = s // WINDOW
    q = rms_norm(q, q_g)
    k = rms_norm(k, k_g)
    scale = HEAD_DIM ** -0.5
    qb = q.reshape(bsz, nblk, WINDOW, N_KV_HEADS, GQA_GROUP, HEAD_DIM)
    kb = k.reshape(bsz, nblk, WINDOW, N_KV_HEADS, HEAD_DIM)
    vb = v.reshape(bsz, nblk, WINDOW, N_KV_HEADS, HEAD_DIM)
    shift = lambda t: jnp.concatenate([jnp.zeros_like(t[:, :1]), t[:, :-1]], axis=1)
    kw = jnp.concatenate([shift(kb), kb], axis=2)
    vw = jnp.concatenate([shift(vb), vb], axis=2)

    qi = jnp.arange(WINDOW)[:, None]
    kj = jnp.arange(2 * WINDOW)[None, :]
    dist = WINDOW + qi - kj
    in_window = (dist >= 0) & (dist < WINDOW)
    bucket = t5_causal_bucket(jnp.maximum(dist, 0))
    bias = rel_bias.astype(jnp.float32)[bucket]
    bias = jnp.transpose(bias, (2, 0, 1)).reshape(N_KV_HEADS, GQA_GROUP, WINDOW, 2 * WINDOW)
    sink = sinks.astype(jnp.float32).reshape(N_KV_HEADS, GQA_GROUP)[None, :, :, None, None]

    def one_block(args):
        qblk, kblk, vblk, blk = args
        sc = jnp.einsum("bqkgd,bskd->bkgqs", qblk, kblk).astype(jnp.float32) * scale + bias
        key_ok = (blk * WINDOW - WINDOW + jnp.arange(2 * WINDOW)) >= 0
        mask = in_window & key_ok[None, :]
        sc = jnp.where(mask, sc, -jnp.inf)
        m = jnp.maximum(jnp.max(sc, axis=-1, keepdims=True), sink)
        p = jnp.exp(sc - m)
        denom = jnp.sum(p, axis=-1, keepdims=True) + jnp.exp(sink - m)
        out = jnp.einsum("bkgqs,bskd->bqkgd", (p / denom).astype(vblk.dtype), vblk)
        return out.reshape(bsz, WINDOW, ATTN_WIDTH)

    xs = (jnp.moveaxis(qb, 1, 0), jnp.moveaxis(kw, 1, 0), jnp.moveaxis(vw, 1, 0),
          jnp.arange(nblk))
    out = lax.map(one_block, xs)
    return jnp.moveaxis(out, 0, 1).reshape(bsz, s, ATTN_WIDTH)


def chunked_spatial_gating(v, w_s, b_s):
    bsz, s, _ = v.shape
    nch = s // CHUNK
    vc = v.reshape(bsz, nch, CHUNK, GMLP_GROUPS, GMLP_GROUP_DIM)
    causal = jnp.tril(jnp.ones((CHUNK, CHUNK), dtype=w_s.dtype))
    y = jnp.einsum("gts,bcsgd->bctgd", w_s * causal, vc)
    y = y + jnp.transpose(b_s)[None, None, :, :, None]
    return y.reshape(bsz, s, GMLP_WIDTH)


def setup_inputs(seed: int = 0) -> dict:
    key = jax.random.key(seed)
    ks = jax.random.split(key, 26)
    nrm = lambda k, shape, scale: jax.random.normal(k, shape, jnp.float32) * scale
    x = nrm(ks[0], (BATCH, SEQ, D_MODEL), 1.0)
    norm_a = 1.0 + nrm(ks[1], (N_EVEN, D_MODEL), 0.02)
    w_in_a = nrm(ks[2], (N_EVEN, D_MODEL, IN_A_WIDTH), D_MODEL ** -0.5)
    conv_w = nrm(ks[3], (N_EVEN, CONV_WIDTH, LRU_WIDTH), CONV_WIDTH ** -0.5)
    conv_b = nrm(ks[4], (N_EVEN, LRU_WIDTH), 0.02)
    gate_a_w = nrm(ks[5], (N_EVEN, LRU_BLOCKS, LRU_BLOCK, LRU_BLOCK), LRU_BLOCK ** -0.5)
    gate_a_b = nrm(ks[6], (N_EVEN, LRU_WIDTH), 0.02)
    gate_x_w = nrm(ks[7], (N_EVEN, LRU_BLOCKS, LRU_BLOCK, LRU_BLOCK), LRU_BLOCK ** -0.5)
    gate_x_b = nrm(ks[8], (N_EVEN, LRU_WIDTH), 0.02)
    u = jax.random.uniform(ks[9], (N_EVEN, LRU_WIDTH), jnp.float32, 0.9, 0.999)
    a0 = u ** (1.0 / LRU_C)
    lru_lambda = jnp.log(a0) - jnp.log1p(-a0)
    q_norm_g = 1.0 + nrm(ks[10], (N_EVEN, HEAD_DIM), 0.02)
    k_norm_g = 1.0 + nrm(ks[11], (N_EVEN, HEAD_DIM), 0.02)
    sinks = nrm(ks[12], (N_EVEN, N_Q_HEADS), 0.5)
    w_out_a = nrm(ks[13], (N_EVEN, OUT_A_WIDTH, D_MODEL), OUT_A_WIDTH ** -0.5)
    rel_bias = nrm(ks[14], (REL_BUCKETS, N_Q_HEADS), 0.5)
    norm_c = 1.0 + nrm(ks[15], (N_ODD, D_MODEL), 0.02)
    w_in_c = nrm(ks[16], (N_ODD, D_MODEL, IN_C_WIDTH), D_MODEL ** -0.5)
    ln_v_g = 1.0 + nrm(ks[17], (N_ODD, GMLP_WIDTH), 0.02)
    ln_v_b = nrm(ks[18], (N_ODD, GMLP_WIDTH), 0.02)
    spatial_w = nrm(ks[19], (N_ODD, GMLP_GROUPS, CHUNK, CHUNK), CHUNK ** -0.5)
    spatial_b = 1.0 + nrm(ks[20], (N_ODD, GMLP_GROUPS, CHUNK), 0.02)
    w_out_c = nrm(ks[21], (N_ODD, GMLP_WIDTH, D_MODEL), GMLP_WIDTH ** -0.5)
    return {"x": x, "norm_a": norm_a, "w_in_a": w_in_a, "conv_w": conv_w,
            "conv_b": conv_b, "gate_a_w": gate_a_w, "gate_a_b": gate_a_b,
            "gate_x_w": gate_x_w, "gate_x_b": gate_x_b, "lru_lambda": lru_lambda,
            "q_norm_g": q_norm_g, "k_norm_g": k_norm_g, "sinks": sinks,
            "w_out_a": w_out_a, "rel_bias": rel_bias, "norm_c": norm_c,
            "w_in_c": w_in_c, "ln_v_g": ln_v_g, "ln_v_b": ln_v_b,
            "spatial_w": spatial_w, "spatial_b": spatial_b, "w_out_c": w_out_c}


def reference(x, norm_a, w_in_a, conv_w, conv_b, gate_a_w, gate_a_b, gate_x_w,
              gate_x_b, lru_lambda, q_norm_g, k_norm_g, sinks, w_out_a, rel_bias,
              norm_c, w_in_c, ln_v_g, ln_v_b, spatial_w, spatial_b, w_out_c):
    bsz, s, _ = x.shape
    for layer in range(DEPTH):
        j = layer // 2
        if layer % 2 == 0:
            h = rms_norm(x, norm_a[j])
            z = h @ w_in_a[j]
            za, ga, zq, zk, zv, gb = split_cols(z, IN_A_SPLITS)
            xa = causal_depthwise_conv(za, conv_w[j], conv_b[j])
            ya = rg_lru(xa, gate_a_w[j], gate_a_b[j], gate_x_w[j], gate_x_b[j], lru_lambda[j])
            q = zq.reshape(bsz, s, N_Q_HEADS, HEAD_DIM)
            k = zk.reshape(bsz, s, N_KV_HEADS, HEAD_DIM)
            v = zv.reshape(bsz, s, N_KV_HEADS, HEAD_DIM)
            yb = sliding_window_gqa(q, k, v, q_norm_g[j], k_norm_g[j], sinks[j], rel_bias)
            y = jnp.concatenate([ya * jax.nn.silu(ga), yb * jax.nn.silu(gb)], axis=-1)
            x = x + y @ w_out_a[j]
        else:
            h = rms_norm(x, norm_c[j])
            z = h @ w_in_c[j]
            u, v, g = split_cols(z, (GMLP_WIDTH, GMLP_WIDTH, GMLP_WIDTH))
            v = layer_norm(v, ln_v_g[j], ln_v_b[j])
            sg = chunked_spatial_gating(v, spatial_w[j], spatial_b[j])
            y = u * sg * jax.nn.silu(g)
            x = x + y @ w_out_c[j]
    return x
```

```python
import numpy as np
from contextlib import ExitStack
import concourse.bass as bass
import concourse.mybir as mybir
from concourse.bass_utils import run_bass_kernel_spmd

F32 = mybir.dt.float32
BF16 = mybir.dt.bfloat16
AF = mybir.ActivationFunctionType
ALU = mybir.AluOpType

P = 128
T = 512
NB = 4
D = 1024
KC = 8
EPS = 1e-6
N_CORES = 8
SEQ = 2048
TPS = SEQ // T
NSLOT = 5
SLOT = 4096
NG_L0_IN = 9
NG_TILE = 29

C_CONVW = 0
C_CONVB = 32
C_GAB = 40
C_GXB = 48
C_LAM = 56
C_QG = 64
C_KG = 65
C_SINK = 66
C_LNG = 74
C_LNB = 90
C_ONESB = 128
C_TOTAL = 256
U_GATEW = 0
U_SPW = 2048
U_SPB = 3072
U_AMASK = 4096
U_CMASK = 4352
U_IDENT = 4480
U_TOTAL = 4608


class Sched:
    ENG = ("pe", "act", "dve", "pool", "sp")

    def __init__(self):
        self.q = {e: [] for e in self.ENG}
        self.cnt = {e: 0 for e in self.ENG}
        self.dcnt = {}
        self.lastw = {}
        self.readers = {}
        self.waited = {e: {} for e in self.ENG}
        self.nwaits = 0

    def _deps(self, eng, reads, writes, is_dma):
        need = {}

        def add(t, kind):
            if t[0] == "E" and t[1] == eng and not is_dma:
                if eng == "pe":
                    return
                if kind != "raw" and t[2] != self.cnt[eng]:
                    return
            sk = (t[0], t[1])
            if need.get(sk, 0) < t[2]:
                need[sk] = t[2]

        for k in reads:
            t = self.lastw.get(k)
            if t is not None:
                add(t, "raw")
        for k in writes:
            t = self.lastw.get(k)
            if t is not None:
                add(t, "waw")
            for sk, v in self.readers.get(k, {}).items():
                add((sk[0], sk[1], v), "war")
        waits = []
        wd = self.waited[eng]
        for sk, v in need.items():
            if wd.get(sk, 0) >= v:
                continue
            wd[sk] = v
            waits.append((sk, v))
        self.nwaits += len(waits)
        return waits

    def _commit(self, tok, reads, writes):
        for k in writes:
            self.lastw[k] = tok
            self.readers[k] = {}
        sk = (tok[0], tok[1])
        for k in reads:
            r = self.readers.setdefault(k, {})
            if r.get(sk, 0) < tok[2]:
                r[sk] = tok[2]

    def op(self, eng, fn, reads=(), writes=()):
        waits = self._deps(eng, reads, writes, False)
        self.cnt[eng] += 1
        tok = ("E", eng, self.cnt[eng])
        self.q[eng].append((waits, fn, ("E", eng), 1))
        self._commit(tok, reads, writes)
        return tok

    def dma(self, qeng, sem, fn, reads=(), writes=()):
        waits = self._deps(qeng, reads, writes, True)
        self.dcnt[sem] = self.dcnt.get(sem, 0) + 1
        tok = ("D", sem, 16 * self.dcnt[sem])
        self.q[qeng].append((waits, fn, ("D", sem), 16))
        self._commit(tok, reads, writes)
        return tok

    def final_wait(self, eng, toks):
        waits = [((t[0], t[1]), t[2]) for t in toks]
        self.q[eng].append((waits, None, None, 0))

    def sem_keys(self):
        keys = [("E", e) for e in self.ENG]
        keys += [("D", s) for s in self.dcnt]
        return keys

    def emit(self, nc, sems):
        with nc.Block() as block:
            decos = {"pe": block.tensor, "act": block.scalar, "dve": block.vector,
                     "pool": block.gpsimd, "sp": block.sync}
            for e in self.ENG:
                items = self.q[e]

                def body(engh, items=items):
                    for waits, fn, inc, amt in items:
                        for sk, v in waits:
                            engh.wait_ge(sems[sk], v)
                        if fn is None:
                            continue
                        ins = fn(engh)
                        ins.then_inc(sems[inc], amt)

                decos[e](body)


def zipper(gens, depth):
    gens = list(gens)
    active = []
    i = 0
    while active or i < len(gens):
        while len(active) < depth and i < len(gens):
            active.append(gens[i])
            i += 1
        for g in list(active):
            try:
                next(g)
            except StopIteration:
                active.remove(g)


def multi_zipper(streams):
    st = [{"gens": list(s_[0]), "depth": s_[1], "rate": (s_[2] if len(s_) > 2 else 1), "i": 0, "active": []}
          for s_ in streams]
    while any(x["active"] or x["i"] < len(x["gens"]) for x in st):
        for x in st:
            for _ in range(x["rate"]):
                while len(x["active"]) < x["depth"] and x["i"] < len(x["gens"]):
                    x["active"].append(x["gens"][x["i"]])
                    x["i"] += 1
                for g in list(x["active"]):
                    try:
                        next(g)
                    except StopIteration:
                        x["active"].remove(g)


def build_program(n_tiles, tps=TPS, n_layers=2, dbg=0, l0_order=None, record=None):
    nc = bass.Bass("TRN2", target_bir_lowering=False)
    ntok = n_tiles * T
    x_d = nc.dram_tensor("x", [ntok, D], F32, kind="ExternalInput").ap()
    wst_d = nc.dram_tensor("wst", [NG_TILE, P, SLOT], F32, kind="ExternalInput").ap()
    cst_d = nc.dram_tensor("cst", [P, C_TOTAL], F32, kind="ExternalInput").ap()
    csu_d = nc.dram_tensor("csu", [P, U_TOTAL], F32, kind="ExternalInput").ap()
    gain_d = nc.dram_tensor("gain", [2, P, D], F32, kind="ExternalInput").ap()
    bias_d = nc.dram_tensor("biasT", [P, 2 * 16 * 128], F32, kind="ExternalInput").ap()
    out_d = nc.dram_tensor("out", [ntok, D], F32, kind="ExternalOutput").ap()

    S = Sched()
    st = ExitStack()
    with st:
        def sb(name, shape, dt=F32):
            return st.enter_context(nc.sbuf_tensor("sb_" + name, shape, dt))

        NXB = 2
        xbuf = [sb("xbuf%d" % i, [P, NB, D]) for i in range(NXB)]
        hT = sb("hT", [P, KC, T], BF16)
        ss = sb("ss", [P, 8])
        rs = sb("rs", [P, 8])
        ring = sb("ring", [P, NSLOT, SLOT], BF16)
        cst = sb("cst", [P, C_TOTAL])
        gain = sb("gain", [P, 2, D])
        Et = sb("Et", [P, 2, 16, 128], BF16)
        vz = sb("vz", [P, NB + 1, 2, 128], BF16)
        kz = sb("kz", [P, 2, (NB + 1) * 128], BF16)
        onesz = sb("onesz", [P, 2, 128], BF16)
        qnT = sb("qnT", [P, 8, T], BF16)
        Gs = sb("Gs", [P, 8, T], BF16)
        sqb = [sb("sqb%d" % i, [P, T]) for i in range(2)]
        rsb = [sb("rsb%d" % i, [P, T]) for i in range(2)]
        expS = [sb("expS%d" % i, [P, T]) for i in range(2)]
        PT = [sb("PT%d" % i, [P, T], BF16) for i in range(4)]
        den = [sb("den0", [P, T]), expS[0]]
        yb = [sb("yb0", [P, T]), expS[1]]
        denk = [("den", 0), ("expS", 0)]
        ybk = [("yb", 0), ("expS", 1)]
        junk = den[0][:].bitcast(BF16)
        yT = sb("yT", [P, 16, T], BF16)
        Us = [sb("U%d" % i, [P, 8, T]) for i in range(2)]
        zbuf = [sb("zbuf%d" % i, [P, T + 4]) for i in range(2)]
        xabf = [sb("xabf%d" % i, [P, T], BF16) for i in range(2)]
        halo = sb("halo", [P, 8, 4])
        hstate = sb("hstate", [P, 8])
        Cc = sb("Cc", [P, 16, 128])
        stats = sb("stats", [P, 4, 6])
        mv = sb("mv", [P, 2])
        lnr = sb("lnr", [P, 2])
        kcol = sb("kcol", [P, 64])
        ident_bf = sb("ident_bf", [P, 128], BF16)
        gw_bf = sb("gw_bf", [P, 2, 8, 128], BF16)
        wtril_bf = sb("wtril_bf", [P, 8, 128], BF16)

        NPS = 8
        ps = [st.enter_context(nc.psum_tensor("ps%d" % i, [P, 512], F32)) for i in range(NPS)]
        U = Us[0]
        hb = Gs[:].rearrange("p (b j) t -> p b (j t)", b=NB)
        hbk = lambda b: [("Gs", 2 * b), ("Gs", 2 * b + 1)]
        wtril_f = U[:, 0:2, :].rearrange("p a t -> p (a t)").rearrange("p (g t) -> p g t", g=8)
        vhat = U[:].rearrange("p a t -> p (a t)").bitcast(BF16).rearrange("p (b f) -> p b f", b=NB)
        xflat = xbuf[0][:].rearrange("p b d -> p (b d)")
        usu = U[:, 7, :]
        x0keys = [("x0", b) for b in range(NB)]

        free_banks = list(range(NPS))

        def nbank():
            assert free_banks, "PSUM banks exhausted"
            return free_banks.pop(0)

        def getbank():
            n = 0
            while not free_banks:
                n += 1
                assert n < 10000, "PSUM bank wait deadlock at build time"
                yield
            return free_banks.pop(0)

        def rel(bk):
            assert bk not in free_banks
            free_banks.append(bk)

        def col(c, n=1):
            return cst[:, c:c + n]

        S.dma("sp", "c0", lambda e: e.dma_start(out=cst[:], in_=cst_d), writes=["cst"])
        S.dma("sp", "c1", lambda e: e.dma_start(out=gain[:], in_=gain_d.rearrange("g p d -> p g d")),
              writes=["gain"])
        x1flat = xbuf[1][:].rearrange("p b d -> p (b d)")
        x1keys = [("x1", b) for b in range(NB)]
        S.dma("sp", "c2", lambda e: e.dma_start(out=x1flat, in_=bias_d), writes=x1keys)
        S.dma("sp", "c3", lambda e: e.dma_start(out=xflat, in_=csu_d[:, 0:4096]), writes=x0keys)
        S.dma("sp", "c4", lambda e: e.dma_start(out=usu, in_=csu_d[:, 4096:U_TOTAL]), writes=[("U0", 7)])

        S.op("dve", lambda e: e.tensor_copy(out=ident_bf[:], in_=usu[:, 384:512]),
             reads=[("U0", 7)], writes=["ident_bf"])
        S.op("dve", lambda e: e.memset(onesz[:], 0.0), writes=["onesz"])
        S.op("dve", lambda e: e.memset(onesz[:, 0, 0:64], 1.0), writes=["onesz"])
        S.op("dve", lambda e: e.memset(onesz[:, 1, 64:128], 1.0), writes=["onesz"])
        S.op("dve", lambda e: e.memset(kz[:], 0.0), writes=["kprev", "kcur"])
        S.op("dve", lambda e: e.memset(vz[:], 0.0), writes=["vprev", "vcur"])
        S.op("dve", lambda e: e.tensor_copy(out=gw_bf[:].rearrange("p a n j -> p (a n j)"),
                                            in_=xflat[:, U_GATEW:U_GATEW + 2048]),
             reads=x0keys, writes=["gw_bf"])
        for kt in range(2):
            def f(e, kt=kt):
                m = usu[:, kt * 128:(kt + 1) * 128]
                v = x1flat[:, kt * 2048:(kt + 1) * 2048].rearrange("p (h q) -> p h q", h=16)
                return e.scalar_tensor_tensor(out=v, in0=v, scalar=8.0,
                                              in1=m.unsqueeze(1).to_broadcast([P, 16, 128]),
                                              op0=ALU.mult, op1=ALU.mult)
            S.op("dve", f, reads=x1keys + [("U0", 7)], writes=x1keys)
        S.op("dve", lambda e: e.tensor_scalar(out=usu[:, 0:256], in0=usu[:, 0:256], scalar1=-1.0, scalar2=480.0,
                                              op0=ALU.add, op1=ALU.mult),
             reads=[("U0", 7)], writes=[("U0", 7)])
        for kt in range(2):
            def f(e, kt=kt):
                m = usu[:, kt * 128:(kt + 1) * 128]
                v = x1flat[:, kt * 2048:(kt + 1) * 2048].rearrange("p (h q) -> p h q", h=16)
                return e.tensor_tensor(out=Et[:, kt], in0=v, in1=m.unsqueeze(1).to_broadcast([P, 16, 128]),
                                       op=ALU.add)
            S.op("dve", f, reads=x1keys + [("U0", 7)], writes=["Et"])
        S.op("act", lambda e: e.activation(out=kcol[:, 16:24], in_=col(C_SINK, 8), func=AF.Exp),
             reads=["cst"], writes=["kc_sink"])
        S.op("dve", lambda e: e.tensor_scalar(out=kcol[:, 32:40], in0=col(C_GAB, 8), scalar1=0.5, scalar2=None,
                                              op0=ALU.mult), reads=["cst"], writes=["kc_hb"])
        S.op("dve", lambda e: e.tensor_scalar(out=kcol[:, 40:48], in0=col(C_GXB, 8), scalar1=0.5, scalar2=None,
                                              op0=ALU.mult), reads=["cst"], writes=["kc_hb"])
        S.op("act", lambda e: e.activation(out=kcol[:, 24:32], in_=col(C_LAM, 8), func=AF.Exp, scale=-1.0),
             reads=["cst"], writes=["kc_e"])
        S.op("dve", lambda e: e.tensor_scalar(out=kcol[:, 48:56], in0=kcol[:, 24:32], scalar1=2.0,
                                              scalar2=None, op0=ALU.add),
             reads=["kc_e"], writes=["kc_t"])
        S.op("dve", lambda e: e.reciprocal(out=kcol[:, 48:56], in_=kcol[:, 48:56]),
             reads=["kc_t"], writes=["kc_t"])
        S.op("dve", lambda e: e.tensor_tensor(out=kcol[:, 24:32], in0=kcol[:, 24:32], in1=kcol[:, 48:56],
                                              op=ALU.mult),
             reads=["kc_t", "kc_e"], writes=["kc_e"])
        S.op("dve", lambda e: e.tensor_tensor(out=kcol[:, 48:56], in0=kcol[:, 24:32], in1=kcol[:, 24:32],
                                              op=ALU.mult),
             reads=["kc_e"], writes=["kc_t"])
        S.op("dve", lambda e: e.tensor_scalar(out=kcol[:, 0:8], in0=kcol[:, 48:56], scalar1=1.0 / 9,
                                              scalar2=1.0 / 7, op0=ALU.mult, op1=ALU.add),
             reads=["kc_t"], writes=["kc_p"])
        for cc in (1.0 / 5, 1.0 / 3, 1.0):
            S.op("dve", lambda e: e.tensor_tensor(out=kcol[:, 0:8], in0=kcol[:, 0:8], in1=kcol[:, 48:56],
                                                  op=ALU.mult),
                 reads=["kc_p", "kc_t"], writes=["kc_p"])
            S.op("dve", lambda e, cc=cc: e.tensor_scalar(out=kcol[:, 0:8], in0=kcol[:, 0:8], scalar1=cc,
                                                         scalar2=None, op0=ALU.add),
                 reads=["kc_p"], writes=["kc_p"])
        S.op("dve", lambda e: e.tensor_tensor(out=kcol[:, 0:8], in0=kcol[:, 0:8], in1=kcol[:, 24:32],
                                              op=ALU.mult),
             reads=["kc_p", "kc_e"], writes=["kc_p"])
        S.op("dve", lambda e: e.tensor_scalar(out=kcol[:, 8:16], in0=kcol[:, 0:8], scalar1=-16.0,
                                              scalar2=None, op0=ALU.mult),
             reads=["kc_p"], writes=["kc_K"])
        S.op("dve", lambda e: e.tensor_scalar(out=kcol[:, 0:8], in0=kcol[:, 0:8], scalar1=-8.0,
                                              scalar2=None, op0=ALU.mult),
             reads=["kc_p", "kc_K"], writes=["kc_K"])
        Kh = lambda blk: kcol[:, blk:blk + 1]
        Kf = lambda blk: kcol[:, 8 + blk:9 + blk]

        if n_layers > 1:
            def f(e):
                m = usu[:, 256:384]
                return e.tensor_tensor(out=wtril_f,
                                       in0=xflat[:, U_SPW:U_SPW + 1024].rearrange("p (g t) -> p g t", g=8),
                                       in1=m.unsqueeze(1).to_broadcast([P, 8, 128]), op=ALU.mult)
            S.op("dve", f, reads=x0keys + [("U0", 7)], writes=[("U0", 0), ("U0", 1)])
            S.op("dve", lambda e: e.tensor_copy(out=wtril_bf[:], in_=wtril_f),
                 reads=[("U0", 0), ("U0", 1)], writes=["wtril_bf"])
            S.op("dve", lambda e: e.memset(sqb[0][:, 0:128], 1.0), writes=["sqb0"])
            for g in range(8):
                bk = nbank()

                def f(e, g=g, bk=bk):
                    return e.matmul(ps[bk][:, 0:128], lhsT=sqb[0][:, 0:128], rhs=wtril_f[:, g, :],
                                    start=True, stop=True)
                S.op("pe", f, reads=["sqb0", ("U0", 0), ("U0", 1)], writes=[("ps", bk)])
                for jj in range(2):
                    j = 2 * g + jj

                    def f2(e, g=g, bk=bk, j=j):
                        return e.scalar_tensor_tensor(
                            out=Cc[:, j, :], in0=ps[bk][:, 0:128], scalar=col(C_LNB + j),
                            in1=xflat[:, U_SPB + g * 128:U_SPB + (g + 1) * 128], op0=ALU.mult, op1=ALU.add)
                    S.op("dve", f2, reads=[("ps", bk), "cst"] + x0keys, writes=["Cc"])
                rel(bk)

        GPT = NG_TILE if n_layers > 1 else 13
        rem_tile = [4] * 8 + [2] + [4] * 4 + ([4] * 16 if n_layers > 1 else [])
        seq_groups = [g for t in range(n_tiles) for g in range(GPT)]
        rem = [r for t in range(n_tiles) for r in rem_tile]
        rstate = {"next_load": 0, "oldest": 0}

        def _load():
            i = rstate["next_load"]
            rstate["next_load"] += 1
            if i >= len(seq_groups):
                return
            slot = i % NSLOT
            g = seq_groups[i]
            S.dma("pool", "w%d" % slot,
                  lambda e: e.dma_start(out=ring[:, slot, :], in_=wst_d[g], max_dma_last_dim=8192),
                  writes=[("w", slot)])

        for i in range(NSLOT):
            _load()

        def wslot(i):
            if record is not None:
                return i % NSLOT
            assert rstate["oldest"] <= i < rstate["next_load"], (i, rstate)
            return i % NSLOT

        def wdone(i, n=1):
            if record is not None:
                return
            rem[i] -= n
            assert rem[i] >= 0
            while rstate["oldest"] < len(rem) and rem[rstate["oldest"]] == 0:
                rstate["oldest"] += 1
                _load()

        def mm_group(out_ap, pairs, reads, writes):
            def f(e):
                ins = None
                n = len(pairs)
                for i, (l, r) in enumerate(pairs):
                    ins = e.matmul(out_ap, lhsT=l, rhs=r, start=(i == 0), stop=(i == n - 1))
                return ins
            S.op("pe", f, reads=reads, writes=writes)

        hT_keys = [("hT", kc) for kc in range(KC)]

        def norm_stats_blk(xb, xk, gi, b):
            S.op("act", lambda e: e.activation(out=junk, in_=xb[:, b, :], func=AF.Square,
                                               accum_out=ss[:, b:b + 1]),
                 reads=[(xk, b)], writes=[("ss", b), ("den", 0)])
            S.op("act", lambda e: e.activation(out=rs[:, b:b + 1], in_=ss[:, b:b + 1], func=AF.Sqrt,
                                               scale=1.0 / D, bias=EPS),
                 reads=[("ss", b)], writes=[("rs", b)])
            S.op("dve", lambda e: e.reciprocal(out=rs[:, b:b + 1], in_=rs[:, b:b + 1]),
                 reads=[("rs", b)], writes=[("rs", b)])
            for hf in range(2):
                S.op("dve", lambda e, hf=hf: e.scalar_tensor_tensor(
                    out=hb[:, b, hf * 512:(hf + 1) * 512], in0=xb[:, b, hf * 512:(hf + 1) * 512],
                    scalar=rs[:, b:b + 1], in1=gain[:, gi, hf * 512:(hf + 1) * 512],
                    op0=ALU.mult, op1=ALU.mult),
                    reads=[(xk, b), ("rs", b), "gain"], writes=[("Gs", 2 * b + hf)])

        def norm_stats(xb, xk, gi):
            for b in range(NB):
                norm_stats_blk(xb, xk, gi, b)

        def norm_tr():
            for pi in range(KC // 2):
                bkt = nbank()
                ptb = ps[bkt][:, :].bitcast(BF16)

                def f(e, pi=pi, ptb=ptb):
                    ins = None
                    for kk in range(2):
                        kc = 2 * pi + kk
                        for b in range(NB):
                            ins = e.transpose(out=ptb[:, kk * 512 + b * 128: kk * 512 + (b + 1) * 128],
                                              in_=hb[:, b, kc * 128:(kc + 1) * 128], identity=ident_bf[:])
                    return ins
                S.op("pe", f, reads=[("Gs", 2 * b + pi // 2) for b in range(NB)] + ["ident_bf"],
                     writes=[("ps", bkt)])
                S.op("act", lambda e, pi=pi, ptb=ptb: e.copy(
                    out=hT[:, 2 * pi:2 * pi + 2, :].rearrange("p k t -> p (k t)"), in_=ptb[:, :]),
                    reads=[("ps", bkt)], writes=[("hT", 2 * pi), ("hT", 2 * pi + 1)])
                rel(bkt)

        def norm_T(xb, xk, gi):
            norm_stats(xb, xk, gi)
            norm_tr()

        def inproj_chunk(gi, pos, bk=None):
            slot = wslot(gi)
            if bk is None:
                bk = nbank()
            wv = ring[:, slot, :].rearrange("p (c k n) -> p c k n", c=4, k=KC)
            mm_group(ps[bk][:, :], [(wv[:, pos, kc, :], hT[:, kc, :]) for kc in range(KC)],
                     reads=[("w", slot)] + hT_keys, writes=[("ps", bk)])
            wdone(gi)
            return bk

        def out_proj(xb, xk, gbase, pre=None, blk_hook=None):
            if pre is not None:
                pre[0]()

            def mm(b, jg, gi, bank):
                slot = wslot(gi)
                wv = ring[:, slot, :].rearrange("p (j n) -> p j n", j=8)

                def f(e):
                    ins = None
                    for jj in range(8):
                        j = jg * 8 + jj
                        ins = e.matmul(ps[bank][:, :], lhsT=yT[:, j, b * 128:(b + 1) * 128],
                                       rhs=wv[:, jj, :], start=(j == 0), stop=(j == 15))
                    return ins
                S.op("pe", f, reads=[("w", slot)] + [("yT", jg * 8 + jj) for jj in range(8)],
                     writes=[("ps", bank)])
                wdone(gi)

            def add(b, half, bank):
                S.op("dve", lambda e: e.tensor_tensor(
                    out=xb[:, b, half * 512:(half + 1) * 512], in0=ps[bank][:, :],
                    in1=xb[:, b, half * 512:(half + 1) * 512], op=ALU.add),
                    reads=[("ps", bank), (xk, b)], writes=[(xk, b)])
                rel(bank)

            banks = [nbank() for _ in range(NB)]
            for jg in range(2):
                for b in range(NB):
                    mm(b, jg, gbase + jg, banks[b])
            for b in range(NB):
                add(b, 0, banks[b])
            if pre is not None:
                pre[1]()
            if blk_hook is None:
                banks = [nbank() for _ in range(NB)]
                for jg in range(2):
                    for b in range(NB):
                        mm(b, jg, gbase + 2 + jg, banks[b])
                for b in range(NB):
                    add(b, 1, banks[b])
            else:
                for b in range(NB):
                    bank = nbank()
                    for jg in range(2):
                        mm(b, jg, gbase + 2 + jg, bank)
                    add(b, 1, bank)
                    blk_hook(b)

        def layer0(t, xb, xk, gb0, pre=None):
            first = (t % tps == 0)
            if first:
                S.op("pool", lambda e: e.memset(halo[:], 0.0), writes=[("halo", i) for i in range(8)])
                S.op("pool", lambda e: e.memset(hstate[:], 0.0), writes=[("hstate", i) for i in range(8)])
            def cgrp(ci):
                if record is not None:
                    record.append(ci)
                    return (gb0, 0)
                pos = l0_order.index(ci)
                return (gb0 + pos // 4, pos % 4)

            gi, pos = cgrp(0)
            slot = wslot(gi)
            wv = ring[:, slot, :].rearrange("p (c k n) -> p c k n", c=4, k=KC)
            bkv = nbank()
            for b in range(NB):
                mm_group(ps[bkv][:, b * 128:(b + 1) * 128],
                         [(hT[:, kc, b * 128:(b + 1) * 128], wv[:, 0, kc, :]) for kc in range(KC)],
                         reads=[("w", slot)] + hT_keys, writes=[("ps", bkv)])
            wdone(gi)
            for kv in range(2):
                S.op("act", lambda e, kv=kv: e.copy(
                    out=vz[:, 1:NB + 1, kv, kv * 64:(kv + 1) * 64],
                    in_=ps[bkv][:, :].rearrange("p (b n) -> p b n", b=NB)[:, :, kv * 64:(kv + 1) * 64]),
                    reads=[("ps", bkv)], writes=["vcur"])
            rel(bkv)

            def gen_qk(ci, gcol, out_ap, outkey, idx):
                sq, rr = sqb[idx % 2], rsb[idx % 2]
                sqk, rrk = "sqb%d" % (idx % 2), "rsb%d" % (idx % 2)
                bk = yield from getbank()
                inproj_chunk(*cgrp(ci), bk=bk)
                yield
                S.op("act", lambda e: e.activation(out=sq[:], in_=ps[bk][:, :], func=AF.Square),
                     reads=[("ps", bk)], writes=[sqk])
                yield
                b2 = yield from getbank()
                S.op("pe", lambda e: e.matmul(ps[b2][:, :], lhsT=col(C_ONESB, 128), rhs=sq[:],
                                              start=True, stop=True),
                     reads=[sqk, "cst"], writes=[("ps", b2)])
                yield
                S.op("act", lambda e: e.activation(out=rr[:], in_=ps[b2][:, :], func=AF.Ln, scale=1.0 / 64,
                                                   bias=EPS),
                     reads=[("ps", b2)], writes=[rrk])
                rel(b2)
                yield
                S.op("act", lambda e: e.activation(out=rr[:], in_=rr[:], func=AF.Exp, scale=-0.5),
                     reads=[rrk], writes=[rrk])
                yield
                if out_ap is None:
                    for kv in range(2):
                        pr = slice(kv * 64, (kv + 1) * 64)
                        S.op("dve", lambda e, kv=kv, pr=pr: e.scalar_tensor_tensor(
                            out=kz[pr, kv, 128:], in0=ps[bk][pr, :], scalar=cst[pr, gcol:gcol + 1],
                            in1=rr[pr, :], op0=ALU.mult, op1=ALU.mult),
                            reads=[("ps", bk), rrk, "cst"], writes=[outkey])
                else:
                    S.op("dve", lambda e: e.scalar_tensor_tensor(out=out_ap, in0=ps[bk][:, :], scalar=col(gcol),
                                                                 in1=rr[:], op0=ALU.mult, op1=ALU.mult),
                         reads=[("ps", bk), rrk, "cst"], writes=[outkey])
                rel(bk)
                yield

            def gen_gb(c):
                bk = yield from getbank()
                inproj_chunk(*cgrp(10 + c), bk=bk)
                yield
                S.op("act", lambda e: e.activation(out=Gs[:, c, :], in_=ps[bk][:, :], func=AF.Silu),
                     reads=[("ps", bk)], writes=[("Gs", c)])
                rel(bk)
                yield

            gens = [gen_qk(1, C_KG, None, "kcur", 0)]
            gens += [gen_qk(2 + c, C_QG, qnT[:, c, :], ("qn", c), 1 + c) for c in range(8)]
            gens += [gen_gb(c) for c in range(8)]
            b_gens = gens

            def gen_att_kv(b, quad, kv, bo, bd, stt, u):
                pi_ = (u % 2) * 2 + kv
                pT = PT[pi_]
                pk = ("PT", pi_)
                kts = [1] if (first and b == 0) else [0, 1]
                for kt in kts:
                    keyblk = b + kt
                    bs_ = yield from getbank()
                    kkey = "kprev" if keyblk == 0 else "kcur"
                    vkey = "vprev" if keyblk == 0 else "vcur"
                    h0 = kv * 8 + quad * 4

                    def fS(e, keyblk=keyblk, bs_=bs_, kt=kt, h0=h0):
                        o = ps[bs_][:, :].rearrange("p (c q) -> p c q", c=4)
                        e.matmul(o, lhsT=kz[:, kv, keyblk * 128:(keyblk + 1) * 128],
                                 rhs=qnT[:, quad * 4:(quad + 1) * 4, b * 128:(b + 1) * 128],
                                 start=True, stop=False)
                        return e.matmul(o, lhsT=ident_bf[:], rhs=Et[:, kt, h0:h0 + 4, :], start=False, stop=True)
                    S.op("pe", fS, reads=[kkey, "Et", "ident_bf"] + [("qn", quad * 4 + cc) for cc in range(4)],
                         writes=[("ps", bs_)])
                    yield
                    S.op("act", lambda e, bs_=bs_: e.activation(
                        out=pT[:], in_=ps[bs_][:, :], func=AF.Exp, scale=0.125),
                        reads=[("ps", bs_)], writes=[pk])
                    rel(bs_)
                    yield
                    st_ = (stt["n"] == 0)
                    sp_ = (stt["n"] == stt["total"] - 1)
                    stt["n"] += 1
                    S.op("pe", lambda e, keyblk=keyblk, st_=st_, sp_=sp_: e.matmul(
                        ps[bo][:, :], lhsT=vz[:, keyblk, kv, :], rhs=pT[:], start=st_, stop=sp_),
                        reads=[vkey, pk], writes=[("ps", bo)])
                    S.op("pe", lambda e, st_=st_, sp_=sp_: e.matmul(
                        ps[bd][:, :], lhsT=onesz[:, kv, :], rhs=pT[:], start=st_, stop=sp_),
                        reads=["onesz", pk], writes=[("ps", bd)])
                    yield

            def gen_att(b, quad, u):
                bo = yield from getbank()
                bd = yield from getbank()
                dn, y_ = den[u % 2], yb[u % 2]
                dk, yk = denk[u % 2], ybk[u % 2]
                nk = 1 if (first and b == 0) else 2
                stt = {"n": 0, "total": 2 * nk}
                subs = [gen_att_kv(b, quad, kv, bo, bd, stt, u) for kv in range(2)]
                while subs:
                    for g in list(subs):
                        try:
                            next(g)
                        except StopIteration:
                            subs.remove(g)
                    yield
                S.op("dve", lambda e: e.tensor_tensor(
                    out=dn[:].rearrange("p (c q) -> p c q", c=4),
                    in0=ps[bd][:, :].rearrange("p (c q) -> p c q", c=4),
                    in1=kcol[:, 16 + quad * 4:16 + (quad + 1) * 4].unsqueeze(2).to_broadcast([P, 4, 128]),
                    op=ALU.add),
                    reads=[("ps", bd), "kc_sink"], writes=[dk])
                rel(bd)
                yield
                S.op("act", lambda e: e.activation(out=dn[:], in_=dn[:], func=AF.Ln), reads=[dk], writes=[dk])
                yield
                S.op("act", lambda e: e.activation(out=dn[:], in_=dn[:], func=AF.Exp, scale=-1.0),
                     reads=[dk], writes=[dk])
                yield
                S.op("dve", lambda e: e.tensor_tensor(out=y_[:], in0=ps[bo][:, :], in1=dn[:], op=ALU.mult),
                     reads=[("ps", bo), dk], writes=[yk])
                rel(bo)
                yield
                S.op("dve", lambda e: e.tensor_tensor(
                    out=yT[:, 8 + quad * 4:8 + (quad + 1) * 4, b * 128:(b + 1) * 128],
                    in0=y_[:].rearrange("p (c q) -> p c q", c=4),
                    in1=Gs[:, quad * 4:(quad + 1) * 4, b * 128:(b + 1) * 128], op=ALU.mult),
                    reads=[yk] + [("Gs", quad * 4 + cc) for cc in range(4)],
                    writes=[("yT", 8 + quad * 4 + cc) for cc in range(4)])
                yield

            def gen_att_tail():
                S.op("pool", lambda e: e.tensor_copy(out=kz[:, :, 0:128], in_=kz[:, :, NB * 128:(NB + 1) * 128]),
                     reads=["kcur"], writes=["kprev"])
                S.op("pool", lambda e: e.tensor_copy(out=vz[:, 0, :, :], in_=vz[:, NB, :, :]),
                     reads=["vcur"], writes=["vprev"])
                yield

            units = [(b, quad) for b in range(NB) for quad in range(2)]
            att_gens = [gen_att(b, quad, u) for u, (b, quad) in enumerate(units)] + [gen_att_tail()]
            def gen_A():
                active = []
                i = 0
                while active or i < len(b_gens):
                    while len(active) < 2 and i < len(b_gens):
                        active.append(b_gens[i])
                        i += 1
                    for g in list(active):
                        try:
                            next(g)
                        except StopIteration:
                            active.remove(g)
                    yield
                active = []
                i = 0
                while active or i < len(att_gens):
                    while len(active) < 2 and i < len(att_gens):
                        active.append(att_gens[i])
                        i += 1
                    for g in list(active):
                        try:
                            next(g)
                        except StopIteration:
                            active.remove(g)
                    yield

            def gen_rg(blk):
                si = blk % 2
                Ub = Us[si]
                zb, xb_ = zbuf[si], xabf[si]
                t_r, t_a, t_m, t_i, t_b, t_h, t_s, t_x = [Ub[:, i, :] for i in range(8)]
                Uk = [("U%d" % si, i) for i in range(8)]
                zh, zm, xk_ = ("zb_h", si), ("zb_m", si), ("xabf", si)
                bz = yield from getbank()
                inproj_chunk(*cgrp(18 + 2 * blk), bk=bz)
                S.op("pool", lambda e: e.tensor_copy(out=zb[:, 0:4], in_=halo[:, blk, :]),
                     reads=[("halo", blk)], writes=[zh])
                yield
                S.op("act", lambda e: e.copy(out=zb[:, 4:T + 4], in_=ps[bz][:, :]),
                     reads=[("ps", bz)], writes=[zm])
                S.op("dve", lambda e: e.tensor_scalar(
                    out=t_x, in0=zb[:, 4:T + 4], scalar1=col(C_CONVW + blk * 4 + 3), scalar2=col(C_CONVB + blk),
                    op0=ALU.mult, op1=ALU.add),
                    reads=[zm, "cst"], writes=[Uk[7]])
                rel(bz)
                yield
                for k in (1, 2, 3):
                    S.op("dve", lambda e, k=k: e.scalar_tensor_tensor(
                        out=t_x, in0=zb[:, 4 - k:T + 4 - k], scalar=col(C_CONVW + blk * 4 + 3 - k), in1=t_x,
                        op0=ALU.mult, op1=ALU.add),
                        reads=[zh, zm, Uk[7], "cst"], writes=[Uk[7]])
                    yield
                S.op("pool", lambda e: e.tensor_copy(out=halo[:, blk, :], in_=zb[:, T:T + 4]),
                     reads=[zm, zh], writes=[("halo", blk)])
                S.op("dve", lambda e: e.tensor_copy(out=xb_[:], in_=t_x), reads=[Uk[7]], writes=[xk_])
                yield
                br = yield from getbank()
                S.op("pe", lambda e: e.matmul(ps[br][:, :], lhsT=gw_bf[:, 0, blk, :], rhs=xb_[:],
                                              start=True, stop=True),
                     reads=["gw_bf", xk_], writes=[("ps", br)])
                bi = yield from getbank()
                S.op("pe", lambda e: e.matmul(ps[bi][:, :], lhsT=gw_bf[:, 1, blk, :], rhs=xb_[:],
                                              start=True, stop=True),
                     reads=["gw_bf", xk_], writes=[("ps", bi)])
                yield
                S.op("act", lambda e: e.activation(out=t_r, in_=ps[br][:, :], func=AF.Tanh, scale=0.5,
                                                   bias=kcol[:, 32 + blk:33 + blk]),
                     reads=[("ps", br), "kc_hb"], writes=[Uk[0]])
                rel(br)
                yield
                S.op("act", lambda e: e.activation(out=t_i, in_=ps[bi][:, :], func=AF.Tanh, scale=0.5,
                                                   bias=kcol[:, 40 + blk:41 + blk]),
                     reads=[("ps", bi), "kc_hb"], writes=[Uk[3]])
                rel(bi)
                yield
                S.op("act", lambda e: e.activation(out=t_a, in_=t_r, func=AF.Exp, scale=Kh(blk), bias=Kh(blk)),
                     reads=[Uk[0], "kc_K"], writes=[Uk[1]])
                yield
                S.op("dve", lambda e: e.tensor_tensor(out=t_m, in0=t_a, in1=t_a, op=ALU.mult),
                     reads=[Uk[1]], writes=[Uk[2]])
                yield
                S.op("dve", lambda e: e.scalar_tensor_tensor(out=t_b, in0=t_i, scalar=1.0, in1=t_x,
                                                             op0=ALU.add, op1=ALU.mult),
                     reads=[Uk[3], Uk[7]], writes=[Uk[4]])
                yield
                S.op("dve", lambda e: e.tensor_scalar(out=t_m, in0=t_m, scalar1=1.0, scalar2=-1.0,
                                                      op0=ALU.min, op1=ALU.mult),
                     reads=[Uk[2]], writes=[Uk[2]])
                yield
                S.op("act", lambda e: e.activation(out=t_m, in_=t_m, func=AF.Sqrt, bias=1.0),
                     reads=[Uk[2]], writes=[Uk[2]])
                yield
                S.op("dve", lambda e: e.scalar_tensor_tensor(out=t_b, in0=t_b, scalar=0.5, in1=t_m,
                                                             op0=ALU.mult, op1=ALU.mult),
                     reads=[Uk[4], Uk[2]], writes=[Uk[4]])
                yield
                bg = yield from getbank()
                inproj_chunk(*cgrp(19 + 2 * blk), bk=bg)
                yield
                S.op("act", lambda e: e.activation(out=t_s, in_=ps[bg][:, :], func=AF.Tanh, scale=0.5),
                     reads=[("ps", bg)], writes=[Uk[6]])
                yield
                S.op("dve", lambda e: e.tensor_tensor_scan(
                    out=t_h, data0=t_a, data1=t_b, initial=hstate[:, blk:blk + 1], op0=ALU.mult, op1=ALU.add),
                    reads=[Uk[1], Uk[4], ("hstate", blk)], writes=[Uk[5]])
                yield
                S.op("pool", lambda e: e.tensor_copy(out=hstate[:, blk:blk + 1], in_=Ub[:, 5, T - 1:T]),
                     reads=[Uk[5]], writes=[("hstate", blk)])
                S.op("dve", lambda e: e.scalar_tensor_tensor(out=t_s, in0=t_s, scalar=1.0, in1=ps[bg][:, :],
                                                             op0=ALU.add, op1=ALU.mult),
                     reads=[Uk[6], ("ps", bg)], writes=[Uk[6]])
                rel(bg)
                yield
                S.op("dve", lambda e: e.scalar_tensor_tensor(out=yT[:, blk, :], in0=t_h, scalar=0.5, in1=t_s,
                                                             op0=ALU.mult, op1=ALU.mult),
                     reads=[Uk[5], Uk[6]], writes=[("yT", blk)])
                yield

            multi_zipper([([gen_A()], 1, 2), ([gen_rg(blk) for blk in range(8)], 2, 1)])
            if dbg == 5:
                return
            if n_layers > 1 and dbg == 0:
                out_proj(xb, xk, gb0 + 9, pre, blk_hook=lambda b: norm_stats_blk(xb, xk, 1, b))
            else:
                out_proj(xb, xk, gb0 + 9, pre)

        def layer1(t, xb, xk, gb1, pre=None):
            norm_tr()
            vslots = [wslot(gb1 + vg) for vg in range(4)]
            for b in range(NB):
                banks = []
                for vg in range(4):
                    bk = nbank()
                    banks.append(bk)
                    wv = ring[:, vslots[vg], :].rearrange("p (k n) -> p k n", k=KC)
                    mm_group(ps[bk][:, :], [(hT[:, kc, b * 128:(b + 1) * 128], wv[:, kc, :]) for kc in range(KC)],
                             reads=[("w", vslots[vg])] + hT_keys, writes=[("ps", bk)])
                    wdone(gb1 + vg)
                    S.op("dve", lambda e, bk=bk, vg=vg: e.bn_stats(out=stats[:, vg, :], in_=ps[bk][:, :]),
                         reads=[("ps", bk)], writes=[("stats", vg)])
                S.op("dve", lambda e: e.bn_aggr(out=mv[:], in_=stats[:].rearrange("p a s -> p (a s)")),
                     reads=[("stats", vg) for vg in range(4)], writes=["mv"])
                S.op("act", lambda e: e.activation(out=lnr[:, 0:1], in_=mv[:, 1:2], func=AF.Sqrt, bias=EPS),
                     reads=["mv"], writes=["lnr"])
                S.op("dve", lambda e: e.reciprocal(out=lnr[:, 1:2], in_=lnr[:, 0:1]), reads=["lnr"],
                     writes=["lnr2"])
                for vg in range(4):
                    S.op("dve", lambda e, vg=vg, b=b, bk=banks[vg]: e.tensor_scalar(
                        out=vhat[:, b, vg * 512:(vg + 1) * 512], in0=ps[bk][:, :], scalar1=mv[:, 0:1],
                        scalar2=lnr[:, 1:2], op0=ALU.subtract, op1=ALU.mult),
                        reads=[("ps", banks[vg]), "mv", "lnr2"], writes=[("U0", 2 * b), ("U0", 2 * b + 1)])
                    rel(banks[vg])

            def gen_l1(j):
                g = j // 2
                si = j % 2
                S1, G1, Y1 = sqb[si], rsb[si], expS[si]
                s1k, g1k, y1k = "sqb%d" % si, "rsb%d" % si, ("expS", si)
                gi = gb1 + 4 + j // 2
                bu = yield from getbank()
                inproj_chunk(gi, (j % 2) * 2, bk=bu)
                yield
                bg = yield from getbank()
                inproj_chunk(gi, (j % 2) * 2 + 1, bk=bg)
                yield
                S.op("act", lambda e: e.activation(out=G1[:], in_=ps[bg][:, :], func=AF.Silu),
                     reads=[("ps", bg)], writes=[g1k])
                rel(bg)
                bsg = yield from getbank()
                for b in range(NB):
                    S.op("pe", lambda e, b=b: e.matmul(
                        ps[bsg][:, b * 128:(b + 1) * 128], lhsT=vhat[:, b, j * 128:(j + 1) * 128],
                        rhs=wtril_bf[:, g, :], start=True, stop=True),
                        reads=[("U0", 2 * b), ("U0", 2 * b + 1), "wtril_bf"], writes=[("ps", bsg)])
                yield
                S.op("dve", lambda e: e.scalar_tensor_tensor(
                    out=S1[:].rearrange("p (b t) -> p b t", b=NB),
                    in0=ps[bsg][:, :].rearrange("p (b t) -> p b t", b=NB), scalar=col(C_LNG + j),
                    in1=Cc[:, j:j + 1, :].to_broadcast([P, NB, 128]), op0=ALU.mult, op1=ALU.add),
                    reads=[("ps", bsg), "Cc", "cst"], writes=[s1k])
                rel(bsg)
                yield
                S.op("dve", lambda e: e.tensor_tensor(out=Y1[:], in0=ps[bu][:, :], in1=S1[:], op=ALU.mult),
                     reads=[("ps", bu), s1k], writes=[y1k])
                rel(bu)
                yield
                S.op("dve", lambda e: e.tensor_tensor(out=yT[:, j, :], in0=Y1[:], in1=G1[:], op=ALU.mult),
                     reads=[y1k, g1k], writes=[("yT", j)])
                yield

            zipper([gen_l1(j) for j in range(16)], 2)
            out_proj(xb, xk, gb1 + 12, pre)

        def load_x(t):
            xb = xbuf[t % NXB]
            xk = "x%d" % (t % NXB)
            S.dma("sp", "xl%d" % (t % NXB), lambda e: e.dma_start(
                out=xb[:], in_=x_d[t * T:(t + 1) * T, :].rearrange("(b p) d -> p b d", p=P)),
                writes=[(xk, b) for b in range(NB)])

        load_x(0)
        if dbg != 1:
            norm_T(xbuf[0], "x0", 0)
        for t in range(n_tiles):
            xb = xbuf[t % NXB]
            xk = "x%d" % (t % NXB)
            nxt = None
            if t + 1 < n_tiles:
                load_x(t + 1)
                if dbg != 1:
                    nxt = (lambda t=t: norm_stats(xbuf[(t + 1) % NXB], "x%d" % ((t + 1) % NXB), 0), norm_tr)
            if dbg != 1:
                if n_layers > 1 and dbg == 0:
                    layer0(t, xb, xk, t * GPT)
                    layer1(t, xb, xk, t * GPT + 13, nxt)
                else:
                    layer0(t, xb, xk, t * GPT, nxt)
            S.dma("sp", "xs%d" % (t % NXB), lambda e, t=t, xb=xb: e.dma_start(
                out=out_d[t * T:(t + 1) * T, :].rearrange("(b p) d -> p b d", p=P), in_=xb[:]),
                reads=[(xk, b) for b in range(NB)])
        S.final_wait("sp", [("D", sname, 16 * c) for sname, c in S.dcnt.items()])

        sems = {}
        for k in S.sem_keys():
            sems[k] = st.enter_context(nc.semaphore("s_%s_%s" % (k[0], k[1])))
        S.emit(nc, sems)
    return nc


def t5_bucket(dist):
    max_exact = 16
    df = np.maximum(dist, 1).astype(np.float32)
    large = max_exact + (np.log(df / np.float32(max_exact)) / np.float32(np.log(128 / 16))
                         * np.float32(16)).astype(np.int32)
    large = np.minimum(large, 31)
    return np.where(dist < max_exact, dist, large)


def prepare_shared(inp, l0_order):
    f32 = np.float32
    w_in_a = np.asarray(inp["w_in_a"], f32)[0]
    w_out_a = np.asarray(inp["w_out_a"], f32)[0]
    w_in_c = np.asarray(inp["w_in_c"], f32)[0]
    w_out_c = np.asarray(inp["w_out_c"], f32)[0]

    def chunk_in(w, cols):
        return w[:, cols].reshape(KC, P, len(cols)).transpose(1, 0, 2)

    o_za, o_ga, o_q, o_k, o_v, o_gb = 0, 1024, 2048, 3072, 3200, 3328
    ar = np.arange
    chunks = [chunk_in(w_in_a, o_v + ar(128)), chunk_in(w_in_a, o_k + ar(128))]
    head_cols = lambda base, c: np.concatenate([base + c * 64 + ar(64), base + (8 + c) * 64 + ar(64)])
    for c in range(8):
        chunks.append(chunk_in(w_in_a, head_cols(o_q, c)))
    for c in range(8):
        chunks.append(chunk_in(w_in_a, head_cols(o_gb, c)))
    for blk in range(8):
        chunks.append(chunk_in(w_in_a, o_za + blk * 128 + ar(128)))
        chunks.append(chunk_in(w_in_a, o_ga + blk * 128 + ar(128)))
    chunks = [chunks[ci] for ci in l0_order]
    while len(chunks) % 4:
        chunks.append(np.zeros_like(chunks[0]))
    groups = []
    for g in range(len(chunks) // 4):
        groups.append(np.stack(chunks[4 * g:4 * g + 4], axis=1).reshape(P, SLOT))

    def out_groups(w, rows_of_chunk):
        gs = []
        for half in range(2):
            for jg in range(2):
                blk = np.stack([w[rows_of_chunk(jg * 8 + jj)][:, half * 512:(half + 1) * 512]
                                for jj in range(8)], axis=1)
                gs.append(blk.reshape(P, SLOT))
        return gs

    def rows_a(j):
        if j < 8:
            return j * 128 + ar(128)
        c = j - 8
        return np.concatenate([1024 + c * 64 + ar(64), 1024 + (8 + c) * 64 + ar(64)])
    groups += out_groups(w_out_a, rows_a)
    for vg in range(4):
        cols = 2048 + vg * 512 + ar(512)
        groups.append(w_in_c[:, cols].reshape(KC, P, 512).transpose(1, 0, 2).reshape(P, SLOT))
    for j in range(0, 16, 2):
        cs = [chunk_in(w_in_c, j * 128 + ar(128)), chunk_in(w_in_c, 4096 + j * 128 + ar(128)),
              chunk_in(w_in_c, (j + 1) * 128 + ar(128)), chunk_in(w_in_c, 4096 + (j + 1) * 128 + ar(128))]
        groups.append(np.stack(cs, axis=1).reshape(P, SLOT))
    groups += out_groups(w_out_c, lambda j: j * 128 + ar(128))
    wst = np.ascontiguousarray(np.stack(groups, axis=0), dtype=f32)
    assert wst.shape == (NG_TILE, P, SLOT), wst.shape

    cst = np.zeros((P, C_TOTAL), f32)
    pcol = lambda v: np.asarray(v, f32).reshape(8, P).T
    cw = np.asarray(inp["conv_w"], f32)[0]
    cst[:, C_CONVW:C_CONVW + 32] = cw.reshape(4, 8, P).transpose(2, 1, 0).reshape(P, 32)
    cst[:, C_CONVB:C_CONVB + 8] = pcol(inp["conv_b"][0])
    cst[:, C_GAB:C_GAB + 8] = pcol(inp["gate_a_b"][0])
    cst[:, C_GXB:C_GXB + 8] = pcol(inp["gate_x_b"][0])
    cst[:, C_LAM:C_LAM + 8] = pcol(inp["lru_lambda"][0])
    cst[:, C_QG] = np.tile(np.asarray(inp["q_norm_g"], f32)[0], 2)
    cst[:, C_KG] = np.tile(np.asarray(inp["k_norm_g"], f32)[0], 2)
    sinks = np.asarray(inp["sinks"], f32)[0]
    cst[:64, C_SINK:C_SINK + 8] = sinks[None, 0:8]
    cst[64:, C_SINK:C_SINK + 8] = sinks[None, 8:16]
    cst[:, C_LNG:C_LNG + 16] = np.asarray(inp["ln_v_g"], f32)[0].reshape(16, P).T
    cst[:, C_LNB:C_LNB + 16] = np.asarray(inp["ln_v_b"], f32)[0].reshape(16, P).T
    ga = np.asarray(inp["gate_a_w"], f32)[0]
    gx = np.asarray(inp["gate_x_w"], f32)[0]
    gw = np.stack([ga.transpose(1, 0, 2), gx.transpose(1, 0, 2)], axis=1)
    csu = np.zeros((P, U_TOTAL), f32)
    csu[:, U_GATEW:U_GATEW + 2048] = gw.reshape(P, 2048)
    s_i = ar(128)[:, None]
    q_i = ar(128)[None, :]
    amask = np.stack([(s_i > q_i), (s_i <= q_i)], axis=1).astype(f32)
    csu[:, U_AMASK:U_AMASK + 256] = amask.reshape(P, 256)
    spw = np.asarray(inp["spatial_w"], f32)[0]
    csu[:, U_SPW:U_SPW + 1024] = spw.transpose(2, 0, 1).reshape(P, 1024)
    csu[:, U_CMASK:U_CMASK + 128] = (s_i <= q_i).astype(f32)
    spb = np.asarray(inp["spatial_b"], f32)[0]
    csu[:, U_SPB:U_SPB + 1024] = np.broadcast_to(spb.reshape(1, 1024), (P, 1024))
    csu[:, U_IDENT:U_IDENT + 128] = np.eye(128, dtype=f32)
    ob = np.zeros((128, 128), f32)
    ob[:64, :64] = 1.0
    ob[64:, 64:] = 1.0
    cst[:, C_ONESB:C_ONESB + 128] = ob

    gain = np.stack([np.broadcast_to(np.asarray(inp["norm_a"], f32)[0][None, :], (P, D)),
                     np.broadcast_to(np.asarray(inp["norm_c"], f32)[0][None, :], (P, D))], axis=0)
    gain = np.ascontiguousarray(gain, dtype=f32)

    rel = np.asarray(inp["rel_bias"], f32)
    kj = ar(256)[None, :]
    qi = ar(128)[:, None]
    dist = 128 + qi - kj
    bucket = t5_bucket(np.maximum(dist, 0))
    bias = rel[bucket]
    biasT = bias.transpose(1, 2, 0).reshape(2, 128, 16, 128).transpose(1, 0, 2, 3)
    biasT = np.ascontiguousarray(biasT.reshape(P, 2 * 16 * 128), dtype=f32)
    return {"wst": wst, "cst": cst, "csu": csu, "gain": gain, "biasT": biasT}


_CACHE = {}


def get_l0_order():
    if "order" not in _CACHE:
        rec = []
        build_program(1, 4, record=rec)
        order = []
        for ci in rec:
            if ci not in order:
                order.append(ci)
        assert sorted(order) == list(range(34)), order
        _CACHE["order"] = order
    return _CACHE["order"]


def kernel(**inputs):
    x = np.asarray(inputs["x"], np.float32)
    bsz, s, d = x.shape
    per = bsz // N_CORES
    order = get_l0_order()
    shared = prepare_shared(inputs, order)
    n_tiles = per * s // T
    key = (n_tiles, s // T)
    if key not in _CACHE:
        _CACHE[key] = build_program(n_tiles, s // T, l0_order=order)
    nc = _CACHE[key]
    in_maps = []
    for c in range(N_CORES):
        m = dict(shared)
        m["x"] = np.ascontiguousarray(x[c * per:(c + 1) * per].reshape(per * s, d))
        in_maps.append(m)
    res = run_bass_kernel_spmd(nc, in_maps, core_ids=list(range(N_CORES)))
    outs = [np.asarray(r["out"], np.float32).reshape(per, s, d) for r in res.results]
    return np.concatenate(outs, axis=0)
```

```python
import numpy as np
from contextlib import ExitStack
import concourse.bass as bass
import concourse.mybir as mybir
from concourse.bass_utils import run_bass_kernel_spmd

F32 = mybir.dt.float32
BF16 = mybir.dt.bfloat16
AF = mybir.ActivationFunctionType
ALU = mybir.AluOpType

P = 128
T = 512
NB = 4
D = 1024
KC = 8
EPS = 1e-6
N_CORES = 8
SEQ = 2048
TPS = SEQ // T
NSLOT = 5
SLOT = 4096
NG_L0_IN = 9
NG_TILE = 29

C_CONVW = 0
C_CONVB = 32
C_GAB = 40
C_GXB = 48
C_LAM = 56
C_QG = 64
C_KG = 65
C_SINK = 66
C_LNG = 74
C_LNB = 90
C_ONESB = 128
C_TOTAL = 256
U_GATEW = 0
U_SPW = 2048
U_SPB = 3072
U_AMASK = 4096
U_CMASK = 4352
U_IDENT = 4480
U_TOTAL = 4608


class Sched:
    ENG = ("pe", "act", "dve", "pool", "sp")

    def __init__(self):
        self.q = {e: [] for e in self.ENG}
        self.cnt = {e: 0 for e in self.ENG}
        self.dcnt = {}
        self.lastw = {}
        self.readers = {}
        self.waited = {e: {} for e in self.ENG}
        self.nwaits = 0

    def _deps(self, eng, reads, writes, is_dma):
        need = {}

        def add(t, kind):
            if t[0] == "E" and t[1] == eng and not is_dma:
                if eng == "pe":
                    return
                if kind != "raw" and t[2] != self.cnt[eng]:
                    return
            sk = (t[0], t[1])
            if need.get(sk, 0) < t[2]:
                need[sk] = t[2]

        for k in reads:
            t = self.lastw.get(k)
            if t is not None:
                add(t, "raw")
        for k in writes:
            t = self.lastw.get(k)
            if t is not None:
                add(t, "waw")
            for sk, v in self.readers.get(k, {}).items():
                add((sk[0], sk[1], v), "war")
        waits = []
        wd = self.waited[eng]
        for sk, v in need.items():
            if wd.get(sk, 0) >= v:
                continue
            wd[sk] = v
            waits.append((sk, v))
        self.nwaits += len(waits)
        return waits

    def _commit(self, tok, reads, writes):
        for k in writes:
            self.lastw[k] = tok
            self.readers[k] = {}
        sk = (tok[0], tok[1])
        for k in reads:
            r = self.readers.setdefault(k, {})
            if r.get(sk, 0) < tok[2]:
                r[sk] = tok[2]

    def op(self, eng, fn, reads=(), writes=()):
        waits = self._deps(eng, reads, writes, False)
        self.cnt[eng] += 1
        tok = ("E", eng, self.cnt[eng])
        self.q[eng].append((waits, fn, ("E", eng), 1))
        self._commit(tok, reads, writes)
        return tok

    def dma(self, qeng, sem, fn, reads=(), writes=()):
        waits = self._deps(qeng, reads, writes, True)
        self.dcnt[sem] = self.dcnt.get(sem, 0) + 1
        tok = ("D", sem, 16 * self.dcnt[sem])
        self.q[qeng].append((waits, fn, ("D", sem), 16))
        self._commit(tok, reads, writes)
        return tok

    def final_wait(self, eng, toks):
        waits = [((t[0], t[1]), t[2]) for t in toks]
        self.q[eng].append((waits, None, None, 0))

    def sem_keys(self):
        keys = [("E", e) for e in self.ENG]
        keys += [("D", s) for s in self.dcnt]
        return keys

    def emit(self, nc, sems):
        with nc.Block() as block:
            decos = {"pe": block.tensor, "act": block.scalar, "dve": block.vector,
                     "pool": block.gpsimd, "sp": block.sync}
            for e in self.ENG:
                items = self.q[e]

                def body(engh, items=items):
                    for waits, fn, inc, amt in items:
                        for sk, v in waits:
                            engh.wait_ge(sems[sk], v)
                        if fn is None:
                            continue
                        ins = fn(engh)
                        ins.then_inc(sems[inc], amt)

                decos[e](body)


def zipper(gens, depth):
    gens = list(gens)
    active = []
    i = 0
    while active or i < len(gens):
        while len(active) < depth and i < len(gens):
            active.append(gens[i])
            i += 1
        for g in list(active):
            try:
                next(g)
            except StopIteration:
                active.remove(g)


def multi_zipper(streams):
    st = [{"gens": list(s_[0]), "depth": s_[1], "rate": (s_[2] if len(s_) > 2 else 1), "i": 0, "active": []}
          for s_ in streams]
    while any(x["active"] or x["i"] < len(x["gens"]) for x in st):
        for x in st:
            for _ in range(x["rate"]):
                while len(x["active"]) < x["depth"] and x["i"] < len(x["gens"]):
                    x["active"].append(x["gens"][x["i"]])
                    x["i"] += 1
                for g in list(x["active"]):
                    try:
                        next(g)
                    except StopIteration:
                        x["active"].remove(g)


def build_program(n_tiles, tps=TPS, n_layers=2, dbg=0, l0_order=None, record=None):
    nc = bass.Bass("TRN2", target_bir_lowering=False)
    ntok = n_tiles * T
    x_d = nc.dram_tensor("x", [ntok, D], F32, kind="ExternalInput").ap()
    wst_d = nc.dram_tensor("wst", [NG_TILE, P, SLOT], F32, kind="ExternalInput").ap()
    cst_d = nc.dram_tensor("cst", [P, C_TOTAL], F32, kind="ExternalInput").ap()
    csu_d = nc.dram_tensor("csu", [P, U_TOTAL], F32, kind="ExternalInput").ap()
    gain_d = nc.dram_tensor("gain", [2, P, D], F32, kind="ExternalInput").ap()
    bias_d = nc.dram_tensor("biasT", [P, 2 * 16 * 128], F32, kind="ExternalInput").ap()
    out_d = nc.dram_tensor("out", [ntok, D], F32, kind="ExternalOutput").ap()

    S = Sched()
    st = ExitStack()
    with st:
        def sb(name, shape, dt=F32):
            return st.enter_context(nc.sbuf_tensor("sb_" + name, shape, dt))

        NXB = 2
        xbuf = [sb("xbuf%d" % i, [P, NB, D]) for i in range(NXB)]
        hT = sb("hT", [P, KC, T], BF16)
        ss = sb("ss", [P, 8])
        rs = sb("rs", [P, 8])
        ring = sb("ring", [P, NSLOT, SLOT], BF16)
        cst = sb("cst", [P, C_TOTAL])
        gain = sb("gain", [P, 2, D])
        Et = sb("Et", [P, 2, 16, 128], BF16)
        vz = sb("vz", [P, NB + 1, 2, 128], BF16)
        kz = sb("kz", [P, 2, (NB + 1) * 128], BF16)
        onesz = sb("onesz", [P, 2, 128], BF16)
        qnT = sb("qnT", [P, 8, T], BF16)
        Gs = sb("Gs", [P, 8, T], BF16)
        sqb = [sb("sqb%d" % i, [P, T]) for i in range(2)]
        rsb = [sb("rsb%d" % i, [P, T]) for i in range(2)]
        expS = [sb("expS%d" % i, [P, T]) for i in range(2)]
        PT = [sb("PT%d" % i, [P, T], BF16) for i in range(4)]
        den = [sb("den0", [P, T]), expS[0]]
        yb = [sb("yb0", [P, T]), expS[1]]
        denk = [("den", 0), ("expS", 0)]
        ybk = [("yb", 0), ("expS", 1)]
        junk = den[0][:].bitcast(BF16)
        yT = sb("yT", [P, 16, T], BF16)
        Us = [sb("U%d" % i, [P, 8, T]) for i in range(2)]
        zbuf = [sb("zbuf%d" % i, [P, T + 4]) for i in range(2)]
        xabf = [sb("xabf%d" % i, [P, T], BF16) for i in range(2)]
        halo = sb("halo", [P, 8, 4])
        hstate = sb("hstate", [P, 8])
        Cc = sb("Cc", [P, 16, 128])
        stats = sb("stats", [P, 4, 6])
        mv = sb("mv", [P, 2])
        lnr = sb("lnr", [P, 2])
        kcol = sb("kcol", [P, 64])
        ident_bf = sb("ident_bf", [P, 128], BF16)
        gw_bf = sb("gw_bf", [P, 2, 8, 128], BF16)
        wtril_bf = sb("wtril_bf", [P, 8, 128], BF16)

        NPS = 8
        ps = [st.enter_context(nc.psum_tensor("ps%d" % i, [P, 512], F32)) for i in range(NPS)]
        U = Us[0]
        hb = Gs[:].rearrange("p (b j) t -> p b (j t)", b=NB)
        hbk = lambda b: [("Gs", 2 * b), ("Gs", 2 * b + 1)]
        wtril_f = U[:, 0:2, :].rearrange("p a t -> p (a t)").rearrange("p (g t) -> p g t", g=8)
        vhat = U[:].rearrange("p a t -> p (a t)").bitcast(BF16).rearrange("p (b f) -> p b f", b=NB)
        xflat = xbuf[0][:].rearrange("p b d -> p (b d)")
        usu = U[:, 7, :]
        x0keys = [("x0", b) for b in range(NB)]

        free_banks = list(range(NPS))

        def nbank():
            assert free_banks, "PSUM banks exhausted"
            return free_banks.pop(0)

        def getbank():
            n = 0
            while not free_banks:
                n += 1
                assert n < 10000, "PSUM bank wait deadlock at build time"
                yield
            return free_banks.pop(0)

        def rel(bk):
            assert bk not in free_banks
            free_banks.append(bk)

        def col(c, n=1):
            return cst[:, c:c + n]

        S.dma("sp", "c0", lambda e: e.dma_start(out=cst[:], in_=cst_d), writes=["cst"])
        S.dma("sp", "c1", lambda e: e.dma_start(out=gain[:], in_=gain_d.rearrange("g p d -> p g d")),
              writes=["gain"])
        x1flat = xbuf[1][:].rearrange("p b d -> p (b d)")
        x1keys = [("x1", b) for b in range(NB)]
        S.dma("sp", "c2", lambda e: e.dma_start(out=x1flat, in_=bias_d), writes=x1keys)
        S.dma("sp", "c3", lambda e: e.dma_start(out=xflat, in_=csu_d[:, 0:4096]), writes=x0keys)
        S.dma("sp", "c4", lambda e: e.dma_start(out=usu, in_=csu_d[:, 4096:U_TOTAL]), writes=[("U0", 7)])

        S.op("dve", lambda e: e.tensor_copy(out=ident_bf[:], in_=usu[:, 384:512]),
             reads=[("U0", 7)], writes=["ident_bf"])
        S.op("dve", lambda e: e.memset(onesz[:], 0.0), writes=["onesz"])
        S.op("dve", lambda e: e.memset(onesz[:, 0, 0:64], 1.0), writes=["onesz"])
        S.op("dve", lambda e: e.memset(onesz[:, 1, 64:128], 1.0), writes=["onesz"])
        S.op("dve", lambda e: e.memset(kz[:], 0.0), writes=["kprev", "kcur"])
        S.op("dve", lambda e: e.memset(vz[:], 0.0), writes=["vprev", "vcur"])
        S.op("dve", lambda e: e.tensor_copy(out=gw_bf[:].rearrange("p a n j -> p (a n j)"),
                                            in_=xflat[:, U_GATEW:U_GATEW + 2048]),
             reads=x0keys, writes=["gw_bf"])
        for kt in range(2):
            def f(e, kt=kt):
                m = usu[:, kt * 128:(kt + 1) * 128]
                v = x1flat[:, kt * 2048:(kt + 1) * 2048].rearrange("p (h q) -> p h q", h=16)
                return e.scalar_tensor_tensor(out=v, in0=v, scalar=8.0,
                                              in1=m.unsqueeze(1).to_broadcast([P, 16, 128]),
                                              op0=ALU.mult, op1=ALU.mult)
            S.op("dve", f, reads=x1keys + [("U0", 7)], writes=x1keys)
        S.op("dve", lambda e: e.tensor_scalar(out=usu[:, 0:256], in0=usu[:, 0:256], scalar1=-1.0, scalar2=480.0,
                                              op0=ALU.add, op1=ALU.mult),
             reads=[("U0", 7)], writes=[("U0", 7)])
        for kt in range(2):
            def f(e, kt=kt):
                m = usu[:, kt * 128:(kt + 1) * 128]
                v = x1flat[:, kt * 2048:(kt + 1) * 2048].rearrange("p (h q) -> p h q", h=16)
                return e.tensor_tensor(out=Et[:, kt], in0=v, in1=m.unsqueeze(1).to_broadcast([P, 16, 128]),
                                       op=ALU.add)
            S.op("dve", f, reads=x1keys + [("U0", 7)], writes=["Et"])
        S.op("act", lambda e: e.activation(out=kcol[:, 16:24], in_=col(C_SINK, 8), func=AF.Exp),
             reads=["cst"], writes=["kc_sink"])
        S.op("dve", lambda e: e.tensor_scalar(out=kcol[:, 32:40], in0=col(C_GAB, 8), scalar1=0.5, scalar2=None,
                                              op0=ALU.mult), reads=["cst"], writes=["kc_hb"])
        S.op("dve", lambda e: e.tensor_scalar(out=kcol[:, 40:48], in0=col(C_GXB, 8), scalar1=0.5, scalar2=None,
                                              op0=ALU.mult), reads=["cst"], writes=["kc_hb"])
        S.op("act", lambda e: e.activation(out=kcol[:, 24:32], in_=col(C_LAM, 8), func=AF.Exp, scale=-1.0),
             reads=["cst"], writes=["kc_e"])
        S.op("dve", lambda e: e.tensor_scalar(out=kcol[:, 48:56], in0=kcol[:, 24:32], scalar1=2.0,
                                              scalar2=None, op0=ALU.add),
             reads=["kc_e"], writes=["kc_t"])
        S.op("dve", lambda e: e.reciprocal(out=kcol[:, 48:56], in_=kcol[:, 48:56]),
             reads=["kc_t"], writes=["kc_t"])
        S.op("dve", lambda e: e.tensor_tensor(out=kcol[:, 24:32], in0=kcol[:, 24:32], in1=kcol[:, 48:56],
                                              op=ALU.mult),
             reads=["kc_t", "kc_e"], writes=["kc_e"])
        S.op("dve", lambda e: e.tensor_tensor(out=kcol[:, 48:56], in0=kcol[:, 24:32], in1=kcol[:, 24:32],
                                              op=ALU.mult),
             reads=["kc_e"], writes=["kc_t"])
        S.op("dve", lambda e: e.tensor_scalar(out=kcol[:, 0:8], in0=kcol[:, 48:56], scalar1=1.0 / 9,
                                              scalar2=1.0 / 7, op0=ALU.mult, op1=ALU.add),
             reads=["kc_t"], writes=["kc_p"])
        for cc in (1.0 / 5, 1.0 / 3, 1.0):
            S.op("dve", lambda e: e.tensor_tensor(out=kcol[:, 0:8], in0=kcol[:, 0:8], in1=kcol[:, 48:56],
                                                  op=ALU.mult),
                 reads=["kc_p", "kc_t"], writes=["kc_p"])
            S.op("dve", lambda e, cc=cc: e.tensor_scalar(out=kcol[:, 0:8], in0=kcol[:, 0:8], scalar1=cc,
                                                         scalar2=None, op0=ALU.add),
                 reads=["kc_p"], writes=["kc_p"])
        S.op("dve", lambda e: e.tensor_tensor(out=kcol[:, 0:8], in0=kcol[:, 0:8], in1=kcol[:, 24:32],
                                              op=ALU.mult),
             reads=["kc_p", "kc_e"], writes=["kc_p"])
        S.op("dve", lambda e: e.tensor_scalar(out=kcol[:, 8:16], in0=kcol[:, 0:8], scalar1=-16.0,
                                              scalar2=None, op0=ALU.mult),
             reads=["kc_p"], writes=["kc_K"])
        S.op("dve", lambda e: e.tensor_scalar(out=kcol[:, 0:8], in0=kcol[:, 0:8], scalar1=-8.0,
                                              scalar2=None, op0=ALU.mult),
             reads=["kc_p", "kc_K"], writes=["kc_K"])
        Kh = lambda blk: kcol[:, blk:blk + 1]
        Kf = lambda blk: kcol[:, 8 + blk:9 + blk]

        if n_layers > 1:
            def f(e):
                m = usu[:, 256:384]
                return e.tensor_tensor(out=wtril_f,
                                       in0=xflat[:, U_SPW:U_SPW + 1024].rearrange("p (g t) -> p g t", g=8),
                                       in1=m.unsqueeze(1).to_broadcast([P, 8, 128]), op=ALU.mult)
            S.op("dve", f, reads=x0keys + [("U0", 7)], writes=[("U0", 0), ("U0", 1)])
            S.op("dve", lambda e: e.tensor_copy(out=wtril_bf[:], in_=wtril_f),
                 reads=[("U0", 0), ("U0", 1)], writes=["wtril_bf"])
            S.op("dve", lambda e: e.memset(sqb[0][:, 0:128], 1.0), writes=["sqb0"])
            for g in range(8):
                bk = nbank()

                def f(e, g=g, bk=bk):
                    return e.matmul(ps[bk][:, 0:128], lhsT=sqb[0][:, 0:128], rhs=wtril_f[:, g, :],
                                    start=True, stop=True)
                S.op("pe", f, reads=["sqb0", ("U0", 0), ("U0", 1)], writes=[("ps", bk)])
                for jj in range(2):
                    j = 2 * g + jj

                    def f2(e, g=g, bk=bk, j=j):
                        return e.scalar_tensor_tensor(
                            out=Cc[:, j, :], in0=ps[bk][:, 0:128], scalar=col(C_LNB + j),
                            in1=xflat[:, U_SPB + g * 128:U_SPB + (g + 1) * 128], op0=ALU.mult, op1=ALU.add)
                    S.op("dve", f2, reads=[("ps", bk), "cst"] + x0keys, writes=["Cc"])
                rel(bk)

        GPT = NG_TILE if n_layers > 1 else 13
        rem_tile = [4] * 8 + [2] + [4] * 4 + ([4] * 16 if n_layers > 1 else [])
        seq_groups = [g for t in range(n_tiles) for g in range(GPT)]
        rem = [r for t in range(n_tiles) for r in rem_tile]
        rstate = {"next_load": 0, "oldest": 0}

        def _load():
            i = rstate["next_load"]
            rstate["next_load"] += 1
            if i >= len(seq_groups):
                return
            slot = i % NSLOT
            g = seq_groups[i]
            S.dma("pool", "w%d" % slot,
                  lambda e: e.dma_start(out=ring[:, slot, :], in_=wst_d[g], max_dma_last_dim=8192),
                  writes=[("w", slot)])

        for i in range(NSLOT):
            _load()

        def wslot(i):
            if record is not None:
                return i % NSLOT
            assert rstate["oldest"] <= i < rstate["next_load"], (i, rstate)
            return i % NSLOT

        def wdone(i, n=1):
            if record is not None:
                return
            rem[i] -= n
            assert rem[i] >= 0
            while rstate["oldest"] < len(rem) and rem[rstate["oldest"]] == 0:
                rstate["oldest"] += 1
                _load()

        def mm_group(out_ap, pairs, reads, writes):
            def f(e):
                ins = None
                n = len(pairs)
                for i, (l, r) in enumerate(pairs):
                    ins = e.matmul(out_ap, lhsT=l, rhs=r, start=(i == 0), stop=(i == n - 1))
                return ins
            S.op("pe", f, reads=reads, writes=writes)

        hT_keys = [("hT", kc) for kc in range(KC)]

        def norm_stats_blk(xb, xk, gi, b):
            S.op("act", lambda e: e.activation(out=junk, in_=xb[:, b, :], func=AF.Square,
                                               accum_out=ss[:, b:b + 1]),
                 reads=[(xk, b)], writes=[("ss", b), ("den", 0)])
            S.op("act", lambda e: e.activation(out=rs[:, b:b + 1], in_=ss[:, b:b + 1], func=AF.Sqrt,
                                               scale=1.0 / D, bias=EPS),
                 reads=[("ss", b)], writes=[("rs", b)])
            S.op("dve", lambda e: e.reciprocal(out=rs[:, b:b + 1], in_=rs[:, b:b + 1]),
                 reads=[("rs", b)], writes=[("rs", b)])
            for hf in range(2):
                S.op("dve", lambda e, hf=hf: e.scalar_tensor_tensor(
                    out=hb[:, b, hf * 512:(hf + 1) * 512], in0=xb[:, b, hf * 512:(hf + 1) * 512],
                    scalar=rs[:, b:b + 1], in1=gain[:, gi, hf * 512:(hf + 1) * 512],
                    op0=ALU.mult, op1=ALU.mult),
                    reads=[(xk, b), ("rs", b), "gain"], writes=[("Gs", 2 * b + hf)])

        def norm_stats(xb, xk, gi):
            for b in range(NB):
                norm_stats_blk(xb, xk, gi, b)

        def norm_tr():
            for pi in range(KC // 2):
                bkt = nbank()
                ptb = ps[bkt][:, :].bitcast(BF16)

                def f(e, pi=pi, ptb=ptb):
                    ins = None
                    for kk in range(2):
                        kc = 2 * pi + kk
                        for b in range(NB):
                            ins = e.transpose(out=ptb[:, kk * 512 + b * 128: kk * 512 + (b + 1) * 128],
                                              in_=hb[:, b, kc * 128:(kc + 1) * 128], identity=ident_bf[:])
                    return ins
                S.op("pe", f, reads=[("Gs", 2 * b + pi // 2) for b in range(NB)] + ["ident_bf"],
                     writes=[("ps", bkt)])
                S.op("act", lambda e, pi=pi, ptb=ptb: e.copy(
                    out=hT[:, 2 * pi:2 * pi + 2, :].rearrange("p k t -> p (k t)"), in_=ptb[:, :]),
                    reads=[("ps", bkt)], writes=[("hT", 2 * pi), ("hT", 2 * pi + 1)])
                rel(bkt)

        def norm_T(xb, xk, gi):
            norm_stats(xb, xk, gi)
            norm_tr()

        def inproj_chunk(gi, pos, bk=None):
            slot = wslot(gi)
            if bk is None:
                bk = nbank()
            wv = ring[:, slot, :].rearrange("p (c k n) -> p c k n", c=4, k=KC)
            mm_group(ps[bk][:, :], [(wv[:, pos, kc, :], hT[:, kc, :]) for kc in range(KC)],
                     reads=[("w", slot)] + hT_keys, writes=[("ps", bk)])
            wdone(gi)
            return bk

        def out_proj(xb, xk, gbase, pre=None, blk_hook=None):
            if pre is not None:
                pre[0]()

            def mm(b, jg, gi, bank):
                slot = wslot(gi)
                wv = ring[:, slot, :].rearrange("p (j n) -> p j n", j=8)

                def f(e):
                    ins = None
                    for jj in range(8):
                        j = jg * 8 + jj
                        ins = e.matmul(ps[bank][:, :], lhsT=yT[:, j, b * 128:(b + 1) * 128],
                                       rhs=wv[:, jj, :], start=(j == 0), stop=(j == 15))
                    return ins
                S.op("pe", f, reads=[("w", slot)] + [("yT", jg * 8 + jj) for jj in range(8)],
                     writes=[("ps", bank)])
                wdone(gi)

            def add(b, half, bank):
                S.op("dve", lambda e: e.tensor_tensor(
                    out=xb[:, b, half * 512:(half + 1) * 512], in0=ps[bank][:, :],
                    in1=xb[:, b, half * 512:(half + 1) * 512], op=ALU.add),
                    reads=[("ps", bank), (xk, b)], writes=[(xk, b)])
                rel(bank)

            banks = [nbank() for _ in range(NB)]
            for jg in range(2):
                for b in range(NB):
                    mm(b, jg, gbase + jg, banks[b])
            for b in range(NB):
                add(b, 0, banks[b])
            if pre is not None:
                pre[1]()
            if blk_hook is None:
                banks = [nbank() for _ in range(NB)]
                for jg in range(2):
                    for b in range(NB):
                        mm(b, jg, gbase + 2 + jg, banks[b])
                for b in range(NB):
                    add(b, 1, banks[b])
            else:
                for b in range(NB):
                    bank = nbank()
                    for jg in range(2):
                        mm(b, jg, gbase + 2 + jg, bank)
                    add(b, 1, bank)
                    blk_hook(b)

        def layer0(t, xb, xk, gb0, pre=None):
            first = (t % tps == 0)
            if first:
                S.op("pool", lambda e: e.memset(halo[:], 0.0), writes=[("halo", i) for i in range(8)])
                S.op("pool", lambda e: e.memset(hstate[:], 0.0), writes=[("hstate", i) for i in range(8)])
            def cgrp(ci):
                if record is not None:
                    record.append(ci)
                    return (gb0, 0)
                pos = l0_order.index(ci)
                return (gb0 + pos // 4, pos % 4)

            gi, pos = cgrp(0)
            slot = wslot(gi)
            wv = ring[:, slot, :].rearrange("p (c k n) -> p c k n", c=4, k=KC)
            bkv = nbank()
            for b in range(NB):
                mm_group(ps[bkv][:, b * 128:(b + 1) * 128],
                         [(hT[:, kc, b * 128:(b + 1) * 128], wv[:, 0, kc, :]) for kc in range(KC)],
                         reads=[("w", slot)] + hT_keys, writes=[("ps", bkv)])
            wdone(gi)
            for kv in range(2):
                S.op("act", lambda e, kv=kv: e.copy(
                    out=vz[:, 1:NB + 1, kv, kv * 64:(kv + 1) * 64],
                    in_=ps[bkv][:, :].rearrange("p (b n) -> p b n", b=NB)[:, :, kv * 64:(kv + 1) * 64]),
                    reads=[("ps", bkv)], writes=["vcur"])
            rel(bkv)

            def gen_qk(ci, gcol, out_ap, outkey, idx):
                sq, rr = sqb[idx % 2], rsb[idx % 2]
                sqk, rrk = "sqb%d" % (idx % 2), "rsb%d" % (idx % 2)
                bk = yield from getbank()
                inproj_chunk(*cgrp(ci), bk=bk)
                yield
                S.op("act", lambda e: e.activation(out=sq[:], in_=ps[bk][:, :], func=AF.Square),
                     reads=[("ps", bk)], writes=[sqk])
                yield
                b2 = yield from getbank()
                S.op("pe", lambda e: e.matmul(ps[b2][:, :], lhsT=col(C_ONESB, 128), rhs=sq[:],
                                              start=True, stop=True),
                     reads=[sqk, "cst"], writes=[("ps", b2)])
                yield
                S.op("act", lambda e: e.activation(out=rr[:], in_=ps[b2][:, :], func=AF.Ln, scale=1.0 / 64,
                                                   bias=EPS),
                     reads=[("ps", b2)], writes=[rrk])
                rel(b2)
                yield
                S.op("act", lambda e: e.activation(out=rr[:], in_=rr[:], func=AF.Exp, scale=-0.5),
                     reads=[rrk], writes=[rrk])
                yield
                if out_ap is None:
                    for kv in range(2):
                        pr = slice(kv * 64, (kv + 1) * 64)
                        S.op("dve", lambda e, kv=kv, pr=pr: e.scalar_tensor_tensor(
                            out=kz[pr, kv, 128:], in0=ps[bk][pr, :], scalar=cst[pr, gcol:gcol + 1],
                            in1=rr[pr, :], op0=ALU.mult, op1=ALU.mult),
                            reads=[("ps", bk), rrk, "cst"], writes=[outkey])
                else:
                    S.op("dve", lambda e: e.scalar_tensor_tensor(out=out_ap, in0=ps[bk][:, :], scalar=col(gcol),
                                                                 in1=rr[:], op0=ALU.mult, op1=ALU.mult),
                         reads=[("ps", bk), rrk, "cst"], writes=[outkey])
                rel(bk)
                yield

            def gen_gb(c):
                bk = yield from getbank()
                inproj_chunk(*cgrp(10 + c), bk=bk)
                yield
                S.op("act", lambda e: e.activation(out=Gs[:, c, :], in_=ps[bk][:, :], func=AF.Silu),
                     reads=[("ps", bk)], writes=[("Gs", c)])
                rel(bk)
                yield

            gens = [gen_qk(1, C_KG, None, "kcur", 0)]
            gens += [gen_qk(2 + c, C_QG, qnT[:, c, :], ("qn", c), 1 + c) for c in range(8)]
            gens += [gen_gb(c) for c in range(8)]
            b_gens = gens

            def gen_att_kv(b, quad, kv, bo, bd, stt, u):
                pi_ = (u % 2) * 2 + kv
                pT = PT[pi_]
                pk = ("PT", pi_)
                kts = [1] if (first and b == 0) else [0, 1]
                for kt in kts:
                    keyblk = b + kt
                    bs_ = yield from getbank()
                    kkey = "kprev" if keyblk == 0 else "kcur"
                    vkey = "vprev" if keyblk == 0 else "vcur"
                    h0 = kv * 8 + quad * 4

                    def fS(e, keyblk=keyblk, bs_=bs_, kt=kt, h0=h0):
                        o = ps[bs_][:, :].rearrange("p (c q) -> p c q", c=4)
                        e.matmul(o, lhsT=kz[:, kv, keyblk * 128:(keyblk + 1) * 128],
                                 rhs=qnT[:, quad * 4:(quad + 1) * 4, b * 128:(b + 1) * 128],
                                 start=True, stop=False)
                        return e.matmul(o, lhsT=ident_bf[:], rhs=Et[:, kt, h0:h0 + 4, :], start=False, stop=True)
                    S.op("pe", fS, reads=[kkey, "Et", "ident_bf"] + [("qn", quad * 4 + cc) for cc in range(4)],
                         writes=[("ps", bs_)])
                    yield
                    S.op("act", lambda e, bs_=bs_: e.activation(
                        out=pT[:], in_=ps[bs_][:, :], func=AF.Exp, scale=0.125),
                        reads=[("ps", bs_)], writes=[pk])
                    rel(bs_)
                    yield
                    st_ = (stt["n"] == 0)
                    sp_ = (stt["n"] == stt["total"] - 1)
                    stt["n"] += 1
                    S.op("pe", lambda e, keyblk=keyblk, st_=st_, sp_=sp_: e.matmul(
                        ps[bo][:, :], lhsT=vz[:, keyblk, kv, :], rhs=pT[:], start=st_, stop=sp_),
                        reads=[vkey, pk], writes=[("ps", bo)])
                    S.op("pe", lambda e, st_=st_, sp_=sp_: e.matmul(
                        ps[bd][:, :], lhsT=onesz[:, kv, :], rhs=pT[:], start=st_, stop=sp_),
                        reads=["onesz", pk], writes=[("ps", bd)])
                    yield

            def gen_att(b, quad, u):
                bo = yield from getbank()
                bd = yield from getbank()
                dn, y_ = den[u % 2], yb[u % 2]
                dk, yk = denk[u % 2], ybk[u % 2]
                nk = 1 if (first and b == 0) else 2
                stt = {"n": 0, "total": 2 * nk}
                subs = [gen_att_kv(b, quad, kv, bo, bd, stt, u) for kv in range(2)]
                while subs:
                    for g in list(subs):
                        try:
                            next(g)
                        except StopIteration:
                            subs.remove(g)
                    yield
                S.op("dve", lambda e: e.tensor_tensor(
                    out=dn[:].rearrange("p (c q) -> p c q", c=4),
                    in0=ps[bd][:, :].rearrange("p (c q) -> p c q", c=4),
                    in1=kcol[:, 16 + quad * 4:16 + (quad + 1) * 4].unsqueeze(2).to_broadcast([P, 4, 128]),
                    op=ALU.add),
                    reads=[("ps", bd), "kc_sink"], writes=[dk])
                rel(bd)
                yield
                S.op("act", lambda e: e.activation(out=dn[:], in_=dn[:], func=AF.Ln), reads=[dk], writes=[dk])
                yield
                S.op("act", lambda e: e.activation(out=dn[:], in_=dn[:], func=AF.Exp, scale=-1.0),
                     reads=[dk], writes=[dk])
                yield
                S.op("dve", lambda e: e.tensor_tensor(out=y_[:], in0=ps[bo][:, :], in1=dn[:], op=ALU.mult),
                     reads=[("ps", bo), dk], writes=[yk])
                rel(bo)
                yield
                S.op("dve", lambda e: e.tensor_tensor(
                    out=yT[:, 8 + quad * 4:8 + (quad + 1) * 4, b * 128:(b + 1) * 128],
                    in0=y_[:].rearrange("p (c q) -> p c q", c=4),
                    in1=Gs[:, quad * 4:(quad + 1) * 4, b * 128:(b + 1) * 128], op=ALU.mult),
                    reads=[yk] + [("Gs", quad * 4 + cc) for cc in range(4)],
                    writes=[("yT", 8 + quad * 4 + cc) for cc in range(4)])
                yield

            def gen_att_tail():
                S.op("pool", lambda e: e.tensor_copy(out=kz[:, :, 0:128], in_=kz[:, :, NB * 128:(NB + 1) * 128]),
                     reads=["kcur"], writes=["kprev"])
                S.op("pool", lambda e: e.tensor_copy(out=vz[:, 0, :, :], in_=vz[:, NB, :, :]),
                     reads=["vcur"], writes=["vprev"])
                yield

            units = [(b, quad) for b in range(NB) for quad in range(2)]
            att_gens = [gen_att(b, quad, u) for u, (b, quad) in enumerate(units)] + [gen_att_tail()]
            def gen_A():
                active = []
                i = 0
                while active or i < len(b_gens):
                    while len(active) < 2 and i < len(b_gens):
                        active.append(b_gens[i])
                        i += 1
                    for g in list(active):
                        try:
                            next(g)
                        except StopIteration:
                            active.remove(g)
                    yield
                active = []
                i = 0
                while active or i < len(att_gens):
                    while len(active) < 2 and i < len(att_gens):
                        active.append(att_gens[i])
                        i += 1
                    for g in list(active):
                        try:
                            next(g)
                        except StopIteration:
                            active.remove(g)
                    yield

            def gen_rg(blk):
                si = blk % 2
                Ub = Us[si]
                zb, xb_ = zbuf[si], xabf[si]
                t_r, t_a, t_m, t_i, t_b, t_h, t_s, t_x = [Ub[:, i, :] for i in range(8)]
                Uk = [("U%d" % si, i) for i in range(8)]
                zh, zm, xk_ = ("zb_h", si), ("zb_m", si), ("xabf", si)
                bz = yield from getbank()
                inproj_chunk(*cgrp(18 + 2 * blk), bk=bz)
                S.op("pool", lambda e: e.tensor_copy(out=zb[:, 0:4], in_=halo[:, blk, :]),
                     reads=[("halo", blk)], writes=[zh])
                yield
                S.op("act", lambda e: e.copy(out=zb[:, 4:T + 4], in_=ps[bz][:, :]),
                     reads=[("ps", bz)], writes=[zm])
                S.op("dve", lambda e: e.tensor_scalar(
                    out=t_x, in0=zb[:, 4:T + 4], scalar1=col(C_CONVW + blk * 4 + 3), scalar2=col(C_CONVB + blk),
                    op0=ALU.mult, op1=ALU.add),
                    reads=[zm, "cst"], writes=[Uk[7]])
                rel(bz)
                yield
                for k in (1, 2, 3):
                    S.op("dve", lambda e, k=k: e.scalar_tensor_tensor(
                        out=t_x, in0=zb[:, 4 - k:T + 4 - k], scalar=col(C_CONVW + blk * 4 + 3 - k), in1=t_x,
                        op0=ALU.mult, op1=ALU.add),
                        reads=[zh, zm, Uk[7], "cst"], writes=[Uk[7]])
                    yield
                S.op("pool", lambda e: e.tensor_copy(out=halo[:, blk, :], in_=zb[:, T:T + 4]),
                     reads=[zm, zh], writes=[("halo", blk)])
                S.op("dve", lambda e: e.tensor_copy(out=xb_[:], in_=t_x), reads=[Uk[7]], writes=[xk_])
                yield
                br = yield from getbank()
                S.op("pe", lambda e: e.matmul(ps[br][:, :], lhsT=gw_bf[:, 0, blk, :], rhs=xb_[:],
                                              start=True, stop=True),
                     reads=["gw_bf", xk_], writes=[("ps", br)])
                bi = yield from getbank()
                S.op("pe", lambda e: e.matmul(ps[bi][:, :], lhsT=gw_bf[:, 1, blk, :], rhs=xb_[:],
                                              start=True, stop=True),
                     reads=["gw_bf", xk_], writes=[("ps", bi)])
                yield
                S.op("act", lambda e: e.activation(out=t_r, in_=ps[br][:, :], func=AF.Tanh, scale=0.5,
                                                   bias=kcol[:, 32 + blk:33 + blk]),
                     reads=[("ps", br), "kc_hb"], writes=[Uk[0]])
                rel(br)
                yield
                S.op("act", lambda e: e.activation(out=t_i, in_=ps[bi][:, :], func=AF.Tanh, scale=0.5,
                                                   bias=kcol[:, 40 + blk:41 + blk]),
                     reads=[("ps", bi), "kc_hb"], writes=[Uk[3]])
                rel(bi)
                yield
                S.op("act", lambda e: e.activation(out=t_a, in_=t_r, func=AF.Exp, scale=Kh(blk), bias=Kh(blk)),
                     reads=[Uk[0], "kc_K"], writes=[Uk[1]])
                yield
                S.op("dve", lambda e: e.tensor_tensor(out=t_m, in0=t_a, in1=t_a, op=ALU.mult),
                     reads=[Uk[1]], writes=[Uk[2]])
                yield
                S.op("dve", lambda e: e.scalar_tensor_tensor(out=t_b, in0=t_i, scalar=1.0, in1=t_x,
                                                             op0=ALU.add, op1=ALU.mult),
                     reads=[Uk[3], Uk[7]], writes=[Uk[4]])
                yield
                S.op("dve", lambda e: e.tensor_scalar(out=t_m, in0=t_m, scalar1=1.0, scalar2=-1.0,
                                                      op0=ALU.min, op1=ALU.mult),
                     reads=[Uk[2]], writes=[Uk[2]])
                yield
                S.op("act", lambda e: e.activation(out=t_m, in_=t_m, func=AF.Sqrt, bias=1.0),
                     reads=[Uk[2]], writes=[Uk[2]])
                yield
                S.op("dve", lambda e: e.scalar_tensor_tensor(out=t_b, in0=t_b, scalar=0.5, in1=t_m,
                                                             op0=ALU.mult, op1=ALU.mult),
                     reads=[Uk[4], Uk[2]], writes=[Uk[4]])
                yield
                bg = yield from getbank()
                inproj_chunk(*cgrp(19 + 2 * blk), bk=bg)
                yield
                S.op("act", lambda e: e.activation(out=t_s, in_=ps[bg][:, :], func=AF.Tanh, scale=0.5),
                     reads=[("ps", bg)], writes=[Uk[6]])
                yield
                S.op("dve", lambda e: e.tensor_tensor_scan(
                    out=t_h, data0=t_a, data1=t_b, initial=hstate[:, blk:blk + 1], op0=ALU.mult, op1=ALU.add),
                    reads=[Uk[1], Uk[4], ("hstate", blk)], writes=[Uk[5]])
                yield
                S.op("pool", lambda e: e.tensor_copy(out=hstate[:, blk:blk + 1], in_=Ub[:, 5, T - 1:T]),
                     reads=[Uk[5]], writes=[("hstate", blk)])
                S.op("dve", lambda e: e.scalar_tensor_tensor(out=t_s, in0=t_s, scalar=1.0, in1=ps[bg][:, :],
                                                             op0=ALU.add, op1=ALU.mult),
                     reads=[Uk[6], ("ps", bg)], writes=[Uk[6]])
                rel(bg)
                yield
                S.op("dve", lambda e: e.scalar_tensor_tensor(out=yT[:, blk, :], in0=t_h, scalar=0.5, in1=t_s,
                                                             op0=ALU.mult, op1=ALU.mult),
                     reads=[Uk[5], Uk[6]], writes=[("yT", blk)])
                yield

            multi_zipper([([gen_A()], 1, 1), ([gen_rg(blk) for blk in range(8)], 2, 1)])
            if dbg == 5:
                return
            if n_layers > 1 and dbg == 0:
                out_proj(xb, xk, gb0 + 9, pre, blk_hook=lambda b: norm_stats_blk(xb, xk, 1, b))
            else:
                out_proj(xb, xk, gb0 + 9, pre)

        def layer1(t, xb, xk, gb1, pre=None):
            norm_tr()
            vslots = [wslot(gb1 + vg) for vg in range(4)]
            for b in range(NB):
                banks = []
                for vg in range(4):
                    bk = nbank()
                    banks.append(bk)
                    wv = ring[:, vslots[vg], :].rearrange("p (k n) -> p k n", k=KC)
                    mm_group(ps[bk][:, :], [(hT[:, kc, b * 128:(b + 1) * 128], wv[:, kc, :]) for kc in range(KC)],
                             reads=[("w", vslots[vg])] + hT_keys, writes=[("ps", bk)])
                    wdone(gb1 + vg)
                    S.op("dve", lambda e, bk=bk, vg=vg: e.bn_stats(out=stats[:, vg, :], in_=ps[bk][:, :]),
                         reads=[("ps", bk)], writes=[("stats", vg)])
                S.op("dve", lambda e: e.bn_aggr(out=mv[:], in_=stats[:].rearrange("p a s -> p (a s)")),
                     reads=[("stats", vg) for vg in range(4)], writes=["mv"])
                S.op("act", lambda e: e.activation(out=lnr[:, 0:1], in_=mv[:, 1:2], func=AF.Sqrt, bias=EPS),
                     reads=["mv"], writes=["lnr"])
                S.op("dve", lambda e: e.reciprocal(out=lnr[:, 1:2], in_=lnr[:, 0:1]), reads=["lnr"],
                     writes=["lnr2"])
                for vg in range(4):
                    S.op("dve", lambda e, vg=vg, b=b, bk=banks[vg]: e.tensor_scalar(
                        out=vhat[:, b, vg * 512:(vg + 1) * 512], in0=ps[bk][:, :], scalar1=mv[:, 0:1],
                        scalar2=lnr[:, 1:2], op0=ALU.subtract, op1=ALU.mult),
                        reads=[("ps", banks[vg]), "mv", "lnr2"], writes=[("U0", 2 * b), ("U0", 2 * b + 1)])
                    rel(banks[vg])

            def gen_l1(j):
                g = j // 2
                si = j % 2
                S1, G1, Y1 = sqb[si], rsb[si], expS[si]
                s1k, g1k, y1k = "sqb%d" % si, "rsb%d" % si, ("expS", si)
                gi = gb1 + 4 + j // 2
                bu = yield from getbank()
                inproj_chunk(gi, (j % 2) * 2, bk=bu)
                yield
                bg = yield from getbank()
                inproj_chunk(gi, (j % 2) * 2 + 1, bk=bg)
                yield
                S.op("act", lambda e: e.activation(out=G1[:], in_=ps[bg][:, :], func=AF.Silu),
                     reads=[("ps", bg)], writes=[g1k])
                rel(bg)
                bsg = yield from getbank()
                for b in range(NB):
                    S.op("pe", lambda e, b=b: e.matmul(
                        ps[bsg][:, b * 128:(b + 1) * 128], lhsT=vhat[:, b, j * 128:(j + 1) * 128],
                        rhs=wtril_bf[:, g, :], start=True, stop=True),
                        reads=[("U0", 2 * b), ("U0", 2 * b + 1), "wtril_bf"], writes=[("ps", bsg)])
                yield
                S.op("dve", lambda e: e.scalar_tensor_tensor(
                    out=S1[:].rearrange("p (b t) -> p b t", b=NB),
                    in0=ps[bsg][:, :].rearrange("p (b t) -> p b t", b=NB), scalar=col(C_LNG + j),
                    in1=Cc[:, j:j + 1, :].to_broadcast([P, NB, 128]), op0=ALU.mult, op1=ALU.add),
                    reads=[("ps", bsg), "Cc", "cst"], writes=[s1k])
                rel(bsg)
                yield
                S.op("dve", lambda e: e.tensor_tensor(out=Y1[:], in0=ps[bu][:, :], in1=S1[:], op=ALU.mult),
                     reads=[("ps", bu), s1k], writes=[y1k])
                rel(bu)
                yield
                S.op("dve", lambda e: e.tensor_tensor(out=yT[:, j, :], in0=Y1[:], in1=G1[:], op=ALU.mult),
                     reads=[y1k, g1k], writes=[("yT", j)])
                yield

            zipper([gen_l1(j) for j in range(16)], 2)
            out_proj(xb, xk, gb1 + 12, pre)

        def load_x(t):
            xb = xbuf[t % NXB]
            xk = "x%d" % (t % NXB)
            S.dma("sp", "xl%d" % (t % NXB), lambda e: e.dma_start(
                out=xb[:], in_=x_d[t * T:(t + 1) * T, :].rearrange("(b p) d -> p b d", p=P)),
                writes=[(xk, b) for b in range(NB)])

        load_x(0)
        if dbg != 1:
            norm_T(xbuf[0], "x0", 0)
        for t in range(n_tiles):
            xb = xbuf[t % NXB]
            xk = "x%d" % (t % NXB)
            nxt = None
            if t + 1 < n_tiles:
                load_x(t + 1)
                if dbg != 1:
                    nxt = (lambda t=t: norm_stats(xbuf[(t + 1) % NXB], "x%d" % ((t + 1) % NXB), 0), norm_tr)
            if dbg != 1:
                if n_layers > 1 and dbg == 0:
                    layer0(t, xb, xk, t * GPT)
                    layer1(t, xb, xk, t * GPT + 13, nxt)
                else:
                    layer0(t, xb, xk, t * GPT, nxt)
            S.dma("sp", "xs%d" % (t % NXB), lambda e, t=t, xb=xb: e.dma_start(
                out=out_d[t * T:(t + 1) * T, :].rearrange("(b p) d -> p b d", p=P), in_=xb[:]),
                reads=[(xk, b) for b in range(NB)])
        S.final_wait("sp", [("D", sname, 16 * c) for sname, c in S.dcnt.items()])

        sems = {}
        for k in S.sem_keys():
            sems[k] = st.enter_context(nc.semaphore("s_%s_%s" % (k[0], k[1])))
        S.emit(nc, sems)
    return nc


def t5_bucket(dist):
    max_exact = 16
    df = np.maximum(dist, 1).astype(np.float32)
    large = max_exact + (np.log(df / np.float32(max_exact)) / np.float32(np.log(128 / 16))
                         * np.float32(16)).astype(np.int32)
    large = np.minimum(large, 31)
    return np.where(dist < max_exact, dist, large)


def prepare_shared(inp, l0_order):
    f32 = np.float32
    w_in_a = np.asarray(inp["w_in_a"], f32)[0]
    w_out_a = np.asarray(inp["w_out_a"], f32)[0]
    w_in_c = np.asarray(inp["w_in_c"], f32)[0]
    w_out_c = np.asarray(inp["w_out_c"], f32)[0]

    def chunk_in(w, cols):
        return w[:, cols].reshape(KC, P, len(cols)).transpose(1, 0, 2)

    o_za, o_ga, o_q, o_k, o_v, o_gb = 0, 1024, 2048, 3072, 3200, 3328
    ar = np.arange
    chunks = [chunk_in(w_in_a, o_v + ar(128)), chunk_in(w_in_a, o_k + ar(128))]
    head_cols = lambda base, c: np.concatenate([base + c * 64 + ar(64), base + (8 + c) * 64 + ar(64)])
    for c in range(8):
        chunks.append(chunk_in(w_in_a, head_cols(o_q, c)))
    for c in range(8):
        chunks.append(chunk_in(w_in_a, head_cols(o_gb, c)))
    for blk in range(8):
        chunks.append(chunk_in(w_in_a, o_za + blk * 128 + ar(128)))
        chunks.append(chunk_in(w_in_a, o_ga + blk * 128 + ar(128)))
    chunks = [chunks[ci] for ci in l0_order]
    while len(chunks) % 4:
        chunks.append(np.zeros_like(chunks[0]))
    groups = []
    for g in range(len(chunks) // 4):
        groups.append(np.stack(chunks[4 * g:4 * g + 4], axis=1).reshape(P, SLOT))

    def out_groups(w, rows_of_chunk):
        gs = []
        for half in range(2):
            for jg in range(2):
                blk = np.stack([w[rows_of_chunk(jg * 8 + jj)][:, half * 512:(half + 1) * 512]
                                for jj in range(8)], axis=1)
                gs.append(blk.reshape(P, SLOT))
        return gs

    def rows_a(j):
        if j < 8:
            return j * 128 + ar(128)
        c = j - 8
        return np.concatenate([1024 + c * 64 + ar(64), 1024 + (8 + c) * 64 + ar(64)])
    groups += out_groups(w_out_a, rows_a)
    for vg in range(4):
        cols = 2048 + vg * 512 + ar(512)
        groups.append(w_in_c[:, cols].reshape(KC, P, 512).transpose(1, 0, 2).reshape(P, SLOT))
    for j in range(0, 16, 2):
        cs = [chunk_in(w_in_c, j * 128 + ar(128)), chunk_in(w_in_c, 4096 + j * 128 + ar(128)),
              chunk_in(w_in_c, (j + 1) * 128 + ar(128)), chunk_in(w_in_c, 4096 + (j + 1) * 128 + ar(128))]
        groups.append(np.stack(cs, axis=1).reshape(P, SLOT))
    groups += out_groups(w_out_c, lambda j: j * 128 + ar(128))
    wst = np.ascontiguousarray(np.stack(groups, axis=0), dtype=f32)
    assert wst.shape == (NG_TILE, P, SLOT), wst.shape

    cst = np.zeros((P, C_TOTAL), f32)
    pcol = lambda v: np.asarray(v, f32).reshape(8, P).T
    cw = np.asarray(inp["conv_w"], f32)[0]
    cst[:, C_CONVW:C_CONVW + 32] = cw.reshape(4, 8, P).transpose(2, 1, 0).reshape(P, 32)
    cst[:, C_CONVB:C_CONVB + 8] = pcol(inp["conv_b"][0])
    cst[:, C_GAB:C_GAB + 8] = pcol(inp["gate_a_b"][0])
    cst[:, C_GXB:C_GXB + 8] = pcol(inp["gate_x_b"][0])
    cst[:, C_LAM:C_LAM + 8] = pcol(inp["lru_lambda"][0])
    cst[:, C_QG] = np.tile(np.asarray(inp["q_norm_g"], f32)[0], 2)
    cst[:, C_KG] = np.tile(np.asarray(inp["k_norm_g"], f32)[0], 2)
    sinks = np.asarray(inp["sinks"], f32)[0]
    cst[:64, C_SINK:C_SINK + 8] = sinks[None, 0:8]
    cst[64:, C_SINK:C_SINK + 8] = sinks[None, 8:16]
    cst[:, C_LNG:C_LNG + 16] = np.asarray(inp["ln_v_g"], f32)[0].reshape(16, P).T
    cst[:, C_LNB:C_LNB + 16] = np.asarray(inp["ln_v_b"], f32)[0].reshape(16, P).T
    ga = np.asarray(inp["gate_a_w"], f32)[0]
    gx = np.asarray(inp["gate_x_w"], f32)[0]
    gw = np.stack([ga.transpose(1, 0, 2), gx.transpose(1, 0, 2)], axis=1)
    csu = np.zeros((P, U_TOTAL), f32)
    csu[:, U_GATEW:U_GATEW + 2048] = gw.reshape(P, 2048)
    s_i = ar(128)[:, None]
    q_i = ar(128)[None, :]
    amask = np.stack([(s_i > q_i), (s_i <= q_i)], axis=1).astype(f32)
    csu[:, U_AMASK:U_AMASK + 256] = amask.reshape(P, 256)
    spw = np.asarray(inp["spatial_w"], f32)[0]
    csu[:, U_SPW:U_SPW + 1024] = spw.transpose(2, 0, 1).reshape(P, 1024)
    csu[:, U_CMASK:U_CMASK + 128] = (s_i <= q_i).astype(f32)
    spb = np.asarray(inp["spatial_b"], f32)[0]
    csu[:, U_SPB:U_SPB + 1024] = np.broadcast_to(spb.reshape(1, 1024), (P, 1024))
    csu[:, U_IDENT:U_IDENT + 128] = np.eye(128, dtype=f32)
    ob = np.zeros((128, 128), f32)
    ob[:64, :64] = 1.0
    ob[64:, 64:] = 1.0
    cst[:, C_ONESB:C_ONESB + 128] = ob

    gain = np.stack([np.broadcast_to(np.asarray(inp["norm_a"], f32)[0][None, :], (P, D)),
                     np.broadcast_to(np.asarray(inp["norm_c"], f32)[0][None, :], (P, D))], axis=0)
    gain = np.ascontiguousarray(gain, dtype=f32)

    rel = np.asarray(inp["rel_bias"], f32)
    kj = ar(256)[None, :]
    qi = ar(128)[:, None]
    dist = 128 + qi - kj
    bucket = t5_bucket(np.maximum(dist, 0))
    bias = rel[bucket]
    biasT = bias.transpose(1, 2, 0).reshape(2, 128, 16, 128).transpose(1, 0, 2, 3)
    biasT = np.ascontiguousarray(biasT.reshape(P, 2 * 16 * 128), dtype=f32)
    return {"wst": wst, "cst": cst, "csu": csu, "gain": gain, "biasT": biasT}


_CACHE = {}


def get_l0_order():
    if "order" not in _CACHE:
        rec = []
        build_program(1, 4, record=rec)
        order = []
        for ci in rec:
            if ci not in order:
                order.append(ci)
        assert sorted(order) == list(range(34)), order
        _CACHE["order"] = order
    return _CACHE["order"]


def kernel(**inputs):
    x = np.asarray(inputs["x"], np.float32)
    bsz, s, d = x.shape
    per = bsz // N_CORES
    order = get_l0_order()
    shared = prepare_shared(inputs, order)
    n_tiles = per * s // T
    key = (n_tiles, s // T)
    if key not in _CACHE:
        _CACHE[key] = build_program(n_tiles, s // T, l0_order=order)
    nc = _CACHE[key]
    in_maps = []
    for c in range(N_CORES):
        m = dict(shared)
        m["x"] = np.ascontiguousarray(x[c * per:(c + 1) * per].reshape(per * s, d))
        in_maps.append(m)
    res = run_bass_kernel_spmd(nc, in_maps, core_ids=list(range(N_CORES)))
    outs = [np.asarray(r["out"], np.float32).reshape(per, s, d) for r in res.results]
    return np.concatenate(outs, axis=0)
```

```python
import numpy as np
from contextlib import ExitStack
import concourse.bass as bass
import concourse.mybir as mybir
from concourse.bass_utils import run_bass_kernel_spmd

F32 = mybir.dt.float32
BF16 = mybir.dt.bfloat16
AF = mybir.ActivationFunctionType
ALU = mybir.AluOpType

P = 128
T = 512
NB = 4
D = 1024
KC = 8
EPS = 1e-6
N_CORES = 8
SEQ = 2048
TPS = SEQ // T
NSLOT = 5
SLOT = 4096
NG_L0_IN = 9
NG_TILE = 29

C_CONVW = 0
C_CONVB = 32
C_GAB = 40
C_GXB = 48
C_LAM = 56
C_QG = 64
C_KG = 65
C_SINK = 66
C_LNG = 74
C_LNB = 90
C_ONESB = 128
C_TOTAL = 256
U_GATEW = 0
U_SPW = 2048
U_SPB = 3072
U_AMASK = 4096
U_CMASK = 4352
U_IDENT = 4480
U_TOTAL = 4608


class Sched:
    ENG = ("pe", "act", "dve", "pool", "sp")

    def __init__(self):
        self.q = {e: [] for e in self.ENG}
        self.cnt = {e: 0 for e in self.ENG}
        self.dcnt = {}
        self.lastw = {}
        self.readers = {}
        self.waited = {e: {} for e in self.ENG}
        self.nwaits = 0

    def _deps(self, eng, reads, writes, is_dma):
        need = {}

        def add(t, kind):
            if t[0] == "E" and t[1] == eng and not is_dma:
                if eng == "pe":
                    return
                if kind != "raw" and t[2] != self.cnt[eng]:
                    return
            sk = (t[0], t[1])
            if need.get(sk, 0) < t[2]:
                need[sk] = t[2]

        for k in reads:
            t = self.lastw.get(k)
            if t is not None:
                add(t, "raw")
        for k in writes:
            t = self.lastw.get(k)
            if t is not None:
                add(t, "waw")
            for sk, v in self.readers.get(k, {}).items():
                add((sk[0], sk[1], v), "war")
        waits = []
        wd = self.waited[eng]
        for sk, v in need.items():
            if wd.get(sk, 0) >= v:
                continue
            wd[sk] = v
            waits.append((sk, v))
        self.nwaits += len(waits)
        return waits

    def _commit(self, tok, reads, writes):
        for k in writes:
            self.lastw[k] = tok
            self.readers[k] = {}
        sk = (tok[0], tok[1])
        for k in reads:
            r = self.readers.setdefault(k, {})
            if r.get(sk, 0) < tok[2]:
                r[sk] = tok[2]

    def op(self, eng, fn, reads=(), writes=()):
        waits = self._deps(eng, reads, writes, False)
        self.cnt[eng] += 1
        tok = ("E", eng, self.cnt[eng])
        self.q[eng].append((waits, fn, ("E", eng), 1))
        self._commit(tok, reads, writes)
        return tok

    def dma(self, qeng, sem, fn, reads=(), writes=()):
        waits = self._deps(qeng, reads, writes, True)
        self.dcnt[sem] = self.dcnt.get(sem, 0) + 1
        tok = ("D", sem, 16 * self.dcnt[sem])
        self.q[qeng].append((waits, fn, ("D", sem), 16))
        self._commit(tok, reads, writes)
        return tok

    def final_wait(self, eng, toks):
        waits = [((t[0], t[1]), t[2]) for t in toks]
        self.q[eng].append((waits, None, None, 0))

    def sem_keys(self):
        keys = [("E", e) for e in self.ENG]
        keys += [("D", s) for s in self.dcnt]
        return keys

    def emit(self, nc, sems):
        with nc.Block() as block:
            decos = {"pe": block.tensor, "act": block.scalar, "dve": block.vector,
                     "pool": block.gpsimd, "sp": block.sync}
            for e in self.ENG:
                items = self.q[e]

                def body(engh, items=items):
                    for waits, fn, inc, amt in items:
                        for sk, v in waits:
                            engh.wait_ge(sems[sk], v)
                        if fn is None:
                            continue
                        ins = fn(engh)
                        ins.then_inc(sems[inc], amt)

                decos[e](body)


def zipper(gens, depth):
    gens = list(gens)
    active = []
    i = 0
    while active or i < len(gens):
        while len(active) < depth and i < len(gens):
            active.append(gens[i])
            i += 1
        for g in list(active):
            try:
                next(g)
            except StopIteration:
                active.remove(g)


def multi_zipper(streams):
    st = [{"gens": list(s_[0]), "depth": s_[1], "rate": (s_[2] if len(s_) > 2 else 1), "i": 0, "active": []}
          for s_ in streams]
    while any(x["active"] or x["i"] < len(x["gens"]) for x in st):
        for x in st:
            for _ in range(x["rate"]):
                while len(x["active"]) < x["depth"] and x["i"] < len(x["gens"]):
                    x["active"].append(x["gens"][x["i"]])
                    x["i"] += 1
                for g in list(x["active"]):
                    try:
                        next(g)
                    except StopIteration:
                        x["active"].remove(g)


def build_program(n_tiles, tps=TPS, n_layers=2, dbg=0, l0_order=None, record=None):
    nc = bass.Bass("TRN2", target_bir_lowering=False)
    ntok = n_tiles * T
    x_d = nc.dram_tensor("x", [ntok, D], F32, kind="ExternalInput").ap()
    wst_d = nc.dram_tensor("wst", [NG_TILE, P, SLOT], F32, kind="ExternalInput").ap()
    cst_d = nc.dram_tensor("cst", [P, C_TOTAL], F32, kind="ExternalInput").ap()
    csu_d = nc.dram_tensor("csu", [P, U_TOTAL], F32, kind="ExternalInput").ap()
    gain_d = nc.dram_tensor("gain", [2, P, D], F32, kind="ExternalInput").ap()
    bias_d = nc.dram_tensor("biasT", [P, 2 * 16 * 128], F32, kind="ExternalInput").ap()
    out_d = nc.dram_tensor("out", [ntok, D], F32, kind="ExternalOutput").ap()

    S = Sched()
    st = ExitStack()
    with st:
        def sb(name, shape, dt=F32):
            return st.enter_context(nc.sbuf_tensor("sb_" + name, shape, dt))

        NXB = 2
        xbuf = [sb("xbuf%d" % i, [P, NB, D]) for i in range(NXB)]
        hT = sb("hT", [P, KC, T], BF16)
        ss = sb("ss", [P, 8])
        rs = sb("rs", [P, 8])
        ring = sb("ring", [P, NSLOT, SLOT], BF16)
        cst = sb("cst", [P, C_TOTAL])
        gain = sb("gain", [P, 2, D])
        Et = sb("Et", [P, 2, 16, 128], BF16)
        vz = sb("vz", [P, NB + 1, 2, 128], BF16)
        kz = sb("kz", [P, 2, (NB + 1) * 128], BF16)
        onesz = sb("onesz", [P, 2, 128], BF16)
        qnT = sb("qnT", [P, 8, T], BF16)
        Gs = sb("Gs", [P, 8, T], BF16)
        sqb = [sb("sqb%d" % i, [P, T]) for i in range(2)]
        rsb = [sb("rsb%d" % i, [P, T]) for i in range(2)]
        expS = [sb("expS%d" % i, [P, T]) for i in range(2)]
        PT = [sb("PT%d" % i, [P, T], BF16) for i in range(4)]
        den = [sb("den0", [P, T]), expS[0]]
        yb = [sb("yb0", [P, T]), expS[1]]
        denk = [("den", 0), ("expS", 0)]
        ybk = [("yb", 0), ("expS", 1)]
        junk = den[0][:].bitcast(BF16)
        yT = sb("yT", [P, 16, T], BF16)
        Us = [sb("U%d" % i, [P, 8, T]) for i in range(2)]
        zbuf = [sb("zbuf%d" % i, [P, T + 4]) for i in range(2)]
        xabf = [sb("xabf%d" % i, [P, T], BF16) for i in range(2)]
        halo = sb("halo", [P, 8, 4])
        hstate = sb("hstate", [P, 8])
        Cc = sb("Cc", [P, 16, 128])
        stats = sb("stats", [P, 4, 6])
        mv = sb("mv", [P, 2])
        lnr = sb("lnr", [P, 2])
        kcol = sb("kcol", [P, 64])
        ident_bf = sb("ident_bf", [P, 128], BF16)
        gw_bf = sb("gw_bf", [P, 2, 8, 128], BF16)
        wtril_bf = sb("wtril_bf", [P, 8, 128], BF16)

        NPS = 8
        ps = [st.enter_context(nc.psum_tensor("ps%d" % i, [P, 512], F32)) for i in range(NPS)]
        U = Us[0]
        hb = Gs[:].rearrange("p (b j) t -> p b (j t)", b=NB)
        hbk = lambda b: [("Gs", 2 * b), ("Gs", 2 * b + 1)]
        wtril_f = U[:, 0:2, :].rearrange("p a t -> p (a t)").rearrange("p (g t) -> p g t", g=8)
        vhat = U[:].rearrange("p a t -> p (a t)").bitcast(BF16).rearrange("p (b f) -> p b f", b=NB)
        xflat = xbuf[0][:].rearrange("p b d -> p (b d)")
        usu = U[:, 7, :]
        x0keys = [("x0", b) for b in range(NB)]

        free_banks = list(range(NPS))

        def nbank():
            assert free_banks, "PSUM banks exhausted"
            return free_banks.pop(0)

        def getbank():
            n = 0
            while not free_banks:
                n += 1
                assert n < 10000, "PSUM bank wait deadlock at build time"
                yield
            return free_banks.pop(0)

        def rel(bk):
            assert bk not in free_banks
            free_banks.append(bk)

        def col(c, n=1):
            return cst[:, c:c + n]

        S.dma("sp", "c0", lambda e: e.dma_start(out=cst[:], in_=cst_d), writes=["cst"])
        S.dma("sp", "c1", lambda e: e.dma_start(out=gain[:], in_=gain_d.rearrange("g p d -> p g d")),
              writes=["gain"])
        x1flat = xbuf[1][:].rearrange("p b d -> p (b d)")
        x1keys = [("x1", b) for b in range(NB)]
        S.dma("sp", "c2", lambda e: e.dma_start(out=x1flat, in_=bias_d), writes=x1keys)
        S.dma("sp", "c3", lambda e: e.dma_start(out=xflat, in_=csu_d[:, 0:4096]), writes=x0keys)
        S.dma("sp", "c4", lambda e: e.dma_start(out=usu, in_=csu_d[:, 4096:U_TOTAL]), writes=[("U0", 7)])

        S.op("dve", lambda e: e.tensor_copy(out=ident_bf[:], in_=usu[:, 384:512]),
             reads=[("U0", 7)], writes=["ident_bf"])
        S.op("dve", lambda e: e.memset(onesz[:], 0.0), writes=["onesz"])
        S.op("dve", lambda e: e.memset(onesz[:, 0, 0:64], 1.0), writes=["onesz"])
        S.op("dve", lambda e: e.memset(onesz[:, 1, 64:128], 1.0), writes=["onesz"])
        S.op("dve", lambda e: e.memset(kz[:], 0.0), writes=["kprev", "kcur"])
        S.op("dve", lambda e: e.memset(vz[:], 0.0), writes=["vprev", "vcur"])
        S.op("dve", lambda e: e.tensor_copy(out=gw_bf[:].rearrange("p a n j -> p (a n j)"),
                                            in_=xflat[:, U_GATEW:U_GATEW + 2048]),
             reads=x0keys, writes=["gw_bf"])
        for kt in range(2):
            def f(e, kt=kt):
                m = usu[:, kt * 128:(kt + 1) * 128]
                v = x1flat[:, kt * 2048:(kt + 1) * 2048].rearrange("p (h q) -> p h q", h=16)
                return e.scalar_tensor_tensor(out=v, in0=v, scalar=8.0,
                                              in1=m.unsqueeze(1).to_broadcast([P, 16, 128]),
                                              op0=ALU.mult, op1=ALU.mult)
            S.op("dve", f, reads=x1keys + [("U0", 7)], writes=x1keys)
        S.op("dve", lambda e: e.tensor_scalar(out=usu[:, 0:256], in0=usu[:, 0:256], scalar1=-1.0, scalar2=480.0,
                                              op0=ALU.add, op1=ALU.mult),
             reads=[("U0", 7)], writes=[("U0", 7)])
        for kt in range(2):
            def f(e, kt=kt):
                m = usu[:, kt * 128:(kt + 1) * 128]
                v = x1flat[:, kt * 2048:(kt + 1) * 2048].rearrange("p (h q) -> p h q", h=16)
                return e.tensor_tensor(out=Et[:, kt], in0=v, in1=m.unsqueeze(1).to_broadcast([P, 16, 128]),
                                       op=ALU.add)
            S.op("dve", f, reads=x1keys + [("U0", 7)], writes=["Et"])
        S.op("act", lambda e: e.activation(out=kcol[:, 16:24], in_=col(C_SINK, 8), func=AF.Exp),
             reads=["cst"], writes=["kc_sink"])
        S.op("dve", lambda e: e.tensor_scalar(out=kcol[:, 32:40], in0=col(C_GAB, 8), scalar1=0.5, scalar2=None,
                                              op0=ALU.mult), reads=["cst"], writes=["kc_hb"])
        S.op("dve", lambda e: e.tensor_scalar(out=kcol[:, 40:48], in0=col(C_GXB, 8), scalar1=0.5, scalar2=None,
                                              op0=ALU.mult), reads=["cst"], writes=["kc_hb"])
        S.op("act", lambda e: e.activation(out=kcol[:, 24:32], in_=col(C_LAM, 8), func=AF.Exp, scale=-1.0),
             reads=["cst"], writes=["kc_e"])
        S.op("dve", lambda e: e.tensor_scalar(out=kcol[:, 48:56], in0=kcol[:, 24:32], scalar1=2.0,
                                              scalar2=None, op0=ALU.add),
             reads=["kc_e"], writes=["kc_t"])
        S.op("dve", lambda e: e.reciprocal(out=kcol[:, 48:56], in_=kcol[:, 48:56]),
             reads=["kc_t"], writes=["kc_t"])
        S.op("dve", lambda e: e.tensor_tensor(out=kcol[:, 24:32], in0=kcol[:, 24:32], in1=kcol[:, 48:56],
                                              op=ALU.mult),
             reads=["kc_t", "kc_e"], writes=["kc_e"])
        S.op("dve", lambda e: e.tensor_tensor(out=kcol[:, 48:56], in0=kcol[:, 24:32], in1=kcol[:, 24:32],
                                              op=ALU.mult),
             reads=["kc_e"], writes=["kc_t"])
        S.op("dve", lambda e: e.tensor_scalar(out=kcol[:, 0:8], in0=kcol[:, 48:56], scalar1=1.0 / 9,
                                              scalar2=1.0 / 7, op0=ALU.mult, op1=ALU.add),
             reads=["kc_t"], writes=["kc_p"])
        for cc in (1.0 / 5, 1.0 / 3, 1.0):
            S.op("dve", lambda e: e.tensor_tensor(out=kcol[:, 0:8], in0=kcol[:, 0:8], in1=kcol[:, 48:56],
                                                  op=ALU.mult),
                 reads=["kc_p", "kc_t"], writes=["kc_p"])
            S.op("dve", lambda e, cc=cc: e.tensor_scalar(out=kcol[:, 0:8], in0=kcol[:, 0:8], scalar1=cc,
                                                         scalar2=None, op0=ALU.add),
                 reads=["kc_p"], writes=["kc_p"])
        S.op("dve", lambda e: e.tensor_tensor(out=kcol[:, 0:8], in0=kcol[:, 0:8], in1=kcol[:, 24:32],
                                              op=ALU.mult),
             reads=["kc_p", "kc_e"], writes=["kc_p"])
        S.op("dve", lambda e: e.tensor_scalar(out=kcol[:, 8:16], in0=kcol[:, 0:8], scalar1=-16.0,
                                              scalar2=None, op0=ALU.mult),
             reads=["kc_p"], writes=["kc_K"])
        S.op("dve", lambda e: e.tensor_scalar(out=kcol[:, 0:8], in0=kcol[:, 0:8], scalar1=-8.0,
                                              scalar2=None, op0=ALU.mult),
             reads=["kc_p", "kc_K"], writes=["kc_K"])
        Kh = lambda blk: kcol[:, blk:blk + 1]
        Kf = lambda blk: kcol[:, 8 + blk:9 + blk]

        if n_layers > 1:
            def f(e):
                m = usu[:, 256:384]
                return e.tensor_tensor(out=wtril_f,
                                       in0=xflat[:, U_SPW:U_SPW + 1024].rearrange("p (g t) -> p g t", g=8),
                                       in1=m.unsqueeze(1).to_broadcast([P, 8, 128]), op=ALU.mult)
            S.op("dve", f, reads=x0keys + [("U0", 7)], writes=[("U0", 0), ("U0", 1)])
            S.op("dve", lambda e: e.tensor_copy(out=wtril_bf[:], in_=wtril_f),
                 reads=[("U0", 0), ("U0", 1)], writes=["wtril_bf"])
            S.op("dve", lambda e: e.memset(sqb[0][:, 0:128], 1.0), writes=["sqb0"])
            for g in range(8):
                bk = nbank()

                def f(e, g=g, bk=bk):
                    return e.matmul(ps[bk][:, 0:128], lhsT=sqb[0][:, 0:128], rhs=wtril_f[:, g, :],
                                    start=True, stop=True)
                S.op("pe", f, reads=["sqb0", ("U0", 0), ("U0", 1)], writes=[("ps", bk)])
                for jj in range(2):
                    j = 2 * g + jj

                    def f2(e, g=g, bk=bk, j=j):
                        return e.scalar_tensor_tensor(
                            out=Cc[:, j, :], in0=ps[bk][:, 0:128], scalar=col(C_LNB + j),
                            in1=xflat[:, U_SPB + g * 128:U_SPB + (g + 1) * 128], op0=ALU.mult, op1=ALU.add)
                    S.op("dve", f2, reads=[("ps", bk), "cst"] + x0keys, writes=["Cc"])
                rel(bk)

        GPT = NG_TILE if n_layers > 1 else 13
        rem_tile = [4] * 8 + [2] + [4] * 4 + ([4] * 16 if n_layers > 1 else [])
        seq_groups = [g for t in range(n_tiles) for g in range(GPT)]
        rem = [r for t in range(n_tiles) for r in rem_tile]
        rstate = {"next_load": 0, "oldest": 0}

        def _load():
            i = rstate["next_load"]
            rstate["next_load"] += 1
            if i >= len(seq_groups):
                return
            slot = i % NSLOT
            g = seq_groups[i]
            S.dma("pool", "w%d" % slot,
                  lambda e: e.dma_start(out=ring[:, slot, :], in_=wst_d[g], max_dma_last_dim=8192),
                  writes=[("w", slot)])

        for i in range(NSLOT):
            _load()

        def wslot(i):
            if record is not None:
                return i % NSLOT
            assert rstate["oldest"] <= i < rstate["next_load"], (i, rstate)
            return i % NSLOT

        def wdone(i, n=1):
            if record is not None:
                return
            rem[i] -= n
            assert rem[i] >= 0
            while rstate["oldest"] < len(rem) and rem[rstate["oldest"]] == 0:
                rstate["oldest"] += 1
                _load()

        def mm_group(out_ap, pairs, reads, writes):
            def f(e):
                ins = None
                n = len(pairs)
                for i, (l, r) in enumerate(pairs):
                    ins = e.matmul(out_ap, lhsT=l, rhs=r, start=(i == 0), stop=(i == n - 1))
                return ins
            S.op("pe", f, reads=reads, writes=writes)

        hT_keys = [("hT", kc) for kc in range(KC)]

        def norm_stats_blk(xb, xk, gi, b):
            S.op("act", lambda e: e.activation(out=junk, in_=xb[:, b, :], func=AF.Square,
                                               accum_out=ss[:, b:b + 1]),
                 reads=[(xk, b)], writes=[("ss", b), ("den", 0)])
            S.op("act", lambda e: e.activation(out=rs[:, b:b + 1], in_=ss[:, b:b + 1], func=AF.Sqrt,
                                               scale=1.0 / D, bias=EPS),
                 reads=[("ss", b)], writes=[("rs", b)])
            S.op("dve", lambda e: e.reciprocal(out=rs[:, b:b + 1], in_=rs[:, b:b + 1]),
                 reads=[("rs", b)], writes=[("rs", b)])
            for hf in range(2):
                S.op("dve", lambda e, hf=hf: e.scalar_tensor_tensor(
                    out=hb[:, b, hf * 512:(hf + 1) * 512], in0=xb[:, b, hf * 512:(hf + 1) * 512],
                    scalar=rs[:, b:b + 1], in1=gain[:, gi, hf * 512:(hf + 1) * 512],
                    op0=ALU.mult, op1=ALU.mult),
                    reads=[(xk, b), ("rs", b), "gain"], writes=[("Gs", 2 * b + hf)])

        def norm_stats(xb, xk, gi):
            for b in range(NB):
                norm_stats_blk(xb, xk, gi, b)

        def norm_tr():
            for pi in range(KC // 2):
                bkt = nbank()
                ptb = ps[bkt][:, :].bitcast(BF16)

                def f(e, pi=pi, ptb=ptb):
                    ins = None
                    for kk in range(2):
                        kc = 2 * pi + kk
                        for b in range(NB):
                            ins = e.transpose(out=ptb[:, kk * 512 + b * 128: kk * 512 + (b + 1) * 128],
                                              in_=hb[:, b, kc * 128:(kc + 1) * 128], identity=ident_bf[:])
                    return ins
                S.op("pe", f, reads=[("Gs", 2 * b + pi // 2) for b in range(NB)] + ["ident_bf"],
                     writes=[("ps", bkt)])
                S.op("act", lambda e, pi=pi, ptb=ptb: e.copy(
                    out=hT[:, 2 * pi:2 * pi + 2, :].rearrange("p k t -> p (k t)"), in_=ptb[:, :]),
                    reads=[("ps", bkt)], writes=[("hT", 2 * pi), ("hT", 2 * pi + 1)])
                rel(bkt)

        def norm_T(xb, xk, gi):
            norm_stats(xb, xk, gi)
            norm_tr()

        def inproj_chunk(gi, pos, bk=None):
            slot = wslot(gi)
            if bk is None:
                bk = nbank()
            wv = ring[:, slot, :].rearrange("p (c k n) -> p c k n", c=4, k=KC)
            mm_group(ps[bk][:, :], [(wv[:, pos, kc, :], hT[:, kc, :]) for kc in range(KC)],
                     reads=[("w", slot)] + hT_keys, writes=[("ps", bk)])
            wdone(gi)
            return bk

        def out_proj(xb, xk, gbase, pre=None, blk_hook=None):
            if pre is not None:
                pre[0]()

            def mm(b, jg, gi, bank):
                slot = wslot(gi)
                wv = ring[:, slot, :].rearrange("p (j n) -> p j n", j=8)

                def f(e):
                    ins = None
                    for jj in range(8):
                        j = jg * 8 + jj
                        ins = e.matmul(ps[bank][:, :], lhsT=yT[:, j, b * 128:(b + 1) * 128],
                                       rhs=wv[:, jj, :], start=(j == 0), stop=(j == 15))
                    return ins
                S.op("pe", f, reads=[("w", slot)] + [("yT", jg * 8 + jj) for jj in range(8)],
                     writes=[("ps", bank)])
                wdone(gi)

            def add(b, half, bank):
                S.op("dve", lambda e: e.tensor_tensor(
                    out=xb[:, b, half * 512:(half + 1) * 512], in0=ps[bank][:, :],
                    in1=xb[:, b, half * 512:(half + 1) * 512], op=ALU.add),
                    reads=[("ps", bank), (xk, b)], writes=[(xk, b)])
                rel(bank)

            banks = [nbank() for _ in range(NB)]
            for jg in range(2):
                for b in range(NB):
                    mm(b, jg, gbase + jg, banks[b])
            for b in range(NB):
                add(b, 0, banks[b])
            if pre is not None:
                pre[1]()
            if blk_hook is None:
                banks = [nbank() for _ in range(NB)]
                for jg in range(2):
                    for b in range(NB):
                        mm(b, jg, gbase + 2 + jg, banks[b])
                for b in range(NB):
                    add(b, 1, banks[b])
            else:
                for b in range(NB):
                    bank = nbank()
                    for jg in range(2):
                        mm(b, jg, gbase + 2 + jg, bank)
                    add(b, 1, bank)
                    blk_hook(b)

        def layer0(t, xb, xk, gb0, pre=None):
            first = (t % tps == 0)
            if first:
                S.op("pool", lambda e: e.memset(halo[:], 0.0), writes=[("halo", i) for i in range(8)])
                S.op("pool", lambda e: e.memset(hstate[:], 0.0), writes=[("hstate", i) for i in range(8)])
            def cgrp(ci):
                if record is not None:
                    record.append(ci)
                    return (gb0, 0)
                pos = l0_order.index(ci)
                return (gb0 + pos // 4, pos % 4)

            gi, pos = cgrp(0)
            slot = wslot(gi)
            wv = ring[:, slot, :].rearrange("p (c k n) -> p c k n", c=4, k=KC)
            bkv = nbank()
            for b in range(NB):
                mm_group(ps[bkv][:, b * 128:(b + 1) * 128],
                         [(hT[:, kc, b * 128:(b + 1) * 128], wv[:, 0, kc, :]) for kc in range(KC)],
                         reads=[("w", slot)] + hT_keys, writes=[("ps", bkv)])
            wdone(gi)
            for kv in range(2):
                S.op("act", lambda e, kv=kv: e.copy(
                    out=vz[:, 1:NB + 1, kv, kv * 64:(kv + 1) * 64],
                    in_=ps[bkv][:, :].rearrange("p (b n) -> p b n", b=NB)[:, :, kv * 64:(kv + 1) * 64]),
                    reads=[("ps", bkv)], writes=["vcur"])
            rel(bkv)

            def gen_qk(ci, gcol, out_ap, outkey, idx):
                sq, rr = sqb[idx % 2], rsb[idx % 2]
                sqk, rrk = "sqb%d" % (idx % 2), "rsb%d" % (idx % 2)
                bk = yield from getbank()
                inproj_chunk(*cgrp(ci), bk=bk)
                yield
                S.op("act", lambda e: e.activation(out=sq[:], in_=ps[bk][:, :], func=AF.Square),
                     reads=[("ps", bk)], writes=[sqk])
                yield
                b2 = yield from getbank()
                S.op("pe", lambda e: e.matmul(ps[b2][:, :], lhsT=col(C_ONESB, 128), rhs=sq[:],
                                              start=True, stop=True),
                     reads=[sqk, "cst"], writes=[("ps", b2)])
                yield
                S.op("act", lambda e: e.activation(out=rr[:], in_=ps[b2][:, :], func=AF.Ln, scale=1.0 / 64,
                                                   bias=EPS),
                     reads=[("ps", b2)], writes=[rrk])
                rel(b2)
                yield
                S.op("act", lambda e: e.activation(out=rr[:], in_=rr[:], func=AF.Exp, scale=-0.5),
                     reads=[rrk], writes=[rrk])
                yield
                if out_ap is None:
                    for kv in range(2):
                        pr = slice(kv * 64, (kv + 1) * 64)
                        S.op("dve", lambda e, kv=kv, pr=pr: e.scalar_tensor_tensor(
                            out=kz[pr, kv, 128:], in0=ps[bk][pr, :], scalar=cst[pr, gcol:gcol + 1],
                            in1=rr[pr, :], op0=ALU.mult, op1=ALU.mult),
                            reads=[("ps", bk), rrk, "cst"], writes=[outkey])
                else:
                    S.op("dve", lambda e: e.scalar_tensor_tensor(out=out_ap, in0=ps[bk][:, :], scalar=col(gcol),
                                                                 in1=rr[:], op0=ALU.mult, op1=ALU.mult),
                         reads=[("ps", bk), rrk, "cst"], writes=[outkey])
                rel(bk)
                yield

            def gen_gb(c):
                bk = yield from getbank()
                inproj_chunk(*cgrp(10 + c), bk=bk)
                yield
                S.op("act", lambda e: e.activation(out=Gs[:, c, :], in_=ps[bk][:, :], func=AF.Silu),
                     reads=[("ps", bk)], writes=[("Gs", c)])
                rel(bk)
                yield

            gens = [gen_qk(1, C_KG, None, "kcur", 0)]
            gens += [gen_qk(2 + c, C_QG, qnT[:, c, :], ("qn", c), 1 + c) for c in range(8)]
            gens += [gen_gb(c) for c in range(8)]
            b_gens = gens

            def gen_att_kv(b, quad, kv, bo, bd, stt, u):
                pi_ = (u % 2) * 2 + kv
                pT = PT[pi_]
                pk = ("PT", pi_)
                kts = [1] if (first and b == 0) else [0, 1]
                for kt in kts:
                    keyblk = b + kt
                    bs_ = yield from getbank()
                    kkey = "kprev" if keyblk == 0 else "kcur"
                    vkey = "vprev" if keyblk == 0 else "vcur"
                    h0 = kv * 8 + quad * 4

                    def fS(e, keyblk=keyblk, bs_=bs_, kt=kt, h0=h0):
                        o = ps[bs_][:, :].rearrange("p (c q) -> p c q", c=4)
                        e.matmul(o, lhsT=kz[:, kv, keyblk * 128:(keyblk + 1) * 128],
                                 rhs=qnT[:, quad * 4:(quad + 1) * 4, b * 128:(b + 1) * 128],
                                 start=True, stop=False)
                        return e.matmul(o, lhsT=ident_bf[:], rhs=Et[:, kt, h0:h0 + 4, :], start=False, stop=True)
                    S.op("pe", fS, reads=[kkey, "Et", "ident_bf"] + [("qn", quad * 4 + cc) for cc in range(4)],
                         writes=[("ps", bs_)])
                    yield
                    S.op("act", lambda e, bs_=bs_: e.activation(
                        out=pT[:], in_=ps[bs_][:, :], func=AF.Exp, scale=0.125),
                        reads=[("ps", bs_)], writes=[pk])
                    rel(bs_)
                    yield
                    st_ = (stt["n"] == 0)
                    sp_ = (stt["n"] == stt["total"] - 1)
                    stt["n"] += 1
                    S.op("pe", lambda e, keyblk=keyblk, st_=st_, sp_=sp_: e.matmul(
                        ps[bo][:, :], lhsT=vz[:, keyblk, kv, :], rhs=pT[:], start=st_, stop=sp_),
                        reads=[vkey, pk], writes=[("ps", bo)])
                    S.op("pe", lambda e, st_=st_, sp_=sp_: e.matmul(
                        ps[bd][:, :], lhsT=onesz[:, kv, :], rhs=pT[:], start=st_, stop=sp_),
                        reads=["onesz", pk], writes=[("ps", bd)])
                    yield

            def gen_att(b, quad, u):
                bo = yield from getbank()
                bd = yield from getbank()
                dn, y_ = den[u % 2], yb[u % 2]
                dk, yk = denk[u % 2], ybk[u % 2]
                nk = 1 if (first and b == 0) else 2
                stt = {"n": 0, "total": 2 * nk}
                subs = [gen_att_kv(b, quad, kv, bo, bd, stt, u) for kv in range(2)]
                while subs:
                    for g in list(subs):
                        try:
                            next(g)
                        except StopIteration:
                            subs.remove(g)
                    yield
                S.op("dve", lambda e: e.tensor_tensor(
                    out=dn[:].rearrange("p (c q) -> p c q", c=4),
                    in0=ps[bd][:, :].rearrange("p (c q) -> p c q", c=4),
                    in1=kcol[:, 16 + quad * 4:16 + (quad + 1) * 4].unsqueeze(2).to_broadcast([P, 4, 128]),
                    op=ALU.add),
                    reads=[("ps", bd), "kc_sink"], writes=[dk])
                rel(bd)
                yield
                S.op("act", lambda e: e.activation(out=dn[:], in_=dn[:], func=AF.Ln), reads=[dk], writes=[dk])
                yield
                S.op("act", lambda e: e.activation(out=dn[:], in_=dn[:], func=AF.Exp, scale=-1.0),
                     reads=[dk], writes=[dk])
                yield
                S.op("dve", lambda e: e.tensor_tensor(out=y_[:], in0=ps[bo][:, :], in1=dn[:], op=ALU.mult),
                     reads=[("ps", bo), dk], writes=[yk])
                rel(bo)
                yield
                S.op("dve", lambda e: e.tensor_tensor(
                    out=yT[:, 8 + quad * 4:8 + (quad + 1) * 4, b * 128:(b + 1) * 128],
                    in0=y_[:].rearrange("p (c q) -> p c q", c=4),
                    in1=Gs[:, quad * 4:(quad + 1) * 4, b * 128:(b + 1) * 128], op=ALU.mult),
                    reads=[yk] + [("Gs", quad * 4 + cc) for cc in range(4)],
                    writes=[("yT", 8 + quad * 4 + cc) for cc in range(4)])
                yield

            def gen_att_tail():
                S.op("pool", lambda e: e.tensor_copy(out=kz[:, :, 0:128], in_=kz[:, :, NB * 128:(NB + 1) * 128]),
                     reads=["kcur"], writes=["kprev"])
                S.op("pool", lambda e: e.tensor_copy(out=vz[:, 0, :, :], in_=vz[:, NB, :, :]),
                     reads=["vcur"], writes=["vprev"])
                yield

            units = [(b, quad) for b in range(NB) for quad in range(2)]
            att_gens = [gen_att(b, quad, u) for u, (b, quad) in enumerate(units)] + [gen_att_tail()]
            def gen_A():
                active = []
                i = 0
                while active or i < len(b_gens):
                    while len(active) < 2 and i < len(b_gens):
                        active.append(b_gens[i])
                        i += 1
                    for g in list(active):
                        try:
                            next(g)
                        except StopIteration:
                            active.remove(g)
                    yield
                active = []
                i = 0
                while active or i < len(att_gens):
                    while len(active) < 2 and i < len(att_gens):
                        active.append(att_gens[i])
                        i += 1
                    for g in list(active):
                        try:
                            next(g)
                        except StopIteration:
                            active.remove(g)
                    yield

            def gen_rg(blk):
                si = blk % 2
                Ub = Us[si]
                zb, xb_ = zbuf[si], xabf[si]
                t_r, t_a, t_m, t_i, t_b, t_h, t_s, t_x = [Ub[:, i, :] for i in range(8)]
                Uk = [("U%d" % si, i) for i in range(8)]
                zh, zm, xk_ = ("zb_h", si), ("zb_m", si), ("xabf", si)
                bz = yield from getbank()
                inproj_chunk(*cgrp(18 + 2 * blk), bk=bz)
                S.op("pool", lambda e: e.tensor_copy(out=zb[:, 0:4], in_=halo[:, blk, :]),
                     reads=[("halo", blk)], writes=[zh])
                yield
                S.op("act", lambda e: e.copy(out=zb[:, 4:T + 4], in_=ps[bz][:, :]),
                     reads=[("ps", bz)], writes=[zm])
                S.op("act", lambda e: e.activation(
                    out=t_x, in_=ps[bz][:, :], func=AF.Identity, scale=col(C_CONVW + blk * 4 + 3),
                    bias=col(C_CONVB + blk)),
                    reads=[("ps", bz), "cst"], writes=[Uk[7]])
                rel(bz)
                yield
                for k in (1, 2, 3):
                    S.op("dve", lambda e, k=k: e.scalar_tensor_tensor(
                        out=t_x, in0=zb[:, 4 - k:T + 4 - k], scalar=col(C_CONVW + blk * 4 + 3 - k), in1=t_x,
                        op0=ALU.mult, op1=ALU.add),
                        reads=[zh, zm, Uk[7], "cst"], writes=[Uk[7]])
                    yield
                S.op("pool", lambda e: e.tensor_copy(out=halo[:, blk, :], in_=zb[:, T:T + 4]),
                     reads=[zm, zh], writes=[("halo", blk)])
                S.op("dve", lambda e: e.tensor_copy(out=xb_[:], in_=t_x), reads=[Uk[7]], writes=[xk_])
                yield
                br = yield from getbank()
                S.op("pe", lambda e: e.matmul(ps[br][:, :], lhsT=gw_bf[:, 0, blk, :], rhs=xb_[:],
                                              start=True, stop=True),
                     reads=["gw_bf", xk_], writes=[("ps", br)])
                bi = yield from getbank()
                S.op("pe", lambda e: e.matmul(ps[bi][:, :], lhsT=gw_bf[:, 1, blk, :], rhs=xb_[:],
                                              start=True, stop=True),
                     reads=["gw_bf", xk_], writes=[("ps", bi)])
                yield
                S.op("act", lambda e: e.activation(out=t_r, in_=ps[br][:, :], func=AF.Tanh, scale=0.5,
                                                   bias=kcol[:, 32 + blk:33 + blk]),
                     reads=[("ps", br), "kc_hb"], writes=[Uk[0]])
                rel(br)
                yield
                S.op("act", lambda e: e.activation(out=t_i, in_=ps[bi][:, :], func=AF.Tanh, scale=0.5,
                                                   bias=kcol[:, 40 + blk:41 + blk]),
                     reads=[("ps", bi), "kc_hb"], writes=[Uk[3]])
                rel(bi)
                yield
                S.op("act", lambda e: e.activation(out=t_a, in_=t_r, func=AF.Exp, scale=Kh(blk), bias=Kh(blk)),
                     reads=[Uk[0], "kc_K"], writes=[Uk[1]])
                yield
                S.op("dve", lambda e: e.tensor_tensor(out=t_m, in0=t_a, in1=t_a, op=ALU.mult),
                     reads=[Uk[1]], writes=[Uk[2]])
                yield
                S.op("dve", lambda e: e.scalar_tensor_tensor(out=t_b, in0=t_i, scalar=1.0, in1=t_x,
                                                             op0=ALU.add, op1=ALU.mult),
                     reads=[Uk[3], Uk[7]], writes=[Uk[4]])
                yield
                S.op("dve", lambda e: e.tensor_scalar(out=t_m, in0=t_m, scalar1=1.0, scalar2=-1.0,
                                                      op0=ALU.min, op1=ALU.mult),
                     reads=[Uk[2]], writes=[Uk[2]])
                yield
                S.op("act", lambda e: e.activation(out=t_m, in_=t_m, func=AF.Sqrt, bias=1.0),
                     reads=[Uk[2]], writes=[Uk[2]])
                yield
                S.op("dve", lambda e: e.scalar_tensor_tensor(out=t_b, in0=t_b, scalar=0.5, in1=t_m,
                                                             op0=ALU.mult, op1=ALU.mult),
                     reads=[Uk[4], Uk[2]], writes=[Uk[4]])
                yield
                bg = yield from getbank()
                inproj_chunk(*cgrp(19 + 2 * blk), bk=bg)
                yield
                S.op("act", lambda e: e.activation(out=t_s, in_=ps[bg][:, :], func=AF.Tanh, scale=0.5),
                     reads=[("ps", bg)], writes=[Uk[6]])
                yield
                S.op("dve", lambda e: e.tensor_tensor_scan(
                    out=t_h, data0=t_a, data1=t_b, initial=hstate[:, blk:blk + 1], op0=ALU.mult, op1=ALU.add),
                    reads=[Uk[1], Uk[4], ("hstate", blk)], writes=[Uk[5]])
                yield
                S.op("pool", lambda e: e.tensor_copy(out=hstate[:, blk:blk + 1], in_=Ub[:, 5, T - 1:T]),
                     reads=[Uk[5]], writes=[("hstate", blk)])
                S.op("dve", lambda e: e.scalar_tensor_tensor(out=t_s, in0=t_s, scalar=1.0, in1=ps[bg][:, :],
                                                             op0=ALU.add, op1=ALU.mult),
                     reads=[Uk[6], ("ps", bg)], writes=[Uk[6]])
                rel(bg)
                yield
                S.op("dve", lambda e: e.scalar_tensor_tensor(out=yT[:, blk, :], in0=t_h, scalar=0.5, in1=t_s,
                                                             op0=ALU.mult, op1=ALU.mult),
                     reads=[Uk[5], Uk[6]], writes=[("yT", blk)])
                yield

            multi_zipper([([gen_A()], 1, 1), ([gen_rg(blk) for blk in range(8)], 2, 2)])
            if dbg == 5:
                return
            if n_layers > 1 and dbg == 0:
                out_proj(xb, xk, gb0 + 9, pre, blk_hook=lambda b: norm_stats_blk(xb, xk, 1, b))
            else:
                out_proj(xb, xk, gb0 + 9, pre)

        def layer1(t, xb, xk, gb1, pre=None):
            norm_tr()
            vslots = [wslot(gb1 + vg) for vg in range(4)]
            for b in range(NB):
                banks = []
                for vg in range(4):
                    bk = nbank()
                    banks.append(bk)
                    wv = ring[:, vslots[vg], :].rearrange("p (k n) -> p k n", k=KC)
                    mm_group(ps[bk][:, :], [(hT[:, kc, b * 128:(b + 1) * 128], wv[:, kc, :]) for kc in range(KC)],
                             reads=[("w", vslots[vg])] + hT_keys, writes=[("ps", bk)])
                    wdone(gb1 + vg)
                    S.op("dve", lambda e, bk=bk, vg=vg: e.bn_stats(out=stats[:, vg, :], in_=ps[bk][:, :]),
                         reads=[("ps", bk)], writes=[("stats", vg)])
                S.op("dve", lambda e: e.bn_aggr(out=mv[:], in_=stats[:].rearrange("p a s -> p (a s)")),
                     reads=[("stats", vg) for vg in range(4)], writes=["mv"])
                S.op("act", lambda e: e.activation(out=lnr[:, 0:1], in_=mv[:, 1:2], func=AF.Sqrt, bias=EPS),
                     reads=["mv"], writes=["lnr"])
                S.op("dve", lambda e: e.reciprocal(out=lnr[:, 1:2], in_=lnr[:, 0:1]), reads=["lnr"],
                     writes=["lnr2"])
                for vg in range(4):
                    S.op("dve", lambda e, vg=vg, b=b, bk=banks[vg]: e.tensor_scalar(
                        out=vhat[:, b, vg * 512:(vg + 1) * 512], in0=ps[bk][:, :], scalar1=mv[:, 0:1],
                        scalar2=lnr[:, 1:2], op0=ALU.subtract, op1=ALU.mult),
                        reads=[("ps", banks[vg]), "mv", "lnr2"], writes=[("U0", 2 * b), ("U0", 2 * b + 1)])
                    rel(banks[vg])

            def gen_l1(j):
                g = j // 2
                si = j % 2
                S1, G1, Y1 = sqb[si], rsb[si], expS[si]
                s1k, g1k, y1k = "sqb%d" % si, "rsb%d" % si, ("expS", si)
                gi = gb1 + 4 + j // 2
                bu = yield from getbank()
                inproj_chunk(gi, (j % 2) * 2, bk=bu)
                yield
                bg = yield from getbank()
                inproj_chunk(gi, (j % 2) * 2 + 1, bk=bg)
                yield
                S.op("act", lambda e: e.activation(out=G1[:], in_=ps[bg][:, :], func=AF.Silu),
                     reads=[("ps", bg)], writes=[g1k])
                rel(bg)
                bsg = yield from getbank()
                for b in range(NB):
                    S.op("pe", lambda e, b=b: e.matmul(
                        ps[bsg][:, b * 128:(b + 1) * 128], lhsT=vhat[:, b, j * 128:(j + 1) * 128],
                        rhs=wtril_bf[:, g, :], start=True, stop=True),
                        reads=[("U0", 2 * b), ("U0", 2 * b + 1), "wtril_bf"], writes=[("ps", bsg)])
                yield
                S.op("dve", lambda e: e.scalar_tensor_tensor(
                    out=S1[:].rearrange("p (b t) -> p b t", b=NB),
                    in0=ps[bsg][:, :].rearrange("p (b t) -> p b t", b=NB), scalar=col(C_LNG + j),
                    in1=Cc[:, j:j + 1, :].to_broadcast([P, NB, 128]), op0=ALU.mult, op1=ALU.add),
                    reads=[("ps", bsg), "Cc", "cst"], writes=[s1k])
                rel(bsg)
                yield
                S.op("dve", lambda e: e.tensor_tensor(out=Y1[:], in0=ps[bu][:, :], in1=S1[:], op=ALU.mult),
                     reads=[("ps", bu), s1k], writes=[y1k])
                rel(bu)
                yield
                S.op("dve", lambda e: e.tensor_tensor(out=yT[:, j, :], in0=Y1[:], in1=G1[:], op=ALU.mult),
                     reads=[y1k, g1k], writes=[("yT", j)])
                yield

            zipper([gen_l1(j) for j in range(16)], 2)
            out_proj(xb, xk, gb1 + 12, pre)

        def load_x(t):
            xb = xbuf[t % NXB]
            xk = "x%d" % (t % NXB)
            S.dma("sp", "xl%d" % (t % NXB), lambda e: e.dma_start(
                out=xb[:], in_=x_d[t * T:(t + 1) * T, :].rearrange("(b p) d -> p b d", p=P)),
                writes=[(xk, b) for b in range(NB)])

        load_x(0)
        if dbg != 1:
            norm_T(xbuf[0], "x0", 0)
        for t in range(n_tiles):
            xb = xbuf[t % NXB]
            xk = "x%d" % (t % NXB)
            nxt = None
            if t + 1 < n_tiles:
                load_x(t + 1)
                if dbg != 1:
                    nxt = (lambda t=t: norm_stats(xbuf[(t + 1) % NXB], "x%d" % ((t + 1) % NXB), 0), norm_tr)
            if dbg != 1:
                if n_layers > 1 and dbg == 0:
                    layer0(t, xb, xk, t * GPT)
                    layer1(t, xb, xk, t * GPT + 13, nxt)
                else:
                    layer0(t, xb, xk, t * GPT, nxt)
            S.dma("sp", "xs%d" % (t % NXB), lambda e, t=t, xb=xb: e.dma_start(
                out=out_d[t * T:(t + 1) * T, :].rearrange("(b p) d -> p b d", p=P), in_=xb[:]),
                reads=[(xk, b) for b in range(NB)])
        S.final_wait("sp", [("D", sname, 16 * c) for sname, c in S.dcnt.items()])

        sems = {}
        for k in S.sem_keys():
            sems[k] = st.enter_context(nc.semaphore("s_%s_%s" % (k[0], k[1])))
        S.emit(nc, sems)
    return nc


def t5_bucket(dist):
    max_exact = 16
    df = np.maximum(dist, 1).astype(np.float32)
    large = max_exact + (np.log(df / np.float32(max_exact)) / np.float32(np.log(128 / 16))
                         * np.float32(16)).astype(np.int32)
    large = np.minimum(large, 31)
    return np.where(dist < max_exact, dist, large)


def prepare_shared(inp, l0_order):
    f32 = np.float32
    w_in_a = np.asarray(inp["w_in_a"], f32)[0]
    w_out_a = np.asarray(inp["w_out_a"], f32)[0]
    w_in_c = np.asarray(inp["w_in_c"], f32)[0]
    w_out_c = np.asarray(inp["w_out_c"], f32)[0]

    def chunk_in(w, cols):
        return w[:, cols].reshape(KC, P, len(cols)).transpose(1, 0, 2)

    o_za, o_ga, o_q, o_k, o_v, o_gb = 0, 1024, 2048, 3072, 3200, 3328
    ar = np.arange
    chunks = [chunk_in(w_in_a, o_v + ar(128)), chunk_in(w_in_a, o_k + ar(128))]
    head_cols = lambda base, c: np.concatenate([base + c * 64 + ar(64), base + (8 + c) * 64 + ar(64)])
    for c in range(8):
        chunks.append(chunk_in(w_in_a, head_cols(o_q, c)))
    for c in range(8):
        chunks.append(chunk_in(w_in_a, head_cols(o_gb, c)))
    for blk in range(8):
        chunks.append(chunk_in(w_in_a, o_za + blk * 128 + ar(128)))
        chunks.append(chunk_in(w_in_a, o_ga + blk * 128 + ar(128)))
    chunks = [chunks[ci] for ci in l0_order]
    while len(chunks) % 4:
        chunks.append(np.zeros_like(chunks[0]))
    groups = []
    for g in range(len(chunks) // 4):
        groups.append(np.stack(chunks[4 * g:4 * g + 4], axis=1).reshape(P, SLOT))

    def out_groups(w, rows_of_chunk):
        gs = []
        for half in range(2):
            for jg in range(2):
                blk = np.stack([w[rows_of_chunk(jg * 8 + jj)][:, half * 512:(half + 1) * 512]
                                for jj in range(8)], axis=1)
                gs.append(blk.reshape(P, SLOT))
        return gs

    def rows_a(j):
        if j < 8:
            return j * 128 + ar(128)
        c = j - 8
        return np.concatenate([1024 + c * 64 + ar(64), 1024 + (8 + c) * 64 + ar(64)])
    groups += out_groups(w_out_a, rows_a)
    for vg in range(4):
        cols = 2048 + vg * 512 + ar(512)
        groups.append(w_in_c[:, cols].reshape(KC, P, 512).transpose(1, 0, 2).reshape(P, SLOT))
    for j in range(0, 16, 2):
        cs = [chunk_in(w_in_c, j * 128 + ar(128)), chunk_in(w_in_c, 4096 + j * 128 + ar(128)),
              chunk_in(w_in_c, (j + 1) * 128 + ar(128)), chunk_in(w_in_c, 4096 + (j + 1) * 128 + ar(128))]
        groups.append(np.stack(cs, axis=1).reshape(P, SLOT))
    groups += out_groups(w_out_c, lambda j: j * 128 + ar(128))
    wst = np.ascontiguousarray(np.stack(groups, axis=0), dtype=f32)
    assert wst.shape == (NG_TILE, P, SLOT), wst.shape

    cst = np.zeros((P, C_TOTAL), f32)
    pcol = lambda v: np.asarray(v, f32).reshape(8, P).T
    cw = np.asarray(inp["conv_w"], f32)[0]
    cst[:, C_CONVW:C_CONVW + 32] = cw.reshape(4, 8, P).transpose(2, 1, 0).reshape(P, 32)
    cst[:, C_CONVB:C_CONVB + 8] = pcol(inp["conv_b"][0])
    cst[:, C_GAB:C_GAB + 8] = pcol(inp["gate_a_b"][0])
    cst[:, C_GXB:C_GXB + 8] = pcol(inp["gate_x_b"][0])
    cst[:, C_LAM:C_LAM + 8] = pcol(inp["lru_lambda"][0])
    cst[:, C_QG] = np.tile(np.asarray(inp["q_norm_g"], f32)[0], 2)
    cst[:, C_KG] = np.tile(np.asarray(inp["k_norm_g"], f32)[0], 2)
    sinks = np.asarray(inp["sinks"], f32)[0]
    cst[:64, C_SINK:C_SINK + 8] = sinks[None, 0:8]
    cst[64:, C_SINK:C_SINK + 8] = sinks[None, 8:16]
    cst[:, C_LNG:C_LNG + 16] = np.asarray(inp["ln_v_g"], f32)[0].reshape(16, P).T
    cst[:, C_LNB:C_LNB + 16] = np.asarray(inp["ln_v_b"], f32)[0].reshape(16, P).T
    ga = np.asarray(inp["gate_a_w"], f32)[0]
    gx = np.asarray(inp["gate_x_w"], f32)[0]
    gw = np.stack([ga.transpose(1, 0, 2), gx.transpose(1, 0, 2)], axis=1)
    csu = np.zeros((P, U_TOTAL), f32)
    csu[:, U_GATEW:U_GATEW + 2048] = gw.reshape(P, 2048)
    s_i = ar(128)[:, None]
    q_i = ar(128)[None, :]
    amask = np.stack([(s_i > q_i), (s_i <= q_i)], axis=1).astype(f32)
    csu[:, U_AMASK:U_AMASK + 256] = amask.reshape(P, 256)
    spw = np.asarray(inp["spatial_w"], f32)[0]
    csu[:, U_SPW:U_SPW + 1024] = spw.transpose(2, 0, 1).reshape(P, 1024)
    csu[:, U_CMASK:U_CMASK + 128] = (s_i <= q_i).astype(f32)
    spb = np.asarray(inp["spatial_b"], f32)[0]
    csu[:, U_SPB:U_SPB + 1024] = np.broadcast_to(spb.reshape(1, 1024), (P, 1024))
    csu[:, U_IDENT:U_IDENT + 128] = np.eye(128, dtype=f32)
    ob = np.zeros((128, 128), f32)
    ob[:64, :64] = 1.0
    ob[64:, 64:] = 1.0
    cst[:, C_ONESB:C_ONESB + 128] = ob

    gain = np.stack([np.broadcast_to(np.asarray(inp["norm_a"], f32)[0][None, :], (P, D)),
                     np.broadcast_to(np.asarray(inp["norm_c"], f32)[0][None, :], (P, D))], axis=0)
    gain = np.ascontiguousarray(gain, dtype=f32)

    rel = np.asarray(inp["rel_bias"], f32)
    kj = ar(256)[None, :]
    qi = ar(128)[:, None]
    dist = 128 + qi - kj
    bucket = t5_bucket(np.maximum(dist, 0))
    bias = rel[bucket]
    biasT = bias.transpose(1, 2, 0).reshape(2, 128, 16, 128).transpose(1, 0, 2, 3)
    biasT = np.ascontiguousarray(biasT.reshape(P, 2 * 16 * 128), dtype=f32)
    return {"wst": wst, "cst": cst, "csu": csu, "gain": gain, "biasT": biasT}


_CACHE = {}


def get_l0_order():
    if "order" not in _CACHE:
        rec = []
        build_program(1, 4, record=rec)
        order = []
        for ci in rec:
            if ci not in order:
                order.append(ci)
        assert sorted(order) == list(range(34)), order
        _CACHE["order"] = order
    return _CACHE["order"]


def kernel(**inputs):
    x = np.asarray(inputs["x"], np.float32)
    bsz, s, d = x.shape
    per = bsz // N_CORES
    order = get_l0_order()
    shared = prepare_shared(inputs, order)
    n_tiles = per * s // T
    key = (n_tiles, s // T)
    if key not in _CACHE:
        _CACHE[key] = build_program(n_tiles, s // T, l0_order=order)
    nc = _CACHE[key]
    in_maps = []
    for c in range(N_CORES):
        m = dict(shared)
        m["x"] = np.ascontiguousarray(x[c * per:(c + 1) * per].reshape(per * s, d))
        in_maps.append(m)
    res = run_bass_kernel_spmd(nc, in_maps, core_ids=list(range(N_CORES)))
    outs = [np.asarray(r["out"], np.float32).reshape(per, s, d) for r in res.results]
    return np.concatenate(outs, axis=0)
```

```python
import numpy as np
from contextlib import ExitStack
import concourse.bass as bass
import concourse.mybir as mybir
from concourse.bass_utils import run_bass_kernel_spmd

F32 = mybir.dt.float32
BF16 = mybir.dt.bfloat16
AF = mybir.ActivationFunctionType
ALU = mybir.AluOpType

P = 128
T = 512
NB = 4
D = 1024
KC = 8
EPS = 1e-6
N_CORES = 8
SEQ = 2048
TPS = SEQ // T
NSLOT = 5
SLOT = 4096
NG_L0_IN = 9
NG_TILE = 29

C_CONVW = 0
C_CONVB = 32
C_GAB = 40
C_GXB = 48
C_LAM = 56
C_QG = 64
C_KG = 65
C_SINK = 66
C_LNG = 74
C_LNB = 90
C_ONESB = 128
C_TOTAL = 256
U_GATEW = 0
U_SPW = 2048
U_SPB = 3072
U_AMASK = 4096
U_CMASK = 4352
U_IDENT = 4480
U_TOTAL = 4608


class Sched:
    ENG = ("pe", "act", "dve", "pool", "sp")

    def __init__(self):
        self.q = {e: [] for e in self.ENG}
        self.cnt = {e: 0 for e in self.ENG}
        self.dcnt = {}
        self.lastw = {}
        self.readers = {}
        self.waited = {e: {} for e in self.ENG}
        self.nwaits = 0

    def _deps(self, eng, reads, writes, is_dma):
        need = {}

        def add(t, kind):
            if t[0] == "E" and t[1] == eng and not is_dma:
                if eng == "pe":
                    return
                if kind != "raw" and t[2] != self.cnt[eng]:
                    return
            sk = (t[0], t[1])
            if need.get(sk, 0) < t[2]:
                need[sk] = t[2]

        for k in reads:
            t = self.lastw.get(k)
            if t is not None:
                add(t, "raw")
        for k in writes:
            t = self.lastw.get(k)
            if t is not None:
                add(t, "waw")
            for sk, v in self.readers.get(k, {}).items():
                add((sk[0], sk[1], v), "war")
        waits = []
        wd = self.waited[eng]
        for sk, v in need.items():
            if wd.get(sk, 0) >= v:
                continue
            wd[sk] = v
            waits.append((sk, v))
        self.nwaits += len(waits)
        return waits

    def _commit(self, tok, reads, writes):
        for k in writes:
            self.lastw[k] = tok
            self.readers[k] = {}
        sk = (tok[0], tok[1])
        for k in reads:
            r = self.readers.setdefault(k, {})
            if r.get(sk, 0) < tok[2]:
                r[sk] = tok[2]

    def op(self, eng, fn, reads=(), writes=()):
        waits = self._deps(eng, reads, writes, False)
        self.cnt[eng] += 1
        tok = ("E", eng, self.cnt[eng])
        self.q[eng].append((waits, fn, ("E", eng), 1))
        self._commit(tok, reads, writes)
        return tok

    def dma(self, qeng, sem, fn, reads=(), writes=()):
        waits = self._deps(qeng, reads, writes, True)
        self.dcnt[sem] = self.dcnt.get(sem, 0) + 1
        tok = ("D", sem, 16 * self.dcnt[sem])
        self.q[qeng].append((waits, fn, ("D", sem), 16))
        self._commit(tok, reads, writes)
        return tok

    def final_wait(self, eng, toks):
        waits = [((t[0], t[1]), t[2]) for t in toks]
        self.q[eng].append((waits, None, None, 0))

    def sem_keys(self):
        keys = [("E", e) for e in self.ENG]
        keys += [("D", s) for s in self.dcnt]
        return keys

    def emit(self, nc, sems):
        with nc.Block() as block:
            decos = {"pe": block.tensor, "act": block.scalar, "dve": block.vector,
                     "pool": block.gpsimd, "sp": block.sync}
            for e in self.ENG:
                items = self.q[e]

                def body(engh, items=items):
                    for waits, fn, inc, amt in items:
                        for sk, v in waits:
                            engh.wait_ge(sems[sk], v)
                        if fn is None:
                            continue
                        ins = fn(engh)
                        ins.then_inc(sems[inc], amt)

                decos[e](body)


def zipper(gens, depth):
    gens = list(gens)
    active = []
    i = 0
    while active or i < len(gens):
        while len(active) < depth and i < len(gens):
            active.append(gens[i])
            i += 1
        for g in list(active):
            try:
                next(g)
            except StopIteration:
                active.remove(g)


def multi_zipper(streams):
    st = [{"gens": list(g), "depth": d, "i": 0, "active": []} for g, d in streams]
    while any(x["active"] or x["i"] < len(x["gens"]) for x in st):
        for x in st:
            while len(x["active"]) < x["depth"] and x["i"] < len(x["gens"]):
                x["active"].append(x["gens"][x["i"]])
                x["i"] += 1
            for g in list(x["active"]):
                try:
                    next(g)
                except StopIteration:
                    x["active"].remove(g)


def build_program(n_tiles, tps=TPS, n_layers=2, dbg=0, l0_order=None, record=None):
    nc = bass.Bass("TRN2", target_bir_lowering=False)
    ntok = n_tiles * T
    x_d = nc.dram_tensor("x", [ntok, D], F32, kind="ExternalInput").ap()
    wst_d = nc.dram_tensor("wst", [NG_TILE, P, SLOT], F32, kind="ExternalInput").ap()
    cst_d = nc.dram_tensor("cst", [P, C_TOTAL], F32, kind="ExternalInput").ap()
    csu_d = nc.dram_tensor("csu", [P, U_TOTAL], F32, kind="ExternalInput").ap()
    gain_d = nc.dram_tensor("gain", [2, P, D], F32, kind="ExternalInput").ap()
    bias_d = nc.dram_tensor("biasT", [P, 2 * 16 * 128], F32, kind="ExternalInput").ap()
    out_d = nc.dram_tensor("out", [ntok, D], F32, kind="ExternalOutput").ap()

    S = Sched()
    st = ExitStack()
    with st:
        def sb(name, shape, dt=F32):
            return st.enter_context(nc.sbuf_tensor("sb_" + name, shape, dt))

        NXB = 2
        xbuf = [sb("xbuf%d" % i, [P, NB, D]) for i in range(NXB)]
        hT = sb("hT", [P, KC, T], BF16)
        ss = sb("ss", [P, 8])
        rs = sb("rs", [P, 8])
        ring = sb("ring", [P, NSLOT, SLOT], BF16)
        cst = sb("cst", [P, C_TOTAL])
        gain = sb("gain", [P, 2, D])
        Et = sb("Et", [P, 2, 16, 128], BF16)
        vz = sb("vz", [P, NB + 1, 2, 128], BF16)
        kz = sb("kz", [P, 2, (NB + 1) * 128], BF16)
        onesz = sb("onesz", [P, 2, 128], BF16)
        qnT = sb("qnT", [P, 8, T], BF16)
        Gs = sb("Gs", [P, 8, T], BF16)
        sqb = [sb("sqb%d" % i, [P, T]) for i in range(2)]
        rsb = [sb("rsb%d" % i, [P, T]) for i in range(2)]
        expS = [sb("expS%d" % i, [P, T]) for i in range(2)]
        PT = [sb("PT%d" % i, [P, T], BF16) for i in range(4)]
        den = [sb("den0", [P, T]), expS[0]]
        yb = [sb("yb0", [P, T]), expS[1]]
        denk = [("den", 0), ("expS", 0)]
        ybk = [("yb", 0), ("expS", 1)]
        junk = den[0][:].bitcast(BF16)
        yT = sb("yT", [P, 16, T], BF16)
        Us = [sb("U%d" % i, [P, 8, T]) for i in range(2)]
        zbuf = [sb("zbuf%d" % i, [P, T + 4]) for i in range(2)]
        xabf = [sb("xabf%d" % i, [P, T], BF16) for i in range(2)]
        halo = sb("halo", [P, 8, 4])
        hstate = sb("hstate", [P, 8])
        Cc = sb("Cc", [P, 16, 128])
        stats = sb("stats", [P, 4, 6])
        mv = sb("mv", [P, 2])
        lnr = sb("lnr", [P, 2])
        kcol = sb("kcol", [P, 64])
        ident_bf = sb("ident_bf", [P, 128], BF16)
        gw_bf = sb("gw_bf", [P, 2, 8, 128], BF16)
        wtril_bf = sb("wtril_bf", [P, 8, 128], BF16)

        NPS = 8
        ps = [st.enter_context(nc.psum_tensor("ps%d" % i, [P, 512], F32)) for i in range(NPS)]
        U = Us[0]
        hb = Gs[:].rearrange("p (b j) t -> p b (j t)", b=NB)
        hbk = lambda b: [("Gs", 2 * b), ("Gs", 2 * b + 1)]
        wtril_f = U[:, 0:2, :].rearrange("p a t -> p (a t)").rearrange("p (g t) -> p g t", g=8)
        vhat = U[:].rearrange("p a t -> p (a t)").bitcast(BF16).rearrange("p (b f) -> p b f", b=NB)
        xflat = xbuf[0][:].rearrange("p b d -> p (b d)")
        usu = U[:, 7, :]
        x0keys = [("x0", b) for b in range(NB)]

        free_banks = list(range(NPS))

        def nbank():
            assert free_banks, "PSUM banks exhausted"
            return free_banks.pop(0)

        def getbank():
            n = 0
            while not free_banks:
                n += 1
                assert n < 10000, "PSUM bank wait deadlock at build time"
                yield
            return free_banks.pop(0)

        def rel(bk):
            assert bk not in free_banks
            free_banks.append(bk)

        def col(c, n=1):
            return cst[:, c:c + n]

        u1flat = Us[1][:].rearrange("p a t -> p (a t)")
        u1keys = [("U1", i) for i in range(8)]
        x1flat = xbuf[1][:].rearrange("p b d -> p (b d)")
        x1keys = [("x1", b) for b in range(NB)]

        def load_x(t):
            xb = xbuf[t % NXB]
            xk = "x%d" % (t % NXB)
            S.dma("sp", "xl%d" % (t % NXB), lambda e: e.dma_start(
                out=xb[:], in_=x_d[t * T:(t + 1) * T, :].rearrange("(b p) d -> p b d", p=P)),
                writes=[(xk, b) for b in range(NB)])

        load_x(0)
        S.dma("sp", "c0", lambda e: e.dma_start(out=cst[:], in_=cst_d), writes=["cst"])
        S.dma("sp", "c1", lambda e: e.dma_start(out=gain[:], in_=gain_d.rearrange("g p d -> p g d")),
              writes=["gain"])
        S.dma("sp", "c4", lambda e: e.dma_start(out=usu, in_=csu_d[:, 4096:U_TOTAL]), writes=[("U0", 7)])
        S.dma("sp", "c3", lambda e: e.dma_start(out=u1flat, in_=csu_d[:, 0:4096]), writes=u1keys)
        S.dma("sp", "c2", lambda e: e.dma_start(out=x1flat, in_=bias_d), writes=x1keys)

        S.op("dve", lambda e: e.tensor_copy(out=ident_bf[:], in_=usu[:, 384:512]),
             reads=[("U0", 7)], writes=["ident_bf"])
        S.op("dve", lambda e: e.memset(onesz[:], 0.0), writes=["onesz"])
        S.op("dve", lambda e: e.memset(onesz[:, 0, 0:64], 1.0), writes=["onesz"])
        S.op("dve", lambda e: e.memset(onesz[:, 1, 64:128], 1.0), writes=["onesz"])
        S.op("dve", lambda e: e.memset(kz[:], 0.0), writes=["kprev", "kcur"])
        S.op("dve", lambda e: e.memset(vz[:], 0.0), writes=["vprev", "vcur"])
        S.op("dve", lambda e: e.tensor_copy(out=gw_bf[:].rearrange("p a n j -> p (a n j)"),
                                            in_=u1flat[:, U_GATEW:U_GATEW + 2048]),
             reads=u1keys, writes=["gw_bf"])
        for kt in range(2):
            def f(e, kt=kt):
                v = x1flat[:, kt * 2048:(kt + 1) * 2048]
                return e.tensor_scalar(out=v, in0=v, scalar1=8.0, scalar2=1.0, op0=ALU.mult, op1=ALU.mult)
            S.op("pool", f, reads=x1keys, writes=x1keys)

            def f(e, kt=kt):
                m = usu[:, kt * 128:(kt + 1) * 128]
                v = x1flat[:, kt * 2048:(kt + 1) * 2048].rearrange("p (h q) -> p h q", h=16)
                return e.tensor_tensor(out=v, in0=v, in1=m.unsqueeze(1).to_broadcast([P, 16, 128]), op=ALU.mult)
            S.op("pool", f, reads=x1keys + [("U0", 7)], writes=x1keys)
        S.op("pool", lambda e: e.tensor_scalar(out=usu[:, 0:256], in0=usu[:, 0:256], scalar1=-1.0, scalar2=480.0,
                                               op0=ALU.add, op1=ALU.mult),
             reads=[("U0", 7)], writes=[("U0", 7)])
        for kt in range(2):
            def f(e, kt=kt):
                m = usu[:, kt * 128:(kt + 1) * 128]
                v = x1flat[:, kt * 2048:(kt + 1) * 2048].rearrange("p (h q) -> p h q", h=16)
                return e.tensor_tensor(out=Et[:, kt], in0=v, in1=m.unsqueeze(1).to_broadcast([P, 16, 128]),
                                       op=ALU.add)
            S.op("pool", f, reads=x1keys + [("U0", 7)], writes=["Et"])
        S.op("act", lambda e: e.activation(out=kcol[:, 16:24], in_=col(C_SINK, 8), func=AF.Exp),
             reads=["cst"], writes=["kc_sink"])
        S.op("dve", lambda e: e.tensor_scalar(out=kcol[:, 32:40], in0=col(C_GAB, 8), scalar1=0.5, scalar2=None,
                                              op0=ALU.mult), reads=["cst"], writes=["kc_hb"])
        S.op("dve", lambda e: e.tensor_scalar(out=kcol[:, 40:48], in0=col(C_GXB, 8), scalar1=0.5, scalar2=None,
                                              op0=ALU.mult), reads=["cst"], writes=["kc_hb"])
        S.op("act", lambda e: e.activation(out=kcol[:, 24:32], in_=col(C_LAM, 8), func=AF.Exp, scale=-1.0),
             reads=["cst"], writes=["kc_e"])
        S.op("dve", lambda e: e.tensor_scalar(out=kcol[:, 48:56], in0=kcol[:, 24:32], scalar1=2.0,
                                              scalar2=None, op0=ALU.add),
             reads=["kc_e"], writes=["kc_t"])
        S.op("dve", lambda e: e.reciprocal(out=kcol[:, 48:56], in_=kcol[:, 48:56]),
             reads=["kc_t"], writes=["kc_t"])
        S.op("dve", lambda e: e.tensor_tensor(out=kcol[:, 24:32], in0=kcol[:, 24:32], in1=kcol[:, 48:56],
                                              op=ALU.mult),
             reads=["kc_t", "kc_e"], writes=["kc_e"])
        S.op("dve", lambda e: e.tensor_tensor(out=kcol[:, 48:56], in0=kcol[:, 24:32], in1=kcol[:, 24:32],
                                              op=ALU.mult),
             reads=["kc_e"], writes=["kc_t"])
        S.op("dve", lambda e: e.tensor_scalar(out=kcol[:, 0:8], in0=kcol[:, 48:56], scalar1=1.0 / 9,
                                              scalar2=1.0 / 7, op0=ALU.mult, op1=ALU.add),
             reads=["kc_t"], writes=["kc_p"])
        for cc in (1.0 / 5, 1.0 / 3, 1.0):
            S.op("dve", lambda e: e.tensor_tensor(out=kcol[:, 0:8], in0=kcol[:, 0:8], in1=kcol[:, 48:56],
                                                  op=ALU.mult),
                 reads=["kc_p", "kc_t"], writes=["kc_p"])
            S.op("dve", lambda e, cc=cc: e.tensor_scalar(out=kcol[:, 0:8], in0=kcol[:, 0:8], scalar1=cc,
                                                         scalar2=None, op0=ALU.add),
                 reads=["kc_p"], writes=["kc_p"])
        S.op("dve", lambda e: e.tensor_tensor(out=kcol[:, 0:8], in0=kcol[:, 0:8], in1=kcol[:, 24:32],
                                              op=ALU.mult),
             reads=["kc_p", "kc_e"], writes=["kc_p"])
        S.op("dve", lambda e: e.tensor_scalar(out=kcol[:, 8:16], in0=kcol[:, 0:8], scalar1=-16.0,
                                              scalar2=None, op0=ALU.mult),
             reads=["kc_p"], writes=["kc_K"])
        S.op("dve", lambda e: e.tensor_scalar(out=kcol[:, 0:8], in0=kcol[:, 0:8], scalar1=-8.0,
                                              scalar2=None, op0=ALU.mult),
             reads=["kc_p", "kc_K"], writes=["kc_K"])
        Kh = lambda blk: kcol[:, blk:blk + 1]
        Kf = lambda blk: kcol[:, 8 + blk:9 + blk]

        def l1_setup():
            if n_layers > 1:
                def f(e):
                    m = usu[:, 256:384]
                    return e.tensor_tensor(out=wtril_f,
                                           in0=u1flat[:, U_SPW:U_SPW + 1024].rearrange("p (g t) -> p g t", g=8),
                                           in1=m.unsqueeze(1).to_broadcast([P, 8, 128]), op=ALU.mult)
                S.op("dve", f, reads=u1keys + [("U0", 7)], writes=[("U0", 0), ("U0", 1)])
                S.op("dve", lambda e: e.tensor_copy(out=wtril_bf[:], in_=wtril_f),
                     reads=[("U0", 0), ("U0", 1)], writes=["wtril_bf"])
                S.op("dve", lambda e: e.memset(sqb[0][:, 0:128], 1.0), writes=["sqb0"])
                for g in range(8):
                    bk = nbank()

                    def f(e, g=g, bk=bk):
                        return e.matmul(ps[bk][:, 0:128], lhsT=sqb[0][:, 0:128], rhs=wtril_f[:, g, :],
                                        start=True, stop=True)
                    S.op("pe", f, reads=["sqb0", ("U0", 0), ("U0", 1)], writes=[("ps", bk)])
                    for jj in range(2):
                        j = 2 * g + jj

                        def f2(e, g=g, bk=bk, j=j):
                            return e.scalar_tensor_tensor(
                                out=Cc[:, j, :], in0=ps[bk][:, 0:128], scalar=col(C_LNB + j),
                                in1=u1flat[:, U_SPB + g * 128:U_SPB + (g + 1) * 128], op0=ALU.mult, op1=ALU.add)
                        S.op("dve", f2, reads=[("ps", bk), "cst"] + u1keys, writes=["Cc"])
                    rel(bk)


        GPT = NG_TILE if n_layers > 1 else 13
        rem_tile = [4] * 8 + [2] + [4] * 4 + ([4] * 16 if n_layers > 1 else [])
        seq_groups = [g for t in range(n_tiles) for g in range(GPT)]
        rem = [r for t in range(n_tiles) for r in rem_tile]
        rstate = {"next_load": 0, "oldest": 0}

        def _load():
            i = rstate["next_load"]
            rstate["next_load"] += 1
            if i >= len(seq_groups):
                return
            slot = i % NSLOT
            g = seq_groups[i]
            S.dma("pool", "w%d" % slot,
                  lambda e: e.dma_start(out=ring[:, slot, :], in_=wst_d[g], max_dma_last_dim=8192),
                  writes=[("w", slot)])

        for i in range(NSLOT):
            _load()

        def wslot(i):
            if record is not None:
                return i % NSLOT
            assert rstate["oldest"] <= i < rstate["next_load"], (i, rstate)
            return i % NSLOT

        def wdone(i, n=1):
            if record is not None:
                return
            rem[i] -= n
            assert rem[i] >= 0
            while rstate["oldest"] < len(rem) and rem[rstate["oldest"]] == 0:
                rstate["oldest"] += 1
                _load()

        def mm_group(out_ap, pairs, reads, writes):
            def f(e):
                ins = None
                n = len(pairs)
                for i, (l, r) in enumerate(pairs):
                    ins = e.matmul(out_ap, lhsT=l, rhs=r, start=(i == 0), stop=(i == n - 1))
                return ins
            S.op("pe", f, reads=reads, writes=writes)

        hT_keys = [("hT", kc) for kc in range(KC)]

        def norm_stats_blk(xb, xk, gi, b):
            S.op("act", lambda e: e.activation(out=junk, in_=xb[:, b, :], func=AF.Square,
                                               accum_out=ss[:, b:b + 1]),
                 reads=[(xk, b)], writes=[("ss", b), ("den", 0)])
            S.op("act", lambda e: e.activation(out=rs[:, b:b + 1], in_=ss[:, b:b + 1], func=AF.Sqrt,
                                               scale=1.0 / D, bias=EPS),
                 reads=[("ss", b)], writes=[("rs", b)])
            S.op("dve", lambda e: e.reciprocal(out=rs[:, b:b + 1], in_=rs[:, b:b + 1]),
                 reads=[("rs", b)], writes=[("rs", b)])
            for hf in range(2):
                S.op("dve", lambda e, hf=hf: e.scalar_tensor_tensor(
                    out=hb[:, b, hf * 512:(hf + 1) * 512], in0=xb[:, b, hf * 512:(hf + 1) * 512],
                    scalar=rs[:, b:b + 1], in1=gain[:, gi, hf * 512:(hf + 1) * 512],
                    op0=ALU.mult, op1=ALU.mult),
                    reads=[(xk, b), ("rs", b), "gain"], writes=[("Gs", 2 * b + hf)])

        def norm_stats(xb, xk, gi):
            for b in range(NB):
                norm_stats_blk(xb, xk, gi, b)

        def norm_tr():
            for pi in range(KC // 2):
                bkt = nbank()
                ptb = ps[bkt][:, :].bitcast(BF16)

                def f(e, pi=pi, ptb=ptb):
                    ins = None
                    for kk in range(2):
                        kc = 2 * pi + kk
                        for b in range(NB):
                            ins = e.transpose(out=ptb[:, kk * 512 + b * 128: kk * 512 + (b + 1) * 128],
                                              in_=hb[:, b, kc * 128:(kc + 1) * 128], identity=ident_bf[:])
                    return ins
                S.op("pe", f, reads=[("Gs", 2 * b + pi // 2) for b in range(NB)] + ["ident_bf"],
                     writes=[("ps", bkt)])
                S.op("act", lambda e, pi=pi, ptb=ptb: e.copy(
                    out=hT[:, 2 * pi:2 * pi + 2, :].rearrange("p k t -> p (k t)"), in_=ptb[:, :]),
                    reads=[("ps", bkt)], writes=[("hT", 2 * pi), ("hT", 2 * pi + 1)])
                rel(bkt)

        def norm_T(xb, xk, gi):
            norm_stats(xb, xk, gi)
            norm_tr()

        def inproj_chunk(gi, pos, bk=None):
            slot = wslot(gi)
            if bk is None:
                bk = nbank()
            wv = ring[:, slot, :].rearrange("p (c k n) -> p c k n", c=4, k=KC)
            mm_group(ps[bk][:, :], [(wv[:, pos, kc, :], hT[:, kc, :]) for kc in range(KC)],
                     reads=[("w", slot)] + hT_keys, writes=[("ps", bk)])
            wdone(gi)
            return bk

        def out_proj(xb, xk, gbase, pre=None, blk_hook=None):
            if pre is not None:
                pre[0]()

            def mm(b, jg, gi, bank):
                slot = wslot(gi)
                wv = ring[:, slot, :].rearrange("p (j n) -> p j n", j=8)

                def f(e):
                    ins = None
                    for jj in range(8):
                        j = jg * 8 + jj
                        ins = e.matmul(ps[bank][:, :], lhsT=yT[:, j, b * 128:(b + 1) * 128],
                                       rhs=wv[:, jj, :], start=(j == 0), stop=(j == 15))
                    return ins
                S.op("pe", f, reads=[("w", slot)] + [("yT", jg * 8 + jj) for jj in range(8)],
                     writes=[("ps", bank)])
                wdone(gi)

            def add(b, half, bank):
                S.op("dve", lambda e: e.tensor_tensor(
                    out=xb[:, b, half * 512:(half + 1) * 512], in0=ps[bank][:, :],
                    in1=xb[:, b, half * 512:(half + 1) * 512], op=ALU.add),
                    reads=[("ps", bank), (xk, b)], writes=[(xk, b)])
                rel(bank)

            banks = [nbank() for _ in range(NB)]
            for jg in range(2):
                for b in range(NB):
                    mm(b, jg, gbase + jg, banks[b])
            for b in range(NB):
                add(b, 0, banks[b])
            if pre is not None:
                pre[1]()
            if blk_hook is None:
                banks = [nbank() for _ in range(NB)]
                for jg in range(2):
                    for b in range(NB):
                        mm(b, jg, gbase + 2 + jg, banks[b])
                for b in range(NB):
                    add(b, 1, banks[b])
            else:
                for b in range(NB):
                    bank = nbank()
                    for jg in range(2):
                        mm(b, jg, gbase + 2 + jg, bank)
                    add(b, 1, bank)
                    blk_hook(b)

        def layer0(t, xb, xk, gb0, pre=None):
            first = (t % tps == 0)
            if first:
                S.op("pool", lambda e: e.memset(halo[:], 0.0), writes=[("halo", i) for i in range(8)])
                S.op("pool", lambda e: e.memset(hstate[:], 0.0), writes=[("hstate", i) for i in range(8)])
            def cgrp(ci):
                if record is not None:
                    record.append(ci)
                    return (gb0, 0)
                pos = l0_order.index(ci)
                return (gb0 + pos // 4, pos % 4)

            gi, pos = cgrp(0)
            slot = wslot(gi)
            wv = ring[:, slot, :].rearrange("p (c k n) -> p c k n", c=4, k=KC)
            bkv = nbank()
            for b in range(NB):
                mm_group(ps[bkv][:, b * 128:(b + 1) * 128],
                         [(hT[:, kc, b * 128:(b + 1) * 128], wv[:, 0, kc, :]) for kc in range(KC)],
                         reads=[("w", slot)] + hT_keys, writes=[("ps", bkv)])
            wdone(gi)
            for kv in range(2):
                S.op("act", lambda e, kv=kv: e.copy(
                    out=vz[:, 1:NB + 1, kv, kv * 64:(kv + 1) * 64],
                    in_=ps[bkv][:, :].rearrange("p (b n) -> p b n", b=NB)[:, :, kv * 64:(kv + 1) * 64]),
                    reads=[("ps", bkv)], writes=["vcur"])
            rel(bkv)

            def gen_qk(ci, gcol, out_ap, outkey, idx):
                sq, rr = sqb[idx % 2], rsb[idx % 2]
                sqk, rrk = "sqb%d" % (idx % 2), "rsb%d" % (idx % 2)
                bk = yield from getbank()
                inproj_chunk(*cgrp(ci), bk=bk)
                yield
                S.op("act", lambda e: e.activation(out=sq[:], in_=ps[bk][:, :], func=AF.Square),
                     reads=[("ps", bk)], writes=[sqk])
                yield
                b2 = yield from getbank()
                S.op("pe", lambda e: e.matmul(ps[b2][:, :], lhsT=col(C_ONESB, 128), rhs=sq[:],
                                              start=True, stop=True),
                     reads=[sqk, "cst"], writes=[("ps", b2)])
                yield
                S.op("act", lambda e: e.activation(out=rr[:], in_=ps[b2][:, :], func=AF.Ln, scale=1.0 / 64,
                                                   bias=EPS),
                     reads=[("ps", b2)], writes=[rrk])
                rel(b2)
                yield
                S.op("act", lambda e: e.activation(out=rr[:], in_=rr[:], func=AF.Exp, scale=-0.5),
                     reads=[rrk], writes=[rrk])
                yield
                if out_ap is None:
                    for kv in range(2):
                        pr = slice(kv * 64, (kv + 1) * 64)
                        S.op("dve", lambda e, kv=kv, pr=pr: e.scalar_tensor_tensor(
                            out=kz[pr, kv, 128:], in0=ps[bk][pr, :], scalar=cst[pr, gcol:gcol + 1],
                            in1=rr[pr, :], op0=ALU.mult, op1=ALU.mult),
                            reads=[("ps", bk), rrk, "cst"], writes=[outkey])
                else:
                    S.op("dve", lambda e: e.scalar_tensor_tensor(out=out_ap, in0=ps[bk][:, :], scalar=col(gcol),
                                                                 in1=rr[:], op0=ALU.mult, op1=ALU.mult),
                         reads=[("ps", bk), rrk, "cst"], writes=[outkey])
                rel(bk)
                yield

            def gen_gb(c):
                bk = yield from getbank()
                inproj_chunk(*cgrp(10 + c), bk=bk)
                yield
                S.op("act", lambda e: e.activation(out=Gs[:, c, :], in_=ps[bk][:, :], func=AF.Silu),
                     reads=[("ps", bk)], writes=[("Gs", c)])
                rel(bk)
                yield

            gens = [gen_qk(1, C_KG, None, "kcur", 0)]
            gens += [gen_qk(2 + c, C_QG, qnT[:, c, :], ("qn", c), 1 + c) for c in range(8)]
            gens += [gen_gb(c) for c in range(8)]
            b_gens = gens

            def gen_att_kv(b, quad, kv, bo, bd, stt, u):
                pi_ = (u % 2) * 2 + kv
                pT = PT[pi_]
                pk = ("PT", pi_)
                kts = [1] if (first and b == 0) else [0, 1]
                for kt in kts:
                    keyblk = b + kt
                    bs_ = yield from getbank()
                    kkey = "kprev" if keyblk == 0 else "kcur"
                    vkey = "vprev" if keyblk == 0 else "vcur"
                    h0 = kv * 8 + quad * 4

                    def fS(e, keyblk=keyblk, bs_=bs_, kt=kt, h0=h0):
                        o = ps[bs_][:, :].rearrange("p (c q) -> p c q", c=4)
                        e.matmul(o, lhsT=kz[:, kv, keyblk * 128:(keyblk + 1) * 128],
                                 rhs=qnT[:, quad * 4:(quad + 1) * 4, b * 128:(b + 1) * 128],
                                 start=True, stop=False)
                        return e.matmul(o, lhsT=ident_bf[:], rhs=Et[:, kt, h0:h0 + 4, :], start=False, stop=True)
                    S.op("pe", fS, reads=[kkey, "Et", "ident_bf"] + [("qn", quad * 4 + cc) for cc in range(4)],
                         writes=[("ps", bs_)])
                    yield
                    S.op("act", lambda e, bs_=bs_: e.activation(
                        out=pT[:], in_=ps[bs_][:, :], func=AF.Exp, scale=0.125),
                        reads=[("ps", bs_)], writes=[pk])
                    rel(bs_)
                    yield
                    st_ = (stt["n"] == 0)
                    sp_ = (stt["n"] == stt["total"] - 1)
                    stt["n"] += 1
                    S.op("pe", lambda e, keyblk=keyblk, st_=st_, sp_=sp_: e.matmul(
                        ps[bo][:, :], lhsT=vz[:, keyblk, kv, :], rhs=pT[:], start=st_, stop=sp_),
                        reads=[vkey, pk], writes=[("ps", bo)])
                    S.op("pe", lambda e, st_=st_, sp_=sp_: e.matmul(
                        ps[bd][:, :], lhsT=onesz[:, kv, :], rhs=pT[:], start=st_, stop=sp_),
                        reads=["onesz", pk], writes=[("ps", bd)])
                    yield

            def gen_att(b, quad, u):
                bo = yield from getbank()
                bd = yield from getbank()
                dn, y_ = den[u % 2], yb[u % 2]
                dk, yk = denk[u % 2], ybk[u % 2]
                nk = 1 if (first and b == 0) else 2
                stt = {"n": 0, "total": 2 * nk}
                subs = [gen_att_kv(b, quad, kv, bo, bd, stt, u) for kv in range(2)]
                while subs:
                    for g in list(subs):
                        try:
                            next(g)
                        except StopIteration:
                            subs.remove(g)
                    yield
                S.op("dve", lambda e: e.tensor_tensor(
                    out=dn[:].rearrange("p (c q) -> p c q", c=4),
                    in0=ps[bd][:, :].rearrange("p (c q) -> p c q", c=4),
                    in1=kcol[:, 16 + quad * 4:16 + (quad + 1) * 4].unsqueeze(2).to_broadcast([P, 4, 128]),
                    op=ALU.add),
                    reads=[("ps", bd), "kc_sink"], writes=[dk])
                rel(bd)
                yield
                S.op("act", lambda e: e.activation(out=dn[:], in_=dn[:], func=AF.Ln), reads=[dk], writes=[dk])
                yield
                S.op("act", lambda e: e.activation(out=dn[:], in_=dn[:], func=AF.Exp, scale=-1.0),
                     reads=[dk], writes=[dk])
                yield
                S.op("dve", lambda e: e.tensor_tensor(out=y_[:], in0=ps[bo][:, :], in1=dn[:], op=ALU.mult),
                     reads=[("ps", bo), dk], writes=[yk])
                rel(bo)
                yield
                S.op("dve", lambda e: e.tensor_tensor(
                    out=yT[:, 8 + quad * 4:8 + (quad + 1) * 4, b * 128:(b + 1) * 128],
                    in0=y_[:].rearrange("p (c q) -> p c q", c=4),
                    in1=Gs[:, quad * 4:(quad + 1) * 4, b * 128:(b + 1) * 128], op=ALU.mult),
                    reads=[yk] + [("Gs", quad * 4 + cc) for cc in range(4)],
                    writes=[("yT", 8 + quad * 4 + cc) for cc in range(4)])
                yield

            def gen_att_tail():
                S.op("pool", lambda e: e.tensor_copy(out=kz[:, :, 0:128], in_=kz[:, :, NB * 128:(NB + 1) * 128]),
                     reads=["kcur"], writes=["kprev"])
                S.op("pool", lambda e: e.tensor_copy(out=vz[:, 0, :, :], in_=vz[:, NB, :, :]),
                     reads=["vcur"], writes=["vprev"])
                yield

            units = [(b, quad) for b in range(NB) for quad in range(2)]
            att_gens = [gen_att(b, quad, u) for u, (b, quad) in enumerate(units)] + [gen_att_tail()]
            def gen_A():
                active = []
                i = 0
                while active or i < len(b_gens):
                    while len(active) < 2 and i < len(b_gens):
                        active.append(b_gens[i])
                        i += 1
                    for g in list(active):
                        try:
                            next(g)
                        except StopIteration:
                            active.remove(g)
                    yield
                active = []
                i = 0
                while active or i < len(att_gens):
                    while len(active) < 2 and i < len(att_gens):
                        active.append(att_gens[i])
                        i += 1
                    for g in list(active):
                        try:
                            next(g)
                        except StopIteration:
                            active.remove(g)
                    yield

            def gen_rg(blk):
                si = blk % 2
                Ub = Us[si]
                zb, xb_ = zbuf[si], xabf[si]
                t_r, t_a, t_m, t_i, t_b, t_h, t_s, t_x = [Ub[:, i, :] for i in range(8)]
                Uk = [("U%d" % si, i) for i in range(8)]
                zh, zm, xk_ = ("zb_h", si), ("zb_m", si), ("xabf", si)
                bz = yield from getbank()
                inproj_chunk(*cgrp(18 + 2 * blk), bk=bz)
                S.op("pool", lambda e: e.tensor_copy(out=zb[:, 0:4], in_=halo[:, blk, :]),
                     reads=[("halo", blk)], writes=[zh])
                yield
                S.op("act", lambda e: e.copy(out=zb[:, 4:T + 4], in_=ps[bz][:, :]),
                     reads=[("ps", bz)], writes=[zm])
                S.op("act", lambda e: e.activation(
                    out=t_x, in_=ps[bz][:, :], func=AF.Identity, scale=col(C_CONVW + blk * 4 + 3),
                    bias=col(C_CONVB + blk)),
                    reads=[("ps", bz), "cst"], writes=[Uk[7]])
                rel(bz)
                yield
                for k in (1, 2, 3):
                    S.op("dve", lambda e, k=k: e.scalar_tensor_tensor(
                        out=t_x, in0=zb[:, 4 - k:T + 4 - k], scalar=col(C_CONVW + blk * 4 + 3 - k), in1=t_x,
                        op0=ALU.mult, op1=ALU.add),
                        reads=[zh, zm, Uk[7], "cst"], writes=[Uk[7]])
                    yield
                S.op("pool", lambda e: e.tensor_copy(out=halo[:, blk, :], in_=zb[:, T:T + 4]),
                     reads=[zm, zh], writes=[("halo", blk)])
                S.op("dve", lambda e: e.tensor_copy(out=xb_[:], in_=t_x), reads=[Uk[7]], writes=[xk_])
                yield
                br = yield from getbank()
                S.op("pe", lambda e: e.matmul(ps[br][:, :], lhsT=gw_bf[:, 0, blk, :], rhs=xb_[:],
                                              start=True, stop=True),
                     reads=["gw_bf", xk_], writes=[("ps", br)])
                bi = yield from getbank()
                S.op("pe", lambda e: e.matmul(ps[bi][:, :], lhsT=gw_bf[:, 1, blk, :], rhs=xb_[:],
                                              start=True, stop=True),
                     reads=["gw_bf", xk_], writes=[("ps", bi)])
                yield
                S.op("act", lambda e: e.activation(out=t_r, in_=ps[br][:, :], func=AF.Tanh, scale=0.5,
                                                   bias=kcol[:, 32 + blk:33 + blk]),
                     reads=[("ps", br), "kc_hb"], writes=[Uk[0]])
                rel(br)
                yield
                S.op("act", lambda e: e.activation(out=t_i, in_=ps[bi][:, :], func=AF.Tanh, scale=0.5,
                                                   bias=kcol[:, 40 + blk:41 + blk]),
                     reads=[("ps", bi), "kc_hb"], writes=[Uk[3]])
                rel(bi)
                yield
                S.op("act", lambda e: e.activation(out=t_a, in_=t_r, func=AF.Exp, scale=Kh(blk), bias=Kh(blk)),
                     reads=[Uk[0], "kc_K"], writes=[Uk[1]])
                yield
                S.op("dve", lambda e: e.tensor_tensor(out=t_m, in0=t_a, in1=t_a, op=ALU.mult),
                     reads=[Uk[1]], writes=[Uk[2]])
                yield
                S.op("dve", lambda e: e.scalar_tensor_tensor(out=t_b, in0=t_i, scalar=1.0, in1=t_x,
                                                             op0=ALU.add, op1=ALU.mult),
                     reads=[Uk[3], Uk[7]], writes=[Uk[4]])
                yield
                S.op("dve", lambda e: e.tensor_scalar(out=t_m, in0=t_m, scalar1=1.0, scalar2=-1.0,
                                                      op0=ALU.min, op1=ALU.mult),
                     reads=[Uk[2]], writes=[Uk[2]])
                yield
                S.op("act", lambda e: e.activation(out=t_m, in_=t_m, func=AF.Sqrt, bias=1.0),
                     reads=[Uk[2]], writes=[Uk[2]])
                yield
                S.op("dve", lambda e: e.scalar_tensor_tensor(out=t_b, in0=t_b, scalar=0.5, in1=t_m,
                                                             op0=ALU.mult, op1=ALU.mult),
                     reads=[Uk[4], Uk[2]], writes=[Uk[4]])
                yield
                bg = yield from getbank()
                inproj_chunk(*cgrp(19 + 2 * blk), bk=bg)
                yield
                S.op("act", lambda e: e.activation(out=t_s, in_=ps[bg][:, :], func=AF.Tanh, scale=0.5),
                     reads=[("ps", bg)], writes=[Uk[6]])
                yield
                S.op("dve", lambda e: e.tensor_tensor_scan(
                    out=t_h, data0=t_a, data1=t_b, initial=hstate[:, blk:blk + 1], op0=ALU.mult, op1=ALU.add),
                    reads=[Uk[1], Uk[4], ("hstate", blk)], writes=[Uk[5]])
                yield
                S.op("pool", lambda e: e.tensor_copy(out=hstate[:, blk:blk + 1], in_=Ub[:, 5, T - 1:T]),
                     reads=[Uk[5]], writes=[("hstate", blk)])
                S.op("dve", lambda e: e.scalar_tensor_tensor(out=t_s, in0=t_s, scalar=1.0, in1=ps[bg][:, :],
                                                             op0=ALU.add, op1=ALU.mult),
                     reads=[Uk[6], ("ps", bg)], writes=[Uk[6]])
                rel(bg)
                yield
                S.op("dve", lambda e: e.scalar_tensor_tensor(out=yT[:, blk, :], in0=t_h, scalar=0.5, in1=t_s,
                                                             op0=ALU.mult, op1=ALU.mult),
                     reads=[Uk[5], Uk[6]], writes=[("yT", blk)])
                yield

            multi_zipper([([gen_A()], 1), ([gen_rg(blk) for blk in range(8)], 2)])
            if dbg == 5:
                return
            if n_layers > 1 and dbg == 0:
                out_proj(xb, xk, gb0 + 9, pre, blk_hook=lambda b: norm_stats_blk(xb, xk, 1, b))
            else:
                out_proj(xb, xk, gb0 + 9, pre)

        def layer1(t, xb, xk, gb1, pre=None):
            norm_tr()
            vslots = [wslot(gb1 + vg) for vg in range(4)]
            for b in range(NB):
                banks = []
                for vg in range(4):
                    bk = nbank()
                    banks.append(bk)
                    wv = ring[:, vslots[vg], :].rearrange("p (k n) -> p k n", k=KC)
                    mm_group(ps[bk][:, :], [(hT[:, kc, b * 128:(b + 1) * 128], wv[:, kc, :]) for kc in range(KC)],
                             reads=[("w", vslots[vg])] + hT_keys, writes=[("ps", bk)])
                    wdone(gb1 + vg)
                    S.op("dve", lambda e, bk=bk, vg=vg: e.bn_stats(out=stats[:, vg, :], in_=ps[bk][:, :]),
                         reads=[("ps", bk)], writes=[("stats", vg)])
                S.op("dve", lambda e: e.bn_aggr(out=mv[:], in_=stats[:].rearrange("p a s -> p (a s)")),
                     reads=[("stats", vg) for vg in range(4)], writes=["mv"])
                S.op("act", lambda e: e.activation(out=lnr[:, 0:1], in_=mv[:, 1:2], func=AF.Sqrt, bias=EPS),
                     reads=["mv"], writes=["lnr"])
                S.op("dve", lambda e: e.reciprocal(out=lnr[:, 1:2], in_=lnr[:, 0:1]), reads=["lnr"],
                     writes=["lnr2"])
                for vg in range(4):
                    S.op("dve", lambda e, vg=vg, b=b, bk=banks[vg]: e.tensor_scalar(
                        out=vhat[:, b, vg * 512:(vg + 1) * 512], in0=ps[bk][:, :], scalar1=mv[:, 0:1],
                        scalar2=lnr[:, 1:2], op0=ALU.subtract, op1=ALU.mult),
                        reads=[("ps", banks[vg]), "mv", "lnr2"], writes=[("U0", 2 * b), ("U0", 2 * b + 1)])
                    rel(banks[vg])

            def gen_l1(j):
                g = j // 2
                si = j % 2
                S1, G1, Y1 = sqb[si], rsb[si], expS[si]
                s1k, g1k, y1k = "sqb%d" % si, "rsb%d" % si, ("expS", si)
                gi = gb1 + 4 + j // 2
                bu = yield from getbank()
                inproj_chunk(gi, (j % 2) * 2, bk=bu)
                yield
                bg = yield from getbank()
                inproj_chunk(gi, (j % 2) * 2 + 1, bk=bg)
                yield
                S.op("act", lambda e: e.activation(out=G1[:], in_=ps[bg][:, :], func=AF.Silu),
                     reads=[("ps", bg)], writes=[g1k])
                rel(bg)
                bsg = yield from getbank()
                for b in range(NB):
                    S.op("pe", lambda e, b=b: e.matmul(
                        ps[bsg][:, b * 128:(b + 1) * 128], lhsT=vhat[:, b, j * 128:(j + 1) * 128],
                        rhs=wtril_bf[:, g, :], start=True, stop=True),
                        reads=[("U0", 2 * b), ("U0", 2 * b + 1), "wtril_bf"], writes=[("ps", bsg)])
                yield
                S.op("dve", lambda e: e.scalar_tensor_tensor(
                    out=S1[:].rearrange("p (b t) -> p b t", b=NB),
                    in0=ps[bsg][:, :].rearrange("p (b t) -> p b t", b=NB), scalar=col(C_LNG + j),
                    in1=Cc[:, j:j + 1, :].to_broadcast([P, NB, 128]), op0=ALU.mult, op1=ALU.add),
                    reads=[("ps", bsg), "Cc", "cst"], writes=[s1k])
                rel(bsg)
                yield
                S.op("dve", lambda e: e.tensor_tensor(out=Y1[:], in0=ps[bu][:, :], in1=S1[:], op=ALU.mult),
                     reads=[("ps", bu), s1k], writes=[y1k])
                rel(bu)
                yield
                S.op("dve", lambda e: e.tensor_tensor(out=yT[:, j, :], in0=Y1[:], in1=G1[:], op=ALU.mult),
                     reads=[y1k, g1k], writes=[("yT", j)])
                yield

            zipper([gen_l1(j) for j in range(16)], 2)
            out_proj(xb, xk, gb1 + 12, pre)

        if dbg != 1:
            norm_T(xbuf[0], "x0", 0)
        l1_setup()
        for t in range(n_tiles):
            xb = xbuf[t % NXB]
            xk = "x%d" % (t % NXB)
            nxt = None
            if t + 1 < n_tiles:
                load_x(t + 1)
                if dbg != 1:
                    nxt = (lambda t=t: norm_stats(xbuf[(t + 1) % NXB], "x%d" % ((t + 1) % NXB), 0), norm_tr)
            if dbg != 1:
                if n_layers > 1 and dbg == 0:
                    layer0(t, xb, xk, t * GPT)
                    layer1(t, xb, xk, t * GPT + 13, nxt)
                else:
                    layer0(t, xb, xk, t * GPT, nxt)
            S.dma("sp", "xs%d" % (t % NXB), lambda e, t=t, xb=xb: e.dma_start(
                out=out_d[t * T:(t + 1) * T, :].rearrange("(b p) d -> p b d", p=P), in_=xb[:]),
                reads=[(xk, b) for b in range(NB)])
        S.final_wait("sp", [("D", sname, 16 * c) for sname, c in S.dcnt.items()])

        sems = {}
        for k in S.sem_keys():
            sems[k] = st.enter_context(nc.semaphore("s_%s_%s" % (k[0], k[1])))
        S.emit(nc, sems)
    return nc


def t5_bucket(dist):
    max_exact = 16
    df = np.maximum(dist, 1).astype(np.float32)
    large = max_exact + (np.log(df / np.float32(max_exact)) / np.float32(np.log(128 / 16))
                         * np.float32(16)).astype(np.int32)
    large = np.minimum(large, 31)
    return np.where(dist < max_exact, dist, large)


def prepare_shared(inp, l0_order):
    f32 = np.float32
    w_in_a = np.asarray(inp["w_in_a"], f32)[0]
    w_out_a = np.asarray(inp["w_out_a"], f32)[0]
    w_in_c = np.asarray(inp["w_in_c"], f32)[0]
    w_out_c = np.asarray(inp["w_out_c"], f32)[0]

    def chunk_in(w, cols):
        return w[:, cols].reshape(KC, P, len(cols)).transpose(1, 0, 2)

    o_za, o_ga, o_q, o_k, o_v, o_gb = 0, 1024, 2048, 3072, 3200, 3328
    ar = np.arange
    chunks = [chunk_in(w_in_a, o_v + ar(128)), chunk_in(w_in_a, o_k + ar(128))]
    head_cols = lambda base, c: np.concatenate([base + c * 64 + ar(64), base + (8 + c) * 64 + ar(64)])
    for c in range(8):
        chunks.append(chunk_in(w_in_a, head_cols(o_q, c)))
    for c in range(8):
        chunks.append(chunk_in(w_in_a, head_cols(o_gb, c)))
    for blk in range(8):
        chunks.append(chunk_in(w_in_a, o_za + blk * 128 + ar(128)))
        chunks.append(chunk_in(w_in_a, o_ga + blk * 128 + ar(128)))
    chunks = [chunks[ci] for ci in l0_order]
    while len(chunks) % 4:
        chunks.append(np.zeros_like(chunks[0]))
    groups = []
    for g in range(len(chunks) // 4):
        groups.append(np.stack(chunks[4 * g:4 * g + 4], axis=1).reshape(P, SLOT))

    def out_groups(w, rows_of_chunk):
        gs = []
        for half in range(2):
            for jg in range(2):
                blk = np.stack([w[rows_of_chunk(jg * 8 + jj)][:, half * 512:(half + 1) * 512]
                                for jj in range(8)], axis=1)
                gs.append(blk.reshape(P, SLOT))
        return gs

    def rows_a(j):
        if j < 8:
            return j * 128 + ar(128)
        c = j - 8
        return np.concatenate([1024 + c * 64 + ar(64), 1024 + (8 + c) * 64 + ar(64)])
    groups += out_groups(w_out_a, rows_a)
    for vg in range(4):
        cols = 2048 + vg * 512 + ar(512)
        groups.append(w_in_c[:, cols].reshape(KC, P, 512).transpose(1, 0, 2).reshape(P, SLOT))
    for j in range(0, 16, 2):
        cs = [chunk_in(w_in_c, j * 128 + ar(128)), chunk_in(w_in_c, 4096 + j * 128 + ar(128)),
              chunk_in(w_in_c, (j + 1) * 128 + ar(128)), chunk_in(w_in_c, 4096 + (j + 1) * 128 + ar(128))]
        groups.append(np.stack(cs, axis=1).reshape(P, SLOT))
    groups += out_groups(w_out_c, lambda j: j * 128 + ar(128))
    wst = np.ascontiguousarray(np.stack(groups, axis=0), dtype=f32)
    assert wst.shape == (NG_TILE, P, SLOT), wst.shape

    cst = np.zeros((P, C_TOTAL), f32)
    pcol = lambda v: np.asarray(v, f32).reshape(8, P).T
    cw = np.asarray(inp["conv_w"], f32)[0]
    cst[:, C_CONVW:C_CONVW + 32] = cw.reshape(4, 8, P).transpose(2, 1, 0).reshape(P, 32)
    cst[:, C_CONVB:C_CONVB + 8] = pcol(inp["conv_b"][0])
    cst[:, C_GAB:C_GAB + 8] = pcol(inp["gate_a_b"][0])
    cst[:, C_GXB:C_GXB + 8] = pcol(inp["gate_x_b"][0])
    cst[:, C_LAM:C_LAM + 8] = pcol(inp["lru_lambda"][0])
    cst[:, C_QG] = np.tile(np.asarray(inp["q_norm_g"], f32)[0], 2)
    cst[:, C_KG] = np.tile(np.asarray(inp["k_norm_g"], f32)[0], 2)
    sinks = np.asarray(inp["sinks"], f32)[0]
    cst[:64, C_SINK:C_SINK + 8] = sinks[None, 0:8]
    cst[64:, C_SINK:C_SINK + 8] = sinks[None, 8:16]
    cst[:, C_LNG:C_LNG + 16] = np.asarray(inp["ln_v_g"], f32)[0].reshape(16, P).T
    cst[:, C_LNB:C_LNB + 16] = np.asarray(inp["ln_v_b"], f32)[0].reshape(16, P).T
    ga = np.asarray(inp["gate_a_w"], f32)[0]
    gx = np.asarray(inp["gate_x_w"], f32)[0]
    gw = np.stack([ga.transpose(1, 0, 2), gx.transpose(1, 0, 2)], axis=1)
    csu = np.zeros((P, U_TOTAL), f32)
    csu[:, U_GATEW:U_GATEW + 2048] = gw.reshape(P, 2048)
    s_i = ar(128)[:, None]
    q_i = ar(128)[None, :]
    amask = np.stack([(s_i > q_i), (s_i <= q_i)], axis=1).astype(f32)
    csu[:, U_AMASK:U_AMASK + 256] = amask.reshape(P, 256)
    spw = np.asarray(inp["spatial_w"], f32)[0]
    csu[:, U_SPW:U_SPW + 1024] = spw.transpose(2, 0, 1).reshape(P, 1024)
    csu[:, U_CMASK:U_CMASK + 128] = (s_i <= q_i).astype(f32)
    spb = np.asarray(inp["spatial_b"], f32)[0]
    csu[:, U_SPB:U_SPB + 1024] = np.broadcast_to(spb.reshape(1, 1024), (P, 1024))
    csu[:, U_IDENT:U_IDENT + 128] = np.eye(128, dtype=f32)
    ob = np.zeros((128, 128), f32)
    ob[:64, :64] = 1.0
    ob[64:, 64:] = 1.0
    cst[:, C_ONESB:C_ONESB + 128] = ob

    gain = np.stack([np.broadcast_to(np.asarray(inp["norm_a"], f32)[0][None, :], (P, D)),
                     np.broadcast_to(np.asarray(inp["norm_c"], f32)[0][None, :], (P, D))], axis=0)
    gain = np.ascontiguousarray(gain, dtype=f32)

    rel = np.asarray(inp["rel_bias"], f32)
    kj = ar(256)[None, :]
    qi = ar(128)[:, None]
    dist = 128 + qi - kj
    bucket = t5_bucket(np.maximum(dist, 0))
    bias = rel[bucket]
    biasT = bias.transpose(1, 2, 0).reshape(2, 128, 16, 128).transpose(1, 0, 2, 3)
    biasT = np.ascontiguousarray(biasT.reshape(P, 2 * 16 * 128), dtype=f32)
    return {"wst": wst, "cst": cst, "csu": csu, "gain": gain, "biasT": biasT}


_CACHE = {}


def get_l0_order():
    if "order" not in _CACHE:
        rec = []
        build_program(1, 4, record=rec)
        order = []
        for ci in rec:
            if ci not in order:
                order.append(ci)
        assert sorted(order) == list(range(34)), order
        _CACHE["order"] = order
    return _CACHE["order"]


def kernel(**inputs):
    x = np.asarray(inputs["x"], np.float32)
    bsz, s, d = x.shape
    per = bsz // N_CORES
    order = get_l0_order()
    shared = prepare_shared(inputs, order)
    n_tiles = per * s // T
    key = (n_tiles, s // T)
    if key not in _CACHE:
        _CACHE[key] = build_program(n_tiles, s // T, l0_order=order)
    nc = _CACHE[key]
    in_maps = []
    for c in range(N_CORES):
        m = dict(shared)
        m["x"] = np.ascontiguousarray(x[c * per:(c + 1) * per].reshape(per * s, d))
        in_maps.append(m)
    res = run_bass_kernel_spmd(nc, in_maps, core_ids=list(range(N_CORES)))
    outs = [np.asarray(r["out"], np.float32).reshape(per, s, d) for r in res.results]
    return np.concatenate(outs, axis=0)
```

```python
import numpy as np
from contextlib import ExitStack
import concourse.bass as bass
import concourse.mybir as mybir
from concourse.bass_utils import run_bass_kernel_spmd

F32 = mybir.dt.float32
BF16 = mybir.dt.bfloat16
AF = mybir.ActivationFunctionType
ALU = mybir.AluOpType

P = 128
T = 512
NB = 4
D = 1024
KC = 8
EPS = 1e-6
N_CORES = 8
SEQ = 2048
TPS = SEQ // T
NSLOT = 5
SLOT = 4096
NG_L0_IN = 9
NG_TILE = 29

C_CONVW = 0
C_CONVB = 32
C_GAB = 40
C_GXB = 48
C_LAM = 56
C_QG = 64
C_KG = 65
C_SINK = 66
C_LNG = 74
C_LNB = 90
C_ONESB = 128
C_TOTAL = 256
U_GATEW = 0
U_SPW = 2048
U_SPB = 3072
U_AMASK = 4096
U_CMASK = 4352
U_IDENT = 4480
U_TOTAL = 4608


class Sched:
    ENG = ("pe", "act", "dve", "pool", "sp")

    def __init__(self):
        self.q = {e: [] for e in self.ENG}
        self.cnt = {e: 0 for e in self.ENG}
        self.dcnt = {}
        self.lastw = {}
        self.readers = {}
        self.waited = {e: {} for e in self.ENG}
        self.nwaits = 0

    def _deps(self, eng, reads, writes, is_dma):
        need = {}

        def add(t, kind):
            if t[0] == "E" and t[1] == eng and not is_dma:
                if eng == "pe":
                    return
                if kind != "raw" and t[2] != self.cnt[eng]:
                    return
            sk = (t[0], t[1])
            if need.get(sk, 0) < t[2]:
                need[sk] = t[2]

        for k in reads:
            t = self.lastw.get(k)
            if t is not None:
                add(t, "raw")
        for k in writes:
            t = self.lastw.get(k)
            if t is not None:
                add(t, "waw")
            for sk, v in self.readers.get(k, {}).items():
                add((sk[0], sk[1], v), "war")
        waits = []
        wd = self.waited[eng]
        for sk, v in need.items():
            if wd.get(sk, 0) >= v:
                continue
            wd[sk] = v
            waits.append((sk, v))
        self.nwaits += len(waits)
        return waits

    def _commit(self, tok, reads, writes):
        for k in writes:
            self.lastw[k] = tok
            self.readers[k] = {}
        sk = (tok[0], tok[1])
        for k in reads:
            r = self.readers.setdefault(k, {})
            if r.get(sk, 0) < tok[2]:
                r[sk] = tok[2]

    def op(self, eng, fn, reads=(), writes=()):
        waits = self._deps(eng, reads, writes, False)
        self.cnt[eng] += 1
        tok = ("E", eng, self.cnt[eng])
        self.q[eng].append((waits, fn, ("E", eng), 1))
        self._commit(tok, reads, writes)
        return tok

    def dma(self, qeng, sem, fn, reads=(), writes=()):
        waits = self._deps(qeng, reads, writes, True)
        self.dcnt[sem] = self.dcnt.get(sem, 0) + 1
        tok = ("D", sem, 16 * self.dcnt[sem])
        self.q[qeng].append((waits, fn, ("D", sem), 16))
        self._commit(tok, reads, writes)
        return tok

    def final_wait(self, eng, toks):
        waits = [((t[0], t[1]), t[2]) for t in toks]
        self.q[eng].append((waits, None, None, 0))

    def sem_keys(self):
        keys = [("E", e) for e in self.ENG]
        keys += [("D", s) for s in self.dcnt]
        return keys

    def emit(self, nc, sems):
        with nc.Block() as block:
            decos = {"pe": block.tensor, "act": block.scalar, "dve": block.vector,
                     "pool": block.gpsimd, "sp": block.sync}
            for e in self.ENG:
                items = self.q[e]

                def body(engh, items=items):
                    for waits, fn, inc, amt in items:
                        for sk, v in waits:
                            engh.wait_ge(sems[sk], v)
                        if fn is None:
                            continue
                        ins = fn(engh)
                        ins.then_inc(sems[inc], amt)

                decos[e](body)


def zipper(gens, depth):
    gens = list(gens)
    active = []
    i = 0
    while active or i < len(gens):
        while len(active) < depth and i < len(gens):
            active.append(gens[i])
            i += 1
        for g in list(active):
            try:
                next(g)
            except StopIteration:
                active.remove(g)


def multi_zipper(streams):
    st = [{"gens": list(g), "depth": d, "i": 0, "active": []} for g, d in streams]
    while any(x["active"] or x["i"] < len(x["gens"]) for x in st):
        for x in st:
            while len(x["active"]) < x["depth"] and x["i"] < len(x["gens"]):
                x["active"].append(x["gens"][x["i"]])
                x["i"] += 1
            for g in list(x["active"]):
                try:
                    next(g)
                except StopIteration:
                    x["active"].remove(g)


def build_program(n_tiles, tps=TPS, n_layers=2, dbg=0, l0_order=None, record=None):
    nc = bass.Bass("TRN2", target_bir_lowering=False)
    ntok = n_tiles * T
    x_d = nc.dram_tensor("x", [ntok, D], F32, kind="ExternalInput").ap()
    wst_d = nc.dram_tensor("wst", [NG_TILE, P, SLOT], F32, kind="ExternalInput").ap()
    cst_d = nc.dram_tensor("cst", [P, C_TOTAL], F32, kind="ExternalInput").ap()
    csu_d = nc.dram_tensor("csu", [P, U_TOTAL], F32, kind="ExternalInput").ap()
    gain_d = nc.dram_tensor("gain", [2, P, D], F32, kind="ExternalInput").ap()
    bias_d = nc.dram_tensor("biasT", [P, 2 * 16 * 128], F32, kind="ExternalInput").ap()
    out_d = nc.dram_tensor("out", [ntok, D], F32, kind="ExternalOutput").ap()

    S = Sched()
    st = ExitStack()
    with st:
        def sb(name, shape, dt=F32):
            return st.enter_context(nc.sbuf_tensor("sb_" + name, shape, dt))

        NXB = 2
        xbuf = [sb("xbuf%d" % i, [P, NB, D]) for i in range(NXB)]
        hT = sb("hT", [P, KC, T], BF16)
        ss = sb("ss", [P, 8])
        rs = sb("rs", [P, 8])
        ring = sb("ring", [P, NSLOT, SLOT], BF16)
        cst = sb("cst", [P, C_TOTAL])
        gain = sb("gain", [P, 2, D])
        Et = sb("Et", [P, 2, 16, 128], BF16)
        vz = sb("vz", [P, NB + 1, 2, 128], BF16)
        kz = sb("kz", [P, 2, (NB + 1) * 128], BF16)
        onesz = sb("onesz", [P, 2, 128], BF16)
        qnT = sb("qnT", [P, 8, T], BF16)
        Gs = sb("Gs", [P, 8, T], BF16)
        sqb = [sb("sqb%d" % i, [P, T]) for i in range(2)]
        rsb = [sb("rsb%d" % i, [P, T]) for i in range(2)]
        expS = [sb("expS%d" % i, [P, T]) for i in range(2)]
        PT = [sb("PT%d" % i, [P, T], BF16) for i in range(4)]
        den = [sb("den0", [P, T]), expS[0]]
        yb = [sb("yb0", [P, T]), expS[1]]
        denk = [("den", 0), ("expS", 0)]
        ybk = [("yb", 0), ("expS", 1)]
        junk = den[0][:].bitcast(BF16)
        yT = sb("yT", [P, 16, T], BF16)
        Us = [sb("U%d" % i, [P, 8, T]) for i in range(2)]
        zbuf = [sb("zbuf%d" % i, [P, T + 4]) for i in range(2)]
        xabf = [sb("xabf%d" % i, [P, T], BF16) for i in range(2)]
        halo = sb("halo", [P, 8, 4])
        hstate = sb("hstate", [P, 8])
        Cc = sb("Cc", [P, 16, 128])
        stats = sb("stats", [P, 4, 6])
        mv = sb("mv", [P, 2])
        lnr = sb("lnr", [P, 2])
        kcol = sb("kcol", [P, 64])
        ident_bf = sb("ident_bf", [P, 128], BF16)
        gw_bf = sb("gw_bf", [P, 2, 8, 128], BF16)
        wtril_bf = sb("wtril_bf", [P, 8, 128], BF16)

        NPS = 8
        ps = [st.enter_context(nc.psum_tensor("ps%d" % i, [P, 512], F32)) for i in range(NPS)]
        U = Us[0]
        hb = Gs[:].rearrange("p (b j) t -> p b (j t)", b=NB)
        hbk = lambda b: [("Gs", 2 * b), ("Gs", 2 * b + 1)]
        wtril_f = U[:, 0:2, :].rearrange("p a t -> p (a t)").rearrange("p (g t) -> p g t", g=8)
        vhat = U[:].rearrange("p a t -> p (a t)").bitcast(BF16).rearrange("p (b f) -> p b f", b=NB)
        xflat = xbuf[0][:].rearrange("p b d -> p (b d)")
        usu = U[:, 7, :]
        x0keys = [("x0", b) for b in range(NB)]

        free_banks = list(range(NPS))

        def nbank():
            assert free_banks, "PSUM banks exhausted"
            return free_banks.pop(0)

        def getbank():
            n = 0
            while not free_banks:
                n += 1
                assert n < 10000, "PSUM bank wait deadlock at build time"
                yield
            return free_banks.pop(0)

        def rel(bk):
            assert bk not in free_banks
            free_banks.append(bk)

        def col(c, n=1):
            return cst[:, c:c + n]

        u1flat = Us[1][:].rearrange("p a t -> p (a t)")
        u1keys = [("U1", i) for i in range(8)]
        x1flat = xbuf[1][:].rearrange("p b d -> p (b d)")
        x1keys = [("x1", b) for b in range(NB)]

        def load_x(t):
            xb = xbuf[t % NXB]
            xk = "x%d" % (t % NXB)
            S.dma("sp", "xl%d" % (t % NXB), lambda e: e.dma_start(
                out=xb[:], in_=x_d[t * T:(t + 1) * T, :].rearrange("(b p) d -> p b d", p=P)),
                writes=[(xk, b) for b in range(NB)])

        load_x(0)
        S.dma("sp", "c0", lambda e: e.dma_start(out=cst[:], in_=cst_d), writes=["cst"])
        S.dma("sp", "c1", lambda e: e.dma_start(out=gain[:], in_=gain_d.rearrange("g p d -> p g d")),
              writes=["gain"])
        S.dma("sp", "c4", lambda e: e.dma_start(out=usu, in_=csu_d[:, 4096:U_TOTAL]), writes=[("U0", 7)])
        S.dma("sp", "c3", lambda e: e.dma_start(out=u1flat, in_=csu_d[:, 0:4096]), writes=u1keys)
        S.dma("sp", "c2", lambda e: e.dma_start(out=x1flat, in_=bias_d), writes=x1keys)

        S.op("dve", lambda e: e.tensor_copy(out=ident_bf[:], in_=usu[:, 384:512]),
             reads=[("U0", 7)], writes=["ident_bf"])
        S.op("dve", lambda e: e.memset(onesz[:], 0.0), writes=["onesz"])
        S.op("dve", lambda e: e.memset(onesz[:, 0, 0:64], 1.0), writes=["onesz"])
        S.op("dve", lambda e: e.memset(onesz[:, 1, 64:128], 1.0), writes=["onesz"])
        S.op("dve", lambda e: e.memset(kz[:], 0.0), writes=["kprev", "kcur"])
        S.op("dve", lambda e: e.memset(vz[:], 0.0), writes=["vprev", "vcur"])
        S.op("dve", lambda e: e.tensor_copy(out=gw_bf[:].rearrange("p a n j -> p (a n j)"),
                                            in_=u1flat[:, U_GATEW:U_GATEW + 2048]),
             reads=u1keys, writes=["gw_bf"])
        for kt in range(2):
            def f(e, kt=kt):
                v = x1flat[:, kt * 2048:(kt + 1) * 2048]
                return e.tensor_scalar(out=v, in0=v, scalar1=8.0, scalar2=1.0, op0=ALU.mult, op1=ALU.mult)
            S.op("pool", f, reads=x1keys, writes=x1keys)

            def f(e, kt=kt):
                m = usu[:, kt * 128:(kt + 1) * 128]
                v = x1flat[:, kt * 2048:(kt + 1) * 2048].rearrange("p (h q) -> p h q", h=16)
                return e.tensor_tensor(out=v, in0=v, in1=m.unsqueeze(1).to_broadcast([P, 16, 128]), op=ALU.mult)
            S.op("pool", f, reads=x1keys + [("U0", 7)], writes=x1keys)
        S.op("pool", lambda e: e.tensor_scalar(out=usu[:, 0:256], in0=usu[:, 0:256], scalar1=-1.0, scalar2=480.0,
                                               op0=ALU.add, op1=ALU.mult),
             reads=[("U0", 7)], writes=[("U0", 7)])
        for kt in range(2):
            def f(e, kt=kt):
                m = usu[:, kt * 128:(kt + 1) * 128]
                v = x1flat[:, kt * 2048:(kt + 1) * 2048].rearrange("p (h q) -> p h q", h=16)
                return e.tensor_tensor(out=Et[:, kt], in0=v, in1=m.unsqueeze(1).to_broadcast([P, 16, 128]),
                                       op=ALU.add)
            S.op("pool", f, reads=x1keys + [("U0", 7)], writes=["Et"])
        S.op("act", lambda e: e.activation(out=kcol[:, 16:24], in_=col(C_SINK, 8), func=AF.Exp),
             reads=["cst"], writes=["kc_sink"])
        S.op("dve", lambda e: e.tensor_scalar(out=kcol[:, 32:40], in0=col(C_GAB, 8), scalar1=0.5, scalar2=None,
                                              op0=ALU.mult), reads=["cst"], writes=["kc_hb"])
        S.op("dve", lambda e: e.tensor_scalar(out=kcol[:, 40:48], in0=col(C_GXB, 8), scalar1=0.5, scalar2=None,
                                              op0=ALU.mult), reads=["cst"], writes=["kc_hb"])
        S.op("act", lambda e: e.activation(out=kcol[:, 24:32], in_=col(C_LAM, 8), func=AF.Exp, scale=-1.0),
             reads=["cst"], writes=["kc_e"])
        S.op("dve", lambda e: e.tensor_scalar(out=kcol[:, 48:56], in0=kcol[:, 24:32], scalar1=2.0,
                                              scalar2=None, op0=ALU.add),
             reads=["kc_e"], writes=["kc_t"])
        S.op("dve", lambda e: e.reciprocal(out=kcol[:, 48:56], in_=kcol[:, 48:56]),
             reads=["kc_t"], writes=["kc_t"])
        S.op("dve", lambda e: e.tensor_tensor(out=kcol[:, 24:32], in0=kcol[:, 24:32], in1=kcol[:, 48:56],
                                              op=ALU.mult),
             reads=["kc_t", "kc_e"], writes=["kc_e"])
        S.op("dve", lambda e: e.tensor_tensor(out=kcol[:, 48:56], in0=kcol[:, 24:32], in1=kcol[:, 24:32],
                                              op=ALU.mult),
             reads=["kc_e"], writes=["kc_t"])
        S.op("dve", lambda e: e.tensor_scalar(out=kcol[:, 0:8], in0=kcol[:, 48:56], scalar1=1.0 / 9,
                                              scalar2=1.0 / 7, op0=ALU.mult, op1=ALU.add),
             reads=["kc_t"], writes=["kc_p"])
        for cc in (1.0 / 5, 1.0 / 3, 1.0):
            S.op("dve", lambda e: e.tensor_tensor(out=kcol[:, 0:8], in0=kcol[:, 0:8], in1=kcol[:, 48:56],
                                                  op=ALU.mult),
                 reads=["kc_p", "kc_t"], writes=["kc_p"])
            S.op("dve", lambda e, cc=cc: e.tensor_scalar(out=kcol[:, 0:8], in0=kcol[:, 0:8], scalar1=cc,
                                                         scalar2=None, op0=ALU.add),
                 reads=["kc_p"], writes=["kc_p"])
        S.op("dve", lambda e: e.tensor_tensor(out=kcol[:, 0:8], in0=kcol[:, 0:8], in1=kcol[:, 24:32],
                                              op=ALU.mult),
             reads=["kc_p", "kc_e"], writes=["kc_p"])
        S.op("dve", lambda e: e.tensor_scalar(out=kcol[:, 8:16], in0=kcol[:, 0:8], scalar1=-16.0,
                                              scalar2=None, op0=ALU.mult),
             reads=["kc_p"], writes=["kc_K"])
        S.op("dve", lambda e: e.tensor_scalar(out=kcol[:, 0:8], in0=kcol[:, 0:8], scalar1=-8.0,
                                              scalar2=None, op0=ALU.mult),
             reads=["kc_p", "kc_K"], writes=["kc_K"])
        Kh = lambda blk: kcol[:, blk:blk + 1]
        Kf = lambda blk: kcol[:, 8 + blk:9 + blk]

        def l1_setup():
            if n_layers > 1:
                def f(e):
                    m = usu[:, 256:384]
                    return e.tensor_tensor(out=wtril_f,
                                           in0=u1flat[:, U_SPW:U_SPW + 1024].rearrange("p (g t) -> p g t", g=8),
                                           in1=m.unsqueeze(1).to_broadcast([P, 8, 128]), op=ALU.mult)
                S.op("dve", f, reads=u1keys + [("U0", 7)], writes=[("U0", 0), ("U0", 1)])
                S.op("dve", lambda e: e.tensor_copy(out=wtril_bf[:], in_=wtril_f),
                     reads=[("U0", 0), ("U0", 1)], writes=["wtril_bf"])
                S.op("dve", lambda e: e.memset(sqb[0][:, 0:128], 1.0), writes=["sqb0"])
                for g in range(8):
                    bk = nbank()

                    def f(e, g=g, bk=bk):
                        return e.matmul(ps[bk][:, 0:128], lhsT=sqb[0][:, 0:128], rhs=wtril_f[:, g, :],
                                        start=True, stop=True)
                    S.op("pe", f, reads=["sqb0", ("U0", 0), ("U0", 1)], writes=[("ps", bk)])
                    for jj in range(2):
                        j = 2 * g + jj

                        def f2(e, g=g, bk=bk, j=j):
                            return e.scalar_tensor_tensor(
                                out=Cc[:, j, :], in0=ps[bk][:, 0:128], scalar=col(C_LNB + j),
                                in1=u1flat[:, U_SPB + g * 128:U_SPB + (g + 1) * 128], op0=ALU.mult, op1=ALU.add)
                        S.op("dve", f2, reads=[("ps", bk), "cst"] + u1keys, writes=["Cc"])
                    rel(bk)


        GPT = NG_TILE if n_layers > 1 else 13
        rem_tile = [4] * 8 + [2] + [4] * 4 + ([4] * 16 if n_layers > 1 else [])
        seq_groups = [g for t in range(n_tiles) for g in range(GPT)]
        rem = [r for t in range(n_tiles) for r in rem_tile]
        rstate = {"next_load": 0, "oldest": 0}

        def _load():
            i = rstate["next_load"]
            rstate["next_load"] += 1
            if i >= len(seq_groups):
                return
            slot = i % NSLOT
            g = seq_groups[i]
            S.dma("pool", "w%d" % slot,
                  lambda e: e.dma_start(out=ring[:, slot, :], in_=wst_d[g], max_dma_last_dim=8192),
                  writes=[("w", slot)])

        for i in range(NSLOT):
            _load()

        def wslot(i):
            if record is not None:
                return i % NSLOT
            assert rstate["oldest"] <= i < rstate["next_load"], (i, rstate)
            return i % NSLOT

        def wdone(i, n=1):
            if record is not None:
                return
            rem[i] -= n
            assert rem[i] >= 0
            while rstate["oldest"] < len(rem) and rem[rstate["oldest"]] == 0:
                rstate["oldest"] += 1
                _load()

        def mm_group(out_ap, pairs, reads, writes):
            def f(e):
                ins = None
                n = len(pairs)
                for i, (l, r) in enumerate(pairs):
                    ins = e.matmul(out_ap, lhsT=l, rhs=r, start=(i == 0), stop=(i == n - 1))
                return ins
            S.op("pe", f, reads=reads, writes=writes)

        hT_keys = [("hT", kc) for kc in range(KC)]

        def norm_stats_blk(xb, xk, gi, b):
            S.op("act", lambda e: e.activation(out=junk, in_=xb[:, b, :], func=AF.Square,
                                               accum_out=ss[:, b:b + 1]),
                 reads=[(xk, b)], writes=[("ss", b), ("den", 0)])
            S.op("act", lambda e: e.activation(out=rs[:, b:b + 1], in_=ss[:, b:b + 1], func=AF.Sqrt,
                                               scale=1.0 / D, bias=EPS),
                 reads=[("ss", b)], writes=[("rs", b)])
            S.op("dve", lambda e: e.reciprocal(out=rs[:, b:b + 1], in_=rs[:, b:b + 1]),
                 reads=[("rs", b)], writes=[("rs", b)])
            for hf in range(2):
                S.op("dve", lambda e, hf=hf: e.scalar_tensor_tensor(
                    out=hb[:, b, hf * 512:(hf + 1) * 512], in0=xb[:, b, hf * 512:(hf + 1) * 512],
                    scalar=rs[:, b:b + 1], in1=gain[:, gi, hf * 512:(hf + 1) * 512],
                    op0=ALU.mult, op1=ALU.mult),
                    reads=[(xk, b), ("rs", b), "gain"], writes=[("Gs", 2 * b + hf)])

        def norm_stats(xb, xk, gi):
            for b in range(NB):
                norm_stats_blk(xb, xk, gi, b)

        def norm_tr():
            for pi in range(KC // 2):
                bkt = nbank()
                ptb = ps[bkt][:, :].bitcast(BF16)

                def f(e, pi=pi, ptb=ptb):
                    ins = None
                    for kk in range(2):
                        kc = 2 * pi + kk
                        for b in range(NB):
                            ins = e.transpose(out=ptb[:, kk * 512 + b * 128: kk * 512 + (b + 1) * 128],
                                              in_=hb[:, b, kc * 128:(kc + 1) * 128], identity=ident_bf[:])
                    return ins
                S.op("pe", f, reads=[("Gs", 2 * b + pi // 2) for b in range(NB)] + ["ident_bf"],
                     writes=[("ps", bkt)])
                S.op("act", lambda e, pi=pi, ptb=ptb: e.copy(
                    out=hT[:, 2 * pi:2 * pi + 2, :].rearrange("p k t -> p (k t)"), in_=ptb[:, :]),
                    reads=[("ps", bkt)], writes=[("hT", 2 * pi), ("hT", 2 * pi + 1)])
                rel(bkt)

        def norm_T(xb, xk, gi):
            norm_stats(xb, xk, gi)
            norm_tr()

        def inproj_chunk(gi, pos, bk=None):
            slot = wslot(gi)
            if bk is None:
                bk = nbank()
            wv = ring[:, slot, :].rearrange("p (c k n) -> p c k n", c=4, k=KC)
            mm_group(ps[bk][:, :], [(wv[:, pos, kc, :], hT[:, kc, :]) for kc in range(KC)],
                     reads=[("w", slot)] + hT_keys, writes=[("ps", bk)])
            wdone(gi)
            return bk

        def out_proj(xb, xk, gbase, pre=None, blk_hook=None):
            if pre is not None:
                pre[0]()

            def mm(b, jg, gi, bank):
                slot = wslot(gi)
                wv = ring[:, slot, :].rearrange("p (j n) -> p j n", j=8)

                def f(e):
                    ins = None
                    for jj in range(8):
                        j = jg * 8 + jj
                        ins = e.matmul(ps[bank][:, :], lhsT=yT[:, j, b * 128:(b + 1) * 128],
                                       rhs=wv[:, jj, :], start=(j == 0), stop=(j == 15))
                    return ins
                S.op("pe", f, reads=[("w", slot)] + [("yT", jg * 8 + jj) for jj in range(8)],
                     writes=[("ps", bank)])
                wdone(gi)

            def add(b, half, bank):
                S.op("dve", lambda e: e.tensor_tensor(
                    out=xb[:, b, half * 512:(half + 1) * 512], in0=ps[bank][:, :],
                    in1=xb[:, b, half * 512:(half + 1) * 512], op=ALU.add),
                    reads=[("ps", bank), (xk, b)], writes=[(xk, b)])
                rel(bank)

            banks = [nbank() for _ in range(NB)]
            for jg in range(2):
                for b in range(NB):
                    mm(b, jg, gbase + jg, banks[b])
            for b in range(NB):
                add(b, 0, banks[b])
            if pre is not None:
                pre[1]()
            if blk_hook is None:
                banks = [nbank() for _ in range(NB)]
                for jg in range(2):
                    for b in range(NB):
                        mm(b, jg, gbase + 2 + jg, banks[b])
                for b in range(NB):
                    add(b, 1, banks[b])
            else:
                for b in range(NB):
                    bank = nbank()
                    for jg in range(2):
                        mm(b, jg, gbase + 2 + jg, bank)
                    add(b, 1, bank)
                    blk_hook(b)

        def layer0(t, xb, xk, gb0, pre=None):
            first = (t % tps == 0)
            if first:
                S.op("pool", lambda e: e.memset(halo[:], 0.0), writes=[("halo", i) for i in range(8)])
                S.op("pool", lambda e: e.memset(hstate[:], 0.0), writes=[("hstate", i) for i in range(8)])
            def cgrp(ci):
                if record is not None:
                    record.append(ci)
                    return (gb0, 0)
                pos = l0_order.index(ci)
                return (gb0 + pos // 4, pos % 4)

            gi, pos = cgrp(0)
            slot = wslot(gi)
            wv = ring[:, slot, :].rearrange("p (c k n) -> p c k n", c=4, k=KC)
            bkv = nbank()
            for b in range(NB):
                mm_group(ps[bkv][:, b * 128:(b + 1) * 128],
                         [(hT[:, kc, b * 128:(b + 1) * 128], wv[:, 0, kc, :]) for kc in range(KC)],
                         reads=[("w", slot)] + hT_keys, writes=[("ps", bkv)])
            wdone(gi)
            for kv in range(2):
                S.op("act", lambda e, kv=kv: e.copy(
                    out=vz[:, 1:NB + 1, kv, kv * 64:(kv + 1) * 64],
                    in_=ps[bkv][:, :].rearrange("p (b n) -> p b n", b=NB)[:, :, kv * 64:(kv + 1) * 64]),
                    reads=[("ps", bkv)], writes=["vcur"])
            rel(bkv)

            def gen_qk(ci, gcol, out_ap, outkey, idx):
                sq, rr = sqb[idx % 2], rsb[idx % 2]
                sqk, rrk = "sqb%d" % (idx % 2), "rsb%d" % (idx % 2)
                bk = yield from getbank()
                inproj_chunk(*cgrp(ci), bk=bk)
                yield
                S.op("act", lambda e: e.activation(out=sq[:], in_=ps[bk][:, :], func=AF.Square),
                     reads=[("ps", bk)], writes=[sqk])
                yield
                b2 = yield from getbank()
                S.op("pe", lambda e: e.matmul(ps[b2][:, :], lhsT=col(C_ONESB, 128), rhs=sq[:],
                                              start=True, stop=True),
                     reads=[sqk, "cst"], writes=[("ps", b2)])
                yield
                S.op("act", lambda e: e.activation(out=rr[:], in_=ps[b2][:, :], func=AF.Ln, scale=1.0 / 64,
                                                   bias=EPS),
                     reads=[("ps", b2)], writes=[rrk])
                rel(b2)
                yield
                S.op("act", lambda e: e.activation(out=rr[:], in_=rr[:], func=AF.Exp, scale=-0.5),
                     reads=[rrk], writes=[rrk])
                yield
                if out_ap is None:
                    for kv in range(2):
                        pr = slice(kv * 64, (kv + 1) * 64)
                        S.op("dve", lambda e, kv=kv, pr=pr: e.scalar_tensor_tensor(
                            out=kz[pr, kv, 128:], in0=ps[bk][pr, :], scalar=cst[pr, gcol:gcol + 1],
                            in1=rr[pr, :], op0=ALU.mult, op1=ALU.mult),
                            reads=[("ps", bk), rrk, "cst"], writes=[outkey])
                else:
                    S.op("dve", lambda e: e.scalar_tensor_tensor(out=out_ap, in0=ps[bk][:, :], scalar=col(gcol),
                                                                 in1=rr[:], op0=ALU.mult, op1=ALU.mult),
                         reads=[("ps", bk), rrk, "cst"], writes=[outkey])
                rel(bk)
                yield

            def gen_gb(c):
                bk = yield from getbank()
                inproj_chunk(*cgrp(10 + c), bk=bk)
                yield
                S.op("act", lambda e: e.activation(out=Gs[:, c, :], in_=ps[bk][:, :], func=AF.Silu),
                     reads=[("ps", bk)], writes=[("Gs", c)])
                rel(bk)
                yield

            gens = [gen_qk(1, C_KG, None, "kcur", 0)]
            gens += [gen_qk(2 + c, C_QG, qnT[:, c, :], ("qn", c), 1 + c) for c in range(8)]
            gens += [gen_gb(c) for c in range(8)]
            b_gens = gens

            def gen_att_kv(b, quad, kv, bo, bd, stt, u):
                pi_ = (u % 2) * 2 + kv
                pT = PT[pi_]
                pk = ("PT", pi_)
                kts = [1] if (first and b == 0) else [0, 1]
                for kt in kts:
                    keyblk = b + kt
                    bs_ = yield from getbank()
                    kkey = "kprev" if keyblk == 0 else "kcur"
                    vkey = "vprev" if keyblk == 0 else "vcur"
                    h0 = kv * 8 + quad * 4

                    def fS(e, keyblk=keyblk, bs_=bs_, kt=kt, h0=h0):
                        o = ps[bs_][:, :].rearrange("p (c q) -> p c q", c=4)
                        e.matmul(o, lhsT=kz[:, kv, keyblk * 128:(keyblk + 1) * 128],
                                 rhs=qnT[:, quad * 4:(quad + 1) * 4, b * 128:(b + 1) * 128],
                                 start=True, stop=False)
                        return e.matmul(o, lhsT=ident_bf[:], rhs=Et[:, kt, h0:h0 + 4, :], start=False, stop=True)
                    S.op("pe", fS, reads=[kkey, "Et", "ident_bf"] + [("qn", quad * 4 + cc) for cc in range(4)],
                         writes=[("ps", bs_)])
                    yield
                    S.op("act", lambda e, bs_=bs_: e.activation(
                        out=pT[:], in_=ps[bs_][:, :], func=AF.Exp, scale=0.125),
                        reads=[("ps", bs_)], writes=[pk])
                    rel(bs_)
                    yield
                    st_ = (stt["n"] == 0)
                    sp_ = (stt["n"] == stt["total"] - 1)
                    stt["n"] += 1
                    S.op("pe", lambda e, keyblk=keyblk, st_=st_, sp_=sp_: e.matmul(
                        ps[bo][:, :], lhsT=vz[:, keyblk, kv, :], rhs=pT[:], start=st_, stop=sp_),
                        reads=[vkey, pk], writes=[("ps", bo)])
                    S.op("pe", lambda e, st_=st_, sp_=sp_: e.matmul(
                        ps[bd][:, :], lhsT=onesz[:, kv, :], rhs=pT[:], start=st_, stop=sp_),
                        reads=["onesz", pk], writes=[("ps", bd)])
                    yield

            def gen_att(b, quad, u):
                bo = yield from getbank()
                bd = yield from getbank()
                dn, y_ = den[u % 2], yb[u % 2]
                dk, yk = denk[u % 2], ybk[u % 2]
                nk = 1 if (first and b == 0) else 2
                stt = {"n": 0, "total": 2 * nk}
                subs = [gen_att_kv(b, quad, kv, bo, bd, stt, u) for kv in range(2)]
                while subs:
                    for g in list(subs):
                        try:
                            next(g)
                        except StopIteration:
                            subs.remove(g)
                    yield
                S.op("dve", lambda e: e.tensor_tensor(
                    out=dn[:].rearrange("p (c q) -> p c q", c=4),
                    in0=ps[bd][:, :].rearrange("p (c q) -> p c q", c=4),
                    in1=kcol[:, 16 + quad * 4:16 + (quad + 1) * 4].unsqueeze(2).to_broadcast([P, 4, 128]),
                    op=ALU.add),
                    reads=[("ps", bd), "kc_sink"], writes=[dk])
                rel(bd)
                yield
                S.op("act", lambda e: e.activation(out=dn[:], in_=dn[:], func=AF.Ln), reads=[dk], writes=[dk])
                yield
                S.op("act", lambda e: e.activation(out=dn[:], in_=dn[:], func=AF.Exp, scale=-1.0),
                     reads=[dk], writes=[dk])
                yield
                S.op("dve", lambda e: e.tensor_tensor(out=y_[:], in0=ps[bo][:, :], in1=dn[:], op=ALU.mult),
                     reads=[("ps", bo), dk], writes=[yk])
                rel(bo)
                yield
                S.op("dve", lambda e: e.tensor_tensor(
                    out=yT[:, 8 + quad * 4:8 + (quad + 1) * 4, b * 128:(b + 1) * 128],
                    in0=y_[:].rearrange("p (c q) -> p c q", c=4),
                    in1=Gs[:, quad * 4:(quad + 1) * 4, b * 128:(b + 1) * 128], op=ALU.mult),
                    reads=[yk] + [("Gs", quad * 4 + cc) for cc in range(4)],
                    writes=[("yT", 8 + quad * 4 + cc) for cc in range(4)])
                yield

            def gen_att_tail():
                S.op("pool", lambda e: e.tensor_copy(out=kz[:, :, 0:128], in_=kz[:, :, NB * 128:(NB + 1) * 128]),
                     reads=["kcur"], writes=["kprev"])
                S.op("pool", lambda e: e.tensor_copy(out=vz[:, 0, :, :], in_=vz[:, NB, :, :]),
                     reads=["vcur"], writes=["vprev"])
                yield

            units = [(b, quad) for b in range(NB) for quad in range(2)]
            att_gens = [gen_att(b, quad, u) for u, (b, quad) in enumerate(units)] + [gen_att_tail()]
            def gen_A():
                active = []
                i = 0
                while active or i < len(b_gens):
                    while len(active) < 2 and i < len(b_gens):
                        active.append(b_gens[i])
                        i += 1
                    for g in list(active):
                        try:
                            next(g)
                        except StopIteration:
                            active.remove(g)
                    yield
                active = []
                i = 0
                while active or i < len(att_gens):
                    while len(active) < 2 and i < len(att_gens):
                        active.append(att_gens[i])
                        i += 1
                    for g in list(active):
                        try:
                            next(g)
                        except StopIteration:
                            active.remove(g)
                    yield

            def gen_rg(blk):
                si = blk % 2
                Ub = Us[si]
                zb, xb_ = zbuf[si], xabf[si]
                t_r, t_a, t_m, t_i, t_b, t_h, t_s, t_x = [Ub[:, i, :] for i in range(8)]
                Uk = [("U%d" % si, i) for i in range(8)]
                zh, zm, xk_ = ("zb_h", si), ("zb_m", si), ("xabf", si)
                bz = yield from getbank()
                inproj_chunk(*cgrp(18 + 2 * blk), bk=bz)
                S.op("pool", lambda e: e.tensor_copy(out=zb[:, 0:4], in_=halo[:, blk, :]),
                     reads=[("halo", blk)], writes=[zh])
                yield
                S.op("act", lambda e: e.copy(out=zb[:, 4:T + 4], in_=ps[bz][:, :]),
                     reads=[("ps", bz)], writes=[zm])
                S.op("act", lambda e: e.activation(
                    out=t_x, in_=ps[bz][:, :], func=AF.Identity, scale=col(C_CONVW + blk * 4 + 3),
                    bias=col(C_CONVB + blk)),
                    reads=[("ps", bz), "cst"], writes=[Uk[7]])
                rel(bz)
                yield
                for k in (1, 2, 3):
                    S.op("dve", lambda e, k=k: e.scalar_tensor_tensor(
                        out=t_x, in0=zb[:, 4 - k:T + 4 - k], scalar=col(C_CONVW + blk * 4 + 3 - k), in1=t_x,
                        op0=ALU.mult, op1=ALU.add),
                        reads=[zh, zm, Uk[7], "cst"], writes=[Uk[7]])
                    yield
                S.op("pool", lambda e: e.tensor_copy(out=halo[:, blk, :], in_=zb[:, T:T + 4]),
                     reads=[zm, zh], writes=[("halo", blk)])
                S.op("dve", lambda e: e.tensor_copy(out=xb_[:], in_=t_x), reads=[Uk[7]], writes=[xk_])
                yield
                br = yield from getbank()
                S.op("pe", lambda e: e.matmul(ps[br][:, :], lhsT=gw_bf[:, 0, blk, :], rhs=xb_[:],
                                              start=True, stop=True),
                     reads=["gw_bf", xk_], writes=[("ps", br)])
                bi = yield from getbank()
                S.op("pe", lambda e: e.matmul(ps[bi][:, :], lhsT=gw_bf[:, 1, blk, :], rhs=xb_[:],
                                              start=True, stop=True),
                     reads=["gw_bf", xk_], writes=[("ps", bi)])
                yield
                S.op("act", lambda e: e.activation(out=t_r, in_=ps[br][:, :], func=AF.Tanh, scale=0.5,
                                                   bias=kcol[:, 32 + blk:33 + blk]),
                     reads=[("ps", br), "kc_hb"], writes=[Uk[0]])
                rel(br)
                yield
                S.op("act", lambda e: e.activation(out=t_i, in_=ps[bi][:, :], func=AF.Tanh, scale=0.5,
                                                   bias=kcol[:, 40 + blk:41 + blk]),
                     reads=[("ps", bi), "kc_hb"], writes=[Uk[3]])
                rel(bi)
                yield
                S.op("act", lambda e: e.activation(out=t_a, in_=t_r, func=AF.Exp, scale=Kh(blk), bias=Kh(blk)),
                     reads=[Uk[0], "kc_K"], writes=[Uk[1]])
                yield
                S.op("dve", lambda e: e.tensor_tensor(out=t_m, in0=t_a, in1=t_a, op=ALU.mult),
                     reads=[Uk[1]], writes=[Uk[2]])
                yield
                S.op("dve", lambda e: e.scalar_tensor_tensor(out=t_b, in0=t_i, scalar=1.0, in1=t_x,
                                                             op0=ALU.add, op1=ALU.mult),
                     reads=[Uk[3], Uk[7]], writes=[Uk[4]])
                yield
                S.op("dve", lambda e: e.tensor_scalar(out=t_m, in0=t_m, scalar1=1.0, scalar2=-1.0,
                                                      op0=ALU.min, op1=ALU.mult),
                     reads=[Uk[2]], writes=[Uk[2]])
                yield
                S.op("act", lambda e: e.activation(out=t_m, in_=t_m, func=AF.Sqrt, bias=1.0),
                     reads=[Uk[2]], writes=[Uk[2]])
                yield
                S.op("dve", lambda e: e.scalar_tensor_tensor(out=t_b, in0=t_b, scalar=0.5, in1=t_m,
                                                             op0=ALU.mult, op1=ALU.mult),
                     reads=[Uk[4], Uk[2]], writes=[Uk[4]])
                yield
                bg = yield from getbank()
                inproj_chunk(*cgrp(19 + 2 * blk), bk=bg)
                yield
                S.op("act", lambda e: e.activation(out=t_s, in_=ps[bg][:, :], func=AF.Tanh, scale=0.5),
                     reads=[("ps", bg)], writes=[Uk[6]])
                yield
                S.op("dve", lambda e: e.tensor_tensor_scan(
                    out=t_h, data0=t_a, data1=t_b, initial=hstate[:, blk:blk + 1], op0=ALU.mult, op1=ALU.add),
                    reads=[Uk[1], Uk[4], ("hstate", blk)], writes=[Uk[5]])
                yield
                S.op("pool", lambda e: e.tensor_copy(out=hstate[:, blk:blk + 1], in_=Ub[:, 5, T - 1:T]),
                     reads=[Uk[5]], writes=[("hstate", blk)])
                S.op("dve", lambda e: e.scalar_tensor_tensor(out=t_s, in0=t_s, scalar=1.0, in1=ps[bg][:, :],
                                                             op0=ALU.add, op1=ALU.mult),
                     reads=[Uk[6], ("ps", bg)], writes=[Uk[6]])
                rel(bg)
                yield
                S.op("dve", lambda e: e.scalar_tensor_tensor(out=yT[:, blk, :], in0=t_h, scalar=0.5, in1=t_s,
                                                             op0=ALU.mult, op1=ALU.mult),
                     reads=[Uk[5], Uk[6]], writes=[("yT", blk)])
                yield

            multi_zipper([([gen_A()], 1), ([gen_rg(blk) for blk in range(8)], 2)])
            if dbg == 5:
                return
            if n_layers > 1 and dbg == 0:
                out_proj(xb, xk, gb0 + 9, pre, blk_hook=lambda b: norm_stats_blk(xb, xk, 1, b))
            else:
                out_proj(xb, xk, gb0 + 9, pre)

        def layer1(t, xb, xk, gb1, pre=None):
            norm_tr()
            vslots = [wslot(gb1 + vg) for vg in range(4)]
            for b in range(NB):
                banks = []
                for vg in range(4):
                    bk = nbank()
                    banks.append(bk)
                    wv = ring[:, vslots[vg], :].rearrange("p (k n) -> p k n", k=KC)
                    if b == 0 and vg == 0:
                        for pi in range(KC // 2):
                            def f(e, pi=pi, bk=bk, wv=wv):
                                ins = None
                                for kc in (2 * pi, 2 * pi + 1):
                                    ins = e.matmul(ps[bk][:, :], lhsT=hT[:, kc, 0:128], rhs=wv[:, kc, :],
                                                   start=(kc == 0), stop=(kc == KC - 1))
                                return ins
                            S.op("pe", f, reads=[("w", vslots[vg]), ("hT", 2 * pi), ("hT", 2 * pi + 1)],
                                 writes=[("ps", bk)])
                    else:
                        mm_group(ps[bk][:, :],
                                 [(hT[:, kc, b * 128:(b + 1) * 128], wv[:, kc, :]) for kc in range(KC)],
                                 reads=[("w", vslots[vg])] + hT_keys, writes=[("ps", bk)])
                    wdone(gb1 + vg)
                    S.op("dve", lambda e, bk=bk, vg=vg: e.bn_stats(out=stats[:, vg, :], in_=ps[bk][:, :]),
                         reads=[("ps", bk)], writes=[("stats", vg)])
                S.op("dve", lambda e: e.bn_aggr(out=mv[:], in_=stats[:].rearrange("p a s -> p (a s)")),
                     reads=[("stats", vg) for vg in range(4)], writes=["mv"])
                S.op("act", lambda e: e.activation(out=lnr[:, 0:1], in_=mv[:, 1:2], func=AF.Sqrt, bias=EPS),
                     reads=["mv"], writes=["lnr"])
                S.op("dve", lambda e: e.reciprocal(out=lnr[:, 1:2], in_=lnr[:, 0:1]), reads=["lnr"],
                     writes=["lnr2"])
                for vg in range(4):
                    S.op("dve", lambda e, vg=vg, b=b, bk=banks[vg]: e.tensor_scalar(
                        out=vhat[:, b, vg * 512:(vg + 1) * 512], in0=ps[bk][:, :], scalar1=mv[:, 0:1],
                        scalar2=lnr[:, 1:2], op0=ALU.subtract, op1=ALU.mult),
                        reads=[("ps", banks[vg]), "mv", "lnr2"], writes=[("U0", 2 * b), ("U0", 2 * b + 1)])
                    rel(banks[vg])

            def gen_l1(j):
                g = j // 2
                si = j % 2
                S1, G1, Y1 = sqb[si], rsb[si], expS[si]
                s1k, g1k, y1k = "sqb%d" % si, "rsb%d" % si, ("expS", si)
                gi = gb1 + 4 + j // 2
                bu = yield from getbank()
                inproj_chunk(gi, (j % 2) * 2, bk=bu)
                yield
                bg = yield from getbank()
                inproj_chunk(gi, (j % 2) * 2 + 1, bk=bg)
                yield
                S.op("act", lambda e: e.activation(out=G1[:], in_=ps[bg][:, :], func=AF.Silu),
                     reads=[("ps", bg)], writes=[g1k])
                rel(bg)
                bsg = yield from getbank()
                for b in range(NB):
                    S.op("pe", lambda e, b=b: e.matmul(
                        ps[bsg][:, b * 128:(b + 1) * 128], lhsT=vhat[:, b, j * 128:(j + 1) * 128],
                        rhs=wtril_bf[:, g, :], start=True, stop=True),
                        reads=[("U0", 2 * b), ("U0", 2 * b + 1), "wtril_bf"], writes=[("ps", bsg)])
                yield
                S.op("dve", lambda e: e.scalar_tensor_tensor(
                    out=S1[:].rearrange("p (b t) -> p b t", b=NB),
                    in0=ps[bsg][:, :].rearrange("p (b t) -> p b t", b=NB), scalar=col(C_LNG + j),
                    in1=Cc[:, j:j + 1, :].to_broadcast([P, NB, 128]), op0=ALU.mult, op1=ALU.add),
                    reads=[("ps", bsg), "Cc", "cst"], writes=[s1k])
                rel(bsg)
                yield
                S.op("dve", lambda e: e.tensor_tensor(out=Y1[:], in0=ps[bu][:, :], in1=S1[:], op=ALU.mult),
                     reads=[("ps", bu), s1k], writes=[y1k])
                rel(bu)
                yield
                S.op("dve", lambda e: e.tensor_tensor(out=yT[:, j, :], in0=Y1[:], in1=G1[:], op=ALU.mult),
                     reads=[y1k, g1k], writes=[("yT", j)])
                yield

            zipper([gen_l1(j) for j in range(16)], 2)
            out_proj(xb, xk, gb1 + 12, pre)

        if dbg != 1:
            norm_T(xbuf[0], "x0", 0)
        l1_setup()
        for t in range(n_tiles):
            xb = xbuf[t % NXB]
            xk = "x%d" % (t % NXB)
            nxt = None
            if t + 1 < n_tiles:
                load_x(t + 1)
                if dbg != 1:
                    nxt = (lambda t=t: norm_stats(xbuf[(t + 1) % NXB], "x%d" % ((t + 1) % NXB), 0), norm_tr)
            if dbg != 1:
                if n_layers > 1 and dbg == 0:
                    layer0(t, xb, xk, t * GPT)
                    layer1(t, xb, xk, t * GPT + 13, nxt)
                else:
                    layer0(t, xb, xk, t * GPT, nxt)
            S.dma("sp", "xs%d" % (t % NXB), lambda e, t=t, xb=xb: e.dma_start(
                out=out_d[t * T:(t + 1) * T, :].rearrange("(b p) d -> p b d", p=P), in_=xb[:]),
                reads=[(xk, b) for b in range(NB)])
        S.final_wait("sp", [("D", sname, 16 * c) for sname, c in S.dcnt.items()])

        sems = {}
        for k in S.sem_keys():
            sems[k] = st.enter_context(nc.semaphore("s_%s_%s" % (k[0], k[1])))
        S.emit(nc, sems)
    return nc


def t5_bucket(dist):
    max_exact = 16
    df = np.maximum(dist, 1).astype(np.float32)
    large = max_exact + (np.log(df / np.float32(max_exact)) / np.float32(np.log(128 / 16))
                         * np.float32(16)).astype(np.int32)
    large = np.minimum(large, 31)
    return np.where(dist < max_exact, dist, large)


def prepare_shared(inp, l0_order):
    f32 = np.float32
    w_in_a = np.asarray(inp["w_in_a"], f32)[0]
    w_out_a = np.asarray(inp["w_out_a"], f32)[0]
    w_in_c = np.asarray(inp["w_in_c"], f32)[0]
    w_out_c = np.asarray(inp["w_out_c"], f32)[0]

    def chunk_in(w, cols):
        return w[:, cols].reshape(KC, P, len(cols)).transpose(1, 0, 2)

    o_za, o_ga, o_q, o_k, o_v, o_gb = 0, 1024, 2048, 3072, 3200, 3328
    ar = np.arange
    chunks = [chunk_in(w_in_a, o_v + ar(128)), chunk_in(w_in_a, o_k + ar(128))]
    head_cols = lambda base, c: np.concatenate([base + c * 64 + ar(64), base + (8 + c) * 64 + ar(64)])
    for c in range(8):
        chunks.append(chunk_in(w_in_a, head_cols(o_q, c)))
    for c in range(8):
        chunks.append(chunk_in(w_in_a, head_cols(o_gb, c)))
    for blk in range(8):
        chunks.append(chunk_in(w_in_a, o_za + blk * 128 + ar(128)))
        chunks.append(chunk_in(w_in_a, o_ga + blk * 128 + ar(128)))
    chunks = [chunks[ci] for ci in l0_order]
    while len(chunks) % 4:
        chunks.append(np.zeros_like(chunks[0]))
    groups = []
    for g in range(len(chunks) // 4):
        groups.append(np.stack(chunks[4 * g:4 * g + 4], axis=1).reshape(P, SLOT))

    def out_groups(w, rows_of_chunk):
        gs = []
        for half in range(2):
            for jg in range(2):
                blk = np.stack([w[rows_of_chunk(jg * 8 + jj)][:, half * 512:(half + 1) * 512]
                                for jj in range(8)], axis=1)
                gs.append(blk.reshape(P, SLOT))
        return gs

    def rows_a(j):
        if j < 8:
            return j * 128 + ar(128)
        c = j - 8
        return np.concatenate([1024 + c * 64 + ar(64), 1024 + (8 + c) * 64 + ar(64)])
    groups += out_groups(w_out_a, rows_a)
    for vg in range(4):
        cols = 2048 + vg * 512 + ar(512)
        groups.append(w_in_c[:, cols].reshape(KC, P, 512).transpose(1, 0, 2).reshape(P, SLOT))
    for j in range(0, 16, 2):
        cs = [chunk_in(w_in_c, j * 128 + ar(128)), chunk_in(w_in_c, 4096 + j * 128 + ar(128)),
              chunk_in(w_in_c, (j + 1) * 128 + ar(128)), chunk_in(w_in_c, 4096 + (j + 1) * 128 + ar(128))]
        groups.append(np.stack(cs, axis=1).reshape(P, SLOT))
    groups += out_groups(w_out_c, lambda j: j * 128 + ar(128))
    wst = np.ascontiguousarray(np.stack(groups, axis=0), dtype=f32)
    assert wst.shape == (NG_TILE, P, SLOT), wst.shape

    cst = np.zeros((P, C_TOTAL), f32)
    pcol = lambda v: np.asarray(v, f32).reshape(8, P).T
    cw = np.asarray(inp["conv_w"], f32)[0]
    cst[:, C_CONVW:C_CONVW + 32] = cw.reshape(4, 8, P).transpose(2, 1, 0).reshape(P, 32)
    cst[:, C_CONVB:C_CONVB + 8] = pcol(inp["conv_b"][0])
    cst[:, C_GAB:C_GAB + 8] = pcol(inp["gate_a_b"][0])
    cst[:, C_GXB:C_GXB + 8] = pcol(inp["gate_x_b"][0])
    cst[:, C_LAM:C_LAM + 8] = pcol(inp["lru_lambda"][0])
    cst[:, C_QG] = np.tile(np.asarray(inp["q_norm_g"], f32)[0], 2)
    cst[:, C_KG] = np.tile(np.asarray(inp["k_norm_g"], f32)[0], 2)
    sinks = np.asarray(inp["sinks"], f32)[0]
    cst[:64, C_SINK:C_SINK + 8] = sinks[None, 0:8]
    cst[64:, C_SINK:C_SINK + 8] = sinks[None, 8:16]
    cst[:, C_LNG:C_LNG + 16] = np.asarray(inp["ln_v_g"], f32)[0].reshape(16, P).T
    cst[:, C_LNB:C_LNB + 16] = np.asarray(inp["ln_v_b"], f32)[0].reshape(16, P).T
    ga = np.asarray(inp["gate_a_w"], f32)[0]
    gx = np.asarray(inp["gate_x_w"], f32)[0]
    gw = np.stack([ga.transpose(1, 0, 2), gx.transpose(1, 0, 2)], axis=1)
    csu = np.zeros((P, U_TOTAL), f32)
    csu[:, U_GATEW:U_GATEW + 2048] = gw.reshape(P, 2048)
    s_i = ar(128)[:, None]
    q_i = ar(128)[None, :]
    amask = np.stack([(s_i > q_i), (s_i <= q_i)], axis=1).astype(f32)
    csu[:, U_AMASK:U_AMASK + 256] = amask.reshape(P, 256)
    spw = np.asarray(inp["spatial_w"], f32)[0]
    csu[:, U_SPW:U_SPW + 1024] = spw.transpose(2, 0, 1).reshape(P, 1024)
    csu[:, U_CMASK:U_CMASK + 128] = (s_i <= q_i).astype(f32)
    spb = np.asarray(inp["spatial_b"], f32)[0]
    csu[:, U_SPB:U_SPB + 1024] = np.broadcast_to(spb.reshape(1, 1024), (P, 1024))
    csu[:, U_IDENT:U_IDENT + 128] = np.eye(128, dtype=f32)
    ob = np.zeros((128, 128), f32)
    ob[:64, :64] = 1.0
    ob[64:, 64:] = 1.0
    cst[:, C_ONESB:C_ONESB + 128] = ob

    gain = np.stack([np.broadcast_to(np.asarray(inp["norm_a"], f32)[0][None, :], (P, D)),
                     np.broadcast_to(np.asarray(inp["norm_c"], f32)[0][None, :], (P, D))], axis=0)
    gain = np.ascontiguousarray(gain, dtype=f32)

    rel = np.asarray(inp["rel_bias"], f32)
    kj = ar(256)[None, :]
    qi = ar(128)[:, None]
    dist = 128 + qi - kj
    bucket = t5_bucket(np.maximum(dist, 0))
    bias = rel[bucket]
    biasT = bias.transpose(1, 2, 0).reshape(2, 128, 16, 128).transpose(1, 0, 2, 3)
    biasT = np.ascontiguousarray(biasT.reshape(P, 2 * 16 * 128), dtype=f32)
    return {"wst": wst, "cst": cst, "csu": csu, "gain": gain, "biasT": biasT}


_CACHE = {}


def get_l0_order():
    if "order" not in _CACHE:
        rec = []
        build_program(1, 4, record=rec)
        order = []
        for ci in rec:
            if ci not in order:
                order.append(ci)
        assert sorted(order) == list(range(34)), order
        _CACHE["order"] = order
    return _CACHE["order"]


def kernel(**inputs):
    x = np.asarray(inputs["x"], np.float32)
    bsz, s, d = x.shape
    per = bsz // N_CORES
    order = get_l0_order()
    shared = prepare_shared(inputs, order)
    n_tiles = per * s // T
    key = (n_tiles, s // T)
    if key not in _CACHE:
        _CACHE[key] = build_program(n_tiles, s // T, l0_order=order)
    nc = _CACHE[key]
    in_maps = []
    for c in range(N_CORES):
        m = dict(shared)
        m["x"] = np.ascontiguousarray(x[c * per:(c + 1) * per].reshape(per * s, d))
        in_maps.append(m)
    res = run_bass_kernel_spmd(nc, in_maps, core_ids=list(range(N_CORES)))
    outs = [np.asarray(r["out"], np.float32).reshape(per, s, d) for r in res.results]
    return np.concatenate(outs, axis=0)
```

```python
import numpy as np
from contextlib import ExitStack
import concourse.bass as bass
import concourse.mybir as mybir
from concourse.bass_utils import run_bass_kernel_spmd

F32 = mybir.dt.float32
BF16 = mybir.dt.bfloat16
AF = mybir.ActivationFunctionType
ALU = mybir.AluOpType

P = 128
T = 512
NB = 4
D = 1024
KC = 8
EPS = 1e-6
N_CORES = 8
SEQ = 2048
TPS = SEQ // T
NSLOT = 5
SLOT = 4096
NG_L0_IN = 9
NG_TILE = 29

C_CONVW = 0
C_CONVB = 32
C_GAB = 40
C_GXB = 48
C_LAM = 56
C_QG = 64
C_KG = 65
C_SINK = 66
C_LNG = 74
C_LNB = 90
C_ONESB = 128
C_TOTAL = 256
U_GATEW = 0
U_SPW = 2048
U_SPB = 3072
U_AMASK = 4096
U_CMASK = 4352
U_IDENT = 4480
U_TOTAL = 4608


class Sched:
    ENG = ("pe", "act", "dve", "pool", "sp")

    def __init__(self):
        self.q = {e: [] for e in self.ENG}
        self.cnt = {e: 0 for e in self.ENG}
        self.dcnt = {}
        self.lastw = {}
        self.readers = {}
        self.waited = {e: {} for e in self.ENG}
        self.nwaits = 0

    def _deps(self, eng, reads, writes, is_dma):
        need = {}

        def add(t, kind):
            if t[0] == "E" and t[1] == eng and not is_dma:
                if eng == "pe":
                    return
                if kind != "raw" and t[2] != self.cnt[eng]:
                    return
            sk = (t[0], t[1])
            if need.get(sk, 0) < t[2]:
                need[sk] = t[2]

        for k in reads:
            t = self.lastw.get(k)
            if t is not None:
                add(t, "raw")
        for k in writes:
            t = self.lastw.get(k)
            if t is not None:
                add(t, "waw")
            for sk, v in self.readers.get(k, {}).items():
                add((sk[0], sk[1], v), "war")
        waits = []
        wd = self.waited[eng]
        for sk, v in need.items():
            if wd.get(sk, 0) >= v:
                continue
            wd[sk] = v
            waits.append((sk, v))
        self.nwaits += len(waits)
        return waits

    def _commit(self, tok, reads, writes):
        for k in writes:
            self.lastw[k] = tok
            self.readers[k] = {}
        sk = (tok[0], tok[1])
        for k in reads:
            r = self.readers.setdefault(k, {})
            if r.get(sk, 0) < tok[2]:
                r[sk] = tok[2]

    def op(self, eng, fn, reads=(), writes=()):
        waits = self._deps(eng, reads, writes, False)
        self.cnt[eng] += 1
        tok = ("E", eng, self.cnt[eng])
        self.q[eng].append((waits, fn, ("E", eng), 1))
        self._commit(tok, reads, writes)
        return tok

    def dma(self, qeng, sem, fn, reads=(), writes=()):
        waits = self._deps(qeng, reads, writes, True)
        self.dcnt[sem] = self.dcnt.get(sem, 0) + 1
        tok = ("D", sem, 16 * self.dcnt[sem])
        self.q[qeng].append((waits, fn, ("D", sem), 16))
        self._commit(tok, reads, writes)
        return tok

    def final_wait(self, eng, toks):
        waits = [((t[0], t[1]), t[2]) for t in toks]
        self.q[eng].append((waits, None, None, 0))

    def sem_keys(self):
        keys = [("E", e) for e in self.ENG]
        keys += [("D", s) for s in self.dcnt]
        return keys

    def emit(self, nc, sems):
        with nc.Block() as block:
            decos = {"pe": block.tensor, "act": block.scalar, "dve": block.vector,
                     "pool": block.gpsimd, "sp": block.sync}
            for e in self.ENG:
                items = self.q[e]

                def body(engh, items=items):
                    for waits, fn, inc, amt in items:
                        for sk, v in waits:
                            engh.wait_ge(sems[sk], v)
                        if fn is None:
                            continue
                        ins = fn(engh)
                        ins.then_inc(sems[inc], amt)

                decos[e](body)


def zipper(gens, depth):
    gens = list(gens)
    active = []
    i = 0
    while active or i < len(gens):
        while len(active) < depth and i < len(gens):
            active.append(gens[i])
            i += 1
        for g in list(active):
            try:
                next(g)
            except StopIteration:
                active.remove(g)


def multi_zipper(streams):
    st = [{"gens": list(g), "depth": d, "i": 0, "active": []} for g, d in streams]
    while any(x["active"] or x["i"] < len(x["gens"]) for x in st):
        for x in st:
            while len(x["active"]) < x["depth"] and x["i"] < len(x["gens"]):
                x["active"].append(x["gens"][x["i"]])
                x["i"] += 1
            for g in list(x["active"]):
                try:
                    next(g)
                except StopIteration:
                    x["active"].remove(g)


def build_program(n_tiles, tps=TPS, n_layers=2, dbg=0, l0_order=None, record=None):
    nc = bass.Bass("TRN2", target_bir_lowering=False)
    ntok = n_tiles * T
    x_d = nc.dram_tensor("x", [ntok, D], F32, kind="ExternalInput").ap()
    wst_d = nc.dram_tensor("wst", [NG_TILE, P, SLOT], F32, kind="ExternalInput").ap()
    cst_d = nc.dram_tensor("cst", [P, C_TOTAL], F32, kind="ExternalInput").ap()
    csu_d = nc.dram_tensor("csu", [P, U_TOTAL], F32, kind="ExternalInput").ap()
    gain_d = nc.dram_tensor("gain", [2, P, D], F32, kind="ExternalInput").ap()
    bias_d = nc.dram_tensor("biasT", [P, 2 * 16 * 128], F32, kind="ExternalInput").ap()
    out_d = nc.dram_tensor("out", [ntok, D], F32, kind="ExternalOutput").ap()

    S = Sched()
    st = ExitStack()
    with st:
        def sb(name, shape, dt=F32):
            return st.enter_context(nc.sbuf_tensor("sb_" + name, shape, dt))

        NXB = 2
        xbuf = [sb("xbuf%d" % i, [P, NB, D]) for i in range(NXB)]
        hT = sb("hT", [P, KC, T], BF16)
        ss = sb("ss", [P, 8])
        rs = sb("rs", [P, 8])
        ring = sb("ring", [P, NSLOT, SLOT], BF16)
        cst = sb("cst", [P, C_TOTAL])
        gain = sb("gain", [P, 2, D])
        Et = sb("Et", [P, 2, 16, 128], BF16)
        vz = sb("vz", [P, NB + 1, 2, 128], BF16)
        kz = sb("kz", [P, 2, (NB + 1) * 128], BF16)
        onesz = sb("onesz", [P, 2, 128], BF16)
        qnT = sb("qnT", [P, 8, T], BF16)
        Gs = sb("Gs", [P, 8, T], BF16)
        sqb = [sb("sqb%d" % i, [P, T]) for i in range(2)]
        rsb = [sb("rsb%d" % i, [P, T]) for i in range(2)]
        expS = [sb("expS%d" % i, [P, T]) for i in range(2)]
        PT = [sb("PT%d" % i, [P, T], BF16) for i in range(4)]
        den = [sb("den0", [P, T]), expS[0]]
        yb = [sb("yb0", [P, T]), expS[1]]
        denk = [("den", 0), ("expS", 0)]
        ybk = [("yb", 0), ("expS", 1)]
        junk = den[0][:].bitcast(BF16)
        yT = sb("yT", [P, 16, T], BF16)
        Us = [sb("U%d" % i, [P, 8, T]) for i in range(2)]
        zbuf = [sb("zbuf%d" % i, [P, T + 4]) for i in range(2)]
        xabf = [sb("xabf%d" % i, [P, T], BF16) for i in range(2)]
        halo = sb("halo", [P, 8, 4])
        hstate = sb("hstate", [P, 8])
        Cc = sb("Cc", [P, 16, 128])
        stats = sb("stats", [P, 4, 6])
        mv = sb("mv", [P, 2])
        lnr = sb("lnr", [P, 2])
        kcol = sb("kcol", [P, 64])
        ident_bf = sb("ident_bf", [P, 128], BF16)
        gw_bf = sb("gw_bf", [P, 2, 8, 128], BF16)
        wtril_bf = sb("wtril_bf", [P, 8, 128], BF16)

        NPS = 8
        ps = [st.enter_context(nc.psum_tensor("ps%d" % i, [P, 512], F32)) for i in range(NPS)]
        U = Us[0]
        hb = Gs[:].rearrange("p (b j) t -> p b (j t)", b=NB)
        hbk = lambda b: [("Gs", 2 * b), ("Gs", 2 * b + 1)]
        wtril_f = U[:, 0:2, :].rearrange("p a t -> p (a t)").rearrange("p (g t) -> p g t", g=8)
        vhat = U[:].rearrange("p a t -> p (a t)").bitcast(BF16).rearrange("p (b f) -> p b f", b=NB)
        xflat = xbuf[0][:].rearrange("p b d -> p (b d)")
        usu = U[:, 7, :]
        x0keys = [("x0", b) for b in range(NB)]

        free_banks = list(range(NPS))

        def nbank():
            assert free_banks, "PSUM banks exhausted"
            return free_banks.pop(0)

        def getbank():
            n = 0
            while not free_banks:
                n += 1
                assert n < 10000, "PSUM bank wait deadlock at build time"
                yield
            return free_banks.pop(0)

        def rel(bk):
            assert bk not in free_banks
            free_banks.append(bk)

        def col(c, n=1):
            return cst[:, c:c + n]

        u1flat = Us[1][:].rearrange("p a t -> p (a t)")
        u1keys = [("U1", i) for i in range(8)]
        x1flat = xbuf[1][:].rearrange("p b d -> p (b d)")
        x1keys = [("x1", b) for b in range(NB)]

        def load_x(t):
            xb = xbuf[t % NXB]
            xk = "x%d" % (t % NXB)
            S.dma("sp", "xl%d" % (t % NXB), lambda e: e.dma_start(
                out=xb[:], in_=x_d[t * T:(t + 1) * T, :].rearrange("(b p) d -> p b d", p=P)),
                writes=[(xk, b) for b in range(NB)])

        load_x(0)
        S.dma("sp", "c0", lambda e: e.dma_start(out=cst[:], in_=cst_d), writes=["cst"])
        S.dma("sp", "c1", lambda e: e.dma_start(out=gain[:], in_=gain_d.rearrange("g p d -> p g d")),
              writes=["gain"])
        S.dma("sp", "c4", lambda e: e.dma_start(out=usu, in_=csu_d[:, 4096:U_TOTAL]), writes=[("U0", 7)])
        S.dma("sp", "c3", lambda e: e.dma_start(out=u1flat, in_=csu_d[:, 0:4096]), writes=u1keys)
        S.dma("sp", "c2", lambda e: e.dma_start(out=x1flat, in_=bias_d), writes=x1keys)

        S.op("dve", lambda e: e.tensor_copy(out=ident_bf[:], in_=usu[:, 384:512]),
             reads=[("U0", 7)], writes=["ident_bf"])
        S.op("dve", lambda e: e.memset(onesz[:], 0.0), writes=["onesz"])
        S.op("dve", lambda e: e.memset(onesz[:, 0, 0:64], 1.0), writes=["onesz"])
        S.op("dve", lambda e: e.memset(onesz[:, 1, 64:128], 1.0), writes=["onesz"])
        S.op("dve", lambda e: e.memset(kz[:], 0.0), writes=["kprev", "kcur"])
        S.op("dve", lambda e: e.memset(vz[:], 0.0), writes=["vprev", "vcur"])
        S.op("dve", lambda e: e.tensor_copy(out=gw_bf[:].rearrange("p a n j -> p (a n j)"),
                                            in_=u1flat[:, U_GATEW:U_GATEW + 2048]),
             reads=u1keys, writes=["gw_bf"])
        for kt in range(2):
            def f(e, kt=kt):
                v = x1flat[:, kt * 2048:(kt + 1) * 2048]
                return e.tensor_scalar(out=v, in0=v, scalar1=8.0, scalar2=1.0, op0=ALU.mult, op1=ALU.mult)
            S.op("pool", f, reads=x1keys, writes=x1keys)

            def f(e, kt=kt):
                m = usu[:, kt * 128:(kt + 1) * 128]
                v = x1flat[:, kt * 2048:(kt + 1) * 2048].rearrange("p (h q) -> p h q", h=16)
                return e.tensor_tensor(out=v, in0=v, in1=m.unsqueeze(1).to_broadcast([P, 16, 128]), op=ALU.mult)
            S.op("pool", f, reads=x1keys + [("U0", 7)], writes=x1keys)
        S.op("pool", lambda e: e.tensor_scalar(out=usu[:, 0:256], in0=usu[:, 0:256], scalar1=-1.0, scalar2=480.0,
                                               op0=ALU.add, op1=ALU.mult),
             reads=[("U0", 7)], writes=[("U0", 7)])
        for kt in range(2):
            def f(e, kt=kt):
                m = usu[:, kt * 128:(kt + 1) * 128]
                v = x1flat[:, kt * 2048:(kt + 1) * 2048].rearrange("p (h q) -> p h q", h=16)
                return e.tensor_tensor(out=Et[:, kt], in0=v, in1=m.unsqueeze(1).to_broadcast([P, 16, 128]),
                                       op=ALU.add)
            S.op("pool", f, reads=x1keys + [("U0", 7)], writes=["Et"])
        S.op("act", lambda e: e.activation(out=kcol[:, 16:24], in_=col(C_SINK, 8), func=AF.Exp),
             reads=["cst"], writes=["kc_sink"])
        S.op("dve", lambda e: e.tensor_scalar(out=kcol[:, 32:40], in0=col(C_GAB, 8), scalar1=0.5, scalar2=None,
                                              op0=ALU.mult), reads=["cst"], writes=["kc_hb"])
        S.op("dve", lambda e: e.tensor_scalar(out=kcol[:, 40:48], in0=col(C_GXB, 8), scalar1=0.5, scalar2=None,
                                              op0=ALU.mult), reads=["cst"], writes=["kc_hb"])
        S.op("act", lambda e: e.activation(out=kcol[:, 24:32], in_=col(C_LAM, 8), func=AF.Exp, scale=-1.0),
             reads=["cst"], writes=["kc_e"])
        S.op("dve", lambda e: e.tensor_scalar(out=kcol[:, 48:56], in0=kcol[:, 24:32], scalar1=2.0,
                                              scalar2=None, op0=ALU.add),
             reads=["kc_e"], writes=["kc_t"])
        S.op("dve", lambda e: e.reciprocal(out=kcol[:, 48:56], in_=kcol[:, 48:56]),
             reads=["kc_t"], writes=["kc_t"])
        S.op("dve", lambda e: e.tensor_tensor(out=kcol[:, 24:32], in0=kcol[:, 24:32], in1=kcol[:, 48:56],
                                              op=ALU.mult),
             reads=["kc_t", "kc_e"], writes=["kc_e"])
        S.op("dve", lambda e: e.tensor_tensor(out=kcol[:, 48:56], in0=kcol[:, 24:32], in1=kcol[:, 24:32],
                                              op=ALU.mult),
             reads=["kc_e"], writes=["kc_t"])
        S.op("dve", lambda e: e.tensor_scalar(out=kcol[:, 0:8], in0=kcol[:, 48:56], scalar1=1.0 / 9,
                                              scalar2=1.0 / 7, op0=ALU.mult, op1=ALU.add),
             reads=["kc_t"], writes=["kc_p"])
        for cc in (1.0 / 5, 1.0 / 3, 1.0):
            S.op("dve", lambda e: e.tensor_tensor(out=kcol[:, 0:8], in0=kcol[:, 0:8], in1=kcol[:, 48:56],
                                                  op=ALU.mult),
                 reads=["kc_p", "kc_t"], writes=["kc_p"])
            S.op("dve", lambda e, cc=cc: e.tensor_scalar(out=kcol[:, 0:8], in0=kcol[:, 0:8], scalar1=cc,
                                                         scalar2=None, op0=ALU.add),
                 reads=["kc_p"], writes=["kc_p"])
        S.op("dve", lambda e: e.tensor_tensor(out=kcol[:, 0:8], in0=kcol[:, 0:8], in1=kcol[:, 24:32],
                                              op=ALU.mult),
             reads=["kc_p", "kc_e"], writes=["kc_p"])
        S.op("dve", lambda e: e.tensor_scalar(out=kcol[:, 8:16], in0=kcol[:, 0:8], scalar1=-16.0,
                                              scalar2=None, op0=ALU.mult),
             reads=["kc_p"], writes=["kc_K"])
        S.op("dve", lambda e: e.tensor_scalar(out=kcol[:, 0:8], in0=kcol[:, 0:8], scalar1=-8.0,
                                              scalar2=None, op0=ALU.mult),
             reads=["kc_p", "kc_K"], writes=["kc_K"])
        Kh = lambda blk: kcol[:, blk:blk + 1]
        Kf = lambda blk: kcol[:, 8 + blk:9 + blk]

        def l1_setup():
            if n_layers > 1:
                def f(e):
                    m = usu[:, 256:384]
                    return e.tensor_tensor(out=wtril_f,
                                           in0=u1flat[:, U_SPW:U_SPW + 1024].rearrange("p (g t) -> p g t", g=8),
                                           in1=m.unsqueeze(1).to_broadcast([P, 8, 128]), op=ALU.mult)
                S.op("dve", f, reads=u1keys + [("U0", 7)], writes=[("U0", 0), ("U0", 1)])
                S.op("dve", lambda e: e.tensor_copy(out=wtril_bf[:], in_=wtril_f),
                     reads=[("U0", 0), ("U0", 1)], writes=["wtril_bf"])
                S.op("dve", lambda e: e.memset(sqb[0][:, 0:128], 1.0), writes=["sqb0"])
                for g in range(8):
                    bk = nbank()

                    def f(e, g=g, bk=bk):
                        return e.matmul(ps[bk][:, 0:128], lhsT=sqb[0][:, 0:128], rhs=wtril_f[:, g, :],
                                        start=True, stop=True)
                    S.op("pe", f, reads=["sqb0", ("U0", 0), ("U0", 1)], writes=[("ps", bk)])
                    for jj in range(2):
                        j = 2 * g + jj

                        def f2(e, g=g, bk=bk, j=j):
                            return e.scalar_tensor_tensor(
                                out=Cc[:, j, :], in0=ps[bk][:, 0:128], scalar=col(C_LNB + j),
                                in1=u1flat[:, U_SPB + g * 128:U_SPB + (g + 1) * 128], op0=ALU.mult, op1=ALU.add)
                        S.op("dve", f2, reads=[("ps", bk), "cst"] + u1keys, writes=["Cc"])
                    rel(bk)


        GPT = NG_TILE if n_layers > 1 else 13
        rem_tile = [4] * 8 + [2] + [4] * 4 + ([4] * 16 if n_layers > 1 else [])
        seq_groups = [g for t in range(n_tiles) for g in range(GPT)]
        rem = [r for t in range(n_tiles) for r in rem_tile]
        rstate = {"next_load": 0, "oldest": 0}

        def _load():
            i = rstate["next_load"]
            rstate["next_load"] += 1
            if i >= len(seq_groups):
                return
            slot = i % NSLOT
            g = seq_groups[i]
            S.dma("pool", "w%d" % slot,
                  lambda e: e.dma_start(out=ring[:, slot, :], in_=wst_d[g], max_dma_last_dim=8192),
                  writes=[("w", slot)])

        for i in range(NSLOT):
            _load()

        def wslot(i):
            if record is not None:
                return i % NSLOT
            assert rstate["oldest"] <= i < rstate["next_load"], (i, rstate)
            return i % NSLOT

        def wdone(i, n=1):
            if record is not None:
                return
            rem[i] -= n
            assert rem[i] >= 0
            while rstate["oldest"] < len(rem) and rem[rstate["oldest"]] == 0:
                rstate["oldest"] += 1
                _load()

        def mm_group(out_ap, pairs, reads, writes):
            def f(e):
                ins = None
                n = len(pairs)
                for i, (l, r) in enumerate(pairs):
                    ins = e.matmul(out_ap, lhsT=l, rhs=r, start=(i == 0), stop=(i == n - 1))
                return ins
            S.op("pe", f, reads=reads, writes=writes)

        hT_keys = [("hT", b) for b in range(NB)]

        def norm_stats_blk(xb, xk, gi, b):
            S.op("act", lambda e: e.activation(out=junk, in_=xb[:, b, :], func=AF.Square,
                                               accum_out=ss[:, b:b + 1]),
                 reads=[(xk, b)], writes=[("ss", b), ("den", 0)])
            S.op("act", lambda e: e.activation(out=rs[:, b:b + 1], in_=ss[:, b:b + 1], func=AF.Sqrt,
                                               scale=1.0 / D, bias=EPS),
                 reads=[("ss", b)], writes=[("rs", b)])
            S.op("dve", lambda e: e.reciprocal(out=rs[:, b:b + 1], in_=rs[:, b:b + 1]),
                 reads=[("rs", b)], writes=[("rs", b)])
            for hf in range(2):
                S.op("dve", lambda e, hf=hf: e.scalar_tensor_tensor(
                    out=hb[:, b, hf * 512:(hf + 1) * 512], in0=xb[:, b, hf * 512:(hf + 1) * 512],
                    scalar=rs[:, b:b + 1], in1=gain[:, gi, hf * 512:(hf + 1) * 512],
                    op0=ALU.mult, op1=ALU.mult),
                    reads=[(xk, b), ("rs", b), "gain"], writes=[("Gs", 2 * b + hf)])

        def norm_stats(xb, xk, gi):
            for b in range(NB):
                norm_stats_blk(xb, xk, gi, b)

        def norm_tr_blk(b):
            bkt = nbank()
            ptb = ps[bkt][:, :].bitcast(BF16)

            def f(e):
                ins = None
                for kc in range(KC):
                    ins = e.transpose(out=ptb[:, kc * 128:(kc + 1) * 128],
                                      in_=hb[:, b, kc * 128:(kc + 1) * 128], identity=ident_bf[:])
                return ins
            S.op("pe", f, reads=[("Gs", 2 * b), ("Gs", 2 * b + 1), "ident_bf"], writes=[("ps", bkt)])
            S.op("act", lambda e: e.copy(out=hT[:, :, b * 128:(b + 1) * 128],
                                         in_=ptb.rearrange("p (k t) -> p k t", k=KC)),
                 reads=[("ps", bkt)], writes=[("hT", b)])
            rel(bkt)

        def norm_tr():
            for b in range(NB):
                norm_tr_blk(b)

        def norm_T(xb, xk, gi):
            norm_stats(xb, xk, gi)
            norm_tr()

        def inproj_chunk(gi, pos, bk=None):
            slot = wslot(gi)
            if bk is None:
                bk = nbank()
            wv = ring[:, slot, :].rearrange("p (c k n) -> p c k n", c=4, k=KC)
            mm_group(ps[bk][:, :], [(wv[:, pos, kc, :], hT[:, kc, :]) for kc in range(KC)],
                     reads=[("w", slot)] + hT_keys, writes=[("ps", bk)])
            wdone(gi)
            return bk

        def out_proj(xb, xk, gbase, pre=None, blk_hook=None):
            if pre is not None:
                pre[0]()

            def mm(b, jg, gi, bank):
                slot = wslot(gi)
                wv = ring[:, slot, :].rearrange("p (j n) -> p j n", j=8)

                def f(e):
                    ins = None
                    for jj in range(8):
                        j = jg * 8 + jj
                        ins = e.matmul(ps[bank][:, :], lhsT=yT[:, j, b * 128:(b + 1) * 128],
                                       rhs=wv[:, jj, :], start=(j == 0), stop=(j == 15))
                    return ins
                S.op("pe", f, reads=[("w", slot)] + [("yT", jg * 8 + jj) for jj in range(8)],
                     writes=[("ps", bank)])
                wdone(gi)

            def add(b, half, bank):
                S.op("dve", lambda e: e.tensor_tensor(
                    out=xb[:, b, half * 512:(half + 1) * 512], in0=ps[bank][:, :],
                    in1=xb[:, b, half * 512:(half + 1) * 512], op=ALU.add),
                    reads=[("ps", bank), (xk, b)], writes=[(xk, b)])
                rel(bank)

            banks = [nbank() for _ in range(NB)]
            for jg in range(2):
                for b in range(NB):
                    mm(b, jg, gbase + jg, banks[b])
            for b in range(NB):
                add(b, 0, banks[b])
            if pre is not None:
                pre[1]()
            if blk_hook is None:
                banks = [nbank() for _ in range(NB)]
                for jg in range(2):
                    for b in range(NB):
                        mm(b, jg, gbase + 2 + jg, banks[b])
                for b in range(NB):
                    add(b, 1, banks[b])
            else:
                for b in range(NB):
                    bank = nbank()
                    for jg in range(2):
                        mm(b, jg, gbase + 2 + jg, bank)
                    add(b, 1, bank)
                    blk_hook(b)

        def layer0(t, xb, xk, gb0, pre=None):
            first = (t % tps == 0)
            if first:
                S.op("pool", lambda e: e.memset(halo[:], 0.0), writes=[("halo", i) for i in range(8)])
                S.op("pool", lambda e: e.memset(hstate[:], 0.0), writes=[("hstate", i) for i in range(8)])
            def cgrp(ci):
                if record is not None:
                    record.append(ci)
                    return (gb0, 0)
                pos = l0_order.index(ci)
                return (gb0 + pos // 4, pos % 4)

            gi, pos = cgrp(0)
            slot = wslot(gi)
            wv = ring[:, slot, :].rearrange("p (c k n) -> p c k n", c=4, k=KC)
            bkv = nbank()
            for b in range(NB):
                mm_group(ps[bkv][:, b * 128:(b + 1) * 128],
                         [(hT[:, kc, b * 128:(b + 1) * 128], wv[:, 0, kc, :]) for kc in range(KC)],
                         reads=[("w", slot), ("hT", b)], writes=[("ps", bkv)])
            wdone(gi)
            for kv in range(2):
                S.op("act", lambda e, kv=kv: e.copy(
                    out=vz[:, 1:NB + 1, kv, kv * 64:(kv + 1) * 64],
                    in_=ps[bkv][:, :].rearrange("p (b n) -> p b n", b=NB)[:, :, kv * 64:(kv + 1) * 64]),
                    reads=[("ps", bkv)], writes=["vcur"])
            rel(bkv)

            def gen_qk(ci, gcol, out_ap, outkey, idx):
                sq, rr = sqb[idx % 2], rsb[idx % 2]
                sqk, rrk = "sqb%d" % (idx % 2), "rsb%d" % (idx % 2)
                bk = yield from getbank()
                inproj_chunk(*cgrp(ci), bk=bk)
                yield
                S.op("act", lambda e: e.activation(out=sq[:], in_=ps[bk][:, :], func=AF.Square),
                     reads=[("ps", bk)], writes=[sqk])
                yield
                b2 = yield from getbank()
                S.op("pe", lambda e: e.matmul(ps[b2][:, :], lhsT=col(C_ONESB, 128), rhs=sq[:],
                                              start=True, stop=True),
                     reads=[sqk, "cst"], writes=[("ps", b2)])
                yield
                S.op("act", lambda e: e.activation(out=rr[:], in_=ps[b2][:, :], func=AF.Ln, scale=1.0 / 64,
                                                   bias=EPS),
                     reads=[("ps", b2)], writes=[rrk])
                rel(b2)
                yield
                S.op("act", lambda e: e.activation(out=rr[:], in_=rr[:], func=AF.Exp, scale=-0.5),
                     reads=[rrk], writes=[rrk])
                yield
                if out_ap is None:
                    for kv in range(2):
                        pr = slice(kv * 64, (kv + 1) * 64)
                        S.op("dve", lambda e, kv=kv, pr=pr: e.scalar_tensor_tensor(
                            out=kz[pr, kv, 128:], in0=ps[bk][pr, :], scalar=cst[pr, gcol:gcol + 1],
                            in1=rr[pr, :], op0=ALU.mult, op1=ALU.mult),
                            reads=[("ps", bk), rrk, "cst"], writes=[outkey])
                else:
                    S.op("dve", lambda e: e.scalar_tensor_tensor(out=out_ap, in0=ps[bk][:, :], scalar=col(gcol),
                                                                 in1=rr[:], op0=ALU.mult, op1=ALU.mult),
                         reads=[("ps", bk), rrk, "cst"], writes=[outkey])
                rel(bk)
                yield

            def gen_gb(c):
                bk = yield from getbank()
                inproj_chunk(*cgrp(10 + c), bk=bk)
                yield
                S.op("act", lambda e: e.activation(out=Gs[:, c, :], in_=ps[bk][:, :], func=AF.Silu),
                     reads=[("ps", bk)], writes=[("Gs", c)])
                rel(bk)
                yield

            gens = [gen_qk(1, C_KG, None, "kcur", 0)]
            gens += [gen_qk(2 + c, C_QG, qnT[:, c, :], ("qn", c), 1 + c) for c in range(8)]
            gens += [gen_gb(c) for c in range(8)]
            b_gens = gens

            def gen_att_kv(b, quad, kv, bo, bd, stt, u):
                pi_ = (u % 2) * 2 + kv
                pT = PT[pi_]
                pk = ("PT", pi_)
                kts = [1] if (first and b == 0) else [0, 1]
                for kt in kts:
                    keyblk = b + kt
                    bs_ = yield from getbank()
                    kkey = "kprev" if keyblk == 0 else "kcur"
                    vkey = "vprev" if keyblk == 0 else "vcur"
                    h0 = kv * 8 + quad * 4

                    def fS(e, keyblk=keyblk, bs_=bs_, kt=kt, h0=h0):
                        o = ps[bs_][:, :].rearrange("p (c q) -> p c q", c=4)
                        e.matmul(o, lhsT=kz[:, kv, keyblk * 128:(keyblk + 1) * 128],
                                 rhs=qnT[:, quad * 4:(quad + 1) * 4, b * 128:(b + 1) * 128],
                                 start=True, stop=False)
                        return e.matmul(o, lhsT=ident_bf[:], rhs=Et[:, kt, h0:h0 + 4, :], start=False, stop=True)
                    S.op("pe", fS, reads=[kkey, "Et", "ident_bf"] + [("qn", quad * 4 + cc) for cc in range(4)],
                         writes=[("ps", bs_)])
                    yield
                    S.op("act", lambda e, bs_=bs_: e.activation(
                        out=pT[:], in_=ps[bs_][:, :], func=AF.Exp, scale=0.125),
                        reads=[("ps", bs_)], writes=[pk])
                    rel(bs_)
                    yield
                    st_ = (stt["n"] == 0)
                    sp_ = (stt["n"] == stt["total"] - 1)
                    stt["n"] += 1
                    S.op("pe", lambda e, keyblk=keyblk, st_=st_, sp_=sp_: e.matmul(
                        ps[bo][:, :], lhsT=vz[:, keyblk, kv, :], rhs=pT[:], start=st_, stop=sp_),
                        reads=[vkey, pk], writes=[("ps", bo)])
                    S.op("pe", lambda e, st_=st_, sp_=sp_: e.matmul(
                        ps[bd][:, :], lhsT=onesz[:, kv, :], rhs=pT[:], start=st_, stop=sp_),
                        reads=["onesz", pk], writes=[("ps", bd)])
                    yield

            def gen_att(b, quad, u):
                bo = yield from getbank()
                bd = yield from getbank()
                dn, y_ = den[u % 2], yb[u % 2]
                dk, yk = denk[u % 2], ybk[u % 2]
                nk = 1 if (first and b == 0) else 2
                stt = {"n": 0, "total": 2 * nk}
                subs = [gen_att_kv(b, quad, kv, bo, bd, stt, u) for kv in range(2)]
                while subs:
                    for g in list(subs):
                        try:
                            next(g)
                        except StopIteration:
                            subs.remove(g)
                    yield
                S.op("dve", lambda e: e.tensor_tensor(
                    out=dn[:].rearrange("p (c q) -> p c q", c=4),
                    in0=ps[bd][:, :].rearrange("p (c q) -> p c q", c=4),
                    in1=kcol[:, 16 + quad * 4:16 + (quad + 1) * 4].unsqueeze(2).to_broadcast([P, 4, 128]),
                    op=ALU.add),
                    reads=[("ps", bd), "kc_sink"], writes=[dk])
                rel(bd)
                yield
                S.op("act", lambda e: e.activation(out=dn[:], in_=dn[:], func=AF.Ln), reads=[dk], writes=[dk])
                yield
                S.op("act", lambda e: e.activation(out=dn[:], in_=dn[:], func=AF.Exp, scale=-1.0),
                     reads=[dk], writes=[dk])
                yield
                S.op("dve", lambda e: e.tensor_tensor(out=y_[:], in0=ps[bo][:, :], in1=dn[:], op=ALU.mult),
                     reads=[("ps", bo), dk], writes=[yk])
                rel(bo)
                yield
                S.op("dve", lambda e: e.tensor_tensor(
                    out=yT[:, 8 + quad * 4:8 + (quad + 1) * 4, b * 128:(b + 1) * 128],
                    in0=y_[:].rearrange("p (c q) -> p c q", c=4),
                    in1=Gs[:, quad * 4:(quad + 1) * 4, b * 128:(b + 1) * 128], op=ALU.mult),
                    reads=[yk] + [("Gs", quad * 4 + cc) for cc in range(4)],
                    writes=[("yT", 8 + quad * 4 + cc) for cc in range(4)])
                yield

            def gen_att_tail():
                S.op("pool", lambda e: e.tensor_copy(out=kz[:, :, 0:128], in_=kz[:, :, NB * 128:(NB + 1) * 128]),
                     reads=["kcur"], writes=["kprev"])
                S.op("pool", lambda e: e.tensor_copy(out=vz[:, 0, :, :], in_=vz[:, NB, :, :]),
                     reads=["vcur"], writes=["vprev"])
                yield

            units = [(b, quad) for b in range(NB) for quad in range(2)]
            att_gens = [gen_att(b, quad, u) for u, (b, quad) in enumerate(units)] + [gen_att_tail()]
            def gen_A():
                active = []
                i = 0
                while active or i < len(b_gens):
                    while len(active) < 2 and i < len(b_gens):
                        active.append(b_gens[i])
                        i += 1
                    for g in list(active):
                        try:
                            next(g)
                        except StopIteration:
                            active.remove(g)
                    yield
                active = []
                i = 0
                while active or i < len(att_gens):
                    while len(active) < 2 and i < len(att_gens):
                        active.append(att_gens[i])
                        i += 1
                    for g in list(active):
                        try:
                            next(g)
                        except StopIteration:
                            active.remove(g)
                    yield

            def gen_rg(blk):
                si = blk % 2
                Ub = Us[si]
                zb, xb_ = zbuf[si], xabf[si]
                t_r, t_a, t_m, t_i, t_b, t_h, t_s, t_x = [Ub[:, i, :] for i in range(8)]
                Uk = [("U%d" % si, i) for i in range(8)]
                zh, zm, xk_ = ("zb_h", si), ("zb_m", si), ("xabf", si)
                bz = yield from getbank()
                inproj_chunk(*cgrp(18 + 2 * blk), bk=bz)
                S.op("pool", lambda e: e.tensor_copy(out=zb[:, 0:4], in_=halo[:, blk, :]),
                     reads=[("halo", blk)], writes=[zh])
                yield
                S.op("act", lambda e: e.copy(out=zb[:, 4:T + 4], in_=ps[bz][:, :]),
                     reads=[("ps", bz)], writes=[zm])
                S.op("act", lambda e: e.activation(
                    out=t_x, in_=ps[bz][:, :], func=AF.Identity, scale=col(C_CONVW + blk * 4 + 3),
                    bias=col(C_CONVB + blk)),
                    reads=[("ps", bz), "cst"], writes=[Uk[7]])
                rel(bz)
                yield
                for k in (1, 2, 3):
                    S.op("dve", lambda e, k=k: e.scalar_tensor_tensor(
                        out=t_x, in0=zb[:, 4 - k:T + 4 - k], scalar=col(C_CONVW + blk * 4 + 3 - k), in1=t_x,
                        op0=ALU.mult, op1=ALU.add),
                        reads=[zh, zm, Uk[7], "cst"], writes=[Uk[7]])
                    yield
                S.op("pool", lambda e: e.tensor_copy(out=halo[:, blk, :], in_=zb[:, T:T + 4]),
                     reads=[zm, zh], writes=[("halo", blk)])
                S.op("dve", lambda e: e.tensor_copy(out=xb_[:], in_=t_x), reads=[Uk[7]], writes=[xk_])
                yield
                br = yield from getbank()
                S.op("pe", lambda e: e.matmul(ps[br][:, :], lhsT=gw_bf[:, 0, blk, :], rhs=xb_[:],
                                              start=True, stop=True),
                     reads=["gw_bf", xk_], writes=[("ps", br)])
                bi = yield from getbank()
                S.op("pe", lambda e: e.matmul(ps[bi][:, :], lhsT=gw_bf[:, 1, blk, :], rhs=xb_[:],
                                              start=True, stop=True),
                     reads=["gw_bf", xk_], writes=[("ps", bi)])
                yield
                S.op("act", lambda e: e.activation(out=t_r, in_=ps[br][:, :], func=AF.Tanh, scale=0.5,
                                                   bias=kcol[:, 32 + blk:33 + blk]),
                     reads=[("ps", br), "kc_hb"], writes=[Uk[0]])
                rel(br)
                yield
                S.op("act", lambda e: e.activation(out=t_i, in_=ps[bi][:, :], func=AF.Tanh, scale=0.5,
                                                   bias=kcol[:, 40 + blk:41 + blk]),
                     reads=[("ps", bi), "kc_hb"], writes=[Uk[3]])
                rel(bi)
                yield
                S.op("act", lambda e: e.activation(out=t_a, in_=t_r, func=AF.Exp, scale=Kh(blk), bias=Kh(blk)),
                     reads=[Uk[0], "kc_K"], writes=[Uk[1]])
                yield
                S.op("dve", lambda e: e.tensor_tensor(out=t_m, in0=t_a, in1=t_a, op=ALU.mult),
                     reads=[Uk[1]], writes=[Uk[2]])
                yield
                S.op("dve", lambda e: e.scalar_tensor_tensor(out=t_b, in0=t_i, scalar=1.0, in1=t_x,
                                                             op0=ALU.add, op1=ALU.mult),
                     reads=[Uk[3], Uk[7]], writes=[Uk[4]])
                yield
                S.op("dve", lambda e: e.tensor_scalar(out=t_m, in0=t_m, scalar1=1.0, scalar2=-1.0,
                                                      op0=ALU.min, op1=ALU.mult),
                     reads=[Uk[2]], writes=[Uk[2]])
                yield
                S.op("act", lambda e: e.activation(out=t_m, in_=t_m, func=AF.Sqrt, bias=1.0),
                     reads=[Uk[2]], writes=[Uk[2]])
                yield
                S.op("dve", lambda e: e.scalar_tensor_tensor(out=t_b, in0=t_b, scalar=0.5, in1=t_m,
                                                             op0=ALU.mult, op1=ALU.mult),
                     reads=[Uk[4], Uk[2]], writes=[Uk[4]])
                yield
                bg = yield from getbank()
                inproj_chunk(*cgrp(19 + 2 * blk), bk=bg)
                yield
                S.op("act", lambda e: e.activation(out=t_s, in_=ps[bg][:, :], func=AF.Tanh, scale=0.5),
                     reads=[("ps", bg)], writes=[Uk[6]])
                yield
                S.op("dve", lambda e: e.tensor_tensor_scan(
                    out=t_h, data0=t_a, data1=t_b, initial=hstate[:, blk:blk + 1], op0=ALU.mult, op1=ALU.add),
                    reads=[Uk[1], Uk[4], ("hstate", blk)], writes=[Uk[5]])
                yield
                S.op("pool", lambda e: e.tensor_copy(out=hstate[:, blk:blk + 1], in_=Ub[:, 5, T - 1:T]),
                     reads=[Uk[5]], writes=[("hstate", blk)])
                S.op("dve", lambda e: e.scalar_tensor_tensor(out=t_s, in0=t_s, scalar=1.0, in1=ps[bg][:, :],
                                                             op0=ALU.add, op1=ALU.mult),
                     reads=[Uk[6], ("ps", bg)], writes=[Uk[6]])
                rel(bg)
                yield
                S.op("dve", lambda e: e.scalar_tensor_tensor(out=yT[:, blk, :], in0=t_h, scalar=0.5, in1=t_s,
                                                             op0=ALU.mult, op1=ALU.mult),
                     reads=[Uk[5], Uk[6]], writes=[("yT", blk)])
                yield

            multi_zipper([([gen_A()], 1), ([gen_rg(blk) for blk in range(8)], 2)])
            if dbg == 5:
                return
            if n_layers > 1 and dbg == 0:
                def hook(b):
                    norm_stats_blk(xb, xk, 1, b)
                    if b >= 1:
                        norm_tr_blk(b - 1)
                out_proj(xb, xk, gb0 + 9, pre, blk_hook=hook)
            else:
                out_proj(xb, xk, gb0 + 9, pre)

        def layer1(t, xb, xk, gb1, pre=None):
            vslots = [wslot(gb1 + vg) for vg in range(4)]
            for b in range(NB):
                banks = []
                for vg in range(4):
                    bk = nbank()
                    banks.append(bk)
                    wv = ring[:, vslots[vg], :].rearrange("p (k n) -> p k n", k=KC)
                    mm_group(ps[bk][:, :],
                             [(hT[:, kc, b * 128:(b + 1) * 128], wv[:, kc, :]) for kc in range(KC)],
                             reads=[("w", vslots[vg]), ("hT", b)], writes=[("ps", bk)])
                    wdone(gb1 + vg)
                    S.op("dve", lambda e, bk=bk, vg=vg: e.bn_stats(out=stats[:, vg, :], in_=ps[bk][:, :]),
                         reads=[("ps", bk)], writes=[("stats", vg)])
                S.op("dve", lambda e: e.bn_aggr(out=mv[:], in_=stats[:].rearrange("p a s -> p (a s)")),
                     reads=[("stats", vg) for vg in range(4)], writes=["mv"])
                S.op("act", lambda e: e.activation(out=lnr[:, 0:1], in_=mv[:, 1:2], func=AF.Sqrt, bias=EPS),
                     reads=["mv"], writes=["lnr"])
                S.op("dve", lambda e: e.reciprocal(out=lnr[:, 1:2], in_=lnr[:, 0:1]), reads=["lnr"],
                     writes=["lnr2"])
                for vg in range(4):
                    S.op("dve", lambda e, vg=vg, b=b, bk=banks[vg]: e.tensor_scalar(
                        out=vhat[:, b, vg * 512:(vg + 1) * 512], in0=ps[bk][:, :], scalar1=mv[:, 0:1],
                        scalar2=lnr[:, 1:2], op0=ALU.subtract, op1=ALU.mult),
                        reads=[("ps", banks[vg]), "mv", "lnr2"], writes=[("U0", 2 * b), ("U0", 2 * b + 1)])
                    rel(banks[vg])
                if b == 0:
                    norm_tr_blk(NB - 1)

            def gen_l1(j):
                g = j // 2
                si = j % 2
                S1, G1, Y1 = sqb[si], rsb[si], expS[si]
                s1k, g1k, y1k = "sqb%d" % si, "rsb%d" % si, ("expS", si)
                gi = gb1 + 4 + j // 2
                bu = yield from getbank()
                inproj_chunk(gi, (j % 2) * 2, bk=bu)
                yield
                bg = yield from getbank()
                inproj_chunk(gi, (j % 2) * 2 + 1, bk=bg)
                yield
                S.op("act", lambda e: e.activation(out=G1[:], in_=ps[bg][:, :], func=AF.Silu),
                     reads=[("ps", bg)], writes=[g1k])
                rel(bg)
                bsg = yield from getbank()
                for b in range(NB):
                    S.op("pe", lambda e, b=b: e.matmul(
                        ps[bsg][:, b * 128:(b + 1) * 128], lhsT=vhat[:, b, j * 128:(j + 1) * 128],
                        rhs=wtril_bf[:, g, :], start=True, stop=True),
                        reads=[("U0", 2 * b), ("U0", 2 * b + 1), "wtril_bf"], writes=[("ps", bsg)])
                yield
                S.op("dve", lambda e: e.scalar_tensor_tensor(
                    out=S1[:].rearrange("p (b t) -> p b t", b=NB),
                    in0=ps[bsg][:, :].rearrange("p (b t) -> p b t", b=NB), scalar=col(C_LNG + j),
                    in1=Cc[:, j:j + 1, :].to_broadcast([P, NB, 128]), op0=ALU.mult, op1=ALU.add),
                    reads=[("ps", bsg), "Cc", "cst"], writes=[s1k])
                rel(bsg)
                yield
                S.op("dve", lambda e: e.tensor_tensor(out=Y1[:], in0=ps[bu][:, :], in1=S1[:], op=ALU.mult),
                     reads=[("ps", bu), s1k], writes=[y1k])
                rel(bu)
                yield
                S.op("dve", lambda e: e.tensor_tensor(out=yT[:, j, :], in0=Y1[:], in1=G1[:], op=ALU.mult),
                     reads=[y1k, g1k], writes=[("yT", j)])
                yield

            zipper([gen_l1(j) for j in range(16)], 2)
            out_proj(xb, xk, gb1 + 12, pre)

        if dbg != 1:
            norm_T(xbuf[0], "x0", 0)
        l1_setup()
        for t in range(n_tiles):
            xb = xbuf[t % NXB]
            xk = "x%d" % (t % NXB)
            nxt = None
            if t + 1 < n_tiles:
                load_x(t + 1)
                if dbg != 1:
                    nxt = (lambda t=t: norm_stats(xbuf[(t + 1) % NXB], "x%d" % ((t + 1) % NXB), 0), norm_tr)
            if dbg != 1:
                if n_layers > 1 and dbg == 0:
                    layer0(t, xb, xk, t * GPT)
                    layer1(t, xb, xk, t * GPT + 13, nxt)
                else:
                    layer0(t, xb, xk, t * GPT, nxt)
            S.dma("sp", "xs%d" % (t % NXB), lambda e, t=t, xb=xb: e.dma_start(
                out=out_d[t * T:(t + 1) * T, :].rearrange("(b p) d -> p b d", p=P), in_=xb[:]),
                reads=[(xk, b) for b in range(NB)])
        S.final_wait("sp", [("D", sname, 16 * c) for sname, c in S.dcnt.items()])

        sems = {}
        for k in S.sem_keys():
            sems[k] = st.enter_context(nc.semaphore("s_%s_%s" % (k[0], k[1])))
        S.emit(nc, sems)
    return nc


def t5_bucket(dist):
    max_exact = 16
    df = np.maximum(dist, 1).astype(np.float32)
    large = max_exact + (np.log(df / np.float32(max_exact)) / np.float32(np.log(128 / 16))
                         * np.float32(16)).astype(np.int32)
    large = np.minimum(large, 31)
    return np.where(dist < max_exact, dist, large)


def prepare_shared(inp, l0_order):
    f32 = np.float32
    w_in_a = np.asarray(inp["w_in_a"], f32)[0]
    w_out_a = np.asarray(inp["w_out_a"], f32)[0]
    w_in_c = np.asarray(inp["w_in_c"], f32)[0]
    w_out_c = np.asarray(inp["w_out_c"], f32)[0]

    def chunk_in(w, cols):
        return w[:, cols].reshape(KC, P, len(cols)).transpose(1, 0, 2)

    o_za, o_ga, o_q, o_k, o_v, o_gb = 0, 1024, 2048, 3072, 3200, 3328
    ar = np.arange
    chunks = [chunk_in(w_in_a, o_v + ar(128)), chunk_in(w_in_a, o_k + ar(128))]
    head_cols = lambda base, c: np.concatenate([base + c * 64 + ar(64), base + (8 + c) * 64 + ar(64)])
    for c in range(8):
        chunks.append(chunk_in(w_in_a, head_cols(o_q, c)))
    for c in range(8):
        chunks.append(chunk_in(w_in_a, head_cols(o_gb, c)))
    for blk in range(8):
        chunks.append(chunk_in(w_in_a, o_za + blk * 128 + ar(128)))
        chunks.append(chunk_in(w_in_a, o_ga + blk * 128 + ar(128)))
    chunks = [chunks[ci] for ci in l0_order]
    while len(chunks) % 4:
        chunks.append(np.zeros_like(chunks[0]))
    groups = []
    for g in range(len(chunks) // 4):
        groups.append(np.stack(chunks[4 * g:4 * g + 4], axis=1).reshape(P, SLOT))

    def out_groups(w, rows_of_chunk):
        gs = []
        for half in range(2):
            for jg in range(2):
                blk = np.stack([w[rows_of_chunk(jg * 8 + jj)][:, half * 512:(half + 1) * 512]
                                for jj in range(8)], axis=1)
                gs.append(blk.reshape(P, SLOT))
        return gs

    def rows_a(j):
        if j < 8:
            return j * 128 + ar(128)
        c = j - 8
        return np.concatenate([1024 + c * 64 + ar(64), 1024 + (8 + c) * 64 + ar(64)])
    groups += out_groups(w_out_a, rows_a)
    for vg in range(4):
        cols = 2048 + vg * 512 + ar(512)
        groups.append(w_in_c[:, cols].reshape(KC, P, 512).transpose(1, 0, 2).reshape(P, SLOT))
    for j in range(0, 16, 2):
        cs = [chunk_in(w_in_c, j * 128 + ar(128)), chunk_in(w_in_c, 4096 + j * 128 + ar(128)),
              chunk_in(w_in_c, (j + 1) * 128 + ar(128)), chunk_in(w_in_c, 4096 + (j + 1) * 128 + ar(128))]
        groups.append(np.stack(cs, axis=1).reshape(P, SLOT))
    groups += out_groups(w_out_c, lambda j: j * 128 + ar(128))
    wst = np.ascontiguousarray(np.stack(groups, axis=0), dtype=f32)
    assert wst.shape == (NG_TILE, P, SLOT), wst.shape

    cst = np.zeros((P, C_TOTAL), f32)
    pcol = lambda v: np.asarray(v, f32).reshape(8, P).T
    cw = np.asarray(inp["conv_w"], f32)[0]
    cst[:, C_CONVW:C_CONVW + 32] = cw.reshape(4, 8, P).transpose(2, 1, 0).reshape(P, 32)
    cst[:, C_CONVB:C_CONVB + 8] = pcol(inp["conv_b"][0])
    cst[:, C_GAB:C_GAB + 8] = pcol(inp["gate_a_b"][0])
    cst[:, C_GXB:C_GXB + 8] = pcol(inp["gate_x_b"][0])
    cst[:, C_LAM:C_LAM + 8] = pcol(inp["lru_lambda"][0])
    cst[:, C_QG] = np.tile(np.asarray(inp["q_norm_g"], f32)[0], 2)
    cst[:, C_KG] = np.tile(np.asarray(inp["k_norm_g"], f32)[0], 2)
    sinks = np.asarray(inp["sinks"], f32)[0]
    cst[:64, C_SINK:C_SINK + 8] = sinks[None, 0:8]
    cst[64:, C_SINK:C_SINK + 8] = sinks[None, 8:16]
    cst[:, C_LNG:C_LNG + 16] = np.asarray(inp["ln_v_g"], f32)[0].reshape(16, P).T
    cst[:, C_LNB:C_LNB + 16] = np.asarray(inp["ln_v_b"], f32)[0].reshape(16, P).T
    ga = np.asarray(inp["gate_a_w"], f32)[0]
    gx = np.asarray(inp["gate_x_w"], f32)[0]
    gw = np.stack([ga.transpose(1, 0, 2), gx.transpose(1, 0, 2)], axis=1)
    csu = np.zeros((P, U_TOTAL), f32)
    csu[:, U_GATEW:U_GATEW + 2048] = gw.reshape(P, 2048)
    s_i = ar(128)[:, None]
    q_i = ar(128)[None, :]
    amask = np.stack([(s_i > q_i), (s_i <= q_i)], axis=1).astype(f32)
    csu[:, U_AMASK:U_AMASK + 256] = amask.reshape(P, 256)
    spw = np.asarray(inp["spatial_w"], f32)[0]
    csu[:, U_SPW:U_SPW + 1024] = spw.transpose(2, 0, 1).reshape(P, 1024)
    csu[:, U_CMASK:U_CMASK + 128] = (s_i <= q_i).astype(f32)
    spb = np.asarray(inp["spatial_b"], f32)[0]
    csu[:, U_SPB:U_SPB + 1024] = np.broadcast_to(spb.reshape(1, 1024), (P, 1024))
    csu[:, U_IDENT:U_IDENT + 128] = np.eye(128, dtype=f32)
    ob = np.zeros((128, 128), f32)
    ob[:64, :64] = 1.0
    ob[64:, 64:] = 1.0
    cst[:, C_ONESB:C_ONESB + 128] = ob

    gain = np.stack([np.broadcast_to(np.asarray(inp["norm_a"], f32)[0][None, :], (P, D)),
                     np.broadcast_to(np.asarray(inp["norm_c"], f32)[0][None, :], (P, D))], axis=0)
    gain = np.ascontiguousarray(gain, dtype=f32)

    rel = np.asarray(inp["rel_bias"], f32)
    kj = ar(256)[None, :]
    qi = ar(128)[:, None]
    dist = 128 + qi - kj
    bucket = t5_bucket(np.maximum(dist, 0))
    bias = rel[bucket]
    biasT = bias.transpose(1, 2, 0).reshape(2, 128, 16, 128).transpose(1, 0, 2, 3)
    biasT = np.ascontiguousarray(biasT.reshape(P, 2 * 16 * 128), dtype=f32)
    return {"wst": wst, "cst": cst, "csu": csu, "gain": gain, "biasT": biasT}


_CACHE = {}


def get_l0_order():
    if "order" not in _CACHE:
        rec = []
        build_program(1, 4, record=rec)
        order = []
        for ci in rec:
            if ci not in order:
                order.append(ci)
        assert sorted(order) == list(range(34)), order
        _CACHE["order"] = order
    return _CACHE["order"]


def kernel(**inputs):
    x = np.asarray(inputs["x"], np.float32)
    bsz, s, d = x.shape
    per = bsz // N_CORES
    order = get_l0_order()
    shared = prepare_shared(inputs, order)
    n_tiles = per * s // T
    key = (n_tiles, s // T)
    if key not in _CACHE:
        _CACHE[key] = build_program(n_tiles, s // T, l0_order=order)
    nc = _CACHE[key]
    in_maps = []
    for c in range(N_CORES):
        m = dict(shared)
        m["x"] = np.ascontiguousarray(x[c * per:(c + 1) * per].reshape(per * s, d))
        in_maps.append(m)
    res = run_bass_kernel_spmd(nc, in_maps, core_ids=list(range(N_CORES)))
    outs = [np.asarray(r["out"], np.float32).reshape(per, s, d) for r in res.results]
    return np.concatenate(outs, axis=0)
```

```python
import numpy as np
from contextlib import ExitStack
import concourse.bass as bass
import concourse.mybir as mybir
from concourse.bass_utils import run_bass_kernel_spmd

F32 = mybir.dt.float32
BF16 = mybir.dt.bfloat16
AF = mybir.ActivationFunctionType
ALU = mybir.AluOpType

P = 128
T = 512
NB = 4
D = 1024
KC = 8
EPS = 1e-6
N_CORES = 8
SEQ = 2048
TPS = SEQ // T
NSLOT = 5
SLOT = 4096
NG_L0_IN = 9
NG_TILE = 29

C_CONVW = 0
C_CONVB = 32
C_GAB = 40
C_GXB = 48
C_LAM = 56
C_QG = 64
C_KG = 65
C_SINK = 66
C_LNG = 74
C_LNB = 90
C_ONESB = 128
C_TOTAL = 256
U_GATEW = 0
U_SPW = 2048
U_SPB = 3072
U_AMASK = 4096
U_CMASK = 4352
U_IDENT = 4480
U_TOTAL = 4608


class Sched:
    ENG = ("pe", "act", "dve", "pool", "sp")

    def __init__(self):
        self.q = {e: [] for e in self.ENG}
        self.cnt = {e: 0 for e in self.ENG}
        self.dcnt = {}
        self.lastw = {}
        self.readers = {}
        self.waited = {e: {} for e in self.ENG}
        self.nwaits = 0

    def _deps(self, eng, reads, writes, is_dma):
        need = {}

        def add(t, kind):
            if t[0] == "E" and t[1] == eng and not is_dma:
                if eng == "pe":
                    return
                if kind != "raw" and t[2] != self.cnt[eng]:
                    return
            sk = (t[0], t[1])
            if need.get(sk, 0) < t[2]:
                need[sk] = t[2]

        for k in reads:
            t = self.lastw.get(k)
            if t is not None:
                add(t, "raw")
        for k in writes:
            t = self.lastw.get(k)
            if t is not None:
                add(t, "waw")
            for sk, v in self.readers.get(k, {}).items():
                add((sk[0], sk[1], v), "war")
        waits = []
        wd = self.waited[eng]
        for sk, v in need.items():
            if wd.get(sk, 0) >= v:
                continue
            wd[sk] = v
            waits.append((sk, v))
        self.nwaits += len(waits)
        return waits

    def _commit(self, tok, reads, writes):
        for k in writes:
            self.lastw[k] = tok
            self.readers[k] = {}
        sk = (tok[0], tok[1])
        for k in reads:
            r = self.readers.setdefault(k, {})
            if r.get(sk, 0) < tok[2]:
                r[sk] = tok[2]

    def op(self, eng, fn, reads=(), writes=()):
        waits = self._deps(eng, reads, writes, False)
        self.cnt[eng] += 1
        tok = ("E", eng, self.cnt[eng])
        self.q[eng].append((waits, fn, ("E", eng), 1))
        self._commit(tok, reads, writes)
        return tok

    def dma(self, qeng, sem, fn, reads=(), writes=()):
        waits = self._deps(qeng, reads, writes, True)
        self.dcnt[sem] = self.dcnt.get(sem, 0) + 1
        tok = ("D", sem, 16 * self.dcnt[sem])
        self.q[qeng].append((waits, fn, ("D", sem), 16))
        self._commit(tok, reads, writes)
        return tok

    def final_wait(self, eng, toks):
        waits = [((t[0], t[1]), t[2]) for t in toks]
        self.q[eng].append((waits, None, None, 0))

    def sem_keys(self):
        keys = [("E", e) for e in self.ENG]
        keys += [("D", s) for s in self.dcnt]
        return keys

    def emit(self, nc, sems):
        with nc.Block() as block:
            decos = {"pe": block.tensor, "act": block.scalar, "dve": block.vector,
                     "pool": block.gpsimd, "sp": block.sync}
            for e in self.ENG:
                items = self.q[e]

                def body(engh, items=items):
                    for waits, fn, inc, amt in items:
                        for sk, v in waits:
                            engh.wait_ge(sems[sk], v)
                        if fn is None:
                            continue
                        ins = fn(engh)
                        ins.then_inc(sems[inc], amt)

                decos[e](body)


def zipper(gens, depth):
    gens = list(gens)
    active = []
    i = 0
    while active or i < len(gens):
        while len(active) < depth and i < len(gens):
            active.append(gens[i])
            i += 1
        for g in list(active):
            try:
                next(g)
            except StopIteration:
                active.remove(g)


def multi_zipper(streams):
    st = [{"gens": list(g), "depth": d, "i": 0, "active": []} for g, d in streams]
    while any(x["active"] or x["i"] < len(x["gens"]) for x in st):
        for x in st:
            while len(x["active"]) < x["depth"] and x["i"] < len(x["gens"]):
                x["active"].append(x["gens"][x["i"]])
                x["i"] += 1
            for g in list(x["active"]):
                try:
                    next(g)
                except StopIteration:
                    x["active"].remove(g)


def build_program(n_tiles, tps=TPS, n_layers=2, dbg=0, l0_order=None, record=None):
    nc = bass.Bass("TRN2", target_bir_lowering=False)
    ntok = n_tiles * T
    x_d = nc.dram_tensor("x", [ntok, D], F32, kind="ExternalInput").ap()
    wst_d = nc.dram_tensor("wst", [NG_TILE, P, SLOT], F32, kind="ExternalInput").ap()
    cst_d = nc.dram_tensor("cst", [P, C_TOTAL], F32, kind="ExternalInput").ap()
    csu_d = nc.dram_tensor("csu", [P, U_TOTAL], F32, kind="ExternalInput").ap()
    gain_d = nc.dram_tensor("gain", [2, P, D], F32, kind="ExternalInput").ap()
    bias_d = nc.dram_tensor("biasT", [P, 2 * 16 * 128], F32, kind="ExternalInput").ap()
    out_d = nc.dram_tensor("out", [ntok, D], F32, kind="ExternalOutput").ap()

    S = Sched()
    st = ExitStack()
    with st:
        def sb(name, shape, dt=F32):
            return st.enter_context(nc.sbuf_tensor("sb_" + name, shape, dt))

        NXB = 2
        xbuf = [sb("xbuf%d" % i, [P, NB, D]) for i in range(NXB)]
        hT = sb("hT", [P, KC, T], BF16)
        ss = sb("ss", [P, 8])
        rs = sb("rs", [P, 8])
        ring = sb("ring", [P, NSLOT, SLOT], BF16)
        cst = sb("cst", [P, C_TOTAL])
        gain = sb("gain", [P, 2, D])
        Et = sb("Et", [P, 2, 16, 128], BF16)
        vz = sb("vz", [P, NB + 1, 2, 128], BF16)
        kz = sb("kz", [P, 2, (NB + 1) * 128], BF16)
        onesz = sb("onesz", [P, 2, 128], BF16)
        qnT = sb("qnT", [P, 8, T], BF16)
        Gs = sb("Gs", [P, 8, T], BF16)
        sqb = [sb("sqb%d" % i, [P, T]) for i in range(2)]
        rsb = [sb("rsb%d" % i, [P, T]) for i in range(2)]
        expS = [sb("expS%d" % i, [P, T]) for i in range(2)]
        PT = [sb("PT%d" % i, [P, T], BF16) for i in range(4)]
        den = [sb("den0", [P, T]), expS[0]]
        yb = [sb("yb0", [P, T]), expS[1]]
        denk = [("den", 0), ("expS", 0)]
        ybk = [("yb", 0), ("expS", 1)]
        junk = den[0][:].bitcast(BF16)
        yT = sb("yT", [P, 16, T], BF16)
        Us = [sb("U%d" % i, [P, 8, T]) for i in range(2)]
        zbuf = [sb("zbuf%d" % i, [P, T + 4]) for i in range(2)]
        xabf = [sb("xabf%d" % i, [P, T], BF16) for i in range(2)]
        halo = sb("halo", [P, 8, 4])
        hstate = sb("hstate", [P, 8])
        Cc = sb("Cc", [P, 16, 128])
        stats = sb("stats", [P, 4, 6])
        mv = sb("mv", [P, 2])
        lnr = sb("lnr", [P, 2])
        kcol = sb("kcol", [P, 64])
        ident_bf = sb("ident_bf", [P, 128], BF16)
        gw_bf = sb("gw_bf", [P, 2, 8, 128], BF16)
        wtril_bf = sb("wtril_bf", [P, 8, 128], BF16)

        NPS = 8
        ps = [st.enter_context(nc.psum_tensor("ps%d" % i, [P, 512], F32)) for i in range(NPS)]
        U = Us[0]
        hb = Gs[:].rearrange("p (b j) t -> p b (j t)", b=NB)
        hbk = lambda b: [("Gs", 2 * b), ("Gs", 2 * b + 1)]
        wtril_f = U[:, 0:2, :].rearrange("p a t -> p (a t)").rearrange("p (g t) -> p g t", g=8)
        vhat = U[:].rearrange("p a t -> p (a t)").bitcast(BF16).rearrange("p (b f) -> p b f", b=NB)
        xflat = xbuf[0][:].rearrange("p b d -> p (b d)")
        usu = U[:, 7, :]
        x0keys = [("x0", b) for b in range(NB)]

        free_banks = list(range(NPS))

        def nbank():
            assert free_banks, "PSUM banks exhausted"
            return free_banks.pop(0)

        def getbank():
            n = 0
            while not free_banks:
                n += 1
                assert n < 10000, "PSUM bank wait deadlock at build time"
                yield
            return free_banks.pop(0)

        def rel(bk):
            assert bk not in free_banks
            free_banks.append(bk)

        def col(c, n=1):
            return cst[:, c:c + n]

        u1flat = Us[1][:].rearrange("p a t -> p (a t)")
        u1keys = [("U1", i) for i in range(8)]
        x1flat = xbuf[1][:].rearrange("p b d -> p (b d)")
        x1keys = [("x1", b) for b in range(NB)]

        def load_x(t):
            xb = xbuf[t % NXB]
            xk = "x%d" % (t % NXB)
            S.dma("sp", "xl%d" % (t % NXB), lambda e: e.dma_start(
                out=xb[:], in_=x_d[t * T:(t + 1) * T, :].rearrange("(b p) d -> p b d", p=P)),
                writes=[(xk, b) for b in range(NB)])

        load_x(0)
        S.dma("sp", "c0", lambda e: e.dma_start(out=cst[:], in_=cst_d), writes=["cst"])
        S.dma("sp", "c1", lambda e: e.dma_start(out=gain[:], in_=gain_d.rearrange("g p d -> p g d")),
              writes=["gain"])
        S.dma("sp", "c4", lambda e: e.dma_start(out=usu, in_=csu_d[:, 4096:U_TOTAL]), writes=[("U0", 7)])
        S.dma("sp", "c3", lambda e: e.dma_start(out=u1flat, in_=csu_d[:, 0:4096]), writes=u1keys)
        S.dma("sp", "c2", lambda e: e.dma_start(out=x1flat, in_=bias_d), writes=x1keys)

        S.op("dve", lambda e: e.tensor_copy(out=ident_bf[:], in_=usu[:, 384:512]),
             reads=[("U0", 7)], writes=["ident_bf"])
        S.op("dve", lambda e: e.memset(onesz[:], 0.0), writes=["onesz"])
        S.op("dve", lambda e: e.memset(onesz[:, 0, 0:64], 1.0), writes=["onesz"])
        S.op("dve", lambda e: e.memset(onesz[:, 1, 64:128], 1.0), writes=["onesz"])
        S.op("dve", lambda e: e.memset(kz[:], 0.0), writes=["kprev", "kcur"])
        S.op("dve", lambda e: e.memset(vz[:], 0.0), writes=["vprev", "vcur"])
        S.op("dve", lambda e: e.tensor_copy(out=gw_bf[:].rearrange("p a n j -> p (a n j)"),
                                            in_=u1flat[:, U_GATEW:U_GATEW + 2048]),
             reads=u1keys, writes=["gw_bf"])
        for kt in range(2):
            def f(e, kt=kt):
                v = x1flat[:, kt * 2048:(kt + 1) * 2048]
                return e.tensor_scalar(out=v, in0=v, scalar1=8.0, scalar2=1.0, op0=ALU.mult, op1=ALU.mult)
            S.op("pool", f, reads=x1keys, writes=x1keys)

            def f(e, kt=kt):
                m = usu[:, kt * 128:(kt + 1) * 128]
                v = x1flat[:, kt * 2048:(kt + 1) * 2048].rearrange("p (h q) -> p h q", h=16)
                return e.tensor_tensor(out=v, in0=v, in1=m.unsqueeze(1).to_broadcast([P, 16, 128]), op=ALU.mult)
            S.op("pool", f, reads=x1keys + [("U0", 7)], writes=x1keys)
        S.op("pool", lambda e: e.tensor_scalar(out=usu[:, 0:256], in0=usu[:, 0:256], scalar1=-1.0, scalar2=480.0,
                                               op0=ALU.add, op1=ALU.mult),
             reads=[("U0", 7)], writes=[("U0", 7)])
        for kt in range(2):
            def f(e, kt=kt):
                m = usu[:, kt * 128:(kt + 1) * 128]
                v = x1flat[:, kt * 2048:(kt + 1) * 2048].rearrange("p (h q) -> p h q", h=16)
                return e.tensor_tensor(out=Et[:, kt], in0=v, in1=m.unsqueeze(1).to_broadcast([P, 16, 128]),
                                       op=ALU.add)
            S.op("pool", f, reads=x1keys + [("U0", 7)], writes=["Et"])
        S.op("act", lambda e: e.activation(out=kcol[:, 16:24], in_=col(C_SINK, 8), func=AF.Exp),
             reads=["cst"], writes=["kc_sink"])
        S.op("dve", lambda e: e.tensor_scalar(out=kcol[:, 32:40], in0=col(C_GAB, 8), scalar1=0.5, scalar2=None,
                                              op0=ALU.mult), reads=["cst"], writes=["kc_hb"])
        S.op("dve", lambda e: e.tensor_scalar(out=kcol[:, 40:48], in0=col(C_GXB, 8), scalar1=0.5, scalar2=None,
                                              op0=ALU.mult), reads=["cst"], writes=["kc_hb"])
        S.op("act", lambda e: e.activation(out=kcol[:, 24:32], in_=col(C_LAM, 8), func=AF.Exp, scale=-1.0),
             reads=["cst"], writes=["kc_e"])
        S.op("dve", lambda e: e.tensor_scalar(out=kcol[:, 48:56], in0=kcol[:, 24:32], scalar1=2.0,
                                              scalar2=None, op0=ALU.add),
             reads=["kc_e"], writes=["kc_t"])
        S.op("dve", lambda e: e.reciprocal(out=kcol[:, 48:56], in_=kcol[:, 48:56]),
             reads=["kc_t"], writes=["kc_t"])
        S.op("dve", lambda e: e.tensor_tensor(out=kcol[:, 24:32], in0=kcol[:, 24:32], in1=kcol[:, 48:56],
                                              op=ALU.mult),
             reads=["kc_t", "kc_e"], writes=["kc_e"])
        S.op("dve", lambda e: e.tensor_tensor(out=kcol[:, 48:56], in0=kcol[:, 24:32], in1=kcol[:, 24:32],
                                              op=ALU.mult),
             reads=["kc_e"], writes=["kc_t"])
        S.op("dve", lambda e: e.tensor_scalar(out=kcol[:, 0:8], in0=kcol[:, 48:56], scalar1=1.0 / 9,
                                              scalar2=1.0 / 7, op0=ALU.mult, op1=ALU.add),
             reads=["kc_t"], writes=["kc_p"])
        for cc in (1.0 / 5, 1.0 / 3, 1.0):
            S.op("dve", lambda e: e.tensor_tensor(out=kcol[:, 0:8], in0=kcol[:, 0:8], in1=kcol[:, 48:56],
                                                  op=ALU.mult),
                 reads=["kc_p", "kc_t"], writes=["kc_p"])
            S.op("dve", lambda e, cc=cc: e.tensor_scalar(out=kcol[:, 0:8], in0=kcol[:, 0:8], scalar1=cc,
                                                         scalar2=None, op0=ALU.add),
                 reads=["kc_p"], writes=["kc_p"])
        S.op("dve", lambda e: e.tensor_tensor(out=kcol[:, 0:8], in0=kcol[:, 0:8], in1=kcol[:, 24:32],
                                              op=ALU.mult),
             reads=["kc_p", "kc_e"], writes=["kc_p"])
        S.op("dve", lambda e: e.tensor_scalar(out=kcol[:, 8:16], in0=kcol[:, 0:8], scalar1=-16.0,
                                              scalar2=None, op0=ALU.mult),
             reads=["kc_p"], writes=["kc_K"])
        S.op("dve", lambda e: e.tensor_scalar(out=kcol[:, 0:8], in0=kcol[:, 0:8], scalar1=-8.0,
                                              scalar2=None, op0=ALU.mult),
             reads=["kc_p", "kc_K"], writes=["kc_K"])
        Kh = lambda blk: kcol[:, blk:blk + 1]
        Kf = lambda blk: kcol[:, 8 + blk:9 + blk]

        def l1_setup():
            if n_layers > 1:
                def f(e):
                    m = usu[:, 256:384]
                    return e.tensor_tensor(out=wtril_f,
                                           in0=u1flat[:, U_SPW:U_SPW + 1024].rearrange("p (g t) -> p g t", g=8),
                                           in1=m.unsqueeze(1).to_broadcast([P, 8, 128]), op=ALU.mult)
                S.op("dve", f, reads=u1keys + [("U0", 7)], writes=[("U0", 0), ("U0", 1)])
                S.op("dve", lambda e: e.tensor_copy(out=wtril_bf[:], in_=wtril_f),
                     reads=[("U0", 0), ("U0", 1)], writes=["wtril_bf"])
                S.op("dve", lambda e: e.memset(sqb[0][:, 0:128], 1.0), writes=["sqb0"])
                for g in range(8):
                    bk = nbank()

                    def f(e, g=g, bk=bk):
                        return e.matmul(ps[bk][:, 0:128], lhsT=sqb[0][:, 0:128], rhs=wtril_f[:, g, :],
                                        start=True, stop=True)
                    S.op("pe", f, reads=["sqb0", ("U0", 0), ("U0", 1)], writes=[("ps", bk)])
                    for jj in range(2):
                        j = 2 * g + jj

                        def f2(e, g=g, bk=bk, j=j):
                            return e.scalar_tensor_tensor(
                                out=Cc[:, j, :], in0=ps[bk][:, 0:128], scalar=col(C_LNB + j),
                                in1=u1flat[:, U_SPB + g * 128:U_SPB + (g + 1) * 128], op0=ALU.mult, op1=ALU.add)
                        S.op("dve", f2, reads=[("ps", bk), "cst"] + u1keys, writes=["Cc"])
                    rel(bk)


        GPT = NG_TILE if n_layers > 1 else 13
        rem_tile = [4] * 8 + [2] + [4] * 4 + ([4] * 16 if n_layers > 1 else [])
        seq_groups = [g for t in range(n_tiles) for g in range(GPT)]
        rem = [r for t in range(n_tiles) for r in rem_tile]
        rstate = {"next_load": 0, "oldest": 0}

        def _load():
            i = rstate["next_load"]
            rstate["next_load"] += 1
            if i >= len(seq_groups):
                return
            slot = i % NSLOT
            g = seq_groups[i]
            S.dma("pool", "w%d" % slot,
                  lambda e: e.dma_start(out=ring[:, slot, :], in_=wst_d[g], max_dma_last_dim=8192),
                  writes=[("w", slot)])

        for i in range(NSLOT):
            _load()

        def wslot(i):
            if record is not None:
                return i % NSLOT
            assert rstate["oldest"] <= i < rstate["next_load"], (i, rstate)
            return i % NSLOT

        def wdone(i, n=1):
            if record is not None:
                return
            rem[i] -= n
            assert rem[i] >= 0
            while rstate["oldest"] < len(rem) and rem[rstate["oldest"]] == 0:
                rstate["oldest"] += 1
                _load()

        def mm_group(out_ap, pairs, reads, writes):
            def f(e):
                ins = None
                n = len(pairs)
                for i, (l, r) in enumerate(pairs):
                    ins = e.matmul(out_ap, lhsT=l, rhs=r, start=(i == 0), stop=(i == n - 1))
                return ins
            S.op("pe", f, reads=reads, writes=writes)

        hT_keys = [("hT", b) for b in range(NB)]

        def norm_stats_blk(xb, xk, gi, b):
            S.op("act", lambda e: e.activation(out=junk, in_=xb[:, b, :], func=AF.Square,
                                               accum_out=ss[:, b:b + 1]),
                 reads=[(xk, b)], writes=[("ss", b), ("den", 0)])
            S.op("act", lambda e: e.activation(out=rs[:, b:b + 1], in_=ss[:, b:b + 1], func=AF.Sqrt,
                                               scale=1.0 / D, bias=EPS),
                 reads=[("ss", b)], writes=[("rs", b)])
            S.op("dve", lambda e: e.reciprocal(out=rs[:, b:b + 1], in_=rs[:, b:b + 1]),
                 reads=[("rs", b)], writes=[("rs", b)])
            for hf in range(2):
                S.op("dve", lambda e, hf=hf: e.scalar_tensor_tensor(
                    out=hb[:, b, hf * 512:(hf + 1) * 512], in0=xb[:, b, hf * 512:(hf + 1) * 512],
                    scalar=rs[:, b:b + 1], in1=gain[:, gi, hf * 512:(hf + 1) * 512],
                    op0=ALU.mult, op1=ALU.mult),
                    reads=[(xk, b), ("rs", b), "gain"], writes=[("Gs", 2 * b + hf)])

        def norm_stats(xb, xk, gi):
            for b in range(NB):
                norm_stats_blk(xb, xk, gi, b)

        def norm_tr_blk(b):
            bkt = nbank()
            ptb = ps[bkt][:, :].bitcast(BF16)

            def f(e):
                ins = None
                for kc in range(KC):
                    ins = e.transpose(out=ptb[:, kc * 128:(kc + 1) * 128],
                                      in_=hb[:, b, kc * 128:(kc + 1) * 128], identity=ident_bf[:])
                return ins
            S.op("pe", f, reads=[("Gs", 2 * b), ("Gs", 2 * b + 1), "ident_bf"], writes=[("ps", bkt)])
            S.op("act", lambda e: e.copy(out=hT[:, :, b * 128:(b + 1) * 128],
                                         in_=ptb.rearrange("p (k t) -> p k t", k=KC)),
                 reads=[("ps", bkt)], writes=[("hT", b)])
            rel(bkt)

        def norm_tr():
            for b in range(NB):
                norm_tr_blk(b)

        def norm_T(xb, xk, gi):
            norm_stats(xb, xk, gi)
            norm_tr()

        def inproj_chunk(gi, pos, bk=None):
            slot = wslot(gi)
            if bk is None:
                bk = nbank()
            wv = ring[:, slot, :].rearrange("p (c k n) -> p c k n", c=4, k=KC)
            mm_group(ps[bk][:, :], [(wv[:, pos, kc, :], hT[:, kc, :]) for kc in range(KC)],
                     reads=[("w", slot)] + hT_keys, writes=[("ps", bk)])
            wdone(gi)
            return bk

        def out_proj(xb, xk, gbase, pre=None, blk_hook=None):
            if pre is not None:
                pre[0]()

            def mm(b, jg, gi, bank):
                slot = wslot(gi)
                wv = ring[:, slot, :].rearrange("p (j n) -> p j n", j=8)

                def f(e):
                    ins = None
                    for jj in range(8):
                        j = jg * 8 + jj
                        ins = e.matmul(ps[bank][:, :], lhsT=yT[:, j, b * 128:(b + 1) * 128],
                                       rhs=wv[:, jj, :], start=(j == 0), stop=(j == 15))
                    return ins
                S.op("pe", f, reads=[("w", slot)] + [("yT", jg * 8 + jj) for jj in range(8)],
                     writes=[("ps", bank)])
                wdone(gi)

            def add(b, half, bank):
                S.op("dve", lambda e: e.tensor_tensor(
                    out=xb[:, b, half * 512:(half + 1) * 512], in0=ps[bank][:, :],
                    in1=xb[:, b, half * 512:(half + 1) * 512], op=ALU.add),
                    reads=[("ps", bank), (xk, b)], writes=[(xk, b)])
                rel(bank)

            banks = [nbank() for _ in range(NB)]
            for jg in range(2):
                for b in range(NB):
                    mm(b, jg, gbase + jg, banks[b])
            for b in range(NB):
                add(b, 0, banks[b])
            if pre is not None:
                pre[1]()
            if blk_hook is None:
                banks = [nbank() for _ in range(NB)]
                for jg in range(2):
                    for b in range(NB):
                        mm(b, jg, gbase + 2 + jg, banks[b])
                for b in range(NB):
                    add(b, 1, banks[b])
            else:
                for b in range(NB):
                    bank = nbank()
                    for jg in range(2):
                        mm(b, jg, gbase + 2 + jg, bank)
                    add(b, 1, bank)
                    blk_hook(b)

        def layer0(t, xb, xk, gb0, pre=None):
            first = (t % tps == 0)
            if first:
                S.op("pool", lambda e: e.memset(halo[:], 0.0), writes=[("halo", i) for i in range(8)])
                S.op("pool", lambda e: e.memset(hstate[:], 0.0), writes=[("hstate", i) for i in range(8)])
            def cgrp(ci):
                if record is not None:
                    record.append(ci)
                    return (gb0, 0)
                pos = l0_order.index(ci)
                return (gb0 + pos // 4, pos % 4)

            gi, pos = cgrp(0)
            slot = wslot(gi)
            wv = ring[:, slot, :].rearrange("p (c k n) -> p c k n", c=4, k=KC)
            bkv = nbank()
            for b in range(NB):
                mm_group(ps[bkv][:, b * 128:(b + 1) * 128],
                         [(hT[:, kc, b * 128:(b + 1) * 128], wv[:, 0, kc, :]) for kc in range(KC)],
                         reads=[("w", slot), ("hT", b)], writes=[("ps", bkv)])
            wdone(gi)
            for kv in range(2):
                S.op("act", lambda e, kv=kv: e.copy(
                    out=vz[:, 1:NB + 1, kv, kv * 64:(kv + 1) * 64],
                    in_=ps[bkv][:, :].rearrange("p (b n) -> p b n", b=NB)[:, :, kv * 64:(kv + 1) * 64]),
                    reads=[("ps", bkv)], writes=["vcur"])
            rel(bkv)

            def gen_qk(ci, gcol, out_ap, outkey, idx):
                sq, rr = sqb[idx % 2], rsb[idx % 2]
                sqk, rrk = "sqb%d" % (idx % 2), "rsb%d" % (idx % 2)
                bk = yield from getbank()
                inproj_chunk(*cgrp(ci), bk=bk)
                yield
                S.op("act", lambda e: e.activation(out=sq[:], in_=ps[bk][:, :], func=AF.Square),
                     reads=[("ps", bk)], writes=[sqk])
                yield
                b2 = yield from getbank()
                S.op("pe", lambda e: e.matmul(ps[b2][:, :], lhsT=col(C_ONESB, 128), rhs=sq[:],
                                              start=True, stop=True),
                     reads=[sqk, "cst"], writes=[("ps", b2)])
                yield
                S.op("act", lambda e: e.activation(out=rr[:], in_=ps[b2][:, :], func=AF.Ln, scale=1.0 / 64,
                                                   bias=EPS),
                     reads=[("ps", b2)], writes=[rrk])
                rel(b2)
                yield
                S.op("act", lambda e: e.activation(out=rr[:], in_=rr[:], func=AF.Exp, scale=-0.5),
                     reads=[rrk], writes=[rrk])
                yield
                if out_ap is None:
                    for kv in range(2):
                        pr = slice(kv * 64, (kv + 1) * 64)
                        S.op("dve", lambda e, kv=kv, pr=pr: e.scalar_tensor_tensor(
                            out=kz[pr, kv, 128:], in0=ps[bk][pr, :], scalar=cst[pr, gcol:gcol + 1],
                            in1=rr[pr, :], op0=ALU.mult, op1=ALU.mult),
                            reads=[("ps", bk), rrk, "cst"], writes=[outkey])
                else:
                    S.op("dve", lambda e: e.scalar_tensor_tensor(out=out_ap, in0=ps[bk][:, :], scalar=col(gcol),
                                                                 in1=rr[:], op0=ALU.mult, op1=ALU.mult),
                         reads=[("ps", bk), rrk, "cst"], writes=[outkey])
                rel(bk)
                yield

            def gen_gb(c):
                bk = yield from getbank()
                inproj_chunk(*cgrp(10 + c), bk=bk)
                yield
                S.op("act", lambda e: e.activation(out=Gs[:, c, :], in_=ps[bk][:, :], func=AF.Silu),
                     reads=[("ps", bk)], writes=[("Gs", c)])
                rel(bk)
                yield

            gens = [gen_qk(1, C_KG, None, "kcur", 0)]
            gens += [gen_qk(2 + c, C_QG, qnT[:, c, :], ("qn", c), 1 + c) for c in range(8)]
            gens += [gen_gb(c) for c in range(8)]
            b_gens = gens

            def gen_att_kv(b, quad, kv, bo, bd, stt, u):
                pi_ = (u % 2) * 2 + kv
                pT = PT[pi_]
                pk = ("PT", pi_)
                kts = [1] if (first and b == 0) else [0, 1]
                for kt in kts:
                    keyblk = b + kt
                    bs_ = yield from getbank()
                    kkey = "kprev" if keyblk == 0 else "kcur"
                    vkey = "vprev" if keyblk == 0 else "vcur"
                    h0 = kv * 8 + quad * 4

                    def fS(e, keyblk=keyblk, bs_=bs_, kt=kt, h0=h0):
                        o = ps[bs_][:, :].rearrange("p (c q) -> p c q", c=4)
                        e.matmul(o, lhsT=kz[:, kv, keyblk * 128:(keyblk + 1) * 128],
                                 rhs=qnT[:, quad * 4:(quad + 1) * 4, b * 128:(b + 1) * 128],
                                 start=True, stop=False)
                        return e.matmul(o, lhsT=ident_bf[:], rhs=Et[:, kt, h0:h0 + 4, :], start=False, stop=True)
                    S.op("pe", fS, reads=[kkey, "Et", "ident_bf"] + [("qn", quad * 4 + cc) for cc in range(4)],
                         writes=[("ps", bs_)])
                    yield
                    S.op("act", lambda e, bs_=bs_: e.activation(
                        out=pT[:], in_=ps[bs_][:, :], func=AF.Exp, scale=0.125),
                        reads=[("ps", bs_)], writes=[pk])
                    rel(bs_)
                    yield
                    st_ = (stt["n"] == 0)
                    sp_ = (stt["n"] == stt["total"] - 1)
                    stt["n"] += 1
                    S.op("pe", lambda e, keyblk=keyblk, st_=st_, sp_=sp_: e.matmul(
                        ps[bo][:, :], lhsT=vz[:, keyblk, kv, :], rhs=pT[:], start=st_, stop=sp_),
                        reads=[vkey, pk], writes=[("ps", bo)])
                    S.op("pe", lambda e, st_=st_, sp_=sp_: e.matmul(
                        ps[bd][:, :], lhsT=onesz[:, kv, :], rhs=pT[:], start=st_, stop=sp_),
                        reads=["onesz", pk], writes=[("ps", bd)])
                    yield

            def gen_att(b, quad, u):
                bo = yield from getbank()
                bd = yield from getbank()
                dn, y_ = den[u % 2], yb[u % 2]
                dk, yk = denk[u % 2], ybk[u % 2]
                nk = 1 if (first and b == 0) else 2
                stt = {"n": 0, "total": 2 * nk}
                subs = [gen_att_kv(b, quad, kv, bo, bd, stt, u) for kv in range(2)]
                while subs:
                    for g in list(subs):
                        try:
                            next(g)
                        except StopIteration:
                            subs.remove(g)
                    yield
                S.op("dve", lambda e: e.tensor_tensor(
                    out=dn[:].rearrange("p (c q) -> p c q", c=4),
                    in0=ps[bd][:, :].rearrange("p (c q) -> p c q", c=4),
                    in1=kcol[:, 16 + quad * 4:16 + (quad + 1) * 4].unsqueeze(2).to_broadcast([P, 4, 128]),
                    op=ALU.add),
                    reads=[("ps", bd), "kc_sink"], writes=[dk])
                rel(bd)
                yield
                S.op("act", lambda e: e.activation(out=dn[:], in_=dn[:], func=AF.Ln), reads=[dk], writes=[dk])
                yield
                S.op("act", lambda e: e.activation(out=dn[:], in_=dn[:], func=AF.Exp, scale=-1.0),
                     reads=[dk], writes=[dk])
                yield
                S.op("dve", lambda e: e.tensor_tensor(out=y_[:], in0=ps[bo][:, :], in1=dn[:], op=ALU.mult),
                     reads=[("ps", bo), dk], writes=[yk])
                rel(bo)
                yield
                S.op("dve", lambda e: e.tensor_tensor(
                    out=yT[:, 8 + quad * 4:8 + (quad + 1) * 4, b * 128:(b + 1) * 128],
                    in0=y_[:].rearrange("p (c q) -> p c q", c=4),
                    in1=Gs[:, quad * 4:(quad + 1) * 4, b * 128:(b + 1) * 128], op=ALU.mult),
                    reads=[yk] + [("Gs", quad * 4 + cc) for cc in range(4)],
                    writes=[("yT", 8 + quad * 4 + cc) for cc in range(4)])
                yield

            def gen_att_tail():
                S.op("pool", lambda e: e.tensor_copy(out=kz[:, :, 0:128], in_=kz[:, :, NB * 128:(NB + 1) * 128]),
                     reads=["kcur"], writes=["kprev"])
                S.op("pool", lambda e: e.tensor_copy(out=vz[:, 0, :, :], in_=vz[:, NB, :, :]),
                     reads=["vcur"], writes=["vprev"])
                yield

            units = [(b, quad) for b in range(NB) for quad in range(2)]
            att_gens = [gen_att(b, quad, u) for u, (b, quad) in enumerate(units)] + [gen_att_tail()]
            def gen_A():
                active = []
                i = 0
                while active or i < len(b_gens):
                    while len(active) < 2 and i < len(b_gens):
                        active.append(b_gens[i])
                        i += 1
                    for g in list(active):
                        try:
                            next(g)
                        except StopIteration:
                            active.remove(g)
                    yield
                active = []
                i = 0
                while active or i < len(att_gens):
                    while len(active) < 2 and i < len(att_gens):
                        active.append(att_gens[i])
                        i += 1
                    for g in list(active):
                        try:
                            next(g)
                        except StopIteration:
                            active.remove(g)
                    yield

            def gen_rg(blk):
                si = blk % 2
                Ub = Us[si]
                zb, xb_ = zbuf[si], xabf[si]
                t_r, t_a, t_m, t_i, t_b, t_h, t_s, t_x = [Ub[:, i, :] for i in range(8)]
                Uk = [("U%d" % si, i) for i in range(8)]
                zh, zm, xk_ = ("zb_h", si), ("zb_m", si), ("xabf", si)
                bz = yield from getbank()
                inproj_chunk(*cgrp(18 + 2 * blk), bk=bz)
                S.op("pool", lambda e: e.tensor_copy(out=zb[:, 0:4], in_=halo[:, blk, :]),
                     reads=[("halo", blk)], writes=[zh])
                yield
                S.op("act", lambda e: e.copy(out=zb[:, 4:T + 4], in_=ps[bz][:, :]),
                     reads=[("ps", bz)], writes=[zm])
                S.op("act", lambda e: e.activation(
                    out=t_x, in_=ps[bz][:, :], func=AF.Identity, scale=col(C_CONVW + blk * 4 + 3),
                    bias=col(C_CONVB + blk)),
                    reads=[("ps", bz), "cst"], writes=[Uk[7]])
                rel(bz)
                yield
                for k in (1, 2, 3):
                    S.op("dve", lambda e, k=k: e.scalar_tensor_tensor(
                        out=t_x, in0=zb[:, 4 - k:T + 4 - k], scalar=col(C_CONVW + blk * 4 + 3 - k), in1=t_x,
                        op0=ALU.mult, op1=ALU.add),
                        reads=[zh, zm, Uk[7], "cst"], writes=[Uk[7]])
                    yield
                S.op("pool", lambda e: e.tensor_copy(out=halo[:, blk, :], in_=zb[:, T:T + 4]),
                     reads=[zm, zh], writes=[("halo", blk)])
                S.op("dve", lambda e: e.tensor_copy(out=xb_[:], in_=t_x), reads=[Uk[7]], writes=[xk_])
                yield
                br = yield from getbank()
                S.op("pe", lambda e: e.matmul(ps[br][:, :], lhsT=gw_bf[:, 0, blk, :], rhs=xb_[:],
                                              start=True, stop=True),
                     reads=["gw_bf", xk_], writes=[("ps", br)])
                bi = yield from getbank()
                S.op("pe", lambda e: e.matmul(ps[bi][:, :], lhsT=gw_bf[:, 1, blk, :], rhs=xb_[:],
                                              start=True, stop=True),
                     reads=["gw_bf", xk_], writes=[("ps", bi)])
                yield
                S.op("act", lambda e: e.activation(out=t_r, in_=ps[br][:, :], func=AF.Tanh, scale=0.5,
                                                   bias=kcol[:, 32 + blk:33 + blk]),
                     reads=[("ps", br), "kc_hb"], writes=[Uk[0]])
                rel(br)
                yield
                S.op("act", lambda e: e.activation(out=t_i, in_=ps[bi][:, :], func=AF.Tanh, scale=0.5,
                                                   bias=kcol[:, 40 + blk:41 + blk]),
                     reads=[("ps", bi), "kc_hb"], writes=[Uk[3]])
                rel(bi)
                yield
                S.op("act", lambda e: e.activation(out=t_a, in_=t_r, func=AF.Exp, scale=Kh(blk), bias=Kh(blk)),
                     reads=[Uk[0], "kc_K"], writes=[Uk[1]])
                yield
                S.op("dve", lambda e: e.tensor_tensor(out=t_m, in0=t_a, in1=t_a, op=ALU.mult),
                     reads=[Uk[1]], writes=[Uk[2]])
                yield
                S.op("dve", lambda e: e.scalar_tensor_tensor(out=t_b, in0=t_i, scalar=1.0, in1=t_x,
                                                             op0=ALU.add, op1=ALU.mult),
                     reads=[Uk[3], Uk[7]], writes=[Uk[4]])
                yield
                S.op("dve", lambda e: e.tensor_scalar(out=t_m, in0=t_m, scalar1=1.0, scalar2=-1.0,
                                                      op0=ALU.min, op1=ALU.mult),
                     reads=[Uk[2]], writes=[Uk[2]])
                yield
                S.op("act", lambda e: e.activation(out=t_m, in_=t_m, func=AF.Sqrt, bias=1.0),
                     reads=[Uk[2]], writes=[Uk[2]])
                yield
                S.op("dve", lambda e: e.scalar_tensor_tensor(out=t_b, in0=t_b, scalar=0.5, in1=t_m,
                                                             op0=ALU.mult, op1=ALU.mult),
                     reads=[Uk[4], Uk[2]], writes=[Uk[4]])
                yield
                bg = yield from getbank()
                inproj_chunk(*cgrp(19 + 2 * blk), bk=bg)
                yield
                S.op("act", lambda e: e.activation(out=t_s, in_=ps[bg][:, :], func=AF.Tanh, scale=0.5),
                     reads=[("ps", bg)], writes=[Uk[6]])
                yield
                S.op("dve", lambda e: e.tensor_tensor_scan(
                    out=t_h, data0=t_a, data1=t_b, initial=hstate[:, blk:blk + 1], op0=ALU.mult, op1=ALU.add),
                    reads=[Uk[1], Uk[4], ("hstate", blk)], writes=[Uk[5]])
                yield
                S.op("pool", lambda e: e.tensor_copy(out=hstate[:, blk:blk + 1], in_=Ub[:, 5, T - 1:T]),
                     reads=[Uk[5]], writes=[("hstate", blk)])
                S.op("dve", lambda e: e.scalar_tensor_tensor(out=t_s, in0=t_s, scalar=1.0, in1=ps[bg][:, :],
                                                             op0=ALU.add, op1=ALU.mult),
                     reads=[Uk[6], ("ps", bg)], writes=[Uk[6]])
                rel(bg)
                yield
                S.op("dve", lambda e: e.scalar_tensor_tensor(out=yT[:, blk, :], in0=t_h, scalar=0.5, in1=t_s,
                                                             op0=ALU.mult, op1=ALU.mult),
                     reads=[Uk[5], Uk[6]], writes=[("yT", blk)])
                yield

            multi_zipper([([gen_rg(blk) for blk in range(8)], 2), ([gen_A()], 1)])
            if dbg == 5:
                return
            if n_layers > 1 and dbg == 0:
                def hook(b):
                    norm_stats_blk(xb, xk, 1, b)
                    if b >= 1:
                        norm_tr_blk(b - 1)
                out_proj(xb, xk, gb0 + 9, pre, blk_hook=hook)
            else:
                out_proj(xb, xk, gb0 + 9, pre)

        def layer1(t, xb, xk, gb1, pre=None):
            vslots = [wslot(gb1 + vg) for vg in range(4)]
            for b in range(NB):
                banks = []
                for vg in range(4):
                    bk = nbank()
                    banks.append(bk)
                    wv = ring[:, vslots[vg], :].rearrange("p (k n) -> p k n", k=KC)
                    mm_group(ps[bk][:, :],
                             [(hT[:, kc, b * 128:(b + 1) * 128], wv[:, kc, :]) for kc in range(KC)],
                             reads=[("w", vslots[vg]), ("hT", b)], writes=[("ps", bk)])
                    wdone(gb1 + vg)
                    S.op("dve", lambda e, bk=bk, vg=vg: e.bn_stats(out=stats[:, vg, :], in_=ps[bk][:, :]),
                         reads=[("ps", bk)], writes=[("stats", vg)])
                S.op("dve", lambda e: e.bn_aggr(out=mv[:], in_=stats[:].rearrange("p a s -> p (a s)")),
                     reads=[("stats", vg) for vg in range(4)], writes=["mv"])
                S.op("act", lambda e: e.activation(out=lnr[:, 0:1], in_=mv[:, 1:2], func=AF.Sqrt, bias=EPS),
                     reads=["mv"], writes=["lnr"])
                S.op("dve", lambda e: e.reciprocal(out=lnr[:, 1:2], in_=lnr[:, 0:1]), reads=["lnr"],
                     writes=["lnr2"])
                for vg in range(4):
                    S.op("dve", lambda e, vg=vg, b=b, bk=banks[vg]: e.tensor_scalar(
                        out=vhat[:, b, vg * 512:(vg + 1) * 512], in0=ps[bk][:, :], scalar1=mv[:, 0:1],
                        scalar2=lnr[:, 1:2], op0=ALU.subtract, op1=ALU.mult),
                        reads=[("ps", banks[vg]), "mv", "lnr2"], writes=[("U0", 2 * b), ("U0", 2 * b + 1)])
                    rel(banks[vg])
                if b == 0:
                    norm_tr_blk(NB - 1)

            def gen_l1(j):
                g = j // 2
                si = j % 2
                S1, G1, Y1 = sqb[si], rsb[si], expS[si]
                s1k, g1k, y1k = "sqb%d" % si, "rsb%d" % si, ("expS", si)
                gi = gb1 + 4 + j // 2
                bu = yield from getbank()
                inproj_chunk(gi, (j % 2) * 2, bk=bu)
                yield
                bg = yield from getbank()
                inproj_chunk(gi, (j % 2) * 2 + 1, bk=bg)
                yield
                S.op("act", lambda e: e.activation(out=G1[:], in_=ps[bg][:, :], func=AF.Silu),
                     reads=[("ps", bg)], writes=[g1k])
                rel(bg)
                bsg = yield from getbank()
                for b in range(NB):
                    S.op("pe", lambda e, b=b: e.matmul(
                        ps[bsg][:, b * 128:(b + 1) * 128], lhsT=vhat[:, b, j * 128:(j + 1) * 128],
                        rhs=wtril_bf[:, g, :], start=True, stop=True),
                        reads=[("U0", 2 * b), ("U0", 2 * b + 1), "wtril_bf"], writes=[("ps", bsg)])
                yield
                S.op("dve", lambda e: e.scalar_tensor_tensor(
                    out=S1[:].rearrange("p (b t) -> p b t", b=NB),
                    in0=ps[bsg][:, :].rearrange("p (b t) -> p b t", b=NB), scalar=col(C_LNG + j),
                    in1=Cc[:, j:j + 1, :].to_broadcast([P, NB, 128]), op0=ALU.mult, op1=ALU.add),
                    reads=[("ps", bsg), "Cc", "cst"], writes=[s1k])
                rel(bsg)
                yield
                S.op("dve", lambda e: e.tensor_tensor(out=Y1[:], in0=ps[bu][:, :], in1=S1[:], op=ALU.mult),
                     reads=[("ps", bu), s1k], writes=[y1k])
                rel(bu)
                yield
                S.op("dve", lambda e: e.tensor_tensor(out=yT[:, j, :], in0=Y1[:], in1=G1[:], op=ALU.mult),
                     reads=[y1k, g1k], writes=[("yT", j)])
                yield

            zipper([gen_l1(j) for j in range(16)], 2)
            out_proj(xb, xk, gb1 + 12, pre)

        if dbg != 1:
            norm_T(xbuf[0], "x0", 0)
        l1_setup()
        for t in range(n_tiles):
            xb = xbuf[t % NXB]
            xk = "x%d" % (t % NXB)
            nxt = None
            if t + 1 < n_tiles:
                load_x(t + 1)
                if dbg != 1:
                    nxt = (lambda t=t: norm_stats(xbuf[(t + 1) % NXB], "x%d" % ((t + 1) % NXB), 0), norm_tr)
            if dbg != 1:
                if n_layers > 1 and dbg == 0:
                    layer0(t, xb, xk, t * GPT)
                    layer1(t, xb, xk, t * GPT + 13, nxt)
                else:
                    layer0(t, xb, xk, t * GPT, nxt)
            S.dma("sp", "xs%d" % (t % NXB), lambda e, t=t, xb=xb: e.dma_start(
                out=out_d[t * T:(t + 1) * T, :].rearrange("(b p) d -> p b d", p=P), in_=xb[:]),
                reads=[(xk, b) for b in range(NB)])
        S.final_wait("sp", [("D", sname, 16 * c) for sname, c in S.dcnt.items()])

        sems = {}
        for k in S.sem_keys():
            sems[k] = st.enter_context(nc.semaphore("s_%s_%s" % (k[0], k[1])))
        S.emit(nc, sems)
    return nc


def t5_bucket(dist):
    max_exact = 16
    df = np.maximum(dist, 1).astype(np.float32)
    large = max_exact + (np.log(df / np.float32(max_exact)) / np.float32(np.log(128 / 16))
                         * np.float32(16)).astype(np.int32)
    large = np.minimum(large, 31)
    return np.where(dist < max_exact, dist, large)


def prepare_shared(inp, l0_order):
    f32 = np.float32
    w_in_a = np.asarray(inp["w_in_a"], f32)[0]
    w_out_a = np.asarray(inp["w_out_a"], f32)[0]
    w_in_c = np.asarray(inp["w_in_c"], f32)[0]
    w_out_c = np.asarray(inp["w_out_c"], f32)[0]

    def chunk_in(w, cols):
        return w[:, cols].reshape(KC, P, len(cols)).transpose(1, 0, 2)

    o_za, o_ga, o_q, o_k, o_v, o_gb = 0, 1024, 2048, 3072, 3200, 3328
    ar = np.arange
    chunks = [chunk_in(w_in_a, o_v + ar(128)), chunk_in(w_in_a, o_k + ar(128))]
    head_cols = lambda base, c: np.concatenate([base + c * 64 + ar(64), base + (8 + c) * 64 + ar(64)])
    for c in range(8):
        chunks.append(chunk_in(w_in_a, head_cols(o_q, c)))
    for c in range(8):
        chunks.append(chunk_in(w_in_a, head_cols(o_gb, c)))
    for blk in range(8):
        chunks.append(chunk_in(w_in_a, o_za + blk * 128 + ar(128)))
        chunks.append(chunk_in(w_in_a, o_ga + blk * 128 + ar(128)))
    chunks = [chunks[ci] for ci in l0_order]
    while len(chunks) % 4:
        chunks.append(np.zeros_like(chunks[0]))
    groups = []
    for g in range(len(chunks) // 4):
        groups.append(np.stack(chunks[4 * g:4 * g + 4], axis=1).reshape(P, SLOT))

    def out_groups(w, rows_of_chunk):
        gs = []
        for half in range(2):
            for jg in range(2):
                blk = np.stack([w[rows_of_chunk(jg * 8 + jj)][:, half * 512:(half + 1) * 512]
                                for jj in range(8)], axis=1)
                gs.append(blk.reshape(P, SLOT))
        return gs

    def rows_a(j):
        if j < 8:
            return j * 128 + ar(128)
        c = j - 8
        return np.concatenate([1024 + c * 64 + ar(64), 1024 + (8 + c) * 64 + ar(64)])
    groups += out_groups(w_out_a, rows_a)
    for vg in range(4):
        cols = 2048 + vg * 512 + ar(512)
        groups.append(w_in_c[:, cols].reshape(KC, P, 512).transpose(1, 0, 2).reshape(P, SLOT))
    for j in range(0, 16, 2):
        cs = [chunk_in(w_in_c, j * 128 + ar(128)), chunk_in(w_in_c, 4096 + j * 128 + ar(128)),
              chunk_in(w_in_c, (j + 1) * 128 + ar(128)), chunk_in(w_in_c, 4096 + (j + 1) * 128 + ar(128))]
        groups.append(np.stack(cs, axis=1).reshape(P, SLOT))
    groups += out_groups(w_out_c, lambda j: j * 128 + ar(128))
    wst = np.ascontiguousarray(np.stack(groups, axis=0), dtype=f32)
    assert wst.shape == (NG_TILE, P, SLOT), wst.shape

    cst = np.zeros((P, C_TOTAL), f32)
    pcol = lambda v: np.asarray(v, f32).reshape(8, P).T
    cw = np.asarray(inp["conv_w"], f32)[0]
    cst[:, C_CONVW:C_CONVW + 32] = cw.reshape(4, 8, P).transpose(2, 1, 0).reshape(P, 32)
    cst[:, C_CONVB:C_CONVB + 8] = pcol(inp["conv_b"][0])
    cst[:, C_GAB:C_GAB + 8] = pcol(inp["gate_a_b"][0])
    cst[:, C_GXB:C_GXB + 8] = pcol(inp["gate_x_b"][0])
    cst[:, C_LAM:C_LAM + 8] = pcol(inp["lru_lambda"][0])
    cst[:, C_QG] = np.tile(np.asarray(inp["q_norm_g"], f32)[0], 2)
    cst[:, C_KG] = np.tile(np.asarray(inp["k_norm_g"], f32)[0], 2)
    sinks = np.asarray(inp["sinks"], f32)[0]
    cst[:64, C_SINK:C_SINK + 8] = sinks[None, 0:8]
    cst[64:, C_SINK:C_SINK + 8] = sinks[None, 8:16]
    cst[:, C_LNG:C_LNG + 16] = np.asarray(inp["ln_v_g"], f32)[0].reshape(16, P).T
    cst[:, C_LNB:C_LNB + 16] = np.asarray(inp["ln_v_b"], f32)[0].reshape(16, P).T
    ga = np.asarray(inp["gate_a_w"], f32)[0]
    gx = np.asarray(inp["gate_x_w"], f32)[0]
    gw = np.stack([ga.transpose(1, 0, 2), gx.transpose(1, 0, 2)], axis=1)
    csu = np.zeros((P, U_TOTAL), f32)
    csu[:, U_GATEW:U_GATEW + 2048] = gw.reshape(P, 2048)
    s_i = ar(128)[:, None]
    q_i = ar(128)[None, :]
    amask = np.stack([(s_i > q_i), (s_i <= q_i)], axis=1).astype(f32)
    csu[:, U_AMASK:U_AMASK + 256] = amask.reshape(P, 256)
    spw = np.asarray(inp["spatial_w"], f32)[0]
    csu[:, U_SPW:U_SPW + 1024] = spw.transpose(2, 0, 1).reshape(P, 1024)
    csu[:, U_CMASK:U_CMASK + 128] = (s_i <= q_i).astype(f32)
    spb = np.asarray(inp["spatial_b"], f32)[0]
    csu[:, U_SPB:U_SPB + 1024] = np.broadcast_to(spb.reshape(1, 1024), (P, 1024))
    csu[:, U_IDENT:U_IDENT + 128] = np.eye(128, dtype=f32)
    ob = np.zeros((128, 128), f32)
    ob[:64, :64] = 1.0
    ob[64:, 64:] = 1.0
    cst[:, C_ONESB:C_ONESB + 128] = ob

    gain = np.stack([np.broadcast_to(np.asarray(inp["norm_a"], f32)[0][None, :], (P, D)),
                     np.broadcast_to(np.asarray(inp["norm_c"], f32)[0][None, :], (P, D))], axis=0)
    gain = np.ascontiguousarray(gain, dtype=f32)

    rel = np.asarray(inp["rel_bias"], f32)
    kj = ar(256)[None, :]
    qi = ar(128)[:, None]
    dist = 128 + qi - kj
    bucket = t5_bucket(np.maximum(dist, 0))
    bias = rel[bucket]
    biasT = bias.transpose(1, 2, 0).reshape(2, 128, 16, 128).transpose(1, 0, 2, 3)
    biasT = np.ascontiguousarray(biasT.reshape(P, 2 * 16 * 128), dtype=f32)
    return {"wst": wst, "cst": cst, "csu": csu, "gain": gain, "biasT": biasT}


_CACHE = {}


def get_l0_order():
    if "order" not in _CACHE:
        rec = []
        build_program(1, 4, record=rec)
        order = []
        for ci in rec:
            if ci not in order:
                order.append(ci)
        assert sorted(order) == list(range(34)), order
        _CACHE["order"] = order
    return _CACHE["order"]


def kernel(**inputs):
    x = np.asarray(inputs["x"], np.float32)
    bsz, s, d = x.shape
    per = bsz // N_CORES
    order = get_l0_order()
    shared = prepare_shared(inputs, order)
    n_tiles = per * s // T
    key = (n_tiles, s // T)
    if key not in _CACHE:
        _CACHE[key] = build_program(n_tiles, s // T, l0_order=order)
    nc = _CACHE[key]
    in_maps = []
    for c in range(N_CORES):
        m = dict(shared)
        m["x"] = np.ascontiguousarray(x[c * per:(c + 1) * per].reshape(per * s, d))
        in_maps.append(m)
    res = run_bass_kernel_spmd(nc, in_maps, core_ids=list(range(N_CORES)))
    outs = [np.asarray(r["out"], np.float32).reshape(per, s, d) for r in res.results]
    return np.concatenate(outs, axis=0)
```

```python
import numpy as np
from contextlib import ExitStack
import concourse.bass as bass
import concourse.mybir as mybir
from concourse.bass_utils import run_bass_kernel_spmd

F32 = mybir.dt.float32
BF16 = mybir.dt.bfloat16
AF = mybir.ActivationFunctionType
ALU = mybir.AluOpType

P = 128
T = 512
NB = 4
D = 1024
KC = 8
EPS = 1e-6
N_CORES = 8
SEQ = 2048
TPS = SEQ // T
NSLOT = 5
SLOT = 4096
NG_L0_IN = 9
NG_TILE = 29

C_CONVW = 0
C_CONVB = 32
C_GAB = 40
C_GXB = 48
C_LAM = 56
C_QG = 64
C_KG = 65
C_SINK = 66
C_LNG = 74
C_LNB = 90
C_ONESB = 128
C_TOTAL = 256
U_GATEW = 0
U_SPW = 2048
U_SPB = 3072
U_AMASK = 4096
U_CMASK = 4352
U_IDENT = 4480
U_TOTAL = 4608


class Sched:
    ENG = ("pe", "act", "dve", "pool", "sp")

    def __init__(self):
        self.q = {e: [] for e in self.ENG}
        self.cnt = {e: 0 for e in self.ENG}
        self.dcnt = {}
        self.lastw = {}
        self.readers = {}
        self.waited = {e: {} for e in self.ENG}
        self.nwaits = 0

    def _deps(self, eng, reads, writes, is_dma):
        need = {}

        def add(t, kind):
            if t[0] == "E" and t[1] == eng and not is_dma:
                if eng == "pe":
                    return
                if kind != "raw" and t[2] != self.cnt[eng]:
                    return
            sk = (t[0], t[1])
            if need.get(sk, 0) < t[2]:
                need[sk] = t[2]

        for k in reads:
            t = self.lastw.get(k)
            if t is not None:
                add(t, "raw")
        for k in writes:
            t = self.lastw.get(k)
            if t is not None:
                add(t, "waw")
            for sk, v in self.readers.get(k, {}).items():
                add((sk[0], sk[1], v), "war")
        waits = []
        wd = self.waited[eng]
        for sk, v in need.items():
            if wd.get(sk, 0) >= v:
                continue
            wd[sk] = v
            waits.append((sk, v))
        self.nwaits += len(waits)
        return waits

    def _commit(self, tok, reads, writes):
        for k in writes:
            self.lastw[k] = tok
            self.readers[k] = {}
        sk = (tok[0], tok[1])
        for k in reads:
            r = self.readers.setdefault(k, {})
            if r.get(sk, 0) < tok[2]:
                r[sk] = tok[2]

    def op(self, eng, fn, reads=(), writes=()):
        waits = self._deps(eng, reads, writes, False)
        self.cnt[eng] += 1
        tok = ("E", eng, self.cnt[eng])
        self.q[eng].append((waits, fn, ("E", eng), 1))
        self._commit(tok, reads, writes)
        return tok

    def dma(self, qeng, sem, fn, reads=(), writes=()):
        waits = self._deps(qeng, reads, writes, True)
        self.dcnt[sem] = self.dcnt.get(sem, 0) + 1
        tok = ("D", sem, 16 * self.dcnt[sem])
        self.q[qeng].append((waits, fn, ("D", sem), 16))
        self._commit(tok, reads, writes)
        return tok

    def final_wait(self, eng, toks):
        waits = [((t[0], t[1]), t[2]) for t in toks]
        self.q[eng].append((waits, None, None, 0))

    def sem_keys(self):
        keys = [("E", e) for e in self.ENG]
        keys += [("D", s) for s in self.dcnt]
        return keys

    def emit(self, nc, sems):
        with nc.Block() as block:
            decos = {"pe": block.tensor, "act": block.scalar, "dve": block.vector,
                     "pool": block.gpsimd, "sp": block.sync}
            for e in self.ENG:
                items = self.q[e]

                def body(engh, items=items):
                    for waits, fn, inc, amt in items:
                        for sk, v in waits:
                            engh.wait_ge(sems[sk], v)
                        if fn is None:
                            continue
                        ins = fn(engh)
                        ins.then_inc(sems[inc], amt)

                decos[e](body)


def zipper(gens, depth):
    gens = list(gens)
    active = []
    i = 0
    while active or i < len(gens):
        while len(active) < depth and i < len(gens):
            active.append(gens[i])
            i += 1
        for g in list(active):
            try:
                next(g)
            except StopIteration:
                active.remove(g)


def multi_zipper(streams):
    st = [{"gens": list(g), "depth": d, "i": 0, "active": []} for g, d in streams]
    while any(x["active"] or x["i"] < len(x["gens"]) for x in st):
        for x in st:
            while len(x["active"]) < x["depth"] and x["i"] < len(x["gens"]):
                x["active"].append(x["gens"][x["i"]])
                x["i"] += 1
            for g in list(x["active"]):
                try:
                    next(g)
                except StopIteration:
                    x["active"].remove(g)


def build_program(n_tiles, tps=TPS, n_layers=2, dbg=0, l0_order=None, record=None):
    nc = bass.Bass("TRN2", target_bir_lowering=False)
    ntok = n_tiles * T
    x_d = nc.dram_tensor("x", [ntok, D], F32, kind="ExternalInput").ap()
    wst_d = nc.dram_tensor("wst", [NG_TILE, P, SLOT], F32, kind="ExternalInput").ap()
    cst_d = nc.dram_tensor("cst", [P, C_TOTAL], F32, kind="ExternalInput").ap()
    csu_d = nc.dram_tensor("csu", [P, U_TOTAL], F32, kind="ExternalInput").ap()
    gain_d = nc.dram_tensor("gain", [2, P, D], F32, kind="ExternalInput").ap()
    bias_d = nc.dram_tensor("biasT", [P, 2 * 16 * 128], F32, kind="ExternalInput").ap()
    out_d = nc.dram_tensor("out", [ntok, D], F32, kind="ExternalOutput").ap()

    S = Sched()
    st = ExitStack()
    with st:
        def sb(name, shape, dt=F32):
            return st.enter_context(nc.sbuf_tensor("sb_" + name, shape, dt))

        NXB = 2
        xbuf = [sb("xbuf%d" % i, [P, NB, D]) for i in range(NXB)]
        hT = sb("hT", [P, KC, T], BF16)
        ss = sb("ss", [P, 8])
        rs = sb("rs", [P, 8])
        ring = sb("ring", [P, NSLOT, SLOT], BF16)
        cst = sb("cst", [P, C_TOTAL])
        gain = sb("gain", [P, 2, D])
        Et = sb("Et", [P, 2, 16, 128], BF16)
        vz = sb("vz", [P, NB + 1, 2, 128], BF16)
        kz = sb("kz", [P, 2, (NB + 1) * 128], BF16)
        onesz = sb("onesz", [P, 2, 128], BF16)
        qnT = sb("qnT", [P, 8, T], BF16)
        Gs = sb("Gs", [P, 8, T], BF16)
        sqb = [sb("sqb%d" % i, [P, T]) for i in range(2)]
        rsb = [sb("rsb%d" % i, [P, T]) for i in range(2)]
        expS = [sb("expS%d" % i, [P, T]) for i in range(2)]
        PT = [sb("PT%d" % i, [P, T], BF16) for i in range(4)]
        den = [sb("den0", [P, T]), expS[0]]
        yb = [sb("yb0", [P, T]), expS[1]]
        denk = [("den", 0), ("expS", 0)]
        ybk = [("yb", 0), ("expS", 1)]
        junk = den[0][:].bitcast(BF16)
        yT = sb("yT", [P, 16, T], BF16)
        Us = [sb("U%d" % i, [P, 8, T]) for i in range(2)]
        zbuf = [sb("zbuf%d" % i, [P, T + 4]) for i in range(2)]
        xabf = [sb("xabf%d" % i, [P, T], BF16) for i in range(2)]
        halo = sb("halo", [P, 8, 4])
        hstate = sb("hstate", [P, 8])
        Cc = sb("Cc", [P, 16, 128])
        stats = sb("stats", [P, 4, 6])
        mv = sb("mv", [P, 2])
        lnr = sb("lnr", [P, 2])
        kcol = sb("kcol", [P, 64])
        ident_bf = sb("ident_bf", [P, 128], BF16)
        gw_bf = sb("gw_bf", [P, 2, 8, 128], BF16)
        wtril_bf = sb("wtril_bf", [P, 8, 128], BF16)

        NPS = 8
        ps = [st.enter_context(nc.psum_tensor("ps%d" % i, [P, 512], F32)) for i in range(NPS)]
        U = Us[0]
        hb = Gs[:].rearrange("p (b j) t -> p b (j t)", b=NB)
        hbk = lambda b: [("Gs", 2 * b), ("Gs", 2 * b + 1)]
        wtril_f = U[:, 0:2, :].rearrange("p a t -> p (a t)").rearrange("p (g t) -> p g t", g=8)
        vhat = U[:].rearrange("p a t -> p (a t)").bitcast(BF16).rearrange("p (b f) -> p b f", b=NB)
        xflat = xbuf[0][:].rearrange("p b d -> p (b d)")
        usu = U[:, 7, :]
        x0keys = [("x0", b) for b in range(NB)]

        free_banks = list(range(NPS))

        def nbank():
            assert free_banks, "PSUM banks exhausted"
            return free_banks.pop(0)

        def getbank():
            n = 0
            while not free_banks:
                n += 1
                assert n < 10000, "PSUM bank wait deadlock at build time"
                yield
            return free_banks.pop(0)

        def rel(bk):
            assert bk not in free_banks
            free_banks.append(bk)

        def col(c, n=1):
            return cst[:, c:c + n]

        u1flat = Us[1][:].rearrange("p a t -> p (a t)")
        u1keys = [("U1", i) for i in range(8)]
        x1flat = xbuf[1][:].rearrange("p b d -> p (b d)")
        x1keys = [("x1", b) for b in range(NB)]

        def load_x(t):
            xb = xbuf[t % NXB]
            xk = "x%d" % (t % NXB)
            S.dma("sp", "xl%d" % (t % NXB), lambda e: e.dma_start(
                out=xb[:], in_=x_d[t * T:(t + 1) * T, :].rearrange("(b p) d -> p b d", p=P)),
                writes=[(xk, b) for b in range(NB)])

        load_x(0)
        S.dma("sp", "c0", lambda e: e.dma_start(out=cst[:], in_=cst_d), writes=["cst"])
        S.dma("sp", "c1", lambda e: e.dma_start(out=gain[:], in_=gain_d.rearrange("g p d -> p g d")),
              writes=["gain"])
        S.dma("sp", "c4", lambda e: e.dma_start(out=usu, in_=csu_d[:, 4096:U_TOTAL]), writes=[("U0", 7)])
        S.dma("sp", "c3", lambda e: e.dma_start(out=u1flat, in_=csu_d[:, 0:4096]), writes=u1keys)
        S.dma("sp", "c2", lambda e: e.dma_start(out=x1flat, in_=bias_d), writes=x1keys)

        S.op("dve", lambda e: e.tensor_copy(out=ident_bf[:], in_=usu[:, 384:512]),
             reads=[("U0", 7)], writes=["ident_bf"])
        S.op("dve", lambda e: e.memset(onesz[:], 0.0), writes=["onesz"])
        S.op("dve", lambda e: e.memset(onesz[:, 0, 0:64], 1.0), writes=["onesz"])
        S.op("dve", lambda e: e.memset(onesz[:, 1, 64:128], 1.0), writes=["onesz"])
        S.op("dve", lambda e: e.memset(kz[:], 0.0), writes=["kprev", "kcur"])
        S.op("dve", lambda e: e.memset(vz[:], 0.0), writes=["vprev", "vcur"])
        S.op("dve", lambda e: e.tensor_copy(out=gw_bf[:].rearrange("p a n j -> p (a n j)"),
                                            in_=u1flat[:, U_GATEW:U_GATEW + 2048]),
             reads=u1keys, writes=["gw_bf"])
        for kt in range(2):
            def f(e, kt=kt):
                v = x1flat[:, kt * 2048:(kt + 1) * 2048]
                return e.tensor_scalar(out=v, in0=v, scalar1=8.0, scalar2=1.0, op0=ALU.mult, op1=ALU.mult)
            S.op("pool", f, reads=x1keys, writes=x1keys)

            def f(e, kt=kt):
                m = usu[:, kt * 128:(kt + 1) * 128]
                v = x1flat[:, kt * 2048:(kt + 1) * 2048].rearrange("p (h q) -> p h q", h=16)
                return e.tensor_tensor(out=v, in0=v, in1=m.unsqueeze(1).to_broadcast([P, 16, 128]), op=ALU.mult)
            S.op("pool", f, reads=x1keys + [("U0", 7)], writes=x1keys)
        S.op("pool", lambda e: e.tensor_scalar(out=usu[:, 0:256], in0=usu[:, 0:256], scalar1=-1.0, scalar2=480.0,
                                               op0=ALU.add, op1=ALU.mult),
             reads=[("U0", 7)], writes=[("U0", 7)])
        for kt in range(2):
            def f(e, kt=kt):
                m = usu[:, kt * 128:(kt + 1) * 128]
                v = x1flat[:, kt * 2048:(kt + 1) * 2048].rearrange("p (h q) -> p h q", h=16)
                return e.tensor_tensor(out=Et[:, kt], in0=v, in1=m.unsqueeze(1).to_broadcast([P, 16, 128]),
                                       op=ALU.add)
            S.op("pool", f, reads=x1keys + [("U0", 7)], writes=["Et"])
        S.op("act", lambda e: e.activation(out=kcol[:, 16:24], in_=col(C_SINK, 8), func=AF.Exp),
             reads=["cst"], writes=["kc_sink"])
        S.op("dve", lambda e: e.tensor_scalar(out=kcol[:, 32:40], in0=col(C_GAB, 8), scalar1=0.5, scalar2=None,
                                              op0=ALU.mult), reads=["cst"], writes=["kc_hb"])
        S.op("dve", lambda e: e.tensor_scalar(out=kcol[:, 40:48], in0=col(C_GXB, 8), scalar1=0.5, scalar2=None,
                                              op0=ALU.mult), reads=["cst"], writes=["kc_hb"])
        S.op("act", lambda e: e.activation(out=kcol[:, 24:32], in_=col(C_LAM, 8), func=AF.Exp, scale=-1.0),
             reads=["cst"], writes=["kc_e"])
        S.op("dve", lambda e: e.tensor_scalar(out=kcol[:, 48:56], in0=kcol[:, 24:32], scalar1=2.0,
                                              scalar2=None, op0=ALU.add),
             reads=["kc_e"], writes=["kc_t"])
        S.op("dve", lambda e: e.reciprocal(out=kcol[:, 48:56], in_=kcol[:, 48:56]),
             reads=["kc_t"], writes=["kc_t"])
        S.op("dve", lambda e: e.tensor_tensor(out=kcol[:, 24:32], in0=kcol[:, 24:32], in1=kcol[:, 48:56],
                                              op=ALU.mult),
             reads=["kc_t", "kc_e"], writes=["kc_e"])
        S.op("dve", lambda e: e.tensor_tensor(out=kcol[:, 48:56], in0=kcol[:, 24:32], in1=kcol[:, 24:32],
                                              op=ALU.mult),
             reads=["kc_e"], writes=["kc_t"])
        S.op("dve", lambda e: e.tensor_scalar(out=kcol[:, 0:8], in0=kcol[:, 48:56], scalar1=1.0 / 9,
                                              scalar2=1.0 / 7, op0=ALU.mult, op1=ALU.add),
             reads=["kc_t"], writes=["kc_p"])
        for cc in (1.0 / 5, 1.0 / 3, 1.0):
            S.op("dve", lambda e: e.tensor_tensor(out=kcol[:, 0:8], in0=kcol[:, 0:8], in1=kcol[:, 48:56],
                                                  op=ALU.mult),
                 reads=["kc_p", "kc_t"], writes=["kc_p"])
            S.op("dve", lambda e, cc=cc: e.tensor_scalar(out=kcol[:, 0:8], in0=kcol[:, 0:8], scalar1=cc,
                                                         scalar2=None, op0=ALU.add),
                 reads=["kc_p"], writes=["kc_p"])
        S.op("dve", lambda e: e.tensor_tensor(out=kcol[:, 0:8], in0=kcol[:, 0:8], in1=kcol[:, 24:32],
                                              op=ALU.mult),
             reads=["kc_p", "kc_e"], writes=["kc_p"])
        S.op("dve", lambda e: e.tensor_scalar(out=kcol[:, 8:16], in0=kcol[:, 0:8], scalar1=-16.0,
                                              scalar2=None, op0=ALU.mult),
             reads=["kc_p"], writes=["kc_K"])
        S.op("dve", lambda e: e.tensor_scalar(out=kcol[:, 0:8], in0=kcol[:, 0:8], scalar1=-8.0,
                                              scalar2=None, op0=ALU.mult),
             reads=["kc_p", "kc_K"], writes=["kc_K"])
        Kh = lambda blk: kcol[:, blk:blk + 1]
        Kf = lambda blk: kcol[:, 8 + blk:9 + blk]

        def l1_setup():
            if n_layers > 1:
                def f(e):
                    m = usu[:, 256:384]
                    return e.tensor_tensor(out=wtril_f,
                                           in0=u1flat[:, U_SPW:U_SPW + 1024].rearrange("p (g t) -> p g t", g=8),
                                           in1=m.unsqueeze(1).to_broadcast([P, 8, 128]), op=ALU.mult)
                S.op("dve", f, reads=u1keys + [("U0", 7)], writes=[("U0", 0), ("U0", 1)])
                S.op("dve", lambda e: e.tensor_copy(out=wtril_bf[:], in_=wtril_f),
                     reads=[("U0", 0), ("U0", 1)], writes=["wtril_bf"])
                S.op("dve", lambda e: e.memset(sqb[0][:, 0:128], 1.0), writes=["sqb0"])
                for g in range(8):
                    bk = nbank()

                    def f(e, g=g, bk=bk):
                        return e.matmul(ps[bk][:, 0:128], lhsT=sqb[0][:, 0:128], rhs=wtril_f[:, g, :],
                                        start=True, stop=True)
                    S.op("pe", f, reads=["sqb0", ("U0", 0), ("U0", 1)], writes=[("ps", bk)])
                    for jj in range(2):
                        j = 2 * g + jj

                        def f2(e, g=g, bk=bk, j=j):
                            return e.scalar_tensor_tensor(
                                out=Cc[:, j, :], in0=ps[bk][:, 0:128], scalar=col(C_LNB + j),
                                in1=u1flat[:, U_SPB + g * 128:U_SPB + (g + 1) * 128], op0=ALU.mult, op1=ALU.add)
                        S.op("dve", f2, reads=[("ps", bk), "cst"] + u1keys, writes=["Cc"])
                    rel(bk)


        GPT = NG_TILE if n_layers > 1 else 13
        rem_tile = [4] * 8 + [2] + [4] * 4 + ([4] * 16 if n_layers > 1 else [])
        seq_groups = [g for t in range(n_tiles) for g in range(GPT)]
        rem = [r for t in range(n_tiles) for r in rem_tile]
        rstate = {"next_load": 0, "oldest": 0}

        def _load():
            i = rstate["next_load"]
            rstate["next_load"] += 1
            if i >= len(seq_groups):
                return
            slot = i % NSLOT
            g = seq_groups[i]
            S.dma("pool", "w%d" % slot,
                  lambda e: e.dma_start(out=ring[:, slot, :], in_=wst_d[g], max_dma_last_dim=8192),
                  writes=[("w", slot)])

        for i in range(NSLOT):
            _load()

        def wslot(i):
            if record is not None:
                return i % NSLOT
            assert rstate["oldest"] <= i < rstate["next_load"], (i, rstate)
            return i % NSLOT

        def wdone(i, n=1):
            if record is not None:
                return
            rem[i] -= n
            assert rem[i] >= 0
            while rstate["oldest"] < len(rem) and rem[rstate["oldest"]] == 0:
                rstate["oldest"] += 1
                _load()

        def mm_group(out_ap, pairs, reads, writes):
            def f(e):
                ins = None
                n = len(pairs)
                for i, (l, r) in enumerate(pairs):
                    ins = e.matmul(out_ap, lhsT=l, rhs=r, start=(i == 0), stop=(i == n - 1))
                return ins
            S.op("pe", f, reads=reads, writes=writes)

        hT_keys = [("hT", b) for b in range(NB)]

        def norm_stats_blk(xb, xk, gi, b):
            S.op("act", lambda e: e.activation(out=junk, in_=xb[:, b, :], func=AF.Square,
                                               accum_out=ss[:, b:b + 1]),
                 reads=[(xk, b)], writes=[("ss", b), ("den", 0)])
            S.op("act", lambda e: e.activation(out=rs[:, b:b + 1], in_=ss[:, b:b + 1], func=AF.Sqrt,
                                               scale=1.0 / D, bias=EPS),
                 reads=[("ss", b)], writes=[("rs", b)])
            S.op("dve", lambda e: e.reciprocal(out=rs[:, b:b + 1], in_=rs[:, b:b + 1]),
                 reads=[("rs", b)], writes=[("rs", b)])
            for hf in range(2):
                S.op("dve", lambda e, hf=hf: e.scalar_tensor_tensor(
                    out=hb[:, b, hf * 512:(hf + 1) * 512], in0=xb[:, b, hf * 512:(hf + 1) * 512],
                    scalar=rs[:, b:b + 1], in1=gain[:, gi, hf * 512:(hf + 1) * 512],
                    op0=ALU.mult, op1=ALU.mult),
                    reads=[(xk, b), ("rs", b), "gain"], writes=[("Gs", 2 * b + hf)])

        def norm_stats(xb, xk, gi):
            for b in range(NB):
                norm_stats_blk(xb, xk, gi, b)

        def norm_tr_blk(b):
            bkt = nbank()
            ptb = ps[bkt][:, :].bitcast(BF16)

            def f(e):
                ins = None
                for kc in range(KC):
                    ins = e.transpose(out=ptb[:, kc * 128:(kc + 1) * 128],
                                      in_=hb[:, b, kc * 128:(kc + 1) * 128], identity=ident_bf[:])
                return ins
            S.op("pe", f, reads=[("Gs", 2 * b), ("Gs", 2 * b + 1), "ident_bf"], writes=[("ps", bkt)])
            S.op("act", lambda e: e.copy(out=hT[:, :, b * 128:(b + 1) * 128],
                                         in_=ptb.rearrange("p (k t) -> p k t", k=KC)),
                 reads=[("ps", bkt)], writes=[("hT", b)])
            rel(bkt)

        def norm_tr():
            for b in range(NB):
                norm_tr_blk(b)

        def norm_T(xb, xk, gi):
            norm_stats(xb, xk, gi)
            norm_tr()

        def inproj_chunk(gi, pos, bk=None):
            slot = wslot(gi)
            if bk is None:
                bk = nbank()
            wv = ring[:, slot, :].rearrange("p (c k n) -> p c k n", c=4, k=KC)
            mm_group(ps[bk][:, :], [(wv[:, pos, kc, :], hT[:, kc, :]) for kc in range(KC)],
                     reads=[("w", slot)] + hT_keys, writes=[("ps", bk)])
            wdone(gi)
            return bk

        def out_proj(xb, xk, gbase, pre=None, blk_hook=None):
            if pre is not None:
                pre[0]()

            def mm(b, jg, gi, bank):
                slot = wslot(gi)
                wv = ring[:, slot, :].rearrange("p (j n) -> p j n", j=8)

                def f(e):
                    ins = None
                    for jj in range(8):
                        j = jg * 8 + jj
                        ins = e.matmul(ps[bank][:, :], lhsT=yT[:, j, b * 128:(b + 1) * 128],
                                       rhs=wv[:, jj, :], start=(j == 0), stop=(j == 15))
                    return ins
                S.op("pe", f, reads=[("w", slot)] + [("yT", jg * 8 + jj) for jj in range(8)],
                     writes=[("ps", bank)])
                wdone(gi)

            def add(b, half, bank):
                S.op("dve", lambda e: e.tensor_tensor(
                    out=xb[:, b, half * 512:(half + 1) * 512], in0=ps[bank][:, :],
                    in1=xb[:, b, half * 512:(half + 1) * 512], op=ALU.add),
                    reads=[("ps", bank), (xk, b)], writes=[(xk, b)])
                rel(bank)

            banks = [nbank() for _ in range(NB)]
            for jg in range(2):
                for b in range(NB):
                    mm(b, jg, gbase + jg, banks[b])
            for b in range(NB):
                add(b, 0, banks[b])
            if pre is not None:
                pre[1]()
            if blk_hook is None:
                banks = [nbank() for _ in range(NB)]
                for jg in range(2):
                    for b in range(NB):
                        mm(b, jg, gbase + 2 + jg, banks[b])
                for b in range(NB):
                    add(b, 1, banks[b])
            else:
                for b in range(NB):
                    bank = nbank()
                    for jg in range(2):
                        mm(b, jg, gbase + 2 + jg, bank)
                    add(b, 1, bank)
                    blk_hook(b)

        def layer0(t, xb, xk, gb0, pre=None):
            first = (t % tps == 0)
            if first:
                S.op("pool", lambda e: e.memset(halo[:], 0.0), writes=[("halo", i) for i in range(8)])
                S.op("pool", lambda e: e.memset(hstate[:], 0.0), writes=[("hstate", i) for i in range(8)])
            def cgrp(ci):
                if record is not None:
                    record.append(ci)
                    return (gb0, 0)
                pos = l0_order.index(ci)
                return (gb0 + pos // 4, pos % 4)

            gi, pos = cgrp(0)
            slot = wslot(gi)
            wv = ring[:, slot, :].rearrange("p (c k n) -> p c k n", c=4, k=KC)
            bkv = nbank()
            for b in range(NB):
                mm_group(ps[bkv][:, b * 128:(b + 1) * 128],
                         [(hT[:, kc, b * 128:(b + 1) * 128], wv[:, 0, kc, :]) for kc in range(KC)],
                         reads=[("w", slot), ("hT", b)], writes=[("ps", bkv)])
            wdone(gi)
            for kv in range(2):
                S.op("act", lambda e, kv=kv: e.copy(
                    out=vz[:, 1:NB + 1, kv, kv * 64:(kv + 1) * 64],
                    in_=ps[bkv][:, :].rearrange("p (b n) -> p b n", b=NB)[:, :, kv * 64:(kv + 1) * 64]),
                    reads=[("ps", bkv)], writes=["vcur"])
            rel(bkv)

            def gen_qk(ci, gcol, out_ap, outkey, idx):
                sq, rr = sqb[idx % 2], rsb[idx % 2]
                sqk, rrk = "sqb%d" % (idx % 2), "rsb%d" % (idx % 2)
                bk = yield from getbank()
                inproj_chunk(*cgrp(ci), bk=bk)
                yield
                S.op("act", lambda e: e.activation(out=sq[:], in_=ps[bk][:, :], func=AF.Square),
                     reads=[("ps", bk)], writes=[sqk])
                yield
                b2 = yield from getbank()
                S.op("pe", lambda e: e.matmul(ps[b2][:, :], lhsT=col(C_ONESB, 128), rhs=sq[:],
                                              start=True, stop=True),
                     reads=[sqk, "cst"], writes=[("ps", b2)])
                yield
                S.op("act", lambda e: e.activation(out=rr[:], in_=ps[b2][:, :], func=AF.Ln, scale=1.0 / 64,
                                                   bias=EPS),
                     reads=[("ps", b2)], writes=[rrk])
                rel(b2)
                yield
                S.op("act", lambda e: e.activation(out=rr[:], in_=rr[:], func=AF.Exp, scale=-0.5),
                     reads=[rrk], writes=[rrk])
                yield
                if out_ap is None:
                    for kv in range(2):
                        pr = slice(kv * 64, (kv + 1) * 64)
                        S.op("dve", lambda e, kv=kv, pr=pr: e.scalar_tensor_tensor(
                            out=kz[pr, kv, 128:], in0=ps[bk][pr, :], scalar=cst[pr, gcol:gcol + 1],
                            in1=rr[pr, :], op0=ALU.mult, op1=ALU.mult),
                            reads=[("ps", bk), rrk, "cst"], writes=[outkey])
                else:
                    S.op("dve", lambda e: e.scalar_tensor_tensor(out=out_ap, in0=ps[bk][:, :], scalar=col(gcol),
                                                                 in1=rr[:], op0=ALU.mult, op1=ALU.mult),
                         reads=[("ps", bk), rrk, "cst"], writes=[outkey])
                rel(bk)
                yield

            def gen_gb(c):
                bk = yield from getbank()
                inproj_chunk(*cgrp(10 + c), bk=bk)
                yield
                S.op("act", lambda e: e.activation(out=Gs[:, c, :], in_=ps[bk][:, :], func=AF.Silu),
                     reads=[("ps", bk)], writes=[("Gs", c)])
                rel(bk)
                yield

            gens = [gen_qk(1, C_KG, None, "kcur", 0)]
            gens += [gen_qk(2 + c, C_QG, qnT[:, c, :], ("qn", c), 1 + c) for c in range(8)]
            gens += [gen_gb(c) for c in range(8)]
            b_gens = gens

            def gen_att_kv(b, quad, kv, bo, bd, stt, u):
                pi_ = (u % 2) * 2 + kv
                pT = PT[pi_]
                pk = ("PT", pi_)
                kts = [1] if (first and b == 0) else [0, 1]
                for kt in kts:
                    keyblk = b + kt
                    bs_ = yield from getbank()
                    kkey = "kprev" if keyblk == 0 else "kcur"
                    vkey = "vprev" if keyblk == 0 else "vcur"
                    h0 = kv * 8 + quad * 4

                    def fS(e, keyblk=keyblk, bs_=bs_, kt=kt, h0=h0):
                        o = ps[bs_][:, :].rearrange("p (c q) -> p c q", c=4)
                        e.matmul(o, lhsT=kz[:, kv, keyblk * 128:(keyblk + 1) * 128],
                                 rhs=qnT[:, quad * 4:(quad + 1) * 4, b * 128:(b + 1) * 128],
                                 start=True, stop=False)
                        return e.matmul(o, lhsT=ident_bf[:], rhs=Et[:, kt, h0:h0 + 4, :], start=False, stop=True)
                    S.op("pe", fS, reads=[kkey, "Et", "ident_bf"] + [("qn", quad * 4 + cc) for cc in range(4)],
                         writes=[("ps", bs_)])
                    yield
                    S.op("act", lambda e, bs_=bs_: e.activation(
                        out=pT[:], in_=ps[bs_][:, :], func=AF.Exp, scale=0.125),
                        reads=[("ps", bs_)], writes=[pk])
                    rel(bs_)
                    yield
                    st_ = (stt["n"] == 0)
                    sp_ = (stt["n"] == stt["total"] - 1)
                    stt["n"] += 1
                    S.op("pe", lambda e, keyblk=keyblk, st_=st_, sp_=sp_: e.matmul(
                        ps[bo][:, :], lhsT=vz[:, keyblk, kv, :], rhs=pT[:], start=st_, stop=sp_),
                        reads=[vkey, pk], writes=[("ps", bo)])
                    S.op("pe", lambda e, st_=st_, sp_=sp_: e.matmul(
                        ps[bd][:, :], lhsT=onesz[:, kv, :], rhs=pT[:], start=st_, stop=sp_),
                        reads=["onesz", pk], writes=[("ps", bd)])
                    yield

            def gen_att(b, quad, u):
                bo = yield from getbank()
                bd = yield from getbank()
                dn, y_ = den[u % 2], yb[u % 2]
                dk, yk = denk[u % 2], ybk[u % 2]
                nk = 1 if (first and b == 0) else 2
                stt = {"n": 0, "total": 2 * nk}
                subs = [gen_att_kv(b, quad, kv, bo, bd, stt, u) for kv in range(2)]
                while subs:
                    for g in list(subs):
                        try:
                            next(g)
                        except StopIteration:
                            subs.remove(g)
                    yield
                S.op("dve", lambda e: e.tensor_tensor(
                    out=dn[:].rearrange("p (c q) -> p c q", c=4),
                    in0=ps[bd][:, :].rearrange("p (c q) -> p c q", c=4),
                    in1=kcol[:, 16 + quad * 4:16 + (quad + 1) * 4].unsqueeze(2).to_broadcast([P, 4, 128]),
                    op=ALU.add),
                    reads=[("ps", bd), "kc_sink"], writes=[dk])
                rel(bd)
                yield
                S.op("act", lambda e: e.activation(out=dn[:], in_=dn[:], func=AF.Ln), reads=[dk], writes=[dk])
                yield
                S.op("act", lambda e: e.activation(out=dn[:], in_=dn[:], func=AF.Exp, scale=-1.0),
                     reads=[dk], writes=[dk])
                yield
                S.op("dve", lambda e: e.tensor_tensor(out=y_[:], in0=ps[bo][:, :], in1=dn[:], op=ALU.mult),
                     reads=[("ps", bo), dk], writes=[yk])
                rel(bo)
                yield
                S.op("dve", lambda e: e.tensor_tensor(
                    out=yT[:, 8 + quad * 4:8 + (quad + 1) * 4, b * 128:(b + 1) * 128],
                    in0=y_[:].rearrange("p (c q) -> p c q", c=4),
                    in1=Gs[:, quad * 4:(quad + 1) * 4, b * 128:(b + 1) * 128], op=ALU.mult),
                    reads=[yk] + [("Gs", quad * 4 + cc) for cc in range(4)],
                    writes=[("yT", 8 + quad * 4 + cc) for cc in range(4)])
                yield

            def gen_att_tail():
                S.op("pool", lambda e: e.tensor_copy(out=kz[:, :, 0:128], in_=kz[:, :, NB * 128:(NB + 1) * 128]),
                     reads=["kcur"], writes=["kprev"])
                S.op("pool", lambda e: e.tensor_copy(out=vz[:, 0, :, :], in_=vz[:, NB, :, :]),
                     reads=["vcur"], writes=["vprev"])
                yield

            units = [(b, quad) for quad in range(2) for b in range(NB)]
            att_gens = [gen_att(b, quad, u) for u, (b, quad) in enumerate(units)] + [gen_att_tail()]
            def gen_A():
                active = []
                i = 0
                while active or i < len(b_gens):
                    while len(active) < 2 and i < len(b_gens):
                        active.append(b_gens[i])
                        i += 1
                    for g in list(active):
                        try:
                            next(g)
                        except StopIteration:
                            active.remove(g)
                    yield
                active = []
                i = 0
                while active or i < len(att_gens):
                    while len(active) < 2 and i < len(att_gens):
                        active.append(att_gens[i])
                        i += 1
                    for g in list(active):
                        try:
                            next(g)
                        except StopIteration:
                            active.remove(g)
                    yield

            def gen_rg(blk):
                si = blk % 2
                Ub = Us[si]
                zb, xb_ = zbuf[si], xabf[si]
                t_r, t_a, t_m, t_i, t_b, t_h, t_s, t_x = [Ub[:, i, :] for i in range(8)]
                Uk = [("U%d" % si, i) for i in range(8)]
                zh, zm, xk_ = ("zb_h", si), ("zb_m", si), ("xabf", si)
                bz = yield from getbank()
                inproj_chunk(*cgrp(18 + 2 * blk), bk=bz)
                S.op("pool", lambda e: e.tensor_copy(out=zb[:, 0:4], in_=halo[:, blk, :]),
                     reads=[("halo", blk)], writes=[zh])
                yield
                S.op("act", lambda e: e.copy(out=zb[:, 4:T + 4], in_=ps[bz][:, :]),
                     reads=[("ps", bz)], writes=[zm])
                S.op("act", lambda e: e.activation(
                    out=t_x, in_=ps[bz][:, :], func=AF.Identity, scale=col(C_CONVW + blk * 4 + 3),
                    bias=col(C_CONVB + blk)),
                    reads=[("ps", bz), "cst"], writes=[Uk[7]])
                rel(bz)
                yield
                for k in (1, 2, 3):
                    S.op("dve", lambda e, k=k: e.scalar_tensor_tensor(
                        out=t_x, in0=zb[:, 4 - k:T + 4 - k], scalar=col(C_CONVW + blk * 4 + 3 - k), in1=t_x,
                        op0=ALU.mult, op1=ALU.add),
                        reads=[zh, zm, Uk[7], "cst"], writes=[Uk[7]])
                    yield
                S.op("pool", lambda e: e.tensor_copy(out=halo[:, blk, :], in_=zb[:, T:T + 4]),
                     reads=[zm, zh], writes=[("halo", blk)])
                S.op("dve", lambda e: e.tensor_copy(out=xb_[:], in_=t_x), reads=[Uk[7]], writes=[xk_])
                yield
                br = yield from getbank()
                S.op("pe", lambda e: e.matmul(ps[br][:, :], lhsT=gw_bf[:, 0, blk, :], rhs=xb_[:],
                                              start=True, stop=True),
                     reads=["gw_bf", xk_], writes=[("ps", br)])
                bi = yield from getbank()
                S.op("pe", lambda e: e.matmul(ps[bi][:, :], lhsT=gw_bf[:, 1, blk, :], rhs=xb_[:],
                                              start=True, stop=True),
                     reads=["gw_bf", xk_], writes=[("ps", bi)])
                yield
                S.op("act", lambda e: e.activation(out=t_r, in_=ps[br][:, :], func=AF.Tanh, scale=0.5,
                                                   bias=kcol[:, 32 + blk:33 + blk]),
                     reads=[("ps", br), "kc_hb"], writes=[Uk[0]])
                rel(br)
                yield
                S.op("act", lambda e: e.activation(out=t_i, in_=ps[bi][:, :], func=AF.Tanh, scale=0.5,
                                                   bias=kcol[:, 40 + blk:41 + blk]),
                     reads=[("ps", bi), "kc_hb"], writes=[Uk[3]])
                rel(bi)
                yield
                S.op("act", lambda e: e.activation(out=t_a, in_=t_r, func=AF.Exp, scale=Kh(blk), bias=Kh(blk)),
                     reads=[Uk[0], "kc_K"], writes=[Uk[1]])
                yield
                S.op("dve", lambda e: e.tensor_tensor(out=t_m, in0=t_a, in1=t_a, op=ALU.mult),
                     reads=[Uk[1]], writes=[Uk[2]])
                yield
                S.op("dve", lambda e: e.scalar_tensor_tensor(out=t_b, in0=t_i, scalar=1.0, in1=t_x,
                                                             op0=ALU.add, op1=ALU.mult),
                     reads=[Uk[3], Uk[7]], writes=[Uk[4]])
                yield
                S.op("dve", lambda e: e.tensor_scalar(out=t_m, in0=t_m, scalar1=1.0, scalar2=-1.0,
                                                      op0=ALU.min, op1=ALU.mult),
                     reads=[Uk[2]], writes=[Uk[2]])
                yield
                S.op("act", lambda e: e.activation(out=t_m, in_=t_m, func=AF.Sqrt, bias=1.0),
                     reads=[Uk[2]], writes=[Uk[2]])
                yield
                S.op("dve", lambda e: e.scalar_tensor_tensor(out=t_b, in0=t_b, scalar=0.5, in1=t_m,
                                                             op0=ALU.mult, op1=ALU.mult),
                     reads=[Uk[4], Uk[2]], writes=[Uk[4]])
                yield
                bg = yield from getbank()
                inproj_chunk(*cgrp(19 + 2 * blk), bk=bg)
                yield
                S.op("act", lambda e: e.activation(out=t_s, in_=ps[bg][:, :], func=AF.Tanh, scale=0.5),
                     reads=[("ps", bg)], writes=[Uk[6]])
                yield
                S.op("dve", lambda e: e.tensor_tensor_scan(
                    out=t_h, data0=t_a, data1=t_b, initial=hstate[:, blk:blk + 1], op0=ALU.mult, op1=ALU.add),
                    reads=[Uk[1], Uk[4], ("hstate", blk)], writes=[Uk[5]])
                yield
                S.op("pool", lambda e: e.tensor_copy(out=hstate[:, blk:blk + 1], in_=Ub[:, 5, T - 1:T]),
                     reads=[Uk[5]], writes=[("hstate", blk)])
                S.op("dve", lambda e: e.scalar_tensor_tensor(out=t_s, in0=t_s, scalar=1.0, in1=ps[bg][:, :],
                                                             op0=ALU.add, op1=ALU.mult),
                     reads=[Uk[6], ("ps", bg)], writes=[Uk[6]])
                rel(bg)
                yield
                S.op("dve", lambda e: e.scalar_tensor_tensor(out=yT[:, blk, :], in0=t_h, scalar=0.5, in1=t_s,
                                                             op0=ALU.mult, op1=ALU.mult),
                     reads=[Uk[5], Uk[6]], writes=[("yT", blk)])
                yield

            multi_zipper([([gen_rg(blk) for blk in range(8)], 2), ([gen_A()], 1)])
            if dbg == 5:
                return
            if n_layers > 1 and dbg == 0:
                def hook(b):
                    norm_stats_blk(xb, xk, 1, b)
                    if b >= 1:
                        norm_tr_blk(b - 1)
                out_proj(xb, xk, gb0 + 9, pre, blk_hook=hook)
            else:
                out_proj(xb, xk, gb0 + 9, pre)

        def layer1(t, xb, xk, gb1, pre=None):
            vslots = [wslot(gb1 + vg) for vg in range(4)]
            for b in range(NB):
                banks = []
                for vg in range(4):
                    bk = nbank()
                    banks.append(bk)
                    wv = ring[:, vslots[vg], :].rearrange("p (k n) -> p k n", k=KC)
                    mm_group(ps[bk][:, :],
                             [(hT[:, kc, b * 128:(b + 1) * 128], wv[:, kc, :]) for kc in range(KC)],
                             reads=[("w", vslots[vg]), ("hT", b)], writes=[("ps", bk)])
                    wdone(gb1 + vg)
                    S.op("dve", lambda e, bk=bk, vg=vg: e.bn_stats(out=stats[:, vg, :], in_=ps[bk][:, :]),
                         reads=[("ps", bk)], writes=[("stats", vg)])
                S.op("dve", lambda e: e.bn_aggr(out=mv[:], in_=stats[:].rearrange("p a s -> p (a s)")),
                     reads=[("stats", vg) for vg in range(4)], writes=["mv"])
                S.op("act", lambda e: e.activation(out=lnr[:, 0:1], in_=mv[:, 1:2], func=AF.Sqrt, bias=EPS),
                     reads=["mv"], writes=["lnr"])
                S.op("dve", lambda e: e.reciprocal(out=lnr[:, 1:2], in_=lnr[:, 0:1]), reads=["lnr"],
                     writes=["lnr2"])
                for vg in range(4):
                    S.op("dve", lambda e, vg=vg, b=b, bk=banks[vg]: e.tensor_scalar(
                        out=vhat[:, b, vg * 512:(vg + 1) * 512], in0=ps[bk][:, :], scalar1=mv[:, 0:1],
                        scalar2=lnr[:, 1:2], op0=ALU.subtract, op1=ALU.mult),
                        reads=[("ps", banks[vg]), "mv", "lnr2"], writes=[("U0", 2 * b), ("U0", 2 * b + 1)])
                    rel(banks[vg])
                if b == 0:
                    norm_tr_blk(NB - 1)

            def gen_l1(j):
                g = j // 2
                si = j % 2
                S1, G1, Y1 = sqb[si], rsb[si], expS[si]
                s1k, g1k, y1k = "sqb%d" % si, "rsb%d" % si, ("expS", si)
                gi = gb1 + 4 + j // 2
                bu = yield from getbank()
                inproj_chunk(gi, (j % 2) * 2, bk=bu)
                yield
                bg = yield from getbank()
                inproj_chunk(gi, (j % 2) * 2 + 1, bk=bg)
                yield
                S.op("act", lambda e: e.activation(out=G1[:], in_=ps[bg][:, :], func=AF.Silu),
                     reads=[("ps", bg)], writes=[g1k])
                rel(bg)
                bsg = yield from getbank()
                for b in range(NB):
                    S.op("pe", lambda e, b=b: e.matmul(
                        ps[bsg][:, b * 128:(b + 1) * 128], lhsT=vhat[:, b, j * 128:(j + 1) * 128],
                        rhs=wtril_bf[:, g, :], start=True, stop=True),
                        reads=[("U0", 2 * b), ("U0", 2 * b + 1), "wtril_bf"], writes=[("ps", bsg)])
                yield
                S.op("dve", lambda e: e.scalar_tensor_tensor(
                    out=S1[:].rearrange("p (b t) -> p b t", b=NB),
                    in0=ps[bsg][:, :].rearrange("p (b t) -> p b t", b=NB), scalar=col(C_LNG + j),
                    in1=Cc[:, j:j + 1, :].to_broadcast([P, NB, 128]), op0=ALU.mult, op1=ALU.add),
                    reads=[("ps", bsg), "Cc", "cst"], writes=[s1k])
                rel(bsg)
                yield
                S.op("dve", lambda e: e.tensor_tensor(out=Y1[:], in0=ps[bu][:, :], in1=S1[:], op=ALU.mult),
                     reads=[("ps", bu), s1k], writes=[y1k])
                rel(bu)
                yield
                S.op("dve", lambda e: e.tensor_tensor(out=yT[:, j, :], in0=Y1[:], in1=G1[:], op=ALU.mult),
                     reads=[y1k, g1k], writes=[("yT", j)])
                yield

            zipper([gen_l1(j) for j in range(16)], 2)
            out_proj(xb, xk, gb1 + 12, pre)

        if dbg != 1:
            norm_T(xbuf[0], "x0", 0)
        l1_setup()
        for t in range(n_tiles):
            xb = xbuf[t % NXB]
            xk = "x%d" % (t % NXB)
            nxt = None
            if t + 1 < n_tiles:
                load_x(t + 1)
                if dbg != 1:
                    nxt = (lambda t=t: norm_stats(xbuf[(t + 1) % NXB], "x%d" % ((t + 1) % NXB), 0), norm_tr)
            if dbg != 1:
                if n_layers > 1 and dbg == 0:
                    layer0(t, xb, xk, t * GPT)
                    layer1(t, xb, xk, t * GPT + 13, nxt)
                else:
                    layer0(t, xb, xk, t * GPT, nxt)
            S.dma("sp", "xs%d" % (t % NXB), lambda e, t=t, xb=xb: e.dma_start(
                out=out_d[t * T:(t + 1) * T, :].rearrange("(b p) d -> p b d", p=P), in_=xb[:]),
                reads=[(xk, b) for b in range(NB)])
        S.final_wait("sp", [("D", sname, 16 * c) for sname, c in S.dcnt.items()])

        sems = {}
        for k in S.sem_keys():
            sems[k] = st.enter_context(nc.semaphore("s_%s_%s" % (k[0], k[1])))
        S.emit(nc, sems)
    return nc


def t5_bucket(dist):
    max_exact = 16
    df = np.maximum(dist, 1).astype(np.float32)
    large = max_exact + (np.log(df / np.float32(max_exact)) / np.float32(np.log(128 / 16))
                         * np.float32(16)).astype(np.int32)
    large = np.minimum(large, 31)
    return np.where(dist < max_exact, dist, large)


def prepare_shared(inp, l0_order):
    f32 = np.float32
    w_in_a = np.asarray(inp["w_in_a"], f32)[0]
    w_out_a = np.asarray(inp["w_out_a"], f32)[0]
    w_in_c = np.asarray(inp["w_in_c"], f32)[0]
    w_out_c = np.asarray(inp["w_out_c"], f32)[0]

    def chunk_in(w, cols):
        return w[:, cols].reshape(KC, P, len(cols)).transpose(1, 0, 2)

    o_za, o_ga, o_q, o_k, o_v, o_gb = 0, 1024, 2048, 3072, 3200, 3328
    ar = np.arange
    chunks = [chunk_in(w_in_a, o_v + ar(128)), chunk_in(w_in_a, o_k + ar(128))]
    head_cols = lambda base, c: np.concatenate([base + c * 64 + ar(64), base + (8 + c) * 64 + ar(64)])
    for c in range(8):
        chunks.append(chunk_in(w_in_a, head_cols(o_q, c)))
    for c in range(8):
        chunks.append(chunk_in(w_in_a, head_cols(o_gb, c)))
    for blk in range(8):
        chunks.append(chunk_in(w_in_a, o_za + blk * 128 + ar(128)))
        chunks.append(chunk_in(w_in_a, o_ga + blk * 128 + ar(128)))
    chunks = [chunks[ci] for ci in l0_order]
    while len(chunks) % 4:
        chunks.append(np.zeros_like(chunks[0]))
    groups = []
    for g in range(len(chunks) // 4):
        groups.append(np.stack(chunks[4 * g:4 * g + 4], axis=1).reshape(P, SLOT))

    def out_groups(w, rows_of_chunk):
        gs = []
        for half in range(2):
            for jg in range(2):
                blk = np.stack([w[rows_of_chunk(jg * 8 + jj)][:, half * 512:(half + 1) * 512]
                                for jj in range(8)], axis=1)
                gs.append(blk.reshape(P, SLOT))
        return gs

    def rows_a(j):
        if j < 8:
            return j * 128 + ar(128)
        c = j - 8
        return np.concatenate([1024 + c * 64 + ar(64), 1024 + (8 + c) * 64 + ar(64)])
    groups += out_groups(w_out_a, rows_a)
    for vg in range(4):
        cols = 2048 + vg * 512 + ar(512)
        groups.append(w_in_c[:, cols].reshape(KC, P, 512).transpose(1, 0, 2).reshape(P, SLOT))
    for j in range(0, 16, 2):
        cs = [chunk_in(w_in_c, j * 128 + ar(128)), chunk_in(w_in_c, 4096 + j * 128 + ar(128)),
              chunk_in(w_in_c, (j + 1) * 128 + ar(128)), chunk_in(w_in_c, 4096 + (j + 1) * 128 + ar(128))]
        groups.append(np.stack(cs, axis=1).reshape(P, SLOT))
    groups += out_groups(w_out_c, lambda j: j * 128 + ar(128))
    wst = np.ascontiguousarray(np.stack(groups, axis=0), dtype=f32)
    assert wst.shape == (NG_TILE, P, SLOT), wst.shape

    cst = np.zeros((P, C_TOTAL), f32)
    pcol = lambda v: np.asarray(v, f32).reshape(8, P).T
    cw = np.asarray(inp["conv_w"], f32)[0]
    cst[:, C_CONVW:C_CONVW + 32] = cw.reshape(4, 8, P).transpose(2, 1, 0).reshape(P, 32)
    cst[:, C_CONVB:C_CONVB + 8] = pcol(inp["conv_b"][0])
    cst[:, C_GAB:C_GAB + 8] = pcol(inp["gate_a_b"][0])
    cst[:, C_GXB:C_GXB + 8] = pcol(inp["gate_x_b"][0])
    cst[:, C_LAM:C_LAM + 8] = pcol(inp["lru_lambda"][0])
    cst[:, C_QG] = np.tile(np.asarray(inp["q_norm_g"], f32)[0], 2)
    cst[:, C_KG] = np.tile(np.asarray(inp["k_norm_g"], f32)[0], 2)
    sinks = np.asarray(inp["sinks"], f32)[0]
    cst[:64, C_SINK:C_SINK + 8] = sinks[None, 0:8]
    cst[64:, C_SINK:C_SINK + 8] = sinks[None, 8:16]
    cst[:, C_LNG:C_LNG + 16] = np.asarray(inp["ln_v_g"], f32)[0].reshape(16, P).T
    cst[:, C_LNB:C_LNB + 16] = np.asarray(inp["ln_v_b"], f32)[0].reshape(16, P).T
    ga = np.asarray(inp["gate_a_w"], f32)[0]
    gx = np.asarray(inp["gate_x_w"], f32)[0]
    gw = np.stack([ga.transpose(1, 0, 2), gx.transpose(1, 0, 2)], axis=1)
    csu = np.zeros((P, U_TOTAL), f32)
    csu[:, U_GATEW:U_GATEW + 2048] = gw.reshape(P, 2048)
    s_i = ar(128)[:, None]
    q_i = ar(128)[None, :]
    amask = np.stack([(s_i > q_i), (s_i <= q_i)], axis=1).astype(f32)
    csu[:, U_AMASK:U_AMASK + 256] = amask.reshape(P, 256)
    spw = np.asarray(inp["spatial_w"], f32)[0]
    csu[:, U_SPW:U_SPW + 1024] = spw.transpose(2, 0, 1).reshape(P, 1024)
    csu[:, U_CMASK:U_CMASK + 128] = (s_i <= q_i).astype(f32)
    spb = np.asarray(inp["spatial_b"], f32)[0]
    csu[:, U_SPB:U_SPB + 1024] = np.broadcast_to(spb.reshape(1, 1024), (P, 1024))
    csu[:, U_IDENT:U_IDENT + 128] = np.eye(128, dtype=f32)
    ob = np.zeros((128, 128), f32)
    ob[:64, :64] = 1.0
    ob[64:, 64:] = 1.0
    cst[:, C_ONESB:C_ONESB + 128] = ob

    gain = np.stack([np.broadcast_to(np.asarray(inp["norm_a"], f32)[0][None, :], (P, D)),
                     np.broadcast_to(np.asarray(inp["norm_c"], f32)[0][None, :], (P, D))], axis=0)
    gain = np.ascontiguousarray(gain, dtype=f32)

    rel = np.asarray(inp["rel_bias"], f32)
    kj = ar(256)[None, :]
    qi = ar(128)[:, None]
    dist = 128 + qi - kj
    bucket = t5_bucket(np.maximum(dist, 0))
    bias = rel[bucket]
    biasT = bias.transpose(1, 2, 0).reshape(2, 128, 16, 128).transpose(1, 0, 2, 3)
    biasT = np.ascontiguousarray(biasT.reshape(P, 2 * 16 * 128), dtype=f32)
    return {"wst": wst, "cst": cst, "csu": csu, "gain": gain, "biasT": biasT}


_CACHE = {}


def get_l0_order():
    if "order" not in _CACHE:
        rec = []
        build_program(1, 4, record=rec)
        order = []
        for ci in rec:
            if ci not in order:
                order.append(ci)
        assert sorted(order) == list(range(34)), order
        _CACHE["order"] = order
    return _CACHE["order"]


def kernel(**inputs):
    x = np.asarray(inputs["x"], np.float32)
    bsz, s, d = x.shape
    per = bsz // N_CORES
    order = get_l0_order()
    shared = prepare_shared(inputs, order)
    n_tiles = per * s // T
    key = (n_tiles, s // T)
    if key not in _CACHE:
        _CACHE[key] = build_program(n_tiles, s // T, l0_order=order)
    nc = _CACHE[key]
    in_maps = []
    for c in range(N_CORES):
        m = dict(shared)
        m["x"] = np.ascontiguousarray(x[c * per:(c + 1) * per].reshape(per * s, d))
        in_maps.append(m)
    res = run_bass_kernel_spmd(nc, in_maps, core_ids=list(range(N_CORES)))
    outs = [np.asarray(r["out"], np.float32).reshape(per, s, d) for r in res.results]
    return np.concatenate(outs, axis=0)
```
